# Optimizing a Trainium2 kernel written in Bass

```python
import math
import numpy as np
import jax
import jax.numpy as jnp
from jax import lax

D_MODEL = 1024
BATCH = 2
SEQ = 8192
DEPTH = 2

DA_HEADS = 4
DA_HEAD_DIM = 64
DA_WIDTH = DA_HEADS * 2 * DA_HEAD_DIM
Q_BLOCK = 128
ROPE_THETA = 10000.0
DN_HEADS = 4
DN_HEAD_DIM = 128
DN_WIDTH = DN_HEADS * DN_HEAD_DIM
CONV_K = 5
CHUNK = 64
D_FF = 4 * D_MODEL
EPS = 1e-6
IN_SPLITS = (DA_WIDTH, DA_WIDTH, DA_WIDTH, DN_WIDTH, DN_WIDTH, DN_WIDTH, DN_WIDTH,
             2 * DN_HEADS, 2 * DN_HEADS, 2 * D_MODEL)
IN_COLS = sum(IN_SPLITS)

kernel_name = 'hybrid_diffattn_gdn_encoder'


def rms_norm(x, gain):
    x32 = x.astype(jnp.float32)
    y = x32 * lax.rsqrt(jnp.mean(x32 * x32, axis=-1, keepdims=True) + EPS)
    return (y * gain.astype(jnp.float32)).astype(x.dtype)


def l2_norm(x):
    x32 = x.astype(jnp.float32)
    return (x32 * lax.rsqrt(jnp.sum(x32 * x32, axis=-1, keepdims=True) + EPS)).astype(x.dtype)


def rope_tables(seq, dim, dtype):
    pos = jnp.arange(seq, dtype=jnp.float32)
    inv = 1.0 / (ROPE_THETA ** (jnp.arange(0, dim, 2, dtype=jnp.float32) / dim))
    ang = pos[:, None] * inv[None, :]
    ang = jnp.concatenate([ang, ang], axis=-1)
    return jnp.cos(ang).astype(dtype), jnp.sin(ang).astype(dtype)


def apply_rope(x, cos, sin):
    half = x.shape[-1] // 2
    rot = jnp.concatenate([-x[..., half:], x[..., :half]], axis=-1)
    return x * cos + rot * sin


def diff_attention(q, k, v, lam, lambda_init, cos, sin, qn_gain, kn_gain, subln_gain):
    B, S, H, _, d = q.shape
    q = apply_rope(rms_norm(q, qn_gain).transpose(0, 2, 3, 1, 4), cos, sin)
    k = apply_rope(rms_norm(k, kn_gain).transpose(0, 2, 3, 1, 4), cos, sin)
    v = v.transpose(0, 2, 1, 3)
    nb = S // Q_BLOCK
    qb = q.reshape(B, H, 2, nb, Q_BLOCK, d).transpose(3, 0, 1, 2, 4, 5)
    scale = d ** -0.5

    def block(qi):
        s = jnp.einsum('bhtqd,bhtkd->bhtqk', qi, k).astype(jnp.float32) * scale
        p = jax.nn.softmax(s, axis=-1)
        a = p[:, :, 0] - lam * p[:, :, 1]
        return jnp.einsum('bhqk,bhkd->bhqd', a.astype(v.dtype), v)

    o = lax.map(block, qb)
    o = o.transpose(1, 0, 3, 2, 4).reshape(B, S, H, 2 * d)
    o = rms_norm(o, subln_gain) * (1.0 - lambda_init)
    return o.reshape(B, S, H * 2 * d)


def gated_delta_rule_chunked(q, k, v, g, beta):
    out_dtype = v.dtype
    q, k, v, g, beta = [t.astype(jnp.float32) for t in (q, k, v, g, beta)]
    B, H, S, Dk = q.shape
    Dv = v.shape[-1]
    n = S // CHUNK
    q = q.reshape(B, H, n, CHUNK, Dk)
    k = k.reshape(B, H, n, CHUNK, Dk)
    v = v.reshape(B, H, n, CHUNK, Dv)
    beta = beta.reshape(B, H, n, CHUNK)
    g = jnp.cumsum(g.reshape(B, H, n, CHUNK), axis=-1)
    idx = jnp.arange(CHUNK)
    incl = idx[:, None] >= idx[None, :]
    strict = idx[:, None] > idx[None, :]
    gdiff = g[..., :, None] - g[..., None, :]
    decay = jnp.where(incl, jnp.exp(jnp.where(incl, gdiff, 0.0)), 0.0)
    kb = k * beta[..., None]
    L = jnp.where(strict, jnp.einsum('bhncd,bhnmd->bhncm', kb, k) * decay, 0.0)
    eye = jnp.eye(CHUNK, dtype=jnp.float32)
    T = lax.linalg.triangular_solve(eye + L, jnp.broadcast_to(eye, L.shape), left_side=True,
                                    lower=True, unit_diagonal=True)
    u = jnp.einsum('bhncm,bhnme->bhnce', T, v * beta[..., None])
    w = jnp.einsum('bhncm,bhnmd->bhncd', T, kb * jnp.exp(g)[..., None])
    qk = jnp.einsum('bhncd,bhnmd->bhncm', q, k) * decay
    q_dec = q * jnp.exp(g)[..., None]
    k_dec = k * jnp.exp(g[..., -1:] - g)[..., None]
    g_last = jnp.exp(g[..., -1])

    def step(state, inp):
        qd_i, kd_i, u_i, w_i, qk_i, gl_i = inp
        v_new = u_i - jnp.einsum('bhcd,bhde->bhce', w_i, state)
        o_i = jnp.einsum('bhcd,bhde->bhce', qd_i, state) + jnp.einsum('bhcm,bhme->bhce', qk_i, v_new)
        state = state * gl_i[..., None, None] + jnp.einsum('bhcd,bhce->bhde', kd_i, v_new)
        return state, o_i

    xs = (jnp.moveaxis(q_dec, 2, 0), jnp.moveaxis(k_dec, 2, 0), jnp.moveaxis(u, 2, 0),
          jnp.moveaxis(w, 2, 0), jnp.moveaxis(qk, 2, 0), jnp.moveaxis(g_last, 2, 0))
    state0 = jnp.zeros((B, H, Dk, Dv), jnp.float32)
    _, o = lax.scan(step, state0, xs)
    return jnp.moveaxis(o, 0, 2).reshape(B, H, S, Dv).astype(out_dtype)


def gated_deltanet(q, k, v, z, b, a, conv_w, A_log, dt_bias, out_gain):
    B, S, _ = q.shape
    qkv = jnp.concatenate([q, k, v], axis=-1)
    qkv = lax.conv_general_dilated(qkv, conv_w[:, None, :].astype(qkv.dtype), window_strides=(1,),
                                   padding=[(CONV_K // 2, CONV_K // 2)],
                                   dimension_numbers=('NWC', 'WIO', 'NWC'),
                                   feature_group_count=qkv.shape[-1])
    q, k, v = jnp.split(jax.nn.silu(qkv), 3, axis=-1)

    def heads(t):
        return t.reshape(B, S, DN_HEADS, DN_HEAD_DIM).transpose(0, 2, 1, 3)

    q = l2_norm(heads(q)) * (DN_HEAD_DIM ** -0.5)
    k = l2_norm(heads(k))
    v = heads(v)
    beta = jax.nn.sigmoid(b.astype(jnp.float32)).reshape(B, S, 2, DN_HEADS).transpose(2, 0, 3, 1)
    a_in = a.astype(jnp.float32).reshape(B, S, 2, DN_HEADS).transpose(2, 0, 3, 1)
    g = -jnp.exp(A_log.astype(jnp.float32))[:, None, :, None] * jax.nn.softplus(
        a_in + dt_bias.astype(jnp.float32)[:, None, :, None])
    o_fwd = gated_delta_rule_chunked(q, k, v, g[0], beta[0])
    o_bwd = jnp.flip(gated_delta_rule_chunked(jnp.flip(q, 2), jnp.flip(k, 2), jnp.flip(v, 2),
                                              jnp.flip(g[1], 2), jnp.flip(beta[1], 2)), 2)
    o = (o_fwd + o_bwd).transpose(0, 2, 1, 3)
    o = rms_norm(o, out_gain) * jax.nn.silu(z.reshape(B, S, DN_HEADS, DN_HEAD_DIM))
    return o.reshape(B, S, DN_WIDTH)


def setup_inputs(seed: int = 0) -> dict:
    key = jax.random.key(seed)
    ks = jax.random.split(key, 20)
    f32 = jnp.float32

    def nrm(k, shape, scale):
        return jax.random.normal(k, shape, f32) * scale

    x = nrm(ks[0], (BATCH, SEQ, D_MODEL), 1.0)
    c = nrm(ks[1], (BATCH, D_MODEL), 1.0)
    ada_w = nrm(ks[2], (DEPTH, D_MODEL, 6 * D_MODEL), D_MODEL ** -0.5)
    ada_b = nrm(ks[3], (DEPTH, 6 * D_MODEL), 0.02)
    norm1_g = 1.0 + nrm(ks[4], (DEPTH, D_MODEL), 0.05)
    norm2_g = 1.0 + nrm(ks[5], (DEPTH, D_MODEL), 0.05)
    w_in = nrm(ks[6], (DEPTH, D_MODEL, IN_COLS), D_MODEL ** -0.5)
    dn_conv = nrm(ks[7], (DEPTH, CONV_K, 3 * DN_WIDTH), CONV_K ** -0.5)
    dn_A_log = jnp.log(jax.random.uniform(ks[8], (DEPTH, 2, DN_HEADS), f32, 1.0, 16.0))
    dt = jnp.exp(jax.random.uniform(ks[9], (DEPTH, 2, DN_HEADS), f32, math.log(1e-3), math.log(1e-1)))
    dn_dt_bias = dt + jnp.log(-jnp.expm1(-dt))
    dn_out_gain = 1.0 + nrm(ks[10], (DEPTH, DN_HEAD_DIM), 0.05)
    qk_norm_gain = 1.0 + nrm(ks[11], (DEPTH, 2, DA_HEAD_DIM), 0.05)
    diff_lambda = nrm(ks[12], (DEPTH, 4, DA_HEAD_DIM), 0.1)
    diff_subln_gain = 1.0 + nrm(ks[13], (DEPTH, 2 * DA_HEAD_DIM), 0.05)
    w_branch_a = nrm(ks[14], (DEPTH, DA_WIDTH, D_MODEL), DA_WIDTH ** -0.5)
    w_branch_b = nrm(ks[15], (DEPTH, DN_WIDTH, D_MODEL), DN_WIDTH ** -0.5)
    w_out = nrm(ks[16], (DEPTH, D_MODEL, D_MODEL), D_MODEL ** -0.5)
    w_mlp1 = nrm(ks[17], (DEPTH, D_MODEL, D_FF), D_MODEL ** -0.5)
    w_mlp2 = nrm(ks[18], (DEPTH, D_FF, D_MODEL), D_FF ** -0.5)
    return {'x': x, 'c': c, 'ada_w': ada_w, 'ada_b': ada_b, 'norm1_g': norm1_g, 'norm2_g': norm2_g,
            'w_in': w_in, 'dn_conv': dn_conv, 'dn_A_log': dn_A_log, 'dn_dt_bias': dn_dt_bias,
            'dn_out_gain': dn_out_gain, 'qk_norm_gain': qk_norm_gain, 'diff_lambda': diff_lambda,
            'diff_subln_gain': diff_subln_gain, 'w_branch_a': w_branch_a, 'w_branch_b': w_branch_b,
            'w_out': w_out, 'w_mlp1': w_mlp1, 'w_mlp2': w_mlp2}


def reference(x, c, ada_w, ada_b, norm1_g, norm2_g, w_in, dn_conv, dn_A_log, dn_dt_bias, dn_out_gain,
              qk_norm_gain, diff_lambda, diff_subln_gain, w_branch_a, w_branch_b, w_out, w_mlp1, w_mlp2):
    B, S, D = x.shape
    cos, sin = rope_tables(S, DA_HEAD_DIM, x.dtype)
    split_points = [int(p) for p in np.cumsum(IN_SPLITS)[:-1]]
    c_act = jax.nn.silu(c)
    for l in range(DEPTH):
        mod = c_act @ ada_w[l] + ada_b[l]
        sh1, sc1, gt1, sh2, sc2, gt2 = [m[:, None, :] for m in jnp.split(mod, 6, axis=-1)]
        h = rms_norm(x, norm1_g[l]) * (1.0 + sc1) + sh1
        proj = h @ w_in[l]
        da_q, da_k, da_v, dn_q, dn_k, dn_v, dn_z, dn_b, dn_a, gates = jnp.split(proj, split_points, axis=-1)
        lambda_init = 0.8 - 0.6 * math.exp(-0.3 * l)
        lq = diff_lambda[l].astype(jnp.float32)
        lam = jnp.exp(jnp.sum(lq[0] * lq[1])) - jnp.exp(jnp.sum(lq[2] * lq[3])) + lambda_init
        y_a = diff_attention(da_q.reshape(B, S, DA_HEADS, 2, DA_HEAD_DIM),
                             da_k.reshape(B, S, DA_HEADS, 2, DA_HEAD_DIM),
                             da_v.reshape(B, S, DA_HEADS, 2 * DA_HEAD_DIM),
                             lam, lambda_init, cos, sin, qk_norm_gain[l, 0], qk_norm_gain[l, 1],
                             diff_subln_gain[l])
        y_b = gated_deltanet(dn_q, dn_k, dn_v, dn_z, dn_b, dn_a, dn_conv[l], dn_A_log[l], dn_dt_bias[l],
                             dn_out_gain[l])
        g_a, g_b = jnp.split(jax.nn.sigmoid(gates), 2, axis=-1)
        merged = g_a * (y_a @ w_branch_a[l]) + g_b * (y_b @ w_branch_b[l])
        x = x + gt1 * (merged @ w_out[l])
        h = rms_norm(x, norm2_g[l]) * (1.0 + sc2) + sh2
        x = x + gt2 * (jnp.square(jax.nn.relu(h @ w_mlp1[l])) @ w_mlp2[l])
    return x
```

```python
from contextlib import ExitStack
import math
import numpy as np
import ml_dtypes
import concourse.bass as bass
import concourse.mybir as mybir
from concourse.bass_utils import run_bass_kernel_spmd

F32 = mybir.dt.float32
BF16 = mybir.dt.bfloat16
ALU = mybir.AluOpType
AF = mybir.ActivationFunctionType
AX = mybir.AxisListType
NPBF = ml_dtypes.bfloat16

ENGS = ("pe", "act", "dve", "pool", "sp")

D = 1024
S = 8192
NB = 2
NTOK = 2048
NT = NTOK // 128
EPS = 1e-6
INC = 5648


class Op:
    __slots__ = ("eng", "fn", "deps", "is_dma", "semkey", "cum", "sig", "waits", "has_dep")

    def __init__(self, eng, fn, is_dma=False, semkey=None):
        self.eng = eng
        self.fn = fn
        self.deps = []
        self.is_dma = is_dma
        self.semkey = semkey
        self.cum = 0
        self.sig = 0
        self.has_dep = False
        self.waits = []


class Prog:
    def __init__(self, nc):
        self.nc = nc
        self.es = ExitStack()
        self.ops = {e: [] for e in ENGS}
        self.lastw = {}
        self.readers = {}
        self.dma_cum = {}

    def sb(self, name, shape, dtype):
        return self.es.enter_context(self.nc.sbuf_tensor(name, list(shape), dtype))

    def ps(self, name, shape, dtype=F32):
        return self.es.enter_context(self.nc.psum_tensor(name, list(shape), dtype))

    def _rec(self, op, r, w):
        deps = []
        for k in r:
            lw = self.lastw.get(k)
            if lw is not None:
                deps.append(lw)
        for k in w:
            lw = self.lastw.get(k)
            rd = self.readers.get(k, [])
            if rd:
                deps.extend(rd)
            elif lw is not None:
                if not (op.is_dma and lw.is_dma and lw.semkey == op.semkey):
                    deps.append(lw)
        for k in r:
            self.readers.setdefault(k, []).append(op)
        for k in w:
            self.lastw[k] = op
            self.readers[k] = []
        seen = set()
        for d in deps:
            if d is op or id(d) in seen:
                continue
            seen.add(id(d))
            if d.eng == "pe" and op.eng == "pe" and not d.is_dma and not op.is_dma:
                continue
            op.deps.append(d)
            d.has_dep = True
        self.ops[op.eng].append(op)
        return op

    def op(self, eng, fn, r=(), w=()):
        return self._rec(Op(eng, fn), r, w)

    def dma(self, q, out, in_, r=(), w=(), semkey=None, **kw):
        assert semkey is not None
        o = Op(q, lambda e: e.dma_start(out=out, in_=in_, **kw), is_dma=True, semkey=semkey)
        self._rec(o, r, w)
        self.dma_cum[semkey] = self.dma_cum.get(semkey, 0) + 16
        o.cum = self.dma_cum[semkey]
        return o

    def mm(self, out, lhsT, rhs, start, stop, r, w, skip=False):
        if skip:
            return self.op("pe", lambda e: e.matmul(out, lhsT, rhs, start=start, stop=stop, skip_group_check=True), r, w)
        return self.op("pe", lambda e: e.matmul(out, lhsT, rhs, start=start, stop=stop), r, w)

    def tr(self, out, in_, ident, r, w):
        return self.op("pe", lambda e: e.transpose(out, in_, ident), r, w)

    def actf(self, out, in_, func, r, w, bias=None, scale=None, accum=None):
        kw = {}
        if bias is not None:
            kw["bias"] = bias
        if scale is not None:
            kw["scale"] = scale
        if accum is not None:
            kw["accum_out"] = accum
        return self.op("act", lambda e: e.activation(out, in_, func, **kw), r, w)

    def tt(self, eng, out, a, b, op, r, w):
        return self.op(eng, lambda e: e.tensor_tensor(out, a, b, op), r, w)

    def ts(self, eng, out, a, s1, s2, op0, op1, r, w):
        if op1 is None:
            return self.op(eng, lambda e: e.tensor_scalar(out, a, s1, None, op0), r, w)
        return self.op(eng, lambda e: e.tensor_scalar(out, a, s1, s2, op0, op1), r, w)

    def stt(self, eng, out, in0, scalar, in1, op0, op1, r, w):
        return self.op(eng, lambda e: e.scalar_tensor_tensor(out, in0, scalar, in1, op0, op1), r, w)

    def cp(self, eng, out, in_, r, w):
        if eng == "act":
            return self.op("act", lambda e: e.copy(out, in_), r, w)
        return self.op(eng, lambda e: e.tensor_copy(out, in_), r, w)

    def red(self, out, in_, op, r, w):
        return self.op("dve", lambda e: e.tensor_reduce(out, in_, AX.X, op), r, w)

    def recip(self, out, in_, r, w):
        return self.op("dve", lambda e: e.reciprocal(out, in_), r, w)

    def memset(self, eng, out, val, w):
        return self.op(eng, lambda e: e.memset(out, val), (), w)

    def emit(self):
        nc = self.nc
        es = self.es
        esem = {e: es.enter_context(nc.semaphore("s_" + e)) for e in ENGS}
        dsem = {k: es.enter_context(nc.semaphore("d_%d" % i)) for i, k in enumerate(self.dma_cum)}
        for e in ENGS:
            c = 0
            for o in self.ops[e]:
                if not o.is_dma and o.has_dep:
                    c += 1
                    o.sig = c
        for e in ENGS:
            waited = {}
            for o in self.ops[e]:
                need = {}
                for d in o.deps:
                    if d.is_dma:
                        key = ("d", d.semkey)
                        val = d.cum
                    else:
                        key = ("e", d.eng)
                        val = d.sig
                    if val > need.get(key, 0):
                        need[key] = val
                for key, val in need.items():
                    if waited.get(key, 0) >= val:
                        continue
                    waited[key] = val
                    sem = dsem[key[1]] if key[0] == "d" else esem[key[1]]
                    o.waits.append((sem, val))
        final = [(dsem[k], v) for k, v in self.dma_cum.items()]
        block = es.enter_context(nc.Block())

        def run(eng, name):
            for o in self.ops[name]:
                for sem, val in o.waits:
                    eng.wait_ge(sem, val)
                ins = o.fn(eng)
                if o.is_dma:
                    ins.then_inc(dsem[o.semkey], 16)
                elif o.has_dep:
                    ins.then_inc(esem[name], 1)
            if name == "sp":
                for sem, val in final:
                    eng.wait_ge(sem, val)

        @block.tensor
        def _(eng):
            run(eng, "pe")

        @block.scalar
        def _(eng):
            run(eng, "act")

        @block.vector
        def _(eng):
            run(eng, "dve")

        @block.gpsimd
        def _(eng):
            run(eng, "pool")

        @block.sync
        def _(eng):
            run(eng, "sp")

        es.close()


def din(nc, name, shape, dt):
    return nc.dram_tensor(name, list(shape), dt, kind="ExternalInput").ap()


def dout(nc, name, shape, dt):
    return nc.dram_tensor(name, list(shape), dt, kind="ExternalOutput").ap()


def emit_mod(p, nc, ada_w, ada_b, cpk, col0, ncols, stage, psb, modbc):
    cact = p.sb("cact", [128, 8], F32)
    crep = p.sb("crep", [128, 8, 128], F32)
    p.dma("sp", cact[:], cpk, w=["cact"], semkey="cact")
    p.dma("sp", modbc[:, 0:ncols], ada_b[:, col0:col0 + ncols].partition_broadcast(128), w=["modbc"], semkey="modbc")
    p.actf(cact[:], cact[:], AF.Silu, r=["cact"], w=["cact"])
    p.cp("dve", crep[:], cact[:].unsqueeze(2).to_broadcast([128, 8, 128]), r=["cact"], w=["crep"])
    npass = (ncols + 2047) // 2048
    for ps_ in range(npass):
        c0 = ps_ * 2048
        cw = min(2048, ncols - c0)
        for kc in range(8):
            sl = kc % 2
            p.dma("sp", stage[sl][:, 0:cw], ada_w[kc * 128:(kc + 1) * 128, col0 + c0:col0 + c0 + cw],
                  w=["stg%d" % sl], semkey="stg%d" % sl)
            for j in range(cw // 512):
                p.mm(psb[j][:, :], crep[:, kc, :], stage[sl][:, j * 512:(j + 1) * 512], kc == 0, kc == 7,
                     r=["crep", "stg%d" % sl], w=["psb%d" % j])
        for j in range(cw // 512):
            cs = c0 + j * 512
            p.tt("dve", modbc[:, cs:cs + 512], modbc[:, cs:cs + 512], psb[j][:, :], ALU.add,
                 r=["psb%d" % j, "modbc"], w=["modbc"])


def emit_rmsnorm_T(p, xt, xkey, G, gkey, SH, shkey, junk, ssk, hb, ptb, ptbkey, hT, t, ident, tag):
    ss = ssk
    p.actf(junk[:], xt, AF.Square, r=[xkey], w=["junk" + tag, "ss" + tag], accum=ss[:, 0:1])
    p.ts("dve", ss[:, 0:1], ss[:, 0:1], 1.0 / D, EPS, ALU.mult, ALU.add, r=["ss" + tag], w=["ss" + tag])
    p.actf(ss[:, 0:1], ss[:, 0:1], AF.Sqrt, r=["ss" + tag], w=["ss" + tag])
    p.recip(ss[:, 0:1], ss[:, 0:1], r=["ss" + tag], w=["ss" + tag])
    p.stt("dve", junk[:], xt, ss[:, 0:1], G, ALU.mult, ALU.mult, r=[xkey, "ss" + tag, gkey], w=["junk" + tag])
    p.tt("dve", hb[:], junk[:], SH, ALU.add, r=["junk" + tag, shkey], w=["hb" + tag])
    for kc in range(8):
        p.tr(ptb[:, kc * 128:(kc + 1) * 128], hb[:, kc * 128:(kc + 1) * 128], ident[:], r=["hb" + tag, "ident"], w=[ptbkey])
    p.cp("act", hT[:, :, t * 128:(t + 1) * 128], ptb[:].rearrange("p (k t) -> p k t", t=128), r=[ptbkey], w=["hT%d" % t])


def build_A():
    nc = bass.Bass("TRN2", target_bir_lowering=False)
    x = din(nc, "x", [NTOK, D], F32)
    cpk = din(nc, "cpk", [128, 8], F32)
    ada_w = din(nc, "ada_w", [D, 6 * D], F32)
    ada_b = din(nc, "ada_b", [1, 6 * D], F32)
    g1 = din(nc, "g1", [1, D], F32)
    w_in = din(nc, "w_in", [D, INC], F32)
    qkg = din(nc, "qkg", [1, 128], F32)
    alog = din(nc, "alog", [1, 8], F32)
    dtb = din(nc, "dtb", [1, 8], F32)
    cosd = din(nc, "cos", [NTOK, 64], F32)
    sind = din(nc, "sin", [NTOK, 64], F32)
    identd = din(nc, "identb", [128, 128], BF16)
    qkT_o = dout(nc, "qkT", [8, 128, NTOK], BF16)
    v_o = dout(nc, "v", [NTOK, 512], BF16)
    dnT_o = dout(nc, "dnT", [12, 128, NTOK], F32)
    z_o = dout(nc, "z", [NTOK, 512], F32)
    bg_o = dout(nc, "bg", [128, NT, 16], F32)
    gT_o = dout(nc, "gT", [16, 128, NTOK], BF16)

    p = Prog(nc)
    ident = p.sb("ident", [128, 128], BF16)
    modbc = p.sb("modbc", [128, 2048], F32)
    G1 = p.sb("G1", [128, D], F32)
    stage = [p.sb("stg%d" % i, [128, 2064], F32) for i in range(2)]
    wb = p.sb("wb", [128, 8, 2064], BF16)
    hT = p.sb("hT", [128, 8, NTOK], BF16)
    xs = [p.sb("xs%d" % i, [128, D], F32) for i in range(2)]
    junk = p.sb("junk", [128, D], F32)
    ssA = p.sb("ssA", [128, 1], F32)
    hb = p.sb("hb", [128, D], BF16)
    gq = p.sb("gq", [128, 128], F32)
    cst = [p.sb("cst%d" % i, [128, 64], F32) for i in range(2)]
    snt = [p.sb("snt%d" % i, [128, 64], F32) for i in range(2)]
    alb = p.sb("alb", [128, 8], F32)
    dtbb = p.sb("dtbb", [128, 8], F32)
    sq = [p.sb("sq%d" % i, [128, 512], F32) for i in range(2)]
    t1 = [p.sb("t1_%d" % i, [128, 512], F32) for i in range(2)]
    t2 = [p.sb("t2_%d" % i, [128, 512], F32) for i in range(2)]
    qr = [p.sb("qr%d" % i, [128, 512], BF16) for i in range(2)]
    ss16 = p.sb("ss16", [128, 16], F32)
    vb = [p.sb("vb%d" % i, [128, 512], BF16) for i in range(2)]
    zf = [p.sb("zf%d" % i, [128, 512], F32) for i in range(2)]
    baraw = p.sb("baraw", [128, NT, 16], F32)
    bgo = p.sb("bgo", [128, NT, 16], F32)
    bat = [p.sb("bat%d" % i, [128, NT, 8], F32) for i in range(3)]
    qkTs = [p.sb("qkTs%d" % i, [128, 8, 512], BF16) for i in range(2)]
    dst = [p.sb("dst%d" % i, [128, 512], F32) for i in range(3)]
    gst = [p.sb("gst%d" % i, [128, 512], BF16) for i in range(3)]
    psb = [p.ps("psb%d" % i, [128, 512], F32) for i in range(6)]
    ptb = [p.ps("ptb%d" % i, [128, 1024], BF16) for i in range(2)]

    p.dma("sp", ident[:], identd, w=["ident"], semkey="ident")
    p.dma("sp", G1[:], g1.partition_broadcast(128), w=["GA"], semkey="G1")
    p.dma("sp", gq[:], qkg.partition_broadcast(128), w=["gq"], semkey="gq")
    p.dma("sp", alb[:], alog.partition_broadcast(128), w=["alb"], semkey="alb")
    p.dma("sp", dtbb[:], dtb.partition_broadcast(128), w=["dtbb"], semkey="dtbb")
    emit_mod(p, nc, ada_w, ada_b, cpk, 0, 2048, stage, psb, modbc)
    p.stt("dve", G1[:], modbc[:, 1024:2048], 1.0, G1[:], ALU.add, ALU.mult, r=["modbc", "GA"], w=["GA"])
    p.actf(alb[:], alb[:], AF.Exp, r=["alb"], w=["alb"])
    p.ts("dve", alb[:], alb[:], -1.0, None, ALU.mult, None, r=["alb"], w=["alb"])

    def load_w(colranges, ncols):
        for kc in range(8):
            sl = kc % 2
            o = 0
            for (a, b) in colranges:
                p.dma("sp", stage[sl][:, o:o + (b - a)], w_in[kc * 128:(kc + 1) * 128, a:b],
                      w=["stg%d" % sl], semkey="stg%d" % sl)
                o += b - a
            p.cp("pool", wb[:, kc, 0:ncols], stage[sl][:, 0:ncols], r=["stg%d" % sl], w=["wb%d" % kc])

    def load_x(t):
        p.dma("sp", xs[t % 2][:], x[t * 128:(t + 1) * 128, :], w=["xs%d" % (t % 2)], semkey="xs%d" % (t % 2))

    load_x(0)
    load_x(1)
    load_w([(0, 1536), (3072, 3600)], 2064)
    for t in range(NT):
        emit_rmsnorm_T(p, xs[t % 2][:], "xs%d" % (t % 2), G1[:], "GA", modbc[:, 0:1024], "modbc", junk, ssA, hb,
                       ptb[t % 2], "ptb%d" % (t % 2), hT, t, ident, "A")
        if t + 2 < NT:
            load_x(t + 2)

    def load_cs(t):
        p.dma("sp", cst[t % 2][:], cosd[t * 128:(t + 1) * 128, :], w=["cst%d" % (t % 2)], semkey="cst%d" % (t % 2))
        p.dma("sp", snt[t % 2][:], sind[t * 128:(t + 1) * 128, :], w=["snt%d" % (t % 2)], semkey="snt%d" % (t % 2))

    load_cs(0)
    load_cs(1)
    wkeys = ["wb%d" % kc for kc in range(8)]
    for t in range(NT):
        for j in range(4):
            for kc in range(8):
                p.mm(psb[j][:, :], hT[:, kc, t * 128:(t + 1) * 128], wb[:, kc, j * 512:(j + 1) * 512], kc == 0, kc == 7,
                     r=["hT%d" % t, "wb%d" % kc], w=["psb%d" % j])
        for kc in range(8):
            p.mm(psb[4][:, 0:16], hT[:, kc, t * 128:(t + 1) * 128], wb[:, kc, 2048:2064], kc == 0, kc == 7,
                 r=["hT%d" % t, "wb%d" % kc], w=["psb4"])
        p.cp("act", vb[t % 2][:], psb[2][:, :], r=["psb2"], w=["vb%d" % (t % 2)])
        p.dma("pool", v_o[t * 128:(t + 1) * 128, :], vb[t % 2][:], r=["vb%d" % (t % 2)], semkey="vb%d" % (t % 2))
        p.cp("act", zf[t % 2][:], psb[3][:, :], r=["psb3"], w=["zf%d" % (t % 2)])
        p.dma("pool", z_o[t * 128:(t + 1) * 128, :], zf[t % 2][:], r=["zf%d" % (t % 2)], semkey="zf%d" % (t % 2))
        p.cp("act", baraw[:, t, :], psb[4][:, 0:16], r=["psb4"], w=["baraw"])
        for j in range(2):
            sqj, t1j, t2j, qrj = sq[j], t1[j], t2[j], qr[j]
            kq = "qk%d" % j
            p.actf(sqj[:], psb[j][:, :], AF.Square, r=["psb%d" % j], w=["sq" + kq])
            p.red(ss16[:, j * 8:(j + 1) * 8], sqj[:].rearrange("p (g d) -> p g d", d=64), ALU.add, r=["sq" + kq], w=["ss16"])
        p.ts("dve", ss16[:], ss16[:], 1.0 / 64, EPS, ALU.mult, ALU.add, r=["ss16"], w=["ss16"])
        p.actf(ss16[:], ss16[:], AF.Sqrt, r=["ss16"], w=["ss16"])
        p.recip(ss16[:], ss16[:], r=["ss16"], w=["ss16"])
        for j in range(2):
            sqj, t1j, t2j, qrj = sq[j], t1[j], t2[j], qr[j]
            kq = "qk%d" % j
            s3 = sqj[:].rearrange("p (g d) -> p g d", d=64)
            a3 = t1j[:].rearrange("p (g d) -> p g d", d=64)
            b3 = t2j[:].rearrange("p (g d) -> p g d", d=64)
            q3 = qrj[:].rearrange("p (g d) -> p g d", d=64)
            p.tt("dve", s3, psb[j][:, :].rearrange("p (g d) -> p g d", d=64),
                 ss16[:, j * 8:(j + 1) * 8].unsqueeze(2).to_broadcast([128, 8, 64]), ALU.mult,
                 r=["psb%d" % j, "ss16"], w=["sq" + kq])
            p.tt("pool", s3, s3, gq[:, j * 64:(j + 1) * 64].unsqueeze(1).to_broadcast([128, 8, 64]), ALU.mult,
                 r=["sq" + kq, "gq"], w=["sq" + kq])
            p.tt("pool", a3, s3, cst[t % 2][:].unsqueeze(1).to_broadcast([128, 8, 64]), ALU.mult,
                 r=["sq" + kq, "cst%d" % (t % 2)], w=["t1" + kq])
            p.tt("dve", b3[:, :, 0:32], s3[:, :, 32:64], snt[t % 2][:, 0:32].unsqueeze(1).to_broadcast([128, 8, 32]), ALU.mult,
                 r=["sq" + kq, "snt%d" % (t % 2)], w=["t2a" + kq])
            p.tt("pool", b3[:, :, 32:64], s3[:, :, 0:32], snt[t % 2][:, 32:64].unsqueeze(1).to_broadcast([128, 8, 32]), ALU.mult,
                 r=["sq" + kq, "snt%d" % (t % 2)], w=["t2b" + kq])
            p.tt("dve", qrj[:], t1j[:], t2j[:], ALU.add, r=["t1" + kq, "t2a" + kq, "t2b" + kq], w=["qr" + kq])
        if t + 2 < NT:
            load_cs(t + 2)
        pk = "ptb%d" % (t % 2)
        for j in range(2):
            for h in range(4):
                p.tr(ptb[t % 2][:, (j * 4 + h) * 128:(j * 4 + h + 1) * 128], qr[j][:, h * 128:(h + 1) * 128], ident[:],
                     r=["qrqk%d" % j, "ident"], w=[pk])
        sl = (t // 4) % 2
        p.cp("act", qkTs[sl][:, :, (t % 4) * 128:(t % 4 + 1) * 128], ptb[t % 2][:].rearrange("p (k t) -> p k t", t=128),
             r=[pk], w=["qkTs%d" % sl])
        if t % 4 == 3:
            tg = t // 4
            p.dma("pool", qkT_o[:, :, tg * 512:(tg + 1) * 512].rearrange("h p t -> p h t"), qkTs[sl][:],
                  r=["qkTs%d" % sl], semkey="qkTs%d" % sl)

    braw = baraw[:, :, 0:8]
    araw = baraw[:, :, 8:16]
    p.actf(bgo[:, :, 0:8], braw, AF.Sigmoid, r=["baraw"], w=["bgo_b"])
    xg, ab, mx = bat
    p.tt("dve", xg[:], araw, dtbb[:].unsqueeze(1).to_broadcast([128, NT, 8]), ALU.add, r=["baraw", "dtbb"], w=["xg"])
    p.ts("dve", ab[:], xg[:], -1.0, None, ALU.mult, None, r=["xg"], w=["ab"])
    p.tt("dve", ab[:], ab[:], xg[:], ALU.max, r=["ab", "xg"], w=["ab"])
    p.actf(ab[:], ab[:], AF.Exp, r=["ab"], w=["ab"], scale=-1.0)
    p.ts("dve", ab[:], ab[:], 1.0, None, ALU.add, None, r=["ab"], w=["ab"])
    p.actf(ab[:], ab[:], AF.Ln, r=["ab"], w=["ab"])
    p.ts("dve", mx[:], xg[:], 0.0, None, ALU.max, None, r=["xg"], w=["mx"])
    p.tt("dve", mx[:], mx[:], ab[:], ALU.add, r=["mx", "ab"], w=["mx"])
    p.tt("dve", bgo[:, :, 8:16], mx[:], alb[:].unsqueeze(1).to_broadcast([128, NT, 8]), ALU.mult, r=["mx", "alb"], w=["bgo_g"])
    p.dma("pool", bg_o, bgo[:], r=["bgo_b", "bgo_g"], semkey="bgo")

    cnt = 0
    for grp, (a, b) in enumerate([(1536, 3072), (3600, 5648)]):
        ncols = b - a
        load_w([(a, b)], ncols)
        for tg in range(4):
            for cc in range(ncols // 128):
                bk = cnt % 4
                for kc in range(8):
                    p.mm(psb[bk][:, :], wb[:, kc, cc * 128:(cc + 1) * 128], hT[:, kc, tg * 512:(tg + 1) * 512], kc == 0, kc == 7,
                         r=["wb%d" % kc] + ["hT%d" % tt_ for tt_ in range(tg * 4, tg * 4 + 4)], w=["psb%d" % bk])
                sl = cnt % 3
                if grp == 0:
                    p.cp("act", dst[sl][:], psb[bk][:, :], r=["psb%d" % bk], w=["dst%d" % sl])
                    p.dma("pool", dnT_o[cc, :, tg * 512:(tg + 1) * 512], dst[sl][:], r=["dst%d" % sl], semkey="dst%d" % sl)
                else:
                    p.actf(gst[sl][:], psb[bk][:, :], AF.Sigmoid, r=["psb%d" % bk], w=["gst%d" % sl])
                    p.dma("pool", gT_o[cc, :, tg * 512:(tg + 1) * 512], gst[sl][:], r=["gst%d" % sl], semkey="gst%d" % sl)
                cnt += 1
    p.emit()
    return nc


_CACHE = {}


def _rope_tables():
    pos = np.arange(S, dtype=np.float32)
    inv = (1.0 / (np.float32(10000.0) ** (np.arange(0, 64, 2, dtype=np.float32) / np.float32(64)))).astype(np.float32)
    ang = (pos[:, None] * inv[None, :]).astype(np.float32)
    ang = np.concatenate([ang, ang], axis=-1)
    cos = np.cos(ang).astype(np.float32)
    sin = np.sin(ang).astype(np.float32)
    sin_signed = sin.copy()
    sin_signed[:, :32] = -sin_signed[:, :32]
    return cos, sin_signed


def _get(name, builder):
    if name not in _CACHE:
        _CACHE[name] = builder()
    return _CACHE[name]


def run_A(inp, l, xsh):
    nc = _get("A", build_A)
    cos, sins = _get("rope", _rope_tables)
    identb = np.eye(128, dtype=np.float32).astype(NPBF)
    maps = []
    for i in range(8):
        b, t0 = i // 4, (i % 4) * NTOK
        maps.append({
            "x": np.ascontiguousarray(xsh[i]),
            "cpk": np.ascontiguousarray(inp["c"][b].reshape(8, 128).T),
            "ada_w": np.ascontiguousarray(inp["ada_w"][l]),
            "ada_b": np.ascontiguousarray(inp["ada_b"][l][None, :]),
            "g1": np.ascontiguousarray(inp["norm1_g"][l][None, :]),
            "w_in": np.ascontiguousarray(inp["w_in"][l]),
            "qkg": np.ascontiguousarray(inp["qk_norm_gain"][l].reshape(1, 128)),
            "alog": np.ascontiguousarray(inp["dn_A_log"][l].reshape(1, 8)),
            "dtb": np.ascontiguousarray(inp["dn_dt_bias"][l].reshape(1, 8)),
            "cos": np.ascontiguousarray(cos[t0:t0 + NTOK]),
            "sin": np.ascontiguousarray(sins[t0:t0 + NTOK]),
            "identb": identb,
        })
    res = run_bass_kernel_spmd(nc, maps, core_ids=list(range(8)))
    return res.results


def emit_C(nc, outer, x_in, yaT, ybT, gT, cpk, ada_w, ada_b, g2, w_a, w_b, w_out, w1, w2, identd, x_out):
    x_res = outer.enter_context(nc.sbuf_tensor("x_res", [128, NT, D], F32))
    modbc = outer.enter_context(nc.sbuf_tensor("modbcC", [128, 4096], F32))
    G2 = outer.enter_context(nc.sbuf_tensor("G2", [128, D], F32))
    ident = outer.enter_context(nc.sbuf_tensor("identC", [128, 128], BF16))

    p = Prog(nc)
    stage = [p.sb("stgC%d" % i, [128, 2048], F32) for i in range(2)]
    psb = [p.ps("psC0_%d" % i, [128, 512], F32) for i in range(4)]
    p.dma("sp", ident[:], identd, w=["ident"], semkey="ident")
    p.dma("sp", G2[:], g2.partition_broadcast(128), w=["G2"], semkey="G2")
    for t in range(NT):
        p.dma("sp", x_res[:, t, :], x_in[t * 128:(t + 1) * 128, :], w=["xr%d" % t], semkey="xres")
    emit_mod(p, nc, ada_w, ada_b, cpk, 2048, 4096, stage, psb, modbc)
    p.stt("dve", G2[:], modbc[:, 2048:3072], 1.0, G2[:], ALU.add, ALU.mult, r=["modbc", "G2"], w=["G2"])
    p.emit()
    gt1 = modbc[:, 0:1024]
    sh2 = modbc[:, 1024:2048]
    gt2 = modbc[:, 3072:4096]

    p = Prog(nc)
    stg = [p.sb("stg1_%d" % i, [128, 1024], F32) for i in range(2)]
    wab = p.sb("wab", [128, 4, D], BF16)
    wbb = p.sb("wbb", [128, 4, D], BF16)
    wob = p.sb("wob", [128, 8, D], BF16)
    yas = [p.sb("yas%d" % i, [128, 4, 512], BF16) for i in range(2)]
    ybs = [p.sb("ybs%d" % i, [128, 4, 512], BF16) for i in range(2)]
    gas = [p.sb("gas%d" % i, [128, 512], BF16) for i in range(2)]
    gbs = [p.sb("gbs%d" % i, [128, 512], BF16) for i in range(2)]
    m1 = [p.sb("m1_%d" % i, [128, 512], F32) for i in range(2)]
    m2 = [p.sb("m2_%d" % i, [128, 512], F32) for i in range(2)]
    mT = [p.sb("mT%d" % i, [128, 8, 512], BF16) for i in range(2)]
    tmp = [p.sb("tmpc%d" % i, [128, 512], F32) for i in range(2)]
    psb = [p.ps("psC1_%d" % i, [128, 512], F32) for i in range(6)]
    si = 0
    for h in range(4):
        for (wsrc, wdst, nm) in ((w_a, wab, "wab"), (w_b, wbb, "wbb")):
            sl = si % 2
            si += 1
            p.dma("sp", stg[sl][:], wsrc[h * 128:(h + 1) * 128, :], w=["stg%d" % sl], semkey="stg%d" % sl)
            p.cp("pool", wdst[:, h, :], stg[sl][:], r=["stg%d" % sl], w=[nm])
    for kc in range(8):
        sl = si % 2
        si += 1
        p.dma("sp", stg[sl][:], w_out[kc * 128:(kc + 1) * 128, :], w=["stg%d" % sl], semkey="stg%d" % sl)
        p.cp("pool", wob[:, kc, :], stg[sl][:], r=["stg%d" % sl], w=["wob"])
    cnt = 0
    for tg in range(4):
        s2 = tg % 2
        p.dma("sp", yas[s2][:], yaT[:, :, tg * 512:(tg + 1) * 512].rearrange("h p t -> p h t"), w=["yas%d" % s2], semkey="yas%d" % s2)
        p.dma("sp", ybs[s2][:], ybT[:, :, tg * 512:(tg + 1) * 512].rearrange("h p t -> p h t"), w=["ybs%d" % s2], semkey="ybs%d" % s2)
        for fc in range(8):
            s3 = fc % 2
            p.dma("sp", gas[s3][:], gT[fc, :, tg * 512:(tg + 1) * 512], w=["gas%d" % s3], semkey="gas%d" % s3)
            p.dma("sp", gbs[s3][:], gT[8 + fc, :, tg * 512:(tg + 1) * 512], w=["gbs%d" % s3], semkey="gbs%d" % s3)
            ba, bb = (0, 1) if fc % 2 == 0 else (2, 3)
            for h in range(4):
                p.mm(psb[ba][:, :], wab[:, h, fc * 128:(fc + 1) * 128], yas[s2][:, h, :], h == 0, h == 3,
                     r=["wab", "yas%d" % s2], w=["ps%d" % ba])
            for h in range(4):
                p.mm(psb[bb][:, :], wbb[:, h, fc * 128:(fc + 1) * 128], ybs[s2][:, h, :], h == 0, h == 3,
                     r=["wbb", "ybs%d" % s2], w=["ps%d" % bb])
            p.tt("dve", m1[s3][:], psb[ba][:, :], gas[s3][:], ALU.mult, r=["ps%d" % ba, "gas%d" % s3], w=["m1_%d" % s3])
            p.tt("dve", m2[s3][:], psb[bb][:, :], gbs[s3][:], ALU.mult, r=["ps%d" % bb, "gbs%d" % s3], w=["m2_%d" % s3])
            p.tt("pool", mT[s2][:, fc, :], m1[s3][:], m2[s3][:], ALU.add, r=["m1_%d" % s3, "m2_%d" % s3], w=["mT%d_%d" % (s2, fc)])
        for i in range(4):
            t = tg * 4 + i
            for j in range(2):
                bk = 4 + (cnt % 2)
                ts_ = cnt % 2
                cnt += 1
                for kc in range(8):
                    p.mm(psb[bk][:, :], mT[s2][:, kc, i * 128:(i + 1) * 128], wob[:, kc, j * 512:(j + 1) * 512], kc == 0, kc == 7,
                         r=["mT%d_%d" % (s2, kc), "wob"], w=["ps%d" % bk])
                p.tt("dve", tmp[ts_][:], psb[bk][:, :], gt1[:, j * 512:(j + 1) * 512], ALU.mult, r=["ps%d" % bk], w=["tmp%d" % ts_])
                p.tt("pool", x_res[:, t, j * 512:(j + 1) * 512], x_res[:, t, j * 512:(j + 1) * 512], tmp[ts_][:], ALU.add,
                     r=["tmp%d" % ts_], w=["xr%d_%d" % (t, j)])
    p.emit()

    p = Prog(nc)
    h2T = p.sb("h2T", [128, 8, NTOK], BF16)
    junk = p.sb("junkC", [128, D], F32)
    ssC = p.sb("ssC", [128, 1], F32)
    hb = p.sb("hbC", [128, D], BF16)
    stg = [p.sb("stg2_%d" % i, [128, 1024], F32) for i in range(2)]
    w1b = [p.sb("w1b%d" % i, [128, 8, 512], BF16) for i in range(2)]
    w2b = [p.sb("w2b%d" % i, [128, 4, D], BF16) for i in range(2)]
    aT = p.sb("aT", [128, 4, NTOK], BF16)
    rl = [p.sb("rl%d" % i, [128, 512], F32) for i in range(2)]
    tmp = [p.sb("tmpd%d" % i, [128, 512], F32) for i in range(2)]
    psb = [p.ps("psC2_%d" % i, [128, 512], F32) for i in range(6)]
    ptb = [p.ps("ptC2_%d" % i, [128, 1024], BF16) for i in range(2)]
    si = 0

    def load_fb(fb):
        nonlocal si
        s = fb % 2
        for k2 in range(4):
            sl = si % 2
            si += 1
            p.dma("sp", stg[sl][:].rearrange("p (k c) -> p k c", k=2),
                  w1[k2 * 256:(k2 + 1) * 256, fb * 512:(fb + 1) * 512].rearrange("(k p) c -> p k c", p=128),
                  w=["stg%d" % sl], semkey="stg%d" % sl)
            p.cp("pool", w1b[s][:, k2 * 2:k2 * 2 + 2, :], stg[sl][:].rearrange("p (k c) -> p k c", k=2), r=["stg%d" % sl], w=["w1b%d" % s])
        for c in range(4):
            sl = si % 2
            si += 1
            p.dma("sp", stg[sl][:], w2[fb * 512 + c * 128:fb * 512 + (c + 1) * 128, :], w=["stg%d" % sl], semkey="stg%d" % sl)
            p.cp("pool", w2b[s][:, c, :], stg[sl][:], r=["stg%d" % sl], w=["w2b%d" % s])

    load_fb(0)
    for t in range(NT):
        emit_rmsnorm_T(p, x_res[:, t, :], "xr%d" % t, G2[:], "G2", sh2, "modbc", junk, ssC, hb,
                       ptb[t % 2], "ptb%d" % (t % 2), h2T, t, ident, "C")
    cnt = 0
    c2 = 0
    for fb in range(8):
        s = fb % 2
        if fb + 1 < 8:
            load_fb(fb + 1)
        for tg in range(4):
            for c in range(4):
                bk = cnt % 4
                rs = cnt % 2
                cnt += 1
                for kc in range(8):
                    p.mm(psb[bk][:, :], w1b[s][:, kc, c * 128:(c + 1) * 128], h2T[:, kc, tg * 512:(tg + 1) * 512], kc == 0, kc == 7,
                         r=["w1b%d" % s] + ["hT%d" % tt_ for tt_ in range(tg * 4, tg * 4 + 4)], w=["ps%d" % bk])
                p.actf(rl[rs][:], psb[bk][:, :], AF.Relu, r=["ps%d" % bk], w=["rl%d" % rs])
                p.tt("pool", aT[:, c, tg * 512:(tg + 1) * 512], rl[rs][:], rl[rs][:], ALU.mult, r=["rl%d" % rs], w=["aT%d_%d" % (c, tg)])
        for t in range(NT):
            for j in range(2):
                bk = 4 + (c2 % 2)
                ts_ = c2 % 2
                c2 += 1
                for c in range(4):
                    p.mm(psb[bk][:, :], aT[:, c, t * 128:(t + 1) * 128], w2b[s][:, c, j * 512:(j + 1) * 512], c == 0, c == 3,
                         r=["aT%d_%d" % (c, t // 4), "w2b%d" % s], w=["ps%d" % bk])
                p.tt("dve", tmp[ts_][:], psb[bk][:, :], gt2[:, j * 512:(j + 1) * 512], ALU.mult, r=["ps%d" % bk], w=["tmp%d" % ts_])
                p.tt("pool", x_res[:, t, j * 512:(j + 1) * 512], x_res[:, t, j * 512:(j + 1) * 512], tmp[ts_][:], ALU.add,
                     r=["tmp%d" % ts_], w=["xr%d" % t])
    for t in range(NT):
        p.dma("sp", x_out[t * 128:(t + 1) * 128, :], x_res[:, t, :], r=["xr%d" % t], semkey="xout")
    p.emit()


def build_C():
    nc = bass.Bass("TRN2", target_bir_lowering=False)
    x = din(nc, "x", [NTOK, D], F32)
    yaT = din(nc, "yaT", [4, 128, NTOK], BF16)
    ybT = din(nc, "ybT", [4, 128, NTOK], BF16)
    gT = din(nc, "gT", [16, 128, NTOK], BF16)
    cpk = din(nc, "cpk", [128, 8], F32)
    ada_w = din(nc, "ada_w", [D, 6 * D], F32)
    ada_b = din(nc, "ada_b", [1, 6 * D], F32)
    g2 = din(nc, "g2", [1, D], F32)
    w_a = din(nc, "w_a", [512, D], F32)
    w_b = din(nc, "w_b", [512, D], F32)
    w_out = din(nc, "w_out", [D, D], F32)
    w1 = din(nc, "w1", [D, 4 * D], F32)
    w2 = din(nc, "w2", [4 * D, D], F32)
    identd = din(nc, "identb", [128, 128], BF16)
    x_out = dout(nc, "x_out", [NTOK, D], F32)
    with ExitStack() as outer:
        emit_C(nc, outer, x, yaT, ybT, gT, cpk, ada_w, ada_b, g2, w_a, w_b, w_out, w1, w2, identd, x_out)
    return nc


def run_C(inp, l, xsh, yaT, ybT, gT):
    nc = _get("C", build_C)
    identb = np.eye(128, dtype=np.float32).astype(NPBF)
    maps = []
    for i in range(8):
        b = i // 4
        maps.append({
            "x": np.ascontiguousarray(xsh[i]),
            "yaT": np.ascontiguousarray(yaT[i]),
            "ybT": np.ascontiguousarray(ybT[i]),
            "gT": np.ascontiguousarray(gT[i]),
            "cpk": np.ascontiguousarray(inp["c"][b].reshape(8, 128).T),
            "ada_w": np.ascontiguousarray(inp["ada_w"][l]),
            "ada_b": np.ascontiguousarray(inp["ada_b"][l][None, :]),
            "g2": np.ascontiguousarray(inp["norm2_g"][l][None, :]),
            "w_a": np.ascontiguousarray(inp["w_branch_a"][l]),
            "w_b": np.ascontiguousarray(inp["w_branch_b"][l]),
            "w_out": np.ascontiguousarray(inp["w_out"][l]),
            "w1": np.ascontiguousarray(inp["w_mlp1"][l]),
            "w2": np.ascontiguousarray(inp["w_mlp2"][l]),
            "identb": identb,
        })
    res = run_bass_kernel_spmd(nc, maps, core_ids=list(range(8)))
    return [r["x_out"] for r in res.results]


DBG_B3 = 0
DBG_DUMP = 0
DBG_NG = 16
DBG_D = 0
DBG_DIR = -1
NKC = S // 128


def emit_B1(nc, qkT, vd, qkg, lqd, lamd, sublnd, identd, yaT_o):
    p = Prog(nc)
    ident = p.sb("identB", [128, 128], BF16)
    qT = p.sb("qT", [128, S], BF16)
    kT = p.sb("kT", [128, S], BF16)
    vaug = p.sb("vaug", [128, NKC, 132], BF16)
    gq = p.sb("gqB", [128, 128], F32)
    gqa = p.sb("gqa", [128, 128], F32)
    mx2 = p.sb("mx2", [128, 2], F32)
    nshift = p.sb("nshift", [128, 1], F32)
    lq = p.sb("lq", [128, 4, 64], F32)
    lqp = p.sb("lqp", [128, 2, 64], F32)
    lsum = p.sb("lsum", [128, 2], F32)
    lam = p.sb("lam", [128, 1], F32)
    lamc = p.sb("lamc", [128, 2], F32)
    gsub = p.sb("gsub", [128, 128], F32)
    pT = [p.sb("pT%d" % i, [128, 512], BF16) for i in range(4)]
    Os = [p.sb("Os%d" % i, [128, 4, 129], F32) for i in range(2)]
    rr = p.sb("rr", [128, 8], F32)
    ob = p.sb("ob", [128, 4, 128], F32)
    ojunk = p.sb("ojunk", [128, 128], F32)
    oss = p.sb("oss", [128, 4], F32)
    yb16 = p.sb("yb16", [128, 4, 128], BF16)
    yst = [p.sb("yst%d" % i, [128, 512], BF16) for i in range(2)]
    sps = [p.ps("sps%d" % i, [128, 512], F32) for i in range(2)]
    acc = [[p.ps("acc%d_%d" % (m, i), [128, 512], F32) for i in range(2)] for m in range(2)]
    ptb = p.ps("ptbB", [128, 1024], BF16)

    p.dma("sp", ident[:], identd, w=["ident"], semkey="ident")
    p.dma("sp", gq[:], qkg.partition_broadcast(128), w=["gq"], semkey="gq")
    p.dma("sp", lq[:].rearrange("p a b -> p (a b)"), lqd.partition_broadcast(128), w=["lq"], semkey="lq")
    p.dma("sp", lamc[:], lamd.partition_broadcast(128), w=["lamc"], semkey="lamc")
    p.dma("sp", gsub[:], sublnd.partition_broadcast(128), w=["gsub"], semkey="gsub")
    for c4 in range(4):
        p.dma("sp", qT[:, c4 * 2048:(c4 + 1) * 2048], qkT[0, :, c4 * 2048:(c4 + 1) * 2048], w=["qT"], semkey="qT")
        p.dma("sp", kT[:, c4 * 2048:(c4 + 1) * 2048], qkT[1, :, c4 * 2048:(c4 + 1) * 2048], w=["kT"], semkey="kT")
    for c4 in range(4):
        p.dma("sp", vaug[:, c4 * 16:(c4 + 1) * 16, 0:128],
              vd[c4 * 2048:(c4 + 1) * 2048, :].rearrange("(k p) d -> p k d", p=128), w=["vaug_d"], semkey="vaug")
    p.memset("pool", vaug[:, :, 128:129], 1.0, w=["vaug_1"])
    p.ts("dve", gqa[:], gq[:], -1.0, None, ALU.mult, None, r=["gq"], w=["gqa"])
    p.tt("dve", gqa[:], gqa[:], gq[:], ALU.max, r=["gqa", "gq"], w=["gqa"])
    p.op("dve", lambda e: e.tensor_reduce(mx2[:], gqa[:].rearrange("p (a b) -> p a b", a=2), AX.X, ALU.max), r=["gqa"], w=["mx2"])
    p.stt("dve", nshift[:], mx2[:, 0:1], -8.0, mx2[:, 1:2], ALU.mult, ALU.mult, r=["mx2"], w=["nshift"])
    p.tt("dve", lqp[:, 0, :], lq[:, 0, :], lq[:, 1, :], ALU.mult, r=["lq"], w=["lqp0"])
    p.tt("dve", lqp[:, 1, :], lq[:, 2, :], lq[:, 3, :], ALU.mult, r=["lq"], w=["lqp1"])
    p.red(lsum[:], lqp[:], ALU.add, r=["lqp0", "lqp1"], w=["lsum"])
    p.actf(lsum[:], lsum[:], AF.Exp, r=["lsum"], w=["lsum"])
    p.tt("dve", lam[:], lsum[:, 0:1], lsum[:, 1:2], ALU.subtract, r=["lsum"], w=["lam"])
    p.tt("dve", lam[:], lam[:], lamc[:, 0:1], ALU.add, r=["lam", "lamc"], w=["lam"])
    p.ts("dve", gsub[:], gsub[:], lamc[:, 1:2], None, ALU.mult, None, r=["gsub", "lamc"], w=["gsub"])

    step = 0
    for qg in range(16):
        for m in range(2):
            def s_mm(kc, stp):
                b = stp % 2
                p.mm(sps[b][:, :], kT[m * 64:(m + 1) * 64, kc * 128:(kc + 1) * 128], qT[m * 64:(m + 1) * 64, qg * 512:(qg + 1) * 512],
                     True, True, r=["qT", "kT"], w=["sps%d" % b])
            s_mm(0, step)
            for kc in range(NKC):
                b = step % 2
                sl = step % 4
                if kc + 1 < NKC:
                    s_mm(kc + 1, step + 1)
                p.actf(pT[sl][:], sps[b][:, :], AF.Exp, r=["sps%d" % b, "nshift"], w=["pT%d" % sl], bias=nshift[:, 0:1], scale=0.125)
                for qb in range(4):
                    a = acc[m][qb // 2]
                    c0 = (qb % 2) * 256
                    p.mm(a[:, c0:c0 + 129], pT[sl][:, qb * 128:(qb + 1) * 128], vaug[:, kc, 0:129],
                         kc == 0 and qb % 2 == 0, kc == NKC - 1,
                         r=["pT%d" % sl, "vaug_d", "vaug_1"], w=["acc%d_%d" % (m, qb // 2)], skip=True)
                step += 1
            for hf in range(2):
                src = acc[m][hf][:, :].rearrange("p (a b) -> p a b", a=2)[:, :, 0:129]
                p.cp("dve", Os[m][:, hf * 2:hf * 2 + 2, :], src, r=["acc%d_%d" % (m, hf)], w=["Os%d_%d" % (m, hf)])
        p.recip(rr[:, 0:4], Os[0][:, :, 128], r=["Os0_0", "Os0_1"], w=["rr0"])
        p.recip(rr[:, 4:8], Os[1][:, :, 128], r=["Os1_0", "Os1_1"], w=["rr1"])
        p.ts("dve", rr[:, 4:8], rr[:, 4:8], lam[:, 0:1], -1.0, ALU.mult, ALU.mult, r=["rr1", "lam"], w=["rr1"])
        p.tt("dve", ob[:], Os[0][:, :, 0:128], rr[:, 0:4].unsqueeze(2).to_broadcast([128, 4, 128]), ALU.mult,
             r=["Os0_0", "Os0_1", "rr0"], w=["ob"])
        p.tt("pool", Os[1][:, :, 0:128], Os[1][:, :, 0:128], rr[:, 4:8].unsqueeze(2).to_broadcast([128, 4, 128]), ALU.mult,
             r=["rr1"], w=["Os1_0", "Os1_1"])
        p.tt("dve", ob[:], ob[:], Os[1][:, :, 0:128], ALU.add, r=["ob", "Os1_0", "Os1_1"], w=["ob"])
        for qb in range(4):
            p.actf(ojunk[:], ob[:, qb, :], AF.Square, r=["ob"], w=["ojunk", "oss"], accum=oss[:, qb:qb + 1])
        p.ts("dve", oss[:], oss[:], 1.0 / 128, EPS, ALU.mult, ALU.add, r=["oss"], w=["oss"])
        p.actf(oss[:], oss[:], AF.Sqrt, r=["oss"], w=["oss"])
        p.recip(oss[:], oss[:], r=["oss"], w=["oss"])
        p.tt("dve", ob[:], ob[:], oss[:].unsqueeze(2).to_broadcast([128, 4, 128]), ALU.mult, r=["ob", "oss"], w=["ob"])
        p.tt("pool", yb16[:], ob[:], gsub[:].unsqueeze(1).to_broadcast([128, 4, 128]), ALU.mult, r=["ob", "gsub"], w=["yb16"])
        for qb in range(4):
            p.tr(ptb[:, qb * 128:(qb + 1) * 128], yb16[:, qb, :], ident[:], r=["yb16", "ident"], w=["ptb"])
        ys = qg % 2
        p.cp("dve", yst[ys][:], ptb[:, 0:512], r=["ptb"], w=["yst%d" % ys])
        p.dma("sp", yaT_o[:, qg * 512:(qg + 1) * 512], yst[ys][:], r=["yst%d" % ys], semkey="yst%d" % ys)
    p.emit()


def bc(ap, shape, axis):
    return ap.unsqueeze(axis).to_broadcast(list(shape))


def emit_B2(nc, pers, dnT, cwd, identd):
    qTn, kTn, ktok, vtok = pers
    p = Prog(nc)
    ident = p.sb("identB2", [128, 128], BF16)
    onesb = p.sb("onesb", [128, 128], BF16)
    epst = p.sb("epst", [128, 1], F32)
    cw = p.sb("cw", [128, 3, 5], F32)
    xin = [p.sb("xin%d" % i, [128, 2052], F32) for i in range(2)]
    acc = [p.sb("cacc%d" % i, [128, 2048], F32) for i in range(2)]
    sqb = p.sb("sqb", [128, 2048], BF16)
    rin = p.sb("rin", [128, 2048], F32)
    vtb = p.sb("vtb", [128, 2048], BF16)
    ps = [p.ps("psB2_%d" % i, [128, 512], F32) for i in range(4)]
    ptb = [p.ps("ptB2_%d" % i, [128, 1024], BF16) for i in range(2)]
    p.dma("sp", ident[:], identd, w=["ident"], semkey="ident")
    p.dma("sp", cw[:], cwd, w=["cw"], semkey="cw")
    p.memset("pool", onesb[:], 1.0, w=["onesb"])
    p.memset("pool", epst[:], EPS, w=["epst"])
    it = 0
    tcnt = 0
    for tg in range(4):
        for X in range(3):
            sl = it % 2
            it += 1
            xk, ak = "xin%d" % sl, "cacc%d" % sl
            lo = tg * 2048 - 2
            hi = tg * 2048 + 2050
            if tg == 0:
                p.memset("pool", xin[sl][:, 0:2], 0.0, w=[xk])
                p.dma("sp", xin[sl][:, 2:2052], dnT[X, :, 0:2050], w=[xk + "d"], r=[xk], semkey=xk)
            elif tg == 3:
                p.memset("pool", xin[sl][:, 2050:2052], 0.0, w=[xk])
                p.dma("sp", xin[sl][:, 0:2050], dnT[X, :, lo:S], w=[xk + "d"], r=[xk], semkey=xk)
            else:
                p.memset("pool", xin[sl][:, 0:1], 0.0, w=[xk])
                p.dma("sp", xin[sl][:, :], dnT[X, :, lo:hi], w=[xk + "d"], r=[xk], semkey=xk)
            eng = "dve"
            a = acc[sl]
            p.ts(eng, a[:], xin[sl][:, 0:2048], cw[:, X, 0:1], None, ALU.mult, None, r=[xk + "d", "cw"], w=[ak])
            for j in range(1, 5):
                p.stt(eng, a[:], xin[sl][:, j:j + 2048], cw[:, X, j:j + 1], a[:], ALU.mult, ALU.add, r=[xk + "d", "cw", ak], w=[ak])
            p.actf(a[:], a[:], AF.Silu, r=[ak, xk + "d"], w=[ak, xk])
            if X == 2:
                p.cp("pool", vtb[:], a[:], r=[ak], w=["vtb"])
                src_b, dst, skey = vtb, vtok, "vtb"
            else:
                p.tt("pool", sqb[:], a[:], a[:], ALU.mult, r=[ak], w=["sqb"])
                for blk in range(4):
                    p.mm(ps[blk][:, :], onesb[:], sqb[:, blk * 512:(blk + 1) * 512], True, True, r=["onesb", "sqb"], w=["psn%d" % blk])
                    p.actf(rin[:, blk * 512:(blk + 1) * 512], ps[blk][:, :], AF.Sqrt, r=["psn%d" % blk, "epst"], w=["rin%d" % blk],
                           bias=epst[:, 0:1])
                p.recip(rin[:], rin[:], r=["rin%d" % b_ for b_ in range(4)], w=["rin"])
                tgt = qTn if X == 0 else kTn
                sc = (128.0 ** -0.5) if X == 0 else 1.0
                p.stt("dve", tgt[:, tg * 2048:(tg + 1) * 2048], a[:], sc, rin[:], ALU.mult, ALU.mult, r=[ak, "rin"],
                      w=["nT%d_%d" % (X, tg)] + ["rin%d" % b_ for b_ in range(4)])
                src_b, dst, skey = kTn, ktok, "nT1_%d" % tg
            if X >= 1:
                for half in range(2):
                    pb = tcnt % 2
                    tcnt += 1
                    for c in range(8):
                        cc = half * 8 + c
                        if X == 2:
                            src = vtb[:, cc * 128:(cc + 1) * 128]
                        else:
                            src = kTn[:, tg * 2048 + cc * 128:tg * 2048 + (cc + 1) * 128]
                        p.tr(ptb[pb][:, c * 128:(c + 1) * 128], src, ident[:], r=[skey, "ident"], w=["ptb%d" % pb])
                    n0 = tg * 16 + half * 8
                    p.cp("act", dst[:, n0:n0 + 8, :], ptb[pb][:].rearrange("p (k t) -> p k t", t=128), r=["ptb%d" % pb],
                         w=["tok%d_%d" % (X, n0)])
    p.emit()


def emit_B3(nc, pers, o_scr, bgd, masksd, identd):
    qTn, kTn, ktok, vtok = pers
    p = Prog(nc)
    identb = p.sb("identB3", [128, 128], BF16)
    identf = p.sb("identfB3", [128, 128], F32)
    msk = p.sb("msk", [128, 8, 128], F32)
    onesf = p.sb("onesf", [128, 128], F32)
    bg = p.sb("bgt", [128, NKC, 4], F32)
    negb = p.sb("negb", [128, NKC, 2], F32)
    EG = [p.sb("EG%d" % d, [128, NKC], F32) for d in range(2)]
    ER = [p.sb("ER%d" % d, [128, NKC], F32) for d in range(2)]
    EL = [p.sb("EL%d" % d, [128, NKC], F32) for d in range(2)]
    Sst = [p.sb("Sst%d" % d, [128, 128], F32) for d in range(2)]
    vnew = [[p.sb("vnew%d_%d" % (d, i), [128, 128], F32) for i in range(2)] for d in range(2)]
    ost = [[p.sb("ost%d_%d" % (d, i), [128, 128], F32) for i in range(2)] for d in range(2)]
    grp = {}
    for d in range(2):
        for s_ in range(2):
            for nm in ("u", "wT", "qdT", "qkT", "kdec"):
                grp[(nm, d, s_)] = p.sb("%s%d_%d" % (nm, d, s_), [128, 4, 128], F32)
    T3 = {}
    for d in range(2):
        for nm in ("gtri", "DTr", "EGr", "tmpf", "X", "XT", "A", "AT", "P", "PT"):
            T3[(nm, d)] = p.sb("%s_%d" % (nm, d), [128, 4, 128], F32)
        for nm in ("kgb", "Pb16"):
            T3[(nm, d)] = p.sb("%s_%d" % (nm, d), [128, 4, 128], BF16)
    psb = [p.ps("psB3_%d" % i, [128, 512], F32) for i in range(3)]
    ptb = p.ps("ptB3", [128, 512], F32)
    pssA = [p.ps("psscA%d" % d, [128, 512], F32) for d in range(2)]
    pssB = [p.ps("psscB%d" % d, [128, 512], F32) for d in range(2)]

    p.dma("sp", identb[:], identd, w=["identb_"], semkey="ident")
    p.cp("dve", identf[:], identb[:], r=["identb_"], w=["ident"])
    p.dma("sp", msk[:], masksd.rearrange("a p f -> p a f"), w=["msk"], semkey="msk")
    p.dma("sp", bg[:], bgd, w=["bg"], semkey="bg")
    p.memset("pool", onesf[:], 1.0, w=["onesf"])
    for d in range(2):
        p.memset("pool", Sst[d][:], 0.0, w=["S%d" % d])
    p.ts("dve", negb[:], bg[:, :, 0:2], -1.0, None, ALU.mult, None, r=["bg"], w=["negb"])
    for d in range(2):
        g = bg[:, :, 2 + d]
        mGC = msk[:, 0, :] if d == 0 else msk[:, 2, :]
        mGR = msk[:, 3, :] if d == 0 else msk[:, 1, :]
        for i, (lhs, dst, nm) in enumerate(((mGC, EG[d], "EG"), (mGR, ER[d], "ER"), (onesf[:], EL[d], "EL"))):
            p.mm(psb[i][:, 0:NKC], lhs, g, True, True, r=["msk", "onesf", "bg"], w=["ps%d" % i])
            p.actf(dst[:], psb[i][:, 0:NKC], AF.Exp, r=["ps%d" % i], w=["%s%d" % (nm, d)])

    sh3 = [128, 4, 128]

    def fl(t):
        return t[:].rearrange("p a b -> p (a b)")

    def d3_stages(d, G):
        s_ = G % 2
        n0 = 4 * G if d == 0 else 60 - 4 * G
        MI = msk[:, 0, :] if d == 0 else msk[:, 2, :]
        MS = msk[:, 1, :] if d == 0 else msk[:, 3, :]
        MG = msk[:, 3, :] if d == 0 else msk[:, 1, :]
        u, wT, qdT, qkT, kdec = [grp[(nm, d, s_)] for nm in ("u", "wT", "qdT", "qkT", "kdec")]
        gk = lambda nm: "%s%d_%d" % (nm, d, s_)
        D = "_%d" % d
        gtri, DTr, EGr, tmpf, X, XT, A, AT, P_, PT, kgb, Pb16 = [T3[(nm, d)] for nm in
            ("gtri", "DTr", "EGr", "tmpf", "X", "XT", "A", "AT", "P", "PT", "kgb", "Pb16")]
        Y, Xc, XcT, Y2 = tmpf, DTr, EGr, gtri
        kY, kXc, kXcT, kY2 = "tmpf" + D, "DTr" + D, "EGr" + D, "gtri" + D
        kX, kXT, kA, kAT, kP, kPT = ["%s%s" % (nm, D) for nm in ("X", "XT", "A", "AT", "P", "PT")]
        st = []

        def mm4(bank, bkey, lhs, rhs, r):
            for j in range(4):
                p.mm(psb[bank][:, j * 128:(j + 1) * 128], lhs[:, j, :], rhs[:, j, :], True, True, r=r, w=[bkey])

        def s1():
            p.tt("pool", gtri[:], bc(MI, sh3, 1), bc(bg[:, n0:n0 + 4, 2 + d], sh3, 2), ALU.mult, r=["msk", "bg"], w=[kY2])
            p.mm(psb[0][:, :], MG, fl(gtri), True, True, r=["msk", kY2], w=["ps0"])
            p.mm(psb[1][:, :], onesf[:], fl(gtri), True, True, r=["onesf", kY2], w=["ps1"])
            p.actf(fl(DTr), psb[0][:, :], AF.Exp, r=["ps0"], w=[kXc])
            p.actf(fl(EGr), psb[1][:, :], AF.Exp, r=["ps1"], w=[kXcT])
        st.append(s1)

        def s2():
            for j in range(4):
                c0 = (n0 + j) * 128
                p.mm(psb[2][:, j * 128:(j + 1) * 128], kTn[:, c0:c0 + 128], kTn[:, c0:c0 + 128], True, True, r=[], w=["ps2"])
            for j in range(4):
                c0 = (n0 + j) * 128
                p.mm(psb[0][:, j * 128:(j + 1) * 128], kTn[:, c0:c0 + 128], qTn[:, c0:c0 + 128], True, True, r=[], w=["ps0"])
            ps2v = psb[2][:, :].rearrange("p (a b) -> p a b", a=4)
            ps0v = psb[0][:, :].rearrange("p (a b) -> p a b", a=4)
            p.tt("dve", tmpf[:], ps2v, DTr[:], ALU.mult, r=["ps2", kXc], w=[kY])
            p.tt("dve", qkT[:], ps0v, DTr[:], ALU.mult, r=["ps0", kXc], w=[gk("qkT")])
            p.tt("pool", tmpf[:], tmpf[:], bc(negb[:, n0:n0 + 4, d], sh3, 2), ALU.mult, r=[kY, "negb"], w=[kY])
            p.tt("pool", X[:], tmpf[:], bc(MS, sh3, 1), ALU.mult, r=[kY, "msk"], w=[kX])
            p.tt("pool", qkT[:], qkT[:], bc(MI, sh3, 1), ALU.mult, r=[gk("qkT"), "msk"], w=[gk("qkT")])
            p.tt("pool", fl(qdT), qTn[:, n0 * 128:(n0 + 4) * 128], fl(EGr), ALU.mult, r=[kXcT], w=[gk("qdT")])
            p.tt("pool", kgb[:], ktok[:, n0:n0 + 4, :], bc(EG[d][:, n0:n0 + 4], sh3, 2), ALU.mult, r=["EG%d" % d], w=["kgb" + D])
            p.tt("pool", kdec[:], ktok[:, n0:n0 + 4, :], bc(ER[d][:, n0:n0 + 4], sh3, 2), ALU.mult, r=["ER%d" % d], w=[gk("kdec")])
            for j in range(4):
                p.tr(ptb[:, j * 128:(j + 1) * 128], X[:, j, :], identf[:], r=[kX, "ident"], w=["ptb"])
            p.cp("act", fl(XT), ptb[:, :], r=["ptb"], w=[kXT])
        st.append(s2)

        def s3():
            BD = bc(msk[:, 4, :], sh3, 1)
            p.tt("pool", A[:], X[:], BD, ALU.mult, r=[kX, "msk"], w=[kA])
            p.tt("pool", AT[:], XT[:], BD, ALU.mult, r=[kXT, "msk"], w=[kAT])
            p.tt("pool", P_[:], A[:], bc(identf[:], sh3, 1), ALU.add, r=[kA, "ident"], w=[kP])
            p.tt("pool", PT[:], AT[:], bc(identf[:], sh3, 1), ALU.add, r=[kAT, "ident"], w=[kPT])
        st.append(s3)

        def base(k):
            def f():
                mm4(0, "ps0", AT, A, [kA, kAT])
                mm4(1, "ps1", A, AT, [kA, kAT])
                p.cp("dve", fl(A), psb[0][:, :], r=["ps0"], w=[kA])
                p.cp("act", fl(AT), psb[1][:, :], r=["ps1"], w=[kAT])
                mm4(2, "ps2", AT, P_, [kAT, kP])
                mm4(0, "ps0", A, PT, [kA, kPT])
                p.tt("dve", fl(P_), fl(P_), psb[2][:, :], ALU.add, r=[kP, "ps2"], w=[kP])
                p.tt("dve", fl(PT), fl(PT), psb[0][:, :], ALU.add, r=[kPT, "ps0"], w=[kPT])
            return f
        for k in range(1, 4):
            st.append(base(k))

        def merge(mi):
            def f():
                last = (mi == 2)
                CM = bc(msk[:, 5 + mi, :], sh3, 1)
                p.tt("pool", Xc[:], X[:], CM, ALU.mult, r=[kX, "msk"], w=[kXc])
                p.tt("pool", XcT[:], XT[:], CM, ALU.mult, r=[kXT, "msk"], w=[kXcT])
                mm4(0, "ps0", XcT, P_, [kXcT, kP])
                p.cp("act", fl(Y), psb[0][:, :], r=["ps0"], w=[kY])
                if not last:
                    mm4(1, "ps1", Xc, PT, [kXc, kPT])
                    p.cp("dve", fl(Y2), psb[1][:, :], r=["ps1"], w=[kY2])
                mm4(2, "ps2", PT, Y, [kPT, kY])
                if not last:
                    mm4(0, "ps0", P_, Y2, [kP, kY2])
                p.tt("dve", fl(P_), fl(P_), psb[2][:, :], ALU.add, r=[kP, "ps2"], w=[kP])
                if not last:
                    p.tt("dve", fl(PT), fl(PT), psb[0][:, :], ALU.add, r=[kPT, "ps0"], w=[kPT])
            return f
        for mi in range(3):
            st.append(merge(mi))

        def s_fin():
            p.cp("pool", Pb16[:], P_[:], r=[kP], w=["Pb16" + D])
            for j in range(4):
                p.mm(psb[2][:, j * 128:(j + 1) * 128], Pb16[:, j, :], vtok[:, n0 + j, :], True, True, r=["Pb16" + D], w=["ps2"])
                p.mm(psb[1][:, j * 128:(j + 1) * 128], kgb[:, j, :], Pb16[:, j, :], True, True, r=["Pb16" + D, "kgb" + D], w=["ps1"])
            p.tt("dve", u[:], psb[2][:, :].rearrange("p (a b) -> p a b", a=4), bc(bg[:, n0:n0 + 4, d], sh3, 2), ALU.mult,
                 r=["ps2", "bg"], w=[gk("u")])
            p.cp("act", fl(wT), psb[1][:, :], r=["ps1"], w=[gk("wT")])
        st.append(s_fin)
        return st

    vcnt = [0, 0]

    def scan_step(d, G, j):
        s_ = G % 2
        n0 = 4 * G if d == 0 else 60 - 4 * G
        n = n0 + j
        u, wT, qdT, qkT, kdec = [grp[(nm, d, s_)] for nm in ("u", "wT", "qdT", "qkT", "kdec")]
        gk = lambda nm: "%s%d_%d" % (nm, d, s_)
        vs = vcnt[d] % 2
        vcnt[d] += 1
        vn = vnew[d][vs]
        vk = "vnew%d_%d" % (d, vs)
        os_ = ost[d][vs]
        ok = "ost%d_%d" % (d, vs)
        ps1, ps3, ps2 = pssA[d][:, 0:128], pssA[d][:, 256:384], pssB[d][:, 0:128]
        kS = "S%d" % d
        kA, kB = "scA%d" % d, "scB%d" % d
        p.mm(ps1, wT[:, j, :], Sst[d][:], True, True, r=[gk("wT"), kS], w=[kA])
        p.mm(ps2, qdT[:, j, :], Sst[d][:], True, False, r=[gk("qdT"), kS], w=[kB])
        p.stt("dve", vn[:], ps1, negb[:, n, d:d + 1], u[:, j, :], ALU.mult, ALU.add, r=[kA, "negb", gk("u")], w=[vk])
        p.mm(ps3, kdec[:, j, :], vn[:], True, True, r=[gk("kdec"), vk], w=[kA])
        p.mm(ps2, qkT[:, j, :], vn[:], False, True, r=[gk("qkT"), vk], w=[kB])
        p.stt("dve", Sst[d][:], Sst[d][:], EL[d][:, n:n + 1], ps3, ALU.mult, ALU.add, r=[kS, "EL%d" % d, kA], w=[kS])
        p.cp("act", os_[:], ps2, r=[kB], w=[ok])
        p.dma("sp", o_scr[d, n, :, :], os_[:], r=[ok], semkey=ok)

    NG = DBG_NG
    lists = [d3_stages(0, 0), d3_stages(1, 0)]
    while lists[0] or lists[1]:
        for L in lists:
            if L:
                L.pop(0)()
    for G in range(NG):
        nxt = [d3_stages(0, G + 1), d3_stages(1, G + 1)] if G + 1 < NG else [[], []]
        steps = []
        for j in range(4):
            steps.append((0, G, j))
            steps.append((1, G, 3 - j))
        while steps or nxt[0] or nxt[1]:
            if nxt[0]:
                nxt[0].pop(0)()
            if steps:
                scan_step(*steps.pop(0))
            if nxt[1]:
                nxt[1].pop(0)()
            if steps:
                scan_step(*steps.pop(0))
    p.emit()


def emit_B4(nc, o_scr, zd, ogd, identd, ybT_o):
    p = Prog(nc)
    ident = p.sb("identB4", [128, 128], BF16)
    og = p.sb("og", [128, 128], F32)
    zt = p.sb("zt", [128, NKC, 128], F32)
    sq = p.sb("sqB4", [128, 4, 128], F32)
    of_ = [p.sb("ofB4_%d" % i, [128, 4, 128], F32) for i in range(2)]
    ob_ = [p.sb("obB4_%d" % i, [128, 4, 128], F32) for i in range(2)]
    ss = p.sb("ssB4", [128, 4], F32)
    yb = p.sb("ybB4", [128, 4, 128], BF16)
    yst = [p.sb("ystB4_%d" % i, [128, 512], BF16) for i in range(2)]
    ptb = p.ps("ptB4", [128, 1024], BF16)
    p.dma("sp", ident[:], identd, w=["ident"], semkey="ident")
    p.dma("sp", og[:], ogd.partition_broadcast(128), w=["og"], semkey="og")
    for c4 in range(4):
        p.dma("sp", zt[:, c4 * 16:(c4 + 1) * 16, :], zd[c4 * 2048:(c4 + 1) * 2048, :].rearrange("(k p) d -> p k d", p=128),
              w=["zt%d" % c4], semkey="zt%d" % c4)
    for c4 in range(4):
        v = zt[:, c4 * 16:(c4 + 1) * 16, :].rearrange("p a b -> p (a b)")
        p.actf(v, v, AF.Silu, r=["zt%d" % c4], w=["zt%d" % c4])
    for G in range(16):
        n0 = 4 * G
        gs = G % 2
        p.dma("sp", of_[gs][:], o_scr[0, n0:n0 + 4, :, :].rearrange("n p e -> p n e"), w=["of%d" % gs], semkey="of%d" % gs)
        p.dma("sp", ob_[gs][:], o_scr[1, n0:n0 + 4, :, :].rearrange("n p e -> p n e"), w=["ob%d" % gs], semkey="ob%d" % gs)
        p.tt("pool", of_[gs][:], of_[gs][:], ob_[gs][:], ALU.add, r=["of%d" % gs, "ob%d" % gs], w=["of%d" % gs])
        o = of_[gs][:]
        p.tt("pool", sq[:], o, o, ALU.mult, r=["of%d" % gs], w=["sq"])
        p.red(ss[:], sq[:], ALU.add, r=["sq"], w=["ss"])
        p.ts("dve", ss[:], ss[:], 1.0 / 128, EPS, ALU.mult, ALU.add, r=["ss"], w=["ss"])
        p.actf(ss[:], ss[:], AF.Sqrt, r=["ss"], w=["ss"])
        p.recip(ss[:], ss[:], r=["ss"], w=["ss"])
        p.tt("dve", sq[:], o, bc(ss[:], [128, 4, 128], 2), ALU.mult, r=["ss", "sq", "of%d" % gs], w=["sq"])
        p.tt("pool", sq[:], sq[:], bc(og[:], [128, 4, 128], 1), ALU.mult, r=["sq", "og"], w=["sq"])
        p.tt("pool", yb[:], sq[:], zt[:, n0:n0 + 4, :], ALU.mult, r=["sq", "zt%d" % (G // 4)], w=["yb"])
        for j in range(4):
            p.tr(ptb[:, j * 128:(j + 1) * 128], yb[:, j, :], ident[:], r=["yb", "ident"], w=["ptb"])
        ys = G % 2
        p.cp("dve", yst[ys][:], ptb[:, 0:512], r=["ptb"], w=["yst%d" % ys])
        p.dma("sp", ybT_o[:, n0 * 128:(n0 + 4) * 128], yst[ys][:], r=["yst%d" % ys], semkey="ystB4_%d" % ys)
    p.emit()


def build_B(parts=("B1", "B2", "B3", "B4")):
    nc = bass.Bass("TRN2", target_bir_lowering=False)
    qkT = din(nc, "qkT", [2, 128, S], BF16)
    vd = din(nc, "v", [S, 128], BF16)
    qkg = din(nc, "qkg", [1, 128], F32)
    lqd = din(nc, "lq_d", [1, 256], F32)
    lamd = din(nc, "lamc_d", [1, 2], F32)
    sub = din(nc, "subln", [1, 128], F32)
    identd = din(nc, "identb", [128, 128], BF16)
    dnT = din(nc, "dnT", [3, 128, S], F32)
    cwd = din(nc, "cw_d", [128, 3, 5], F32)
    bgd = din(nc, "bg_d", [128, NKC, 4], F32)
    masksd = din(nc, "masks", [8, 128, 128], F32)
    zd = din(nc, "z", [S, 128], F32)
    ogd = din(nc, "og_d", [1, 128], F32)
    yaT = dout(nc, "yaT", [128, S], BF16)
    ybT = dout(nc, "ybT", [128, S], BF16)
    if "B1" in parts:
        emit_B1(nc, qkT, vd, qkg, lqd, lamd, sub, identd, yaT)
    with ExitStack() as outer:
        pers = [outer.enter_context(nc.sbuf_tensor(nm, shp, BF16)) for nm, shp in
                (("qTn", [128, S]), ("kTn", [128, S]), ("ktok", [128, NKC, 128]), ("vtok", [128, NKC, 128]))]
        o_scr = nc.dram_tensor("o_scr", [2, NKC, 128, 128], F32).ap()
        if "B2" in parts:
            emit_B2(nc, pers, dnT, cwd, identd)
        if "B3" in parts:
            emit_B3(nc, pers, o_scr, bgd, masksd, identd)
        if DBG_DUMP:
            p = Prog(nc)
            for i, (nm, shp, dt_, t_) in enumerate((("dq", [128, S], BF16, pers[0]), ("dk", [128, S], BF16, pers[1]),
                                            ("dkt", [128, NKC, 128], BF16, pers[2]), ("dvt", [128, NKC, 128], BF16, pers[3]))):
                o_ = dout(nc, nm, shp, dt_)
                p.dma("sp", o_, t_[:], semkey="dump%d" % i)
            p.emit()
        if "B4" in parts:
            emit_B4(nc, o_scr, zd, ogd, identd, ybT)
    return nc


def _masks():
    i = np.arange(128)
    pp, ff = i[:, None], i[None, :]
    bd = lambda b_: (pp // b_) == (ff // b_)
    return np.stack([pp <= ff, pp < ff, pp >= ff, pp > ff, bd(16), bd(32) & ~bd(16), bd(64) & ~bd(32), ~bd(64)]).astype(np.float32)


def run_B(inp, l, resA):
    nc = _get("B", build_B)
    identb = np.eye(128, dtype=np.float32).astype(NPBF)
    masks = _masks()
    li = 0.8 - 0.6 * math.exp(-0.3 * l)
    lamc = np.array([[li, 1.0 - li]], np.float32)
    cw = inp["dn_conv"][l]
    maps = []
    for b in range(NB):
        rs = resA[b * 4:(b + 1) * 4]
        bgc = np.concatenate([r["bg"] for r in rs], axis=1)
        for h in range(4):
            qT = np.concatenate([r["qkT"][h] for r in rs], axis=1)
            kT = np.concatenate([r["qkT"][4 + h] for r in rs], axis=1)
            maps.append({
                "qkT": np.ascontiguousarray(np.stack([qT, kT])),
                "v": np.ascontiguousarray(np.concatenate([r["v"][:, h * 128:(h + 1) * 128] for r in rs], axis=0)),
                "qkg": np.ascontiguousarray(inp["qk_norm_gain"][l].reshape(1, 128)),
                "lq_d": np.ascontiguousarray(inp["diff_lambda"][l].reshape(1, 256)),
                "lamc_d": lamc,
                "subln": np.ascontiguousarray(inp["diff_subln_gain"][l][None, :]),
                "identb": identb,
                "dnT": np.ascontiguousarray(np.stack([np.concatenate([r["dnT"][X * 4 + h] for r in rs], axis=1) for X in range(3)])),
                "cw_d": np.ascontiguousarray(np.stack([cw[:, X * 512 + h * 128:X * 512 + (h + 1) * 128].T for X in range(3)], axis=1)),
                "bg_d": np.ascontiguousarray(bgc[:, :, [h, 4 + h, 8 + h, 12 + h]]),
                "masks": masks,
                "z": np.ascontiguousarray(np.concatenate([r["z"][:, h * 128:(h + 1) * 128] for r in rs], axis=0)),
                "og_d": np.ascontiguousarray(inp["dn_out_gain"][l][None, :]),
            })
    res = run_bass_kernel_spmd(nc, maps, core_ids=list(range(8)))
    return res.results


def kernel(**inp):
    inp = {k: np.asarray(v) for k, v in inp.items()}
    x = inp["x"]
    xsh = [x[i // 4, (i % 4) * NTOK:(i % 4 + 1) * NTOK] for i in range(8)]
    for l in range(2):
        resA = run_A(inp, l, xsh)
        resB = run_B(inp, l, resA)
        yaT, ybT, gT = [], [], []
        for i in range(8):
            b, t0 = i // 4, (i % 4) * NTOK
            yaT.append(np.stack([resB[b * 4 + h]["yaT"][:, t0:t0 + NTOK] for h in range(4)]))
            ybT.append(np.stack([resB[b * 4 + h]["ybT"][:, t0:t0 + NTOK] for h in range(4)]))
            gT.append(resA[i]["gT"])
        xsh = run_C(inp, l, xsh, yaT, ybT, gT)
    out = np.empty((NB, S, D), np.float32)
    for i in range(8):
        out[i // 4, (i % 4) * NTOK:(i % 4 + 1) * NTOK] = xsh[i]
    return out
```

```python
from contextlib import ExitStack
import math
import numpy as np
import ml_dtypes
import concourse.bass as bass
import concourse.mybir as mybir
from concourse.bass_utils import run_bass_kernel_spmd

F32 = mybir.dt.float32
BF16 = mybir.dt.bfloat16
ALU = mybir.AluOpType
AF = mybir.ActivationFunctionType
AX = mybir.AxisListType
NPBF = ml_dtypes.bfloat16

ENGS = ("pe", "act", "dve", "pool", "sp")

D = 1024
S = 8192
NB = 2
NTOK = 2048
NT = NTOK // 128
EPS = 1e-6
INC = 5648


class Op:
    __slots__ = ("eng", "fn", "deps", "is_dma", "semkey", "cum", "sig", "waits", "has_dep")

    def __init__(self, eng, fn, is_dma=False, semkey=None):
        self.eng = eng
        self.fn = fn
        self.deps = []
        self.is_dma = is_dma
        self.semkey = semkey
        self.cum = 0
        self.sig = 0
        self.has_dep = False
        self.waits = []


class Prog:
    def __init__(self, nc):
        self.nc = nc
        self.es = ExitStack()
        self.ops = {e: [] for e in ENGS}
        self.lastw = {}
        self.readers = {}
        self.dma_cum = {}

    def sb(self, name, shape, dtype):
        return self.es.enter_context(self.nc.sbuf_tensor(name, list(shape), dtype))

    def ps(self, name, shape, dtype=F32):
        return self.es.enter_context(self.nc.psum_tensor(name, list(shape), dtype))

    def _rec(self, op, r, w):
        deps = []
        for k in r:
            lw = self.lastw.get(k)
            if lw is not None:
                deps.append(lw)
        for k in w:
            lw = self.lastw.get(k)
            rd = self.readers.get(k, [])
            if rd:
                deps.extend(rd)
            elif lw is not None:
                if not (op.is_dma and lw.is_dma and lw.semkey == op.semkey):
                    deps.append(lw)
        for k in r:
            self.readers.setdefault(k, []).append(op)
        for k in w:
            self.lastw[k] = op
            self.readers[k] = []
        seen = set()
        for d in deps:
            if d is op or id(d) in seen:
                continue
            seen.add(id(d))
            if d.eng == "pe" and op.eng == "pe" and not d.is_dma and not op.is_dma:
                continue
            op.deps.append(d)
            d.has_dep = True
        self.ops[op.eng].append(op)
        return op

    def op(self, eng, fn, r=(), w=()):
        return self._rec(Op(eng, fn), r, w)

    def dma(self, q, out, in_, r=(), w=(), semkey=None, **kw):
        assert semkey is not None
        o = Op(q, lambda e: e.dma_start(out=out, in_=in_, **kw), is_dma=True, semkey=semkey)
        self._rec(o, r, w)
        self.dma_cum[semkey] = self.dma_cum.get(semkey, 0) + 16
        o.cum = self.dma_cum[semkey]
        return o

    def mm(self, out, lhsT, rhs, start, stop, r, w, skip=False):
        if skip:
            return self.op("pe", lambda e: e.matmul(out, lhsT, rhs, start=start, stop=stop, skip_group_check=True), r, w)
        return self.op("pe", lambda e: e.matmul(out, lhsT, rhs, start=start, stop=stop), r, w)

    def tr(self, out, in_, ident, r, w):
        return self.op("pe", lambda e: e.transpose(out, in_, ident), r, w)

    def actf(self, out, in_, func, r, w, bias=None, scale=None, accum=None):
        kw = {}
        if bias is not None:
            kw["bias"] = bias
        if scale is not None:
            kw["scale"] = scale
        if accum is not None:
            kw["accum_out"] = accum
        return self.op("act", lambda e: e.activation(out, in_, func, **kw), r, w)

    def tt(self, eng, out, a, b, op, r, w):
        return self.op(eng, lambda e: e.tensor_tensor(out, a, b, op), r, w)

    def ts(self, eng, out, a, s1, s2, op0, op1, r, w):
        if op1 is None:
            return self.op(eng, lambda e: e.tensor_scalar(out, a, s1, None, op0), r, w)
        return self.op(eng, lambda e: e.tensor_scalar(out, a, s1, s2, op0, op1), r, w)

    def stt(self, eng, out, in0, scalar, in1, op0, op1, r, w):
        return self.op(eng, lambda e: e.scalar_tensor_tensor(out, in0, scalar, in1, op0, op1), r, w)

    def cp(self, eng, out, in_, r, w):
        if eng == "act":
            return self.op("act", lambda e: e.copy(out, in_), r, w)
        return self.op(eng, lambda e: e.tensor_copy(out, in_), r, w)

    def red(self, out, in_, op, r, w):
        return self.op("dve", lambda e: e.tensor_reduce(out, in_, AX.X, op), r, w)

    def recip(self, out, in_, r, w):
        return self.op("dve", lambda e: e.reciprocal(out, in_), r, w)

    def memset(self, eng, out, val, w):
        return self.op(eng, lambda e: e.memset(out, val), (), w)

    def emit(self):
        nc = self.nc
        es = self.es
        esem = {e: es.enter_context(nc.semaphore("s_" + e)) for e in ENGS}
        dsem = {k: es.enter_context(nc.semaphore("d_%d" % i)) for i, k in enumerate(self.dma_cum)}
        for e in ENGS:
            c = 0
            for o in self.ops[e]:
                if not o.is_dma and o.has_dep:
                    c += 1
                    o.sig = c
        for e in ENGS:
            waited = {}
            for o in self.ops[e]:
                need = {}
                for d in o.deps:
                    if d.is_dma:
                        key = ("d", d.semkey)
                        val = d.cum
                    else:
                        key = ("e", d.eng)
                        val = d.sig
                    if val > need.get(key, 0):
                        need[key] = val
                for key, val in need.items():
                    if waited.get(key, 0) >= val:
                        continue
                    waited[key] = val
                    sem = dsem[key[1]] if key[0] == "d" else esem[key[1]]
                    o.waits.append((sem, val))
        final = [(dsem[k], v) for k, v in self.dma_cum.items()]
        block = es.enter_context(nc.Block())

        def run(eng, name):
            for o in self.ops[name]:
                for sem, val in o.waits:
                    eng.wait_ge(sem, val)
                ins = o.fn(eng)
                if o.is_dma:
                    ins.then_inc(dsem[o.semkey], 16)
                elif o.has_dep:
                    ins.then_inc(esem[name], 1)
            if name == "sp":
                for sem, val in final:
                    eng.wait_ge(sem, val)

        @block.tensor
        def _(eng):
            run(eng, "pe")

        @block.scalar
        def _(eng):
            run(eng, "act")

        @block.vector
        def _(eng):
            run(eng, "dve")

        @block.gpsimd
        def _(eng):
            run(eng, "pool")

        @block.sync
        def _(eng):
            run(eng, "sp")

        es.close()


def din(nc, name, shape, dt):
    return nc.dram_tensor(name, list(shape), dt, kind="ExternalInput").ap()


def dout(nc, name, shape, dt):
    return nc.dram_tensor(name, list(shape), dt, kind="ExternalOutput").ap()


def emit_mod(p, nc, ada_w, ada_b, cpk, col0, ncols, stage, psb, modbc):
    cact = p.sb("cact", [128, 8], F32)
    crep = p.sb("crep", [128, 8, 128], F32)
    p.dma("sp", cact[:], cpk, w=["cact"], semkey="cact")
    p.dma("sp", modbc[:, 0:ncols], ada_b[:, col0:col0 + ncols].partition_broadcast(128), w=["modbc"], semkey="modbc")
    p.actf(cact[:], cact[:], AF.Silu, r=["cact"], w=["cact"])
    p.cp("dve", crep[:], cact[:].unsqueeze(2).to_broadcast([128, 8, 128]), r=["cact"], w=["crep"])
    npass = (ncols + 2047) // 2048
    for ps_ in range(npass):
        c0 = ps_ * 2048
        cw = min(2048, ncols - c0)
        for kc in range(8):
            sl = kc % 2
            p.dma("sp", stage[sl][:, 0:cw], ada_w[kc * 128:(kc + 1) * 128, col0 + c0:col0 + c0 + cw],
                  w=["stg%d" % sl], semkey="stg%d" % sl)
            for j in range(cw // 512):
                p.mm(psb[j][:, :], crep[:, kc, :], stage[sl][:, j * 512:(j + 1) * 512], kc == 0, kc == 7,
                     r=["crep", "stg%d" % sl], w=["psb%d" % j])
        for j in range(cw // 512):
            cs = c0 + j * 512
            p.tt("dve", modbc[:, cs:cs + 512], modbc[:, cs:cs + 512], psb[j][:, :], ALU.add,
                 r=["psb%d" % j, "modbc"], w=["modbc"])


def emit_rmsnorm_T(p, xt, xkey, G, gkey, SH, shkey, junk, ssk, hb, ptb, ptbkey, hT, t, ident, tag):
    ss = ssk
    p.actf(junk[:], xt, AF.Square, r=[xkey], w=["junk" + tag, "ss" + tag], accum=ss[:, 0:1])
    p.ts("dve", ss[:, 0:1], ss[:, 0:1], 1.0 / D, EPS, ALU.mult, ALU.add, r=["ss" + tag], w=["ss" + tag])
    p.actf(ss[:, 0:1], ss[:, 0:1], AF.Sqrt, r=["ss" + tag], w=["ss" + tag])
    p.recip(ss[:, 0:1], ss[:, 0:1], r=["ss" + tag], w=["ss" + tag])
    p.stt("dve", junk[:], xt, ss[:, 0:1], G, ALU.mult, ALU.mult, r=[xkey, "ss" + tag, gkey], w=["junk" + tag])
    p.tt("dve", hb[:], junk[:], SH, ALU.add, r=["junk" + tag, shkey], w=["hb" + tag])
    for kc in range(8):
        p.tr(ptb[:, kc * 128:(kc + 1) * 128], hb[:, kc * 128:(kc + 1) * 128], ident[:], r=["hb" + tag, "ident"], w=[ptbkey])
    p.cp("act", hT[:, :, t * 128:(t + 1) * 128], ptb[:].rearrange("p (k t) -> p k t", t=128), r=[ptbkey], w=["hT%d" % t])


def build_A():
    nc = bass.Bass("TRN2", target_bir_lowering=False)
    x = din(nc, "x", [NTOK, D], F32)
    cpk = din(nc, "cpk", [128, 8], F32)
    ada_w = din(nc, "ada_w", [D, 6 * D], F32)
    ada_b = din(nc, "ada_b", [1, 6 * D], F32)
    g1 = din(nc, "g1", [1, D], F32)
    w_in = din(nc, "w_in", [D, INC], F32)
    qkg = din(nc, "qkg", [1, 128], F32)
    alog = din(nc, "alog", [1, 8], F32)
    dtb = din(nc, "dtb", [1, 8], F32)
    cosd = din(nc, "cos", [NTOK, 64], F32)
    sind = din(nc, "sin", [NTOK, 64], F32)
    identd = din(nc, "identb", [128, 128], BF16)
    qkT_o = dout(nc, "qkT", [8, 128, NTOK], BF16)
    v_o = dout(nc, "v", [NTOK, 512], BF16)
    dnT_o = dout(nc, "dnT", [12, 128, NTOK], F32)
    z_o = dout(nc, "z", [NTOK, 512], F32)
    bg_o = dout(nc, "bg", [128, NT, 16], F32)
    gT_o = dout(nc, "gT", [16, 128, NTOK], BF16)

    p = Prog(nc)
    ident = p.sb("ident", [128, 128], BF16)
    modbc = p.sb("modbc", [128, 2048], F32)
    G1 = p.sb("G1", [128, D], F32)
    stage = [p.sb("stg%d" % i, [128, 2064], F32) for i in range(2)]
    wb = p.sb("wb", [128, 8, 2064], BF16)
    hT = p.sb("hT", [128, 8, NTOK], BF16)
    xs = [p.sb("xs%d" % i, [128, D], F32) for i in range(2)]
    junk = p.sb("junk", [128, D], F32)
    ssA = p.sb("ssA", [128, 1], F32)
    hb = p.sb("hb", [128, D], BF16)
    gq = p.sb("gq", [128, 128], F32)
    cst = [p.sb("cst%d" % i, [128, 64], F32) for i in range(2)]
    snt = [p.sb("snt%d" % i, [128, 64], F32) for i in range(2)]
    alb = p.sb("alb", [128, 8], F32)
    dtbb = p.sb("dtbb", [128, 8], F32)
    sq = [p.sb("sq%d" % i, [128, 512], F32) for i in range(2)]
    t1 = [p.sb("t1_%d" % i, [128, 512], F32) for i in range(2)]
    t2 = [p.sb("t2_%d" % i, [128, 512], F32) for i in range(2)]
    qr = [p.sb("qr%d" % i, [128, 512], BF16) for i in range(2)]
    ss16 = p.sb("ss16", [128, 16], F32)
    vb = [p.sb("vb%d" % i, [128, 512], BF16) for i in range(2)]
    zf = [p.sb("zf%d" % i, [128, 512], F32) for i in range(2)]
    baraw = p.sb("baraw", [128, NT, 16], F32)
    bgo = p.sb("bgo", [128, NT, 16], F32)
    bat = [p.sb("bat%d" % i, [128, NT, 8], F32) for i in range(3)]
    qkTs = [p.sb("qkTs%d" % i, [128, 8, 512], BF16) for i in range(2)]
    dst = [p.sb("dst%d" % i, [128, 512], F32) for i in range(3)]
    gst = [p.sb("gst%d" % i, [128, 512], BF16) for i in range(3)]
    psb = [p.ps("psb%d" % i, [128, 512], F32) for i in range(6)]
    ptb = [p.ps("ptb%d" % i, [128, 1024], BF16) for i in range(2)]

    p.dma("sp", ident[:], identd, w=["ident"], semkey="ident")
    p.dma("sp", G1[:], g1.partition_broadcast(128), w=["GA"], semkey="G1")
    p.dma("sp", gq[:], qkg.partition_broadcast(128), w=["gq"], semkey="gq")
    p.dma("sp", alb[:], alog.partition_broadcast(128), w=["alb"], semkey="alb")
    p.dma("sp", dtbb[:], dtb.partition_broadcast(128), w=["dtbb"], semkey="dtbb")
    emit_mod(p, nc, ada_w, ada_b, cpk, 0, 2048, stage, psb, modbc)
    p.stt("dve", G1[:], modbc[:, 1024:2048], 1.0, G1[:], ALU.add, ALU.mult, r=["modbc", "GA"], w=["GA"])
    p.actf(alb[:], alb[:], AF.Exp, r=["alb"], w=["alb"])
    p.ts("dve", alb[:], alb[:], -1.0, None, ALU.mult, None, r=["alb"], w=["alb"])

    def load_w(colranges, ncols):
        for kc in range(8):
            sl = kc % 2
            o = 0
            for (a, b) in colranges:
                p.dma("sp", stage[sl][:, o:o + (b - a)], w_in[kc * 128:(kc + 1) * 128, a:b],
                      w=["stg%d" % sl], semkey="stg%d" % sl)
                o += b - a
            p.cp("pool", wb[:, kc, 0:ncols], stage[sl][:, 0:ncols], r=["stg%d" % sl], w=["wb%d" % kc])

    def load_x(t):
        p.dma("sp", xs[t % 2][:], x[t * 128:(t + 1) * 128, :], w=["xs%d" % (t % 2)], semkey="xs%d" % (t % 2))

    load_x(0)
    load_x(1)
    load_w([(0, 1536), (3072, 3600)], 2064)
    for t in range(NT):
        emit_rmsnorm_T(p, xs[t % 2][:], "xs%d" % (t % 2), G1[:], "GA", modbc[:, 0:1024], "modbc", junk, ssA, hb,
                       ptb[t % 2], "ptb%d" % (t % 2), hT, t, ident, "A")
        if t + 2 < NT:
            load_x(t + 2)

    def load_cs(t):
        p.dma("sp", cst[t % 2][:], cosd[t * 128:(t + 1) * 128, :], w=["cst%d" % (t % 2)], semkey="cst%d" % (t % 2))
        p.dma("sp", snt[t % 2][:], sind[t * 128:(t + 1) * 128, :], w=["snt%d" % (t % 2)], semkey="snt%d" % (t % 2))

    load_cs(0)
    load_cs(1)
    wkeys = ["wb%d" % kc for kc in range(8)]
    for t in range(NT):
        for j in range(4):
            for kc in range(8):
                p.mm(psb[j][:, :], hT[:, kc, t * 128:(t + 1) * 128], wb[:, kc, j * 512:(j + 1) * 512], kc == 0, kc == 7,
                     r=["hT%d" % t, "wb%d" % kc], w=["psb%d" % j])
        for kc in range(8):
            p.mm(psb[4][:, 0:16], hT[:, kc, t * 128:(t + 1) * 128], wb[:, kc, 2048:2064], kc == 0, kc == 7,
                 r=["hT%d" % t, "wb%d" % kc], w=["psb4"])
        p.cp("act", vb[t % 2][:], psb[2][:, :], r=["psb2"], w=["vb%d" % (t % 2)])
        p.dma("pool", v_o[t * 128:(t + 1) * 128, :], vb[t % 2][:], r=["vb%d" % (t % 2)], semkey="vb%d" % (t % 2))
        p.cp("act", zf[t % 2][:], psb[3][:, :], r=["psb3"], w=["zf%d" % (t % 2)])
        p.dma("pool", z_o[t * 128:(t + 1) * 128, :], zf[t % 2][:], r=["zf%d" % (t % 2)], semkey="zf%d" % (t % 2))
        p.cp("act", baraw[:, t, :], psb[4][:, 0:16], r=["psb4"], w=["baraw"])
        for j in range(2):
            sqj, t1j, t2j, qrj = sq[j], t1[j], t2[j], qr[j]
            kq = "qk%d" % j
            p.actf(sqj[:], psb[j][:, :], AF.Square, r=["psb%d" % j], w=["sq" + kq])
            p.red(ss16[:, j * 8:(j + 1) * 8], sqj[:].rearrange("p (g d) -> p g d", d=64), ALU.add, r=["sq" + kq], w=["ss16"])
        p.ts("dve", ss16[:], ss16[:], 1.0 / 64, EPS, ALU.mult, ALU.add, r=["ss16"], w=["ss16"])
        p.actf(ss16[:], ss16[:], AF.Sqrt, r=["ss16"], w=["ss16"])
        p.recip(ss16[:], ss16[:], r=["ss16"], w=["ss16"])
        for j in range(2):
            sqj, t1j, t2j, qrj = sq[j], t1[j], t2[j], qr[j]
            kq = "qk%d" % j
            s3 = sqj[:].rearrange("p (g d) -> p g d", d=64)
            a3 = t1j[:].rearrange("p (g d) -> p g d", d=64)
            b3 = t2j[:].rearrange("p (g d) -> p g d", d=64)
            q3 = qrj[:].rearrange("p (g d) -> p g d", d=64)
            p.tt("dve", s3, psb[j][:, :].rearrange("p (g d) -> p g d", d=64),
                 ss16[:, j * 8:(j + 1) * 8].unsqueeze(2).to_broadcast([128, 8, 64]), ALU.mult,
                 r=["psb%d" % j, "ss16"], w=["sq" + kq])
            p.tt("pool", s3, s3, gq[:, j * 64:(j + 1) * 64].unsqueeze(1).to_broadcast([128, 8, 64]), ALU.mult,
                 r=["sq" + kq, "gq"], w=["sq" + kq])
            p.tt("pool", a3, s3, cst[t % 2][:].unsqueeze(1).to_broadcast([128, 8, 64]), ALU.mult,
                 r=["sq" + kq, "cst%d" % (t % 2)], w=["t1" + kq])
            p.tt("dve", b3[:, :, 0:32], s3[:, :, 32:64], snt[t % 2][:, 0:32].unsqueeze(1).to_broadcast([128, 8, 32]), ALU.mult,
                 r=["sq" + kq, "snt%d" % (t % 2)], w=["t2a" + kq])
            p.tt("pool", b3[:, :, 32:64], s3[:, :, 0:32], snt[t % 2][:, 32:64].unsqueeze(1).to_broadcast([128, 8, 32]), ALU.mult,
                 r=["sq" + kq, "snt%d" % (t % 2)], w=["t2b" + kq])
            p.tt("dve", qrj[:], t1j[:], t2j[:], ALU.add, r=["t1" + kq, "t2a" + kq, "t2b" + kq], w=["qr" + kq])
        if t + 2 < NT:
            load_cs(t + 2)
        pk = "ptb%d" % (t % 2)
        for j in range(2):
            for h in range(4):
                p.tr(ptb[t % 2][:, (j * 4 + h) * 128:(j * 4 + h + 1) * 128], qr[j][:, h * 128:(h + 1) * 128], ident[:],
                     r=["qrqk%d" % j, "ident"], w=[pk])
        sl = (t // 4) % 2
        p.cp("act", qkTs[sl][:, :, (t % 4) * 128:(t % 4 + 1) * 128], ptb[t % 2][:].rearrange("p (k t) -> p k t", t=128),
             r=[pk], w=["qkTs%d" % sl])
        if t % 4 == 3:
            tg = t // 4
            p.dma("pool", qkT_o[:, :, tg * 512:(tg + 1) * 512].rearrange("h p t -> p h t"), qkTs[sl][:],
                  r=["qkTs%d" % sl], semkey="qkTs%d" % sl)

    braw = baraw[:, :, 0:8]
    araw = baraw[:, :, 8:16]
    p.actf(bgo[:, :, 0:8], braw, AF.Sigmoid, r=["baraw"], w=["bgo_b"])
    xg, ab, mx = bat
    p.tt("dve", xg[:], araw, dtbb[:].unsqueeze(1).to_broadcast([128, NT, 8]), ALU.add, r=["baraw", "dtbb"], w=["xg"])
    p.ts("dve", ab[:], xg[:], -1.0, None, ALU.mult, None, r=["xg"], w=["ab"])
    p.tt("dve", ab[:], ab[:], xg[:], ALU.max, r=["ab", "xg"], w=["ab"])
    p.actf(ab[:], ab[:], AF.Exp, r=["ab"], w=["ab"], scale=-1.0)
    p.ts("dve", ab[:], ab[:], 1.0, None, ALU.add, None, r=["ab"], w=["ab"])
    p.actf(ab[:], ab[:], AF.Ln, r=["ab"], w=["ab"])
    p.ts("dve", mx[:], xg[:], 0.0, None, ALU.max, None, r=["xg"], w=["mx"])
    p.tt("dve", mx[:], mx[:], ab[:], ALU.add, r=["mx", "ab"], w=["mx"])
    p.tt("dve", bgo[:, :, 8:16], mx[:], alb[:].unsqueeze(1).to_broadcast([128, NT, 8]), ALU.mult, r=["mx", "alb"], w=["bgo_g"])
    p.dma("pool", bg_o, bgo[:], r=["bgo_b", "bgo_g"], semkey="bgo")

    cnt = 0
    for grp, (a, b) in enumerate([(1536, 3072), (3600, 5648)]):
        ncols = b - a
        load_w([(a, b)], ncols)
        for tg in range(4):
            for cc in range(ncols // 128):
                bk = cnt % 4
                for kc in range(8):
                    p.mm(psb[bk][:, :], wb[:, kc, cc * 128:(cc + 1) * 128], hT[:, kc, tg * 512:(tg + 1) * 512], kc == 0, kc == 7,
                         r=["wb%d" % kc] + ["hT%d" % tt_ for tt_ in range(tg * 4, tg * 4 + 4)], w=["psb%d" % bk])
                sl = cnt % 3
                if grp == 0:
                    p.cp("act", dst[sl][:], psb[bk][:, :], r=["psb%d" % bk], w=["dst%d" % sl])
                    p.dma("pool", dnT_o[cc, :, tg * 512:(tg + 1) * 512], dst[sl][:], r=["dst%d" % sl], semkey="dst%d" % sl)
                else:
                    p.actf(gst[sl][:], psb[bk][:, :], AF.Sigmoid, r=["psb%d" % bk], w=["gst%d" % sl])
                    p.dma("pool", gT_o[cc, :, tg * 512:(tg + 1) * 512], gst[sl][:], r=["gst%d" % sl], semkey="gst%d" % sl)
                cnt += 1
    p.emit()
    return nc


_CACHE = {}


def _rope_tables():
    pos = np.arange(S, dtype=np.float32)
    inv = (1.0 / (np.float32(10000.0) ** (np.arange(0, 64, 2, dtype=np.float32) / np.float32(64)))).astype(np.float32)
    ang = (pos[:, None] * inv[None, :]).astype(np.float32)
    ang = np.concatenate([ang, ang], axis=-1)
    cos = np.cos(ang).astype(np.float32)
    sin = np.sin(ang).astype(np.float32)
    sin_signed = sin.copy()
    sin_signed[:, :32] = -sin_signed[:, :32]
    return cos, sin_signed


def _get(name, builder):
    if name not in _CACHE:
        _CACHE[name] = builder()
    return _CACHE[name]


def run_A(inp, l, xsh):
    nc = _get("A", build_A)
    cos, sins = _get("rope", _rope_tables)
    identb = np.eye(128, dtype=np.float32).astype(NPBF)
    maps = []
    for i in range(8):
        b, t0 = i // 4, (i % 4) * NTOK
        maps.append({
            "x": np.ascontiguousarray(xsh[i]),
            "cpk": np.ascontiguousarray(inp["c"][b].reshape(8, 128).T),
            "ada_w": np.ascontiguousarray(inp["ada_w"][l]),
            "ada_b": np.ascontiguousarray(inp["ada_b"][l][None, :]),
            "g1": np.ascontiguousarray(inp["norm1_g"][l][None, :]),
            "w_in": np.ascontiguousarray(inp["w_in"][l]),
            "qkg": np.ascontiguousarray(inp["qk_norm_gain"][l].reshape(1, 128)),
            "alog": np.ascontiguousarray(inp["dn_A_log"][l].reshape(1, 8)),
            "dtb": np.ascontiguousarray(inp["dn_dt_bias"][l].reshape(1, 8)),
            "cos": np.ascontiguousarray(cos[t0:t0 + NTOK]),
            "sin": np.ascontiguousarray(sins[t0:t0 + NTOK]),
            "identb": identb,
        })
    res = run_bass_kernel_spmd(nc, maps, core_ids=list(range(8)))
    return res.results


def emit_C(nc, outer, x_in, yaT, ybT, gT, cpk, ada_w, ada_b, g2, w_a, w_b, w_out, w1, w2, identd, x_out):
    x_res = outer.enter_context(nc.sbuf_tensor("x_res", [128, NT, D], F32))
    modbc = outer.enter_context(nc.sbuf_tensor("modbcC", [128, 4096], F32))
    G2 = outer.enter_context(nc.sbuf_tensor("G2", [128, D], F32))
    ident = outer.enter_context(nc.sbuf_tensor("identC", [128, 128], BF16))

    p = Prog(nc)
    stage = [p.sb("stgC%d" % i, [128, 2048], F32) for i in range(2)]
    psb = [p.ps("psC0_%d" % i, [128, 512], F32) for i in range(4)]
    p.dma("sp", ident[:], identd, w=["ident"], semkey="ident")
    p.dma("sp", G2[:], g2.partition_broadcast(128), w=["G2"], semkey="G2")
    for t in range(NT):
        p.dma("sp", x_res[:, t, :], x_in[t * 128:(t + 1) * 128, :], w=["xr%d" % t], semkey="xres")
    emit_mod(p, nc, ada_w, ada_b, cpk, 2048, 4096, stage, psb, modbc)
    p.stt("dve", G2[:], modbc[:, 2048:3072], 1.0, G2[:], ALU.add, ALU.mult, r=["modbc", "G2"], w=["G2"])
    p.emit()
    gt1 = modbc[:, 0:1024]
    sh2 = modbc[:, 1024:2048]
    gt2 = modbc[:, 3072:4096]

    p = Prog(nc)
    stg = [p.sb("stg1_%d" % i, [128, 1024], F32) for i in range(2)]
    wab = p.sb("wab", [128, 4, D], BF16)
    wbb = p.sb("wbb", [128, 4, D], BF16)
    wob = p.sb("wob", [128, 8, D], BF16)
    yas = [p.sb("yas%d" % i, [128, 4, 512], BF16) for i in range(2)]
    ybs = [p.sb("ybs%d" % i, [128, 4, 512], BF16) for i in range(2)]
    gas = [p.sb("gas%d" % i, [128, 512], BF16) for i in range(2)]
    gbs = [p.sb("gbs%d" % i, [128, 512], BF16) for i in range(2)]
    m1 = [p.sb("m1_%d" % i, [128, 512], F32) for i in range(2)]
    m2 = [p.sb("m2_%d" % i, [128, 512], F32) for i in range(2)]
    mT = [p.sb("mT%d" % i, [128, 8, 512], BF16) for i in range(2)]
    tmp = [p.sb("tmpc%d" % i, [128, 512], F32) for i in range(2)]
    psb = [p.ps("psC1_%d" % i, [128, 512], F32) for i in range(6)]
    si = 0
    for h in range(4):
        for (wsrc, wdst, nm) in ((w_a, wab, "wab"), (w_b, wbb, "wbb")):
            sl = si % 2
            si += 1
            p.dma("sp", stg[sl][:], wsrc[h * 128:(h + 1) * 128, :], w=["stg%d" % sl], semkey="stg%d" % sl)
            p.cp("pool", wdst[:, h, :], stg[sl][:], r=["stg%d" % sl], w=[nm])
    for kc in range(8):
        sl = si % 2
        si += 1
        p.dma("sp", stg[sl][:], w_out[kc * 128:(kc + 1) * 128, :], w=["stg%d" % sl], semkey="stg%d" % sl)
        p.cp("pool", wob[:, kc, :], stg[sl][:], r=["stg%d" % sl], w=["wob"])
    cnt = 0
    for tg in range(4):
        s2 = tg % 2
        p.dma("sp", yas[s2][:], yaT[:, :, tg * 512:(tg + 1) * 512].rearrange("h p t -> p h t"), w=["yas%d" % s2], semkey="yas%d" % s2)
        p.dma("sp", ybs[s2][:], ybT[:, :, tg * 512:(tg + 1) * 512].rearrange("h p t -> p h t"), w=["ybs%d" % s2], semkey="ybs%d" % s2)
        for fc in range(8):
            s3 = fc % 2
            p.dma("sp", gas[s3][:], gT[fc, :, tg * 512:(tg + 1) * 512], w=["gas%d" % s3], semkey="gas%d" % s3)
            p.dma("sp", gbs[s3][:], gT[8 + fc, :, tg * 512:(tg + 1) * 512], w=["gbs%d" % s3], semkey="gbs%d" % s3)
            ba, bb = (0, 1) if fc % 2 == 0 else (2, 3)
            for h in range(4):
                p.mm(psb[ba][:, :], wab[:, h, fc * 128:(fc + 1) * 128], yas[s2][:, h, :], h == 0, h == 3,
                     r=["wab", "yas%d" % s2], w=["ps%d" % ba])
            for h in range(4):
                p.mm(psb[bb][:, :], wbb[:, h, fc * 128:(fc + 1) * 128], ybs[s2][:, h, :], h == 0, h == 3,
                     r=["wbb", "ybs%d" % s2], w=["ps%d" % bb])
            p.tt("dve", m1[s3][:], psb[ba][:, :], gas[s3][:], ALU.mult, r=["ps%d" % ba, "gas%d" % s3], w=["m1_%d" % s3])
            p.tt("dve", m2[s3][:], psb[bb][:, :], gbs[s3][:], ALU.mult, r=["ps%d" % bb, "gbs%d" % s3], w=["m2_%d" % s3])
            p.tt("pool", mT[s2][:, fc, :], m1[s3][:], m2[s3][:], ALU.add, r=["m1_%d" % s3, "m2_%d" % s3], w=["mT%d_%d" % (s2, fc)])
        for i in range(4):
            t = tg * 4 + i
            for j in range(2):
                bk = 4 + (cnt % 2)
                ts_ = cnt % 2
                cnt += 1
                for kc in range(8):
                    p.mm(psb[bk][:, :], mT[s2][:, kc, i * 128:(i + 1) * 128], wob[:, kc, j * 512:(j + 1) * 512], kc == 0, kc == 7,
                         r=["mT%d_%d" % (s2, kc), "wob"], w=["ps%d" % bk])
                p.tt("dve", tmp[ts_][:], psb[bk][:, :], gt1[:, j * 512:(j + 1) * 512], ALU.mult, r=["ps%d" % bk], w=["tmp%d" % ts_])
                p.tt("pool", x_res[:, t, j * 512:(j + 1) * 512], x_res[:, t, j * 512:(j + 1) * 512], tmp[ts_][:], ALU.add,
                     r=["tmp%d" % ts_], w=["xr%d_%d" % (t, j)])
    p.emit()

    p = Prog(nc)
    h2T = p.sb("h2T", [128, 8, NTOK], BF16)
    junk = p.sb("junkC", [128, D], F32)
    ssC = p.sb("ssC", [128, 1], F32)
    hb = p.sb("hbC", [128, D], BF16)
    stg = [p.sb("stg2_%d" % i, [128, 1024], F32) for i in range(2)]
    w1b = [p.sb("w1b%d" % i, [128, 8, 512], BF16) for i in range(2)]
    w2b = [p.sb("w2b%d" % i, [128, 4, D], BF16) for i in range(2)]
    aT = p.sb("aT", [128, 4, NTOK], BF16)
    rl = [p.sb("rl%d" % i, [128, 512], F32) for i in range(2)]
    tmp = [p.sb("tmpd%d" % i, [128, 512], F32) for i in range(2)]
    psb = [p.ps("psC2_%d" % i, [128, 512], F32) for i in range(6)]
    ptb = [p.ps("ptC2_%d" % i, [128, 1024], BF16) for i in range(2)]
    si = 0

    def load_fb(fb):
        nonlocal si
        s = fb % 2
        for k2 in range(4):
            sl = si % 2
            si += 1
            p.dma("sp", stg[sl][:].rearrange("p (k c) -> p k c", k=2),
                  w1[k2 * 256:(k2 + 1) * 256, fb * 512:(fb + 1) * 512].rearrange("(k p) c -> p k c", p=128),
                  w=["stg%d" % sl], semkey="stg%d" % sl)
            p.cp("pool", w1b[s][:, k2 * 2:k2 * 2 + 2, :], stg[sl][:].rearrange("p (k c) -> p k c", k=2), r=["stg%d" % sl], w=["w1b%d" % s])
        for c in range(4):
            sl = si % 2
            si += 1
            p.dma("sp", stg[sl][:], w2[fb * 512 + c * 128:fb * 512 + (c + 1) * 128, :], w=["stg%d" % sl], semkey="stg%d" % sl)
            p.cp("pool", w2b[s][:, c, :], stg[sl][:], r=["stg%d" % sl], w=["w2b%d" % s])

    load_fb(0)
    for t in range(NT):
        emit_rmsnorm_T(p, x_res[:, t, :], "xr%d" % t, G2[:], "G2", sh2, "modbc", junk, ssC, hb,
                       ptb[t % 2], "ptb%d" % (t % 2), h2T, t, ident, "C")
    cnt = 0
    c2 = 0
    for fb in range(8):
        s = fb % 2
        if fb + 1 < 8:
            load_fb(fb + 1)
        for tg in range(4):
            for c in range(4):
                bk = cnt % 4
                rs = cnt % 2
                cnt += 1
                for kc in range(8):
                    p.mm(psb[bk][:, :], w1b[s][:, kc, c * 128:(c + 1) * 128], h2T[:, kc, tg * 512:(tg + 1) * 512], kc == 0, kc == 7,
                         r=["w1b%d" % s] + ["hT%d" % tt_ for tt_ in range(tg * 4, tg * 4 + 4)], w=["ps%d" % bk])
                p.actf(rl[rs][:], psb[bk][:, :], AF.Relu, r=["ps%d" % bk], w=["rl%d" % rs])
                p.tt("pool", aT[:, c, tg * 512:(tg + 1) * 512], rl[rs][:], rl[rs][:], ALU.mult, r=["rl%d" % rs], w=["aT%d_%d" % (c, tg)])
        for t in range(NT):
            for j in range(2):
                bk = 4 + (c2 % 2)
                ts_ = c2 % 2
                c2 += 1
                for c in range(4):
                    p.mm(psb[bk][:, :], aT[:, c, t * 128:(t + 1) * 128], w2b[s][:, c, j * 512:(j + 1) * 512], c == 0, c == 3,
                         r=["aT%d_%d" % (c, t // 4), "w2b%d" % s], w=["ps%d" % bk])
                p.tt("dve", tmp[ts_][:], psb[bk][:, :], gt2[:, j * 512:(j + 1) * 512], ALU.mult, r=["ps%d" % bk], w=["tmp%d" % ts_])
                p.tt("pool", x_res[:, t, j * 512:(j + 1) * 512], x_res[:, t, j * 512:(j + 1) * 512], tmp[ts_][:], ALU.add,
                     r=["tmp%d" % ts_], w=["xr%d" % t])
    for t in range(NT):
        p.dma("sp", x_out[t * 128:(t + 1) * 128, :], x_res[:, t, :], r=["xr%d" % t], semkey="xout")
    p.emit()


def build_C():
    nc = bass.Bass("TRN2", target_bir_lowering=False)
    x = din(nc, "x", [NTOK, D], F32)
    yaT = din(nc, "yaT", [4, 128, NTOK], BF16)
    ybT = din(nc, "ybT", [4, 128, NTOK], BF16)
    gT = din(nc, "gT", [16, 128, NTOK], BF16)
    cpk = din(nc, "cpk", [128, 8], F32)
    ada_w = din(nc, "ada_w", [D, 6 * D], F32)
    ada_b = din(nc, "ada_b", [1, 6 * D], F32)
    g2 = din(nc, "g2", [1, D], F32)
    w_a = din(nc, "w_a", [512, D], F32)
    w_b = din(nc, "w_b", [512, D], F32)
    w_out = din(nc, "w_out", [D, D], F32)
    w1 = din(nc, "w1", [D, 4 * D], F32)
    w2 = din(nc, "w2", [4 * D, D], F32)
    identd = din(nc, "identb", [128, 128], BF16)
    x_out = dout(nc, "x_out", [NTOK, D], F32)
    with ExitStack() as outer:
        emit_C(nc, outer, x, yaT, ybT, gT, cpk, ada_w, ada_b, g2, w_a, w_b, w_out, w1, w2, identd, x_out)
    return nc


def run_C(inp, l, xsh, yaT, ybT, gT):
    nc = _get("C", build_C)
    identb = np.eye(128, dtype=np.float32).astype(NPBF)
    maps = []
    for i in range(8):
        b = i // 4
        maps.append({
            "x": np.ascontiguousarray(xsh[i]),
            "yaT": np.ascontiguousarray(yaT[i]),
            "ybT": np.ascontiguousarray(ybT[i]),
            "gT": np.ascontiguousarray(gT[i]),
            "cpk": np.ascontiguousarray(inp["c"][b].reshape(8, 128).T),
            "ada_w": np.ascontiguousarray(inp["ada_w"][l]),
            "ada_b": np.ascontiguousarray(inp["ada_b"][l][None, :]),
            "g2": np.ascontiguousarray(inp["norm2_g"][l][None, :]),
            "w_a": np.ascontiguousarray(inp["w_branch_a"][l]),
            "w_b": np.ascontiguousarray(inp["w_branch_b"][l]),
            "w_out": np.ascontiguousarray(inp["w_out"][l]),
            "w1": np.ascontiguousarray(inp["w_mlp1"][l]),
            "w2": np.ascontiguousarray(inp["w_mlp2"][l]),
            "identb": identb,
        })
    res = run_bass_kernel_spmd(nc, maps, core_ids=list(range(8)))
    return [r["x_out"] for r in res.results]


DBG_B3 = 0
B1_DUP = 1
DBG_DUMP = 0
DBG_NG = 16
DBG_D = 0
DBG_DIR = -1
NKC = S // 128


def emit_B1(nc, qkT, vd, qkg, lqd, lamd, sublnd, identd, yaT_o):
    p = Prog(nc)
    qT = p.sb("qT", [128, S], BF16)
    kTz = [p.sb("kTz%d" % m, [128, S], BF16) for m in range(2)]
    vv = p.sb("vv", [128, NKC, 128], BF16)
    gq = p.sb("gqB", [128, 128], F32)
    gqa = p.sb("gqa", [128, 128], F32)
    mx2 = p.sb("mx2", [128, 2], F32)
    nshift = p.sb("nshift", [128, 1], F32)
    lq = p.sb("lq", [128, 4, 64], F32)
    lqp = p.sb("lqp", [128, 2, 64], F32)
    lsum = p.sb("lsum", [128, 2], F32)
    lam = p.sb("lam", [128, 1], F32)
    lamc = p.sb("lamc", [128, 2], F32)
    gsubc = p.sb("gsubc", [128, 1], F32)
    epst = p.sb("epsB1", [128, 1], F32)
    onesf = p.sb("onesfB1", [128, 128], F32)
    onesb = p.sb("onesbB1", [128, 128], BF16)
    pT = [p.sb("pT%d" % i, [128, 1024], BF16) for i in range(3)]
    racc = [[p.sb("racc%d_%d" % (m, e), [128, 1024], F32) for e in range(2)] for m in range(2)]
    r1 = p.sb("r1", [128, 512], F32)
    r2 = p.sb("r2", [128, 512], F32)
    t1 = p.sb("t1B1", [128, 512], F32)
    t2 = p.sb("t2B1", [128, 512], F32)
    sqb = p.sb("sqbB1", [128, 512], BF16)
    rs = p.sb("rsB1", [128, 512], F32)
    yst = [p.sb("yst%d" % i, [128, 512], BF16) for i in range(2)]
    sps = [p.ps("sps%d" % i, [128, 1024], F32) for i in range(2)]
    accO = [p.ps("accO%d" % m, [128, 512], F32) for m in range(2)]
    sums = [p.ps("sums%d" % m, [128, 512], F32) for m in range(2)]
    ssp = sums[0]

    p.dma("sp", gq[:], qkg.partition_broadcast(128), w=["gq"], semkey="gq")
    p.dma("sp", lq[:].rearrange("p a b -> p (a b)"), lqd.partition_broadcast(128), w=["lq"], semkey="lq")
    p.dma("sp", lamc[:], lamd.partition_broadcast(128), w=["lamc"], semkey="lamc")
    p.dma("sp", gsubc[:], sublnd.rearrange("o d -> d o"), w=["gsubc"], semkey="gsubc")
    for c4 in range(4):
        p.dma("sp", qT[:, c4 * 2048:(c4 + 1) * 2048], qkT[0, :, c4 * 2048:(c4 + 1) * 2048], w=["qT"], semkey="qT")
        p.dma("sp", kTz[0][0:64, c4 * 2048:(c4 + 1) * 2048], qkT[1, 0:64, c4 * 2048:(c4 + 1) * 2048], w=["kT0d"], semkey="kT0")
        p.dma("sp", kTz[1][64:128, c4 * 2048:(c4 + 1) * 2048], qkT[1, 64:128, c4 * 2048:(c4 + 1) * 2048], w=["kT1d"], semkey="kT1")
    p.memset("pool", kTz[0][64:128, :], 0.0, w=["kT0z"])
    p.memset("pool", kTz[1][0:64, :], 0.0, w=["kT1z"])
    for c4 in range(4):
        p.dma("sp", vv[:, c4 * 16:(c4 + 1) * 16, :],
              vd[c4 * 2048:(c4 + 1) * 2048, :].rearrange("(k p) d -> p k d", p=128), w=["vv"], semkey="vv")
    p.memset("pool", onesf[:], 1.0, w=["onesf"])
    p.memset("pool", onesb[:], 1.0, w=["onesb"])
    p.memset("pool", epst[:], EPS, w=["epst"])
    p.ts("dve", gqa[:], gq[:], -1.0, None, ALU.mult, None, r=["gq"], w=["gqa"])
    p.tt("dve", gqa[:], gqa[:], gq[:], ALU.max, r=["gqa", "gq"], w=["gqa"])
    p.op("dve", lambda e: e.tensor_reduce(mx2[:], gqa[:].rearrange("p (a b) -> p a b", a=2), AX.X, ALU.max), r=["gqa"], w=["mx2"])
    p.stt("dve", nshift[:], mx2[:, 0:1], -8.0, mx2[:, 1:2], ALU.mult, ALU.mult, r=["mx2"], w=["nshift"])
    p.tt("dve", lqp[:, 0, :], lq[:, 0, :], lq[:, 1, :], ALU.mult, r=["lq"], w=["lqp0"])
    p.tt("dve", lqp[:, 1, :], lq[:, 2, :], lq[:, 3, :], ALU.mult, r=["lq"], w=["lqp1"])
    p.red(lsum[:], lqp[:], ALU.add, r=["lqp0", "lqp1"], w=["lsum"])
    p.actf(lsum[:], lsum[:], AF.Exp, r=["lsum"], w=["lsum"])
    p.tt("dve", lam[:], lsum[:, 0:1], lsum[:, 1:2], ALU.subtract, r=["lsum"], w=["lam"])
    p.tt("dve", lam[:], lam[:], lamc[:, 0:1], ALU.add, r=["lam", "lamc"], w=["lam"])
    p.ts("dve", lam[:], lam[:], -1.0, None, ALU.mult, None, r=["lam"], w=["lam"])
    p.ts("dve", gsubc[:], gsubc[:], lamc[:, 1:2], None, ALU.mult, None, r=["gsubc", "lamc"], w=["gsubc"])

    pair = 0
    NP_ = NKC // 2
    for qg in range(16):
        for m in range(2):
            def s_pair(pi, pr):
                b = pr % 2
                for hf in range(2):
                    kc = pi * 2 + hf
                    p.mm(sps[b][:, hf * 512:(hf + 1) * 512], kTz[m][:, kc * 128:(kc + 1) * 128],
                         qT[:, qg * 512:(qg + 1) * 512], True, True, r=["qT", "kT%dd" % m, "kT%dz" % m], w=["sps%d" % b])
            s_pair(0, pair)
            for pi in range(NP_):
                b = pair % 2
                sl = pair % 3
                if pi + 1 < NP_:
                    s_pair(pi + 1, pair + 1)
                p.actf(pT[sl][:], sps[b][:, :], AF.Exp, r=["sps%d" % b, "nshift"], w=["pT%d" % sl], bias=nshift[:, 0:1], scale=0.125)
                for hf in range(2):
                    kc = pi * 2 + hf
                    p.mm(accO[m][:, :], vv[:, kc, :], pT[sl][:, hf * 512:(hf + 1) * 512], kc == 0, kc == NKC - 1,
                         r=["pT%d" % sl, "vv"], w=["accO%d" % m])
                    p.mm(sums[m][:, :], onesb[:], pT[sl][:, hf * 512:(hf + 1) * 512], kc == 0, kc == NKC - 1,
                         r=["pT%d" % sl, "onesb"], w=["sums%d" % m])
                pair += 1
        p.recip(r1[:], sums[0][:, :], r=["sums0"], w=["r1"])
        p.recip(r2[:], sums[1][:, :], r=["sums1"], w=["r2"])
        p.ts("pool", r2[:], r2[:], lam[:, 0:1], None, ALU.mult, None, r=["r2", "lam"], w=["r2"])
        p.tt("dve", t1[:], accO[0][:, :], r1[:], ALU.mult, r=["accO0", "r1"], w=["t1"])
        p.tt("dve", t2[:], accO[1][:, :], r2[:], ALU.mult, r=["accO1", "r2"], w=["t2"])
        p.tt("pool", t1[:], t1[:], t2[:], ALU.add, r=["t1", "t2"], w=["t1"])
        p.tt("pool", sqb[:], t1[:], t1[:], ALU.mult, r=["t1"], w=["sqb"])
        p.mm(ssp[:, :], onesb[:], sqb[:], True, True, r=["onesb", "sqb"], w=["sums0"])
        p.actf(rs[:], ssp[:, :], AF.Sqrt, r=["sums0", "epst"], w=["rs"], bias=epst[:, 0:1], scale=1.0 / 128)
        p.recip(rs[:], rs[:], r=["rs"], w=["rs"])
        ys = qg % 2
        p.stt("dve", yst[ys][:], t1[:], gsubc[:, 0:1], rs[:], ALU.mult, ALU.mult, r=["t1", "gsubc", "rs"], w=["yst%d" % ys])
        p.dma("sp", yaT_o[:, qg * 512:(qg + 1) * 512], yst[ys][:], r=["yst%d" % ys], semkey="yst%d" % ys)
    p.emit()


def bc(ap, shape, axis):
    return ap.unsqueeze(axis).to_broadcast(list(shape))


def emit_B2(nc, pers, dnT, cwd, identd):
    qTn, kTn, ktok, vtok = pers
    p = Prog(nc)
    ident = p.sb("identB2", [128, 128], BF16)
    onesb = p.sb("onesb", [128, 128], BF16)
    epst = p.sb("epst", [128, 1], F32)
    cw = p.sb("cw", [128, 3, 5], F32)
    xin = [p.sb("xin%d" % i, [128, 2052], F32) for i in range(2)]
    acc = [p.sb("cacc%d" % i, [128, 2048], F32) for i in range(2)]
    sqb = p.sb("sqb", [128, 2048], BF16)
    rin = p.sb("rin", [128, 2048], F32)
    vtb = p.sb("vtb", [128, 2048], BF16)
    ps = [p.ps("psB2_%d" % i, [128, 512], F32) for i in range(4)]
    ptb = [p.ps("ptB2_%d" % i, [128, 1024], BF16) for i in range(2)]
    p.dma("sp", ident[:], identd, w=["ident"], semkey="ident")
    p.dma("sp", cw[:], cwd, w=["cw"], semkey="cw")
    p.memset("pool", onesb[:], 1.0, w=["onesb"])
    p.memset("pool", epst[:], EPS, w=["epst"])
    it = 0
    tcnt = 0
    for tg in range(4):
        for X in range(3):
            sl = it % 2
            it += 1
            xk, ak = "xin%d" % sl, "cacc%d" % sl
            lo = tg * 2048 - 2
            hi = tg * 2048 + 2050
            if tg == 0:
                p.memset("pool", xin[sl][:, 0:2], 0.0, w=[xk])
                p.dma("sp", xin[sl][:, 2:2052], dnT[X, :, 0:2050], w=[xk + "d"], r=[xk], semkey=xk)
            elif tg == 3:
                p.memset("pool", xin[sl][:, 2050:2052], 0.0, w=[xk])
                p.dma("sp", xin[sl][:, 0:2050], dnT[X, :, lo:S], w=[xk + "d"], r=[xk], semkey=xk)
            else:
                p.memset("pool", xin[sl][:, 0:1], 0.0, w=[xk])
                p.dma("sp", xin[sl][:, :], dnT[X, :, lo:hi], w=[xk + "d"], r=[xk], semkey=xk)
            eng = "dve"
            a = acc[sl]
            p.ts(eng, a[:], xin[sl][:, 0:2048], cw[:, X, 0:1], None, ALU.mult, None, r=[xk + "d", "cw"], w=[ak])
            for j in range(1, 5):
                p.stt(eng, a[:], xin[sl][:, j:j + 2048], cw[:, X, j:j + 1], a[:], ALU.mult, ALU.add, r=[xk + "d", "cw", ak], w=[ak])
            p.actf(a[:], a[:], AF.Silu, r=[ak, xk + "d"], w=[ak, xk])
            if X == 2:
                p.cp("pool", vtb[:], a[:], r=[ak], w=["vtb"])
                src_b, dst, skey = vtb, vtok, "vtb"
            else:
                p.tt("pool", sqb[:], a[:], a[:], ALU.mult, r=[ak], w=["sqb"])
                for blk in range(4):
                    p.mm(ps[blk][:, :], onesb[:], sqb[:, blk * 512:(blk + 1) * 512], True, True, r=["onesb", "sqb"], w=["psn%d" % blk])
                    p.actf(rin[:, blk * 512:(blk + 1) * 512], ps[blk][:, :], AF.Sqrt, r=["psn%d" % blk, "epst"], w=["rin%d" % blk],
                           bias=epst[:, 0:1])
                p.recip(rin[:], rin[:], r=["rin%d" % b_ for b_ in range(4)], w=["rin"])
                tgt = qTn if X == 0 else kTn
                sc = (128.0 ** -0.5) if X == 0 else 1.0
                p.stt("dve", tgt[:, tg * 2048:(tg + 1) * 2048], a[:], sc, rin[:], ALU.mult, ALU.mult, r=[ak, "rin"],
                      w=["nT%d_%d" % (X, tg)] + ["rin%d" % b_ for b_ in range(4)])
                src_b, dst, skey = kTn, ktok, "nT1_%d" % tg
            if X >= 1:
                for half in range(2):
                    pb = tcnt % 2
                    tcnt += 1
                    for c in range(8):
                        cc = half * 8 + c
                        if X == 2:
                            src = vtb[:, cc * 128:(cc + 1) * 128]
                        else:
                            src = kTn[:, tg * 2048 + cc * 128:tg * 2048 + (cc + 1) * 128]
                        p.tr(ptb[pb][:, c * 128:(c + 1) * 128], src, ident[:], r=[skey, "ident"], w=["ptb%d" % pb])
                    n0 = tg * 16 + half * 8
                    p.cp("act", dst[:, n0:n0 + 8, :], ptb[pb][:].rearrange("p (k t) -> p k t", t=128), r=["ptb%d" % pb],
                         w=["tok%d_%d" % (X, n0)])
    p.emit()


def emit_B3(nc, pers, o_scr, bgd, masksd, identd):
    qTn, kTn, ktok, vtok = pers
    p = Prog(nc)
    identb = p.sb("identB3", [128, 128], BF16)
    identf = p.sb("identfB3", [128, 128], F32)
    msk = p.sb("msk", [128, 8, 128], F32)
    onesf = p.sb("onesf", [128, 128], F32)
    bg = p.sb("bgt", [128, NKC, 4], F32)
    negb = p.sb("negb", [128, NKC, 2], F32)
    EG = [p.sb("EG%d" % d, [128, NKC], F32) for d in range(2)]
    ER = [p.sb("ER%d" % d, [128, NKC], F32) for d in range(2)]
    EL = [p.sb("EL%d" % d, [128, NKC], F32) for d in range(2)]
    Sst = [p.sb("Sst%d" % d, [128, 128], F32) for d in range(2)]
    vnew = [[p.sb("vnew%d_%d" % (d, i), [128, 128], F32) for i in range(2)] for d in range(2)]
    ost = [[p.sb("ost%d_%d" % (d, i), [128, 128], F32) for i in range(2)] for d in range(2)]
    grp = {}
    for d in range(2):
        for s_ in range(2):
            for nm in ("u", "wT", "qdT", "qkT", "kdec"):
                grp[(nm, d, s_)] = p.sb("%s%d_%d" % (nm, d, s_), [128, 4, 128], F32)
    T3 = {}
    for d in range(2):
        for nm in ("gtri", "DTr", "EGr", "tmpf", "X", "XT", "A", "AT", "P", "PT"):
            T3[(nm, d)] = p.sb("%s_%d" % (nm, d), [128, 4, 128], F32)
        for nm in ("kgb", "Pb16"):
            T3[(nm, d)] = p.sb("%s_%d" % (nm, d), [128, 4, 128], BF16)
    psb = [p.ps("psB3_%d" % i, [128, 512], F32) for i in range(3)]
    ptb = p.ps("ptB3", [128, 512], F32)
    pssA = [p.ps("psscA%d" % d, [128, 512], F32) for d in range(2)]
    pssB = [p.ps("psscB%d" % d, [128, 512], F32) for d in range(2)]

    p.dma("sp", identb[:], identd, w=["identb_"], semkey="ident")
    p.cp("dve", identf[:], identb[:], r=["identb_"], w=["ident"])
    p.dma("sp", msk[:], masksd.rearrange("a p f -> p a f"), w=["msk"], semkey="msk")
    p.dma("sp", bg[:], bgd, w=["bg"], semkey="bg")
    p.memset("pool", onesf[:], 1.0, w=["onesf"])
    for d in range(2):
        p.memset("pool", Sst[d][:], 0.0, w=["S%d" % d])
    p.ts("dve", negb[:], bg[:, :, 0:2], -1.0, None, ALU.mult, None, r=["bg"], w=["negb"])
    for d in range(2):
        g = bg[:, :, 2 + d]
        mGC = msk[:, 0, :] if d == 0 else msk[:, 2, :]
        mGR = msk[:, 3, :] if d == 0 else msk[:, 1, :]
        for i, (lhs, dst, nm) in enumerate(((mGC, EG[d], "EG"), (mGR, ER[d], "ER"), (onesf[:], EL[d], "EL"))):
            p.mm(psb[i][:, 0:NKC], lhs, g, True, True, r=["msk", "onesf", "bg"], w=["ps%d" % i])
            p.actf(dst[:], psb[i][:, 0:NKC], AF.Exp, r=["ps%d" % i], w=["%s%d" % (nm, d)])

    sh3 = [128, 4, 128]

    def fl(t):
        return t[:].rearrange("p a b -> p (a b)")

    def d3_stages(d, G):
        s_ = G % 2
        n0 = 4 * G if d == 0 else 60 - 4 * G
        MI = msk[:, 0, :] if d == 0 else msk[:, 2, :]
        MS = msk[:, 1, :] if d == 0 else msk[:, 3, :]
        MG = msk[:, 3, :] if d == 0 else msk[:, 1, :]
        u, wT, qdT, qkT, kdec = [grp[(nm, d, s_)] for nm in ("u", "wT", "qdT", "qkT", "kdec")]
        gk = lambda nm: "%s%d_%d" % (nm, d, s_)
        D = "_%d" % d
        gtri, DTr, EGr, tmpf, X, XT, A, AT, P_, PT, kgb, Pb16 = [T3[(nm, d)] for nm in
            ("gtri", "DTr", "EGr", "tmpf", "X", "XT", "A", "AT", "P", "PT", "kgb", "Pb16")]
        Y, Xc, XcT, Y2 = tmpf, DTr, EGr, gtri
        kY, kXc, kXcT, kY2 = "tmpf" + D, "DTr" + D, "EGr" + D, "gtri" + D
        kX, kXT, kA, kAT, kP, kPT = ["%s%s" % (nm, D) for nm in ("X", "XT", "A", "AT", "P", "PT")]
        st = []

        def mm4(bank, bkey, lhs, rhs, r):
            for j in range(4):
                p.mm(psb[bank][:, j * 128:(j + 1) * 128], lhs[:, j, :], rhs[:, j, :], True, True, r=r, w=[bkey])

        def s1():
            p.tt("pool", gtri[:], bc(MI, sh3, 1), bc(bg[:, n0:n0 + 4, 2 + d], sh3, 2), ALU.mult, r=["msk", "bg"], w=[kY2])
            p.mm(psb[0][:, :], MG, fl(gtri), True, True, r=["msk", kY2], w=["ps0"])
            p.mm(psb[1][:, :], onesf[:], fl(gtri), True, True, r=["onesf", kY2], w=["ps1"])
            p.actf(fl(DTr), psb[0][:, :], AF.Exp, r=["ps0"], w=[kXc])
            p.actf(fl(EGr), psb[1][:, :], AF.Exp, r=["ps1"], w=[kXcT])
        st.append(s1)

        def s2():
            for j in range(4):
                c0 = (n0 + j) * 128
                p.mm(psb[2][:, j * 128:(j + 1) * 128], kTn[:, c0:c0 + 128], kTn[:, c0:c0 + 128], True, True, r=[], w=["ps2"])
            for j in range(4):
                c0 = (n0 + j) * 128
                p.mm(psb[0][:, j * 128:(j + 1) * 128], kTn[:, c0:c0 + 128], qTn[:, c0:c0 + 128], True, True, r=[], w=["ps0"])
            ps2v = psb[2][:, :].rearrange("p (a b) -> p a b", a=4)
            ps0v = psb[0][:, :].rearrange("p (a b) -> p a b", a=4)
            p.tt("dve", tmpf[:], ps2v, DTr[:], ALU.mult, r=["ps2", kXc], w=[kY])
            p.tt("dve", qkT[:], ps0v, DTr[:], ALU.mult, r=["ps0", kXc], w=[gk("qkT")])
            p.tt("pool", tmpf[:], tmpf[:], bc(negb[:, n0:n0 + 4, d], sh3, 2), ALU.mult, r=[kY, "negb"], w=[kY])
            p.tt("pool", X[:], tmpf[:], bc(MS, sh3, 1), ALU.mult, r=[kY, "msk"], w=[kX])
            p.tt("pool", qkT[:], qkT[:], bc(MI, sh3, 1), ALU.mult, r=[gk("qkT"), "msk"], w=[gk("qkT")])
            p.tt("pool", fl(qdT), qTn[:, n0 * 128:(n0 + 4) * 128], fl(EGr), ALU.mult, r=[kXcT], w=[gk("qdT")])
            p.tt("pool", kgb[:], ktok[:, n0:n0 + 4, :], bc(EG[d][:, n0:n0 + 4], sh3, 2), ALU.mult, r=["EG%d" % d], w=["kgb" + D])
            p.tt("pool", kdec[:], ktok[:, n0:n0 + 4, :], bc(ER[d][:, n0:n0 + 4], sh3, 2), ALU.mult, r=["ER%d" % d], w=[gk("kdec")])
            for j in range(4):
                p.tr(ptb[:, j * 128:(j + 1) * 128], X[:, j, :], identf[:], r=[kX, "ident"], w=["ptb"])
            p.cp("act", fl(XT), ptb[:, :], r=["ptb"], w=[kXT])
        st.append(s2)

        def s3():
            BD = bc(msk[:, 4, :], sh3, 1)
            p.tt("pool", A[:], X[:], BD, ALU.mult, r=[kX, "msk"], w=[kA])
            p.tt("pool", AT[:], XT[:], BD, ALU.mult, r=[kXT, "msk"], w=[kAT])
            p.tt("pool", P_[:], A[:], bc(identf[:], sh3, 1), ALU.add, r=[kA, "ident"], w=[kP])
            p.tt("pool", PT[:], AT[:], bc(identf[:], sh3, 1), ALU.add, r=[kAT, "ident"], w=[kPT])
        st.append(s3)

        def base(k):
            def f():
                mm4(0, "ps0", AT, A, [kA, kAT])
                mm4(1, "ps1", A, AT, [kA, kAT])
                p.cp("dve", fl(A), psb[0][:, :], r=["ps0"], w=[kA])
                p.cp("act", fl(AT), psb[1][:, :], r=["ps1"], w=[kAT])
                mm4(2, "ps2", AT, P_, [kAT, kP])
                mm4(0, "ps0", A, PT, [kA, kPT])
                p.tt("dve", fl(P_), fl(P_), psb[2][:, :], ALU.add, r=[kP, "ps2"], w=[kP])
                p.tt("dve", fl(PT), fl(PT), psb[0][:, :], ALU.add, r=[kPT, "ps0"], w=[kPT])
            return f
        for k in range(1, 4):
            st.append(base(k))

        def merge(mi):
            def f():
                last = (mi == 2)
                CM = bc(msk[:, 5 + mi, :], sh3, 1)
                p.tt("pool", Xc[:], X[:], CM, ALU.mult, r=[kX, "msk"], w=[kXc])
                p.tt("pool", XcT[:], XT[:], CM, ALU.mult, r=[kXT, "msk"], w=[kXcT])
                mm4(0, "ps0", XcT, P_, [kXcT, kP])
                p.cp("act", fl(Y), psb[0][:, :], r=["ps0"], w=[kY])
                if not last:
                    mm4(1, "ps1", Xc, PT, [kXc, kPT])
                    p.cp("dve", fl(Y2), psb[1][:, :], r=["ps1"], w=[kY2])
                mm4(2, "ps2", PT, Y, [kPT, kY])
                if not last:
                    mm4(0, "ps0", P_, Y2, [kP, kY2])
                p.tt("dve", fl(P_), fl(P_), psb[2][:, :], ALU.add, r=[kP, "ps2"], w=[kP])
                if not last:
                    p.tt("dve", fl(PT), fl(PT), psb[0][:, :], ALU.add, r=[kPT, "ps0"], w=[kPT])
            return f
        for mi in range(3):
            st.append(merge(mi))

        def s_fin():
            p.cp("pool", Pb16[:], P_[:], r=[kP], w=["Pb16" + D])
            for j in range(4):
                p.mm(psb[2][:, j * 128:(j + 1) * 128], Pb16[:, j, :], vtok[:, n0 + j, :], True, True, r=["Pb16" + D], w=["ps2"])
                p.mm(psb[1][:, j * 128:(j + 1) * 128], kgb[:, j, :], Pb16[:, j, :], True, True, r=["Pb16" + D, "kgb" + D], w=["ps1"])
            p.tt("dve", u[:], psb[2][:, :].rearrange("p (a b) -> p a b", a=4), bc(bg[:, n0:n0 + 4, d], sh3, 2), ALU.mult,
                 r=["ps2", "bg"], w=[gk("u")])
            p.cp("act", fl(wT), psb[1][:, :], r=["ps1"], w=[gk("wT")])
        st.append(s_fin)
        return st

    vcnt = [0, 0]

    def scan_step(d, G, j):
        s_ = G % 2
        n0 = 4 * G if d == 0 else 60 - 4 * G
        n = n0 + j
        u, wT, qdT, qkT, kdec = [grp[(nm, d, s_)] for nm in ("u", "wT", "qdT", "qkT", "kdec")]
        gk = lambda nm: "%s%d_%d" % (nm, d, s_)
        vs = vcnt[d] % 2
        vcnt[d] += 1
        vn = vnew[d][vs]
        vk = "vnew%d_%d" % (d, vs)
        os_ = ost[d][vs]
        ok = "ost%d_%d" % (d, vs)
        ps1, ps3, ps2 = pssA[d][:, 0:128], pssA[d][:, 256:384], pssB[d][:, 0:128]
        kS = "S%d" % d
        kA, kB = "scA%d" % d, "scB%d" % d
        p.mm(ps1, wT[:, j, :], Sst[d][:], True, True, r=[gk("wT"), kS], w=[kA])
        p.mm(ps2, qdT[:, j, :], Sst[d][:], True, False, r=[gk("qdT"), kS], w=[kB])
        p.stt("dve", vn[:], ps1, negb[:, n, d:d + 1], u[:, j, :], ALU.mult, ALU.add, r=[kA, "negb", gk("u")], w=[vk])
        p.mm(ps3, kdec[:, j, :], vn[:], True, True, r=[gk("kdec"), vk], w=[kA])
        p.mm(ps2, qkT[:, j, :], vn[:], False, True, r=[gk("qkT"), vk], w=[kB])
        p.stt("dve", Sst[d][:], Sst[d][:], EL[d][:, n:n + 1], ps3, ALU.mult, ALU.add, r=[kS, "EL%d" % d, kA], w=[kS])
        p.cp("act", os_[:], ps2, r=[kB], w=[ok])
        p.dma("sp", o_scr[d, n, :, :], os_[:], r=[ok], semkey=ok)

    NG = DBG_NG
    lists = [d3_stages(0, 0), d3_stages(1, 0)]
    while lists[0] or lists[1]:
        for L in lists:
            if L:
                L.pop(0)()
    for G in range(NG):
        nxt = [d3_stages(0, G + 1), d3_stages(1, G + 1)] if G + 1 < NG else [[], []]
        steps = []
        for j in range(4):
            steps.append((0, G, j))
            steps.append((1, G, 3 - j))
        while steps or nxt[0] or nxt[1]:
            if nxt[0]:
                nxt[0].pop(0)()
            if steps:
                scan_step(*steps.pop(0))
            if nxt[1]:
                nxt[1].pop(0)()
            if steps:
                scan_step(*steps.pop(0))
    p.emit()


def emit_B4(nc, o_scr, zd, ogd, identd, ybT_o):
    p = Prog(nc)
    ident = p.sb("identB4", [128, 128], BF16)
    og = p.sb("og", [128, 128], F32)
    zt = p.sb("zt", [128, NKC, 128], F32)
    sq = p.sb("sqB4", [128, 4, 128], F32)
    of_ = [p.sb("ofB4_%d" % i, [128, 4, 128], F32) for i in range(2)]
    ob_ = [p.sb("obB4_%d" % i, [128, 4, 128], F32) for i in range(2)]
    ss = p.sb("ssB4", [128, 4], F32)
    yb = p.sb("ybB4", [128, 4, 128], BF16)
    yst = [p.sb("ystB4_%d" % i, [128, 512], BF16) for i in range(2)]
    ptb = p.ps("ptB4", [128, 1024], BF16)
    p.dma("sp", ident[:], identd, w=["ident"], semkey="ident")
    p.dma("sp", og[:], ogd.partition_broadcast(128), w=["og"], semkey="og")
    for c4 in range(4):
        p.dma("sp", zt[:, c4 * 16:(c4 + 1) * 16, :], zd[c4 * 2048:(c4 + 1) * 2048, :].rearrange("(k p) d -> p k d", p=128),
              w=["zt%d" % c4], semkey="zt%d" % c4)
    for c4 in range(4):
        v = zt[:, c4 * 16:(c4 + 1) * 16, :].rearrange("p a b -> p (a b)")
        p.actf(v, v, AF.Silu, r=["zt%d" % c4], w=["zt%d" % c4])
    for G in range(16):
        n0 = 4 * G
        gs = G % 2
        p.dma("sp", of_[gs][:], o_scr[0, n0:n0 + 4, :, :].rearrange("n p e -> p n e"), w=["of%d" % gs], semkey="of%d" % gs)
        p.dma("sp", ob_[gs][:], o_scr[1, n0:n0 + 4, :, :].rearrange("n p e -> p n e"), w=["ob%d" % gs], semkey="ob%d" % gs)
        p.tt("pool", of_[gs][:], of_[gs][:], ob_[gs][:], ALU.add, r=["of%d" % gs, "ob%d" % gs], w=["of%d" % gs])
        o = of_[gs][:]
        p.tt("pool", sq[:], o, o, ALU.mult, r=["of%d" % gs], w=["sq"])
        p.red(ss[:], sq[:], ALU.add, r=["sq"], w=["ss"])
        p.ts("dve", ss[:], ss[:], 1.0 / 128, EPS, ALU.mult, ALU.add, r=["ss"], w=["ss"])
        p.actf(ss[:], ss[:], AF.Sqrt, r=["ss"], w=["ss"])
        p.recip(ss[:], ss[:], r=["ss"], w=["ss"])
        p.tt("dve", sq[:], o, bc(ss[:], [128, 4, 128], 2), ALU.mult, r=["ss", "sq", "of%d" % gs], w=["sq"])
        p.tt("pool", sq[:], sq[:], bc(og[:], [128, 4, 128], 1), ALU.mult, r=["sq", "og"], w=["sq"])
        p.tt("pool", yb[:], sq[:], zt[:, n0:n0 + 4, :], ALU.mult, r=["sq", "zt%d" % (G // 4)], w=["yb"])
        for j in range(4):
            p.tr(ptb[:, j * 128:(j + 1) * 128], yb[:, j, :], ident[:], r=["yb", "ident"], w=["ptb"])
        ys = G % 2
        p.cp("dve", yst[ys][:], ptb[:, 0:512], r=["ptb"], w=["yst%d" % ys])
        p.dma("sp", ybT_o[:, n0 * 128:(n0 + 4) * 128], yst[ys][:], r=["yst%d" % ys], semkey="ystB4_%d" % ys)
    p.emit()


def build_B(parts=("B1", "B2", "B3", "B4")):
    nc = bass.Bass("TRN2", target_bir_lowering=False)
    qkT = din(nc, "qkT", [2, 128, S], BF16)
    vd = din(nc, "v", [S, 128], BF16)
    qkg = din(nc, "qkg", [1, 128], F32)
    lqd = din(nc, "lq_d", [1, 256], F32)
    lamd = din(nc, "lamc_d", [1, 2], F32)
    sub = din(nc, "subln", [1, 128], F32)
    identd = din(nc, "identb", [128, 128], BF16)
    dnT = din(nc, "dnT", [3, 128, S], F32)
    cwd = din(nc, "cw_d", [128, 3, 5], F32)
    bgd = din(nc, "bg_d", [128, NKC, 4], F32)
    masksd = din(nc, "masks", [8, 128, 128], F32)
    zd = din(nc, "z", [S, 128], F32)
    ogd = din(nc, "og_d", [1, 128], F32)
    yaT = dout(nc, "yaT", [128, S], BF16)
    ybT = dout(nc, "ybT", [128, S], BF16)
    if "B1" in parts:
        emit_B1(nc, qkT, vd, qkg, lqd, lamd, sub, identd, yaT)
    with ExitStack() as outer:
        pers = [outer.enter_context(nc.sbuf_tensor(nm, shp, BF16)) for nm, shp in
                (("qTn", [128, S]), ("kTn", [128, S]), ("ktok", [128, NKC, 128]), ("vtok", [128, NKC, 128]))]
        o_scr = nc.dram_tensor("o_scr", [2, NKC, 128, 128], F32).ap()
        if "B2" in parts:
            emit_B2(nc, pers, dnT, cwd, identd)
        if "B3" in parts:
            emit_B3(nc, pers, o_scr, bgd, masksd, identd)
        if DBG_DUMP:
            p = Prog(nc)
            for i, (nm, shp, dt_, t_) in enumerate((("dq", [128, S], BF16, pers[0]), ("dk", [128, S], BF16, pers[1]),
                                            ("dkt", [128, NKC, 128], BF16, pers[2]), ("dvt", [128, NKC, 128], BF16, pers[3]))):
                o_ = dout(nc, nm, shp, dt_)
                p.dma("sp", o_, t_[:], semkey="dump%d" % i)
            p.emit()
        if "B4" in parts:
            emit_B4(nc, o_scr, zd, ogd, identd, ybT)
    return nc


def _masks():
    i = np.arange(128)
    pp, ff = i[:, None], i[None, :]
    bd = lambda b_: (pp // b_) == (ff // b_)
    return np.stack([pp <= ff, pp < ff, pp >= ff, pp > ff, bd(16), bd(32) & ~bd(16), bd(64) & ~bd(32), ~bd(64)]).astype(np.float32)


def run_B(inp, l, resA):
    nc = _get("B", build_B)
    identb = np.eye(128, dtype=np.float32).astype(NPBF)
    masks = _masks()
    li = 0.8 - 0.6 * math.exp(-0.3 * l)
    lamc = np.array([[li, 1.0 - li]], np.float32)
    cw = inp["dn_conv"][l]
    maps = []
    for b in range(NB):
        rs = resA[b * 4:(b + 1) * 4]
        bgc = np.concatenate([r["bg"] for r in rs], axis=1)
        for h in range(4):
            qT = np.concatenate([r["qkT"][h] for r in rs], axis=1)
            kT = np.concatenate([r["qkT"][4 + h] for r in rs], axis=1)
            maps.append({
                "qkT": np.ascontiguousarray(np.stack([qT, kT])),
                "v": np.ascontiguousarray(np.concatenate([r["v"][:, h * 128:(h + 1) * 128] for r in rs], axis=0)),
                "qkg": np.ascontiguousarray(inp["qk_norm_gain"][l].reshape(1, 128)),
                "lq_d": np.ascontiguousarray(inp["diff_lambda"][l].reshape(1, 256)),
                "lamc_d": lamc,
                "subln": np.ascontiguousarray(inp["diff_subln_gain"][l][None, :]),
                "identb": identb,
                "dnT": np.ascontiguousarray(np.stack([np.concatenate([r["dnT"][X * 4 + h] for r in rs], axis=1) for X in range(3)])),
                "cw_d": np.ascontiguousarray(np.stack([cw[:, X * 512 + h * 128:X * 512 + (h + 1) * 128].T for X in range(3)], axis=1)),
                "bg_d": np.ascontiguousarray(bgc[:, :, [h, 4 + h, 8 + h, 12 + h]]),
                "masks": masks,
                "z": np.ascontiguousarray(np.concatenate([r["z"][:, h * 128:(h + 1) * 128] for r in rs], axis=0)),
                "og_d": np.ascontiguousarray(inp["dn_out_gain"][l][None, :]),
            })
    res = run_bass_kernel_spmd(nc, maps, core_ids=list(range(8)))
    return res.results


def kernel(**inp):
    inp = {k: np.asarray(v) for k, v in inp.items()}
    x = inp["x"]
    xsh = [x[i // 4, (i % 4) * NTOK:(i % 4 + 1) * NTOK] for i in range(8)]
    for l in range(2):
        resA = run_A(inp, l, xsh)
        resB = run_B(inp, l, resA)
        yaT, ybT, gT = [], [], []
        for i in range(8):
            b, t0 = i // 4, (i % 4) * NTOK
            yaT.append(np.stack([resB[b * 4 + h]["yaT"][:, t0:t0 + NTOK] for h in range(4)]))
            ybT.append(np.stack([resB[b * 4 + h]["ybT"][:, t0:t0 + NTOK] for h in range(4)]))
            gT.append(resA[i]["gT"])
        xsh = run_C(inp, l, xsh, yaT, ybT, gT)
    out = np.empty((NB, S, D), np.float32)
    for i in range(8):
        out[i // 4, (i % 4) * NTOK:(i % 4 + 1) * NTOK] = xsh[i]
    return out
```

```python
from contextlib import ExitStack
import math
import numpy as np
import ml_dtypes
import concourse.bass as bass
import concourse.mybir as mybir
from concourse.bass_utils import run_bass_kernel_spmd

F32 = mybir.dt.float32
BF16 = mybir.dt.bfloat16
ALU = mybir.AluOpType
AF = mybir.ActivationFunctionType
AX = mybir.AxisListType
NPBF = ml_dtypes.bfloat16

ENGS = ("pe", "act", "dve", "pool", "sp")

D = 1024
S = 8192
NB = 2
NTOK = 2048
NT = NTOK // 128
EPS = 1e-6
INC = 5648


class Op:
    __slots__ = ("eng", "fn", "deps", "is_dma", "semkey", "cum", "sig", "waits", "has_dep")

    def __init__(self, eng, fn, is_dma=False, semkey=None):
        self.eng = eng
        self.fn = fn
        self.deps = []
        self.is_dma = is_dma
        self.semkey = semkey
        self.cum = 0
        self.sig = 0
        self.has_dep = False
        self.waits = []


class Prog:
    def __init__(self, nc):
        self.nc = nc
        self.es = ExitStack()
        self.ops = {e: [] for e in ENGS}
        self.lastw = {}
        self.readers = {}
        self.dma_cum = {}

    def sb(self, name, shape, dtype):
        return self.es.enter_context(self.nc.sbuf_tensor(name, list(shape), dtype))

    def ps(self, name, shape, dtype=F32):
        return self.es.enter_context(self.nc.psum_tensor(name, list(shape), dtype))

    def _rec(self, op, r, w):
        deps = []
        for k in r:
            lw = self.lastw.get(k)
            if lw is not None:
                deps.append(lw)
        for k in w:
            lw = self.lastw.get(k)
            rd = self.readers.get(k, [])
            if rd:
                deps.extend(rd)
            elif lw is not None:
                if not (op.is_dma and lw.is_dma and lw.semkey == op.semkey):
                    deps.append(lw)
        for k in r:
            self.readers.setdefault(k, []).append(op)
        for k in w:
            self.lastw[k] = op
            self.readers[k] = []
        seen = set()
        for d in deps:
            if d is op or id(d) in seen:
                continue
            seen.add(id(d))
            if d.eng == "pe" and op.eng == "pe" and not d.is_dma and not op.is_dma:
                continue
            op.deps.append(d)
            d.has_dep = True
        self.ops[op.eng].append(op)
        return op

    def op(self, eng, fn, r=(), w=()):
        return self._rec(Op(eng, fn), r, w)

    def dma(self, q, out, in_, r=(), w=(), semkey=None, **kw):
        assert semkey is not None
        o = Op(q, lambda e: e.dma_start(out=out, in_=in_, **kw), is_dma=True, semkey=semkey)
        self._rec(o, r, w)
        self.dma_cum[semkey] = self.dma_cum.get(semkey, 0) + 16
        o.cum = self.dma_cum[semkey]
        return o

    def mm(self, out, lhsT, rhs, start, stop, r, w, skip=False):
        if skip:
            return self.op("pe", lambda e: e.matmul(out, lhsT, rhs, start=start, stop=stop, skip_group_check=True), r, w)
        return self.op("pe", lambda e: e.matmul(out, lhsT, rhs, start=start, stop=stop), r, w)

    def tr(self, out, in_, ident, r, w):
        return self.op("pe", lambda e: e.transpose(out, in_, ident), r, w)

    def actf(self, out, in_, func, r, w, bias=None, scale=None, accum=None):
        kw = {}
        if bias is not None:
            kw["bias"] = bias
        if scale is not None:
            kw["scale"] = scale
        if accum is not None:
            kw["accum_out"] = accum
        return self.op("act", lambda e: e.activation(out, in_, func, **kw), r, w)

    def tt(self, eng, out, a, b, op, r, w):
        return self.op(eng, lambda e: e.tensor_tensor(out, a, b, op), r, w)

    def ts(self, eng, out, a, s1, s2, op0, op1, r, w):
        if op1 is None:
            return self.op(eng, lambda e: e.tensor_scalar(out, a, s1, None, op0), r, w)
        return self.op(eng, lambda e: e.tensor_scalar(out, a, s1, s2, op0, op1), r, w)

    def stt(self, eng, out, in0, scalar, in1, op0, op1, r, w):
        return self.op(eng, lambda e: e.scalar_tensor_tensor(out, in0, scalar, in1, op0, op1), r, w)

    def cp(self, eng, out, in_, r, w):
        if eng == "act":
            return self.op("act", lambda e: e.copy(out, in_), r, w)
        return self.op(eng, lambda e: e.tensor_copy(out, in_), r, w)

    def red(self, out, in_, op, r, w):
        return self.op("dve", lambda e: e.tensor_reduce(out, in_, AX.X, op), r, w)

    def recip(self, out, in_, r, w):
        return self.op("dve", lambda e: e.reciprocal(out, in_), r, w)

    def memset(self, eng, out, val, w):
        return self.op(eng, lambda e: e.memset(out, val), (), w)

    def emit(self):
        nc = self.nc
        es = self.es
        esem = {e: es.enter_context(nc.semaphore("s_" + e)) for e in ENGS}
        dsem = {k: es.enter_context(nc.semaphore("d_%d" % i)) for i, k in enumerate(self.dma_cum)}
        for e in ENGS:
            c = 0
            for o in self.ops[e]:
                if not o.is_dma and o.has_dep:
                    c += 1
                    o.sig = c
        for e in ENGS:
            waited = {}
            for o in self.ops[e]:
                need = {}
                for d in o.deps:
                    if d.is_dma:
                        key = ("d", d.semkey)
                        val = d.cum
                    else:
                        key = ("e", d.eng)
                        val = d.sig
                    if val > need.get(key, 0):
                        need[key] = val
                for key, val in need.items():
                    if waited.get(key, 0) >= val:
                        continue
                    waited[key] = val
                    sem = dsem[key[1]] if key[0] == "d" else esem[key[1]]
                    o.waits.append((sem, val))
        final = [(dsem[k], v) for k, v in self.dma_cum.items()]
        block = es.enter_context(nc.Block())

        def run(eng, name):
            for o in self.ops[name]:
                for sem, val in o.waits:
                    eng.wait_ge(sem, val)
                ins = o.fn(eng)
                if o.is_dma:
                    ins.then_inc(dsem[o.semkey], 16)
                elif o.has_dep:
                    ins.then_inc(esem[name], 1)
            if name == "sp":
                for sem, val in final:
                    eng.wait_ge(sem, val)

        @block.tensor
        def _(eng):
            run(eng, "pe")

        @block.scalar
        def _(eng):
            run(eng, "act")

        @block.vector
        def _(eng):
            run(eng, "dve")

        @block.gpsimd
        def _(eng):
            run(eng, "pool")

        @block.sync
        def _(eng):
            run(eng, "sp")

        es.close()


def din(nc, name, shape, dt):
    return nc.dram_tensor(name, list(shape), dt, kind="ExternalInput").ap()


def dout(nc, name, shape, dt):
    return nc.dram_tensor(name, list(shape), dt, kind="ExternalOutput").ap()


def emit_mod(p, nc, ada_w, ada_b, cpk, col0, ncols, stage, psb, modbc):
    cact = p.sb("cact", [128, 8], F32)
    crep = p.sb("crep", [128, 8, 128], F32)
    p.dma("sp", cact[:], cpk, w=["cact"], semkey="cact")
    p.dma("sp", modbc[:, 0:ncols], ada_b[:, col0:col0 + ncols].partition_broadcast(128), w=["modbc"], semkey="modbc")
    p.actf(cact[:], cact[:], AF.Silu, r=["cact"], w=["cact"])
    p.cp("dve", crep[:], cact[:].unsqueeze(2).to_broadcast([128, 8, 128]), r=["cact"], w=["crep"])
    npass = (ncols + 2047) // 2048
    for ps_ in range(npass):
        c0 = ps_ * 2048
        cw = min(2048, ncols - c0)
        for kc in range(8):
            sl = kc % 2
            p.dma("sp", stage[sl][:, 0:cw], ada_w[kc * 128:(kc + 1) * 128, col0 + c0:col0 + c0 + cw],
                  w=["stg%d" % sl], semkey="stg%d" % sl)
            for j in range(cw // 512):
                p.mm(psb[j][:, :], crep[:, kc, :], stage[sl][:, j * 512:(j + 1) * 512], kc == 0, kc == 7,
                     r=["crep", "stg%d" % sl], w=["psb%d" % j])
        for j in range(cw // 512):
            cs = c0 + j * 512
            p.tt("dve", modbc[:, cs:cs + 512], modbc[:, cs:cs + 512], psb[j][:, :], ALU.add,
                 r=["psb%d" % j, "modbc"], w=["modbc"])


def emit_rmsnorm_T(p, xt, xkey, G, gkey, SH, shkey, junk, ssk, hb, ptb, ptbkey, hT, t, ident, tag):
    ss = ssk
    p.actf(junk[:], xt, AF.Square, r=[xkey], w=["junk" + tag, "ss" + tag], accum=ss[:, 0:1])
    p.ts("dve", ss[:, 0:1], ss[:, 0:1], 1.0 / D, EPS, ALU.mult, ALU.add, r=["ss" + tag], w=["ss" + tag])
    p.actf(ss[:, 0:1], ss[:, 0:1], AF.Sqrt, r=["ss" + tag], w=["ss" + tag])
    p.recip(ss[:, 0:1], ss[:, 0:1], r=["ss" + tag], w=["ss" + tag])
    p.stt("dve", junk[:], xt, ss[:, 0:1], G, ALU.mult, ALU.mult, r=[xkey, "ss" + tag, gkey], w=["junk" + tag])
    p.tt("dve", hb[:], junk[:], SH, ALU.add, r=["junk" + tag, shkey], w=["hb" + tag])
    for kc in range(8):
        p.tr(ptb[:, kc * 128:(kc + 1) * 128], hb[:, kc * 128:(kc + 1) * 128], ident[:], r=["hb" + tag, "ident"], w=[ptbkey])
    p.cp("act", hT[:, :, t * 128:(t + 1) * 128], ptb[:].rearrange("p (k t) -> p k t", t=128), r=[ptbkey], w=["hT%d" % t])


def build_A():
    nc = bass.Bass("TRN2", target_bir_lowering=False)
    x = din(nc, "x", [NTOK, D], F32)
    cpk = din(nc, "cpk", [128, 8], F32)
    ada_w = din(nc, "ada_w", [D, 6 * D], F32)
    ada_b = din(nc, "ada_b", [1, 6 * D], F32)
    g1 = din(nc, "g1", [1, D], F32)
    w_in = din(nc, "w_in", [D, INC], F32)
    qkg = din(nc, "qkg", [1, 128], F32)
    alog = din(nc, "alog", [1, 8], F32)
    dtb = din(nc, "dtb", [1, 8], F32)
    cosd = din(nc, "cos", [NTOK, 64], F32)
    sind = din(nc, "sin", [NTOK, 64], F32)
    identd = din(nc, "identb", [128, 128], BF16)
    qkT_o = dout(nc, "qkT", [8, 128, NTOK], BF16)
    v_o = dout(nc, "v", [NTOK, 512], BF16)
    dnT_o = dout(nc, "dnT", [12, 128, NTOK], F32)
    z_o = dout(nc, "z", [NTOK, 512], F32)
    bg_o = dout(nc, "bg", [128, NT, 16], F32)
    gT_o = dout(nc, "gT", [16, 128, NTOK], BF16)

    p = Prog(nc)
    ident = p.sb("ident", [128, 128], BF16)
    modbc = p.sb("modbc", [128, 2048], F32)
    G1 = p.sb("G1", [128, D], F32)
    stage = [p.sb("stg%d" % i, [128, 2064], F32) for i in range(2)]
    wb = p.sb("wb", [128, 8, 2064], BF16)
    hT = p.sb("hT", [128, 8, NTOK], BF16)
    xs = [p.sb("xs%d" % i, [128, D], F32) for i in range(2)]
    junk = p.sb("junk", [128, D], F32)
    ssA = p.sb("ssA", [128, 1], F32)
    hb = p.sb("hb", [128, D], BF16)
    gq = p.sb("gq", [128, 128], F32)
    cst = [p.sb("cst%d" % i, [128, 64], F32) for i in range(2)]
    snt = [p.sb("snt%d" % i, [128, 64], F32) for i in range(2)]
    alb = p.sb("alb", [128, 8], F32)
    dtbb = p.sb("dtbb", [128, 8], F32)
    sq = [p.sb("sq%d" % i, [128, 512], F32) for i in range(2)]
    t1 = [p.sb("t1_%d" % i, [128, 512], F32) for i in range(2)]
    t2 = [p.sb("t2_%d" % i, [128, 512], F32) for i in range(2)]
    qr = [p.sb("qr%d" % i, [128, 512], BF16) for i in range(2)]
    ss16 = p.sb("ss16", [128, 16], F32)
    vb = [p.sb("vb%d" % i, [128, 512], BF16) for i in range(2)]
    zf = [p.sb("zf%d" % i, [128, 512], F32) for i in range(2)]
    baraw = p.sb("baraw", [128, NT, 16], F32)
    bgo = p.sb("bgo", [128, NT, 16], F32)
    bat = [p.sb("bat%d" % i, [128, NT, 8], F32) for i in range(3)]
    qkTs = [p.sb("qkTs%d" % i, [128, 8, 512], BF16) for i in range(2)]
    dst = [p.sb("dst%d" % i, [128, 512], F32) for i in range(3)]
    gst = [p.sb("gst%d" % i, [128, 512], BF16) for i in range(3)]
    psb = [p.ps("psb%d" % i, [128, 512], F32) for i in range(6)]
    ptb = [p.ps("ptb%d" % i, [128, 1024], BF16) for i in range(2)]

    p.dma("sp", ident[:], identd, w=["ident"], semkey="ident")
    p.dma("sp", G1[:], g1.partition_broadcast(128), w=["GA"], semkey="G1")
    p.dma("sp", gq[:], qkg.partition_broadcast(128), w=["gq"], semkey="gq")
    p.dma("sp", alb[:], alog.partition_broadcast(128), w=["alb"], semkey="alb")
    p.dma("sp", dtbb[:], dtb.partition_broadcast(128), w=["dtbb"], semkey="dtbb")
    emit_mod(p, nc, ada_w, ada_b, cpk, 0, 2048, stage, psb, modbc)
    p.stt("dve", G1[:], modbc[:, 1024:2048], 1.0, G1[:], ALU.add, ALU.mult, r=["modbc", "GA"], w=["GA"])
    p.actf(alb[:], alb[:], AF.Exp, r=["alb"], w=["alb"])
    p.ts("dve", alb[:], alb[:], -1.0, None, ALU.mult, None, r=["alb"], w=["alb"])

    def load_w(colranges, ncols):
        for kc in range(8):
            sl = kc % 2
            o = 0
            for (a, b) in colranges:
                p.dma("sp", stage[sl][:, o:o + (b - a)], w_in[kc * 128:(kc + 1) * 128, a:b],
                      w=["stg%d" % sl], semkey="stg%d" % sl)
                o += b - a
            p.cp("pool", wb[:, kc, 0:ncols], stage[sl][:, 0:ncols], r=["stg%d" % sl], w=["wb%d" % kc])

    def load_x(t):
        p.dma("sp", xs[t % 2][:], x[t * 128:(t + 1) * 128, :], w=["xs%d" % (t % 2)], semkey="xs%d" % (t % 2))

    load_x(0)
    load_x(1)
    load_w([(0, 1536), (3072, 3600)], 2064)
    for t in range(NT):
        emit_rmsnorm_T(p, xs[t % 2][:], "xs%d" % (t % 2), G1[:], "GA", modbc[:, 0:1024], "modbc", junk, ssA, hb,
                       ptb[t % 2], "ptb%d" % (t % 2), hT, t, ident, "A")
        if t + 2 < NT:
            load_x(t + 2)

    def load_cs(t):
        p.dma("sp", cst[t % 2][:], cosd[t * 128:(t + 1) * 128, :], w=["cst%d" % (t % 2)], semkey="cst%d" % (t % 2))
        p.dma("sp", snt[t % 2][:], sind[t * 128:(t + 1) * 128, :], w=["snt%d" % (t % 2)], semkey="snt%d" % (t % 2))

    load_cs(0)
    load_cs(1)
    wkeys = ["wb%d" % kc for kc in range(8)]
    for t in range(NT):
        for j in range(4):
            for kc in range(8):
                p.mm(psb[j][:, :], hT[:, kc, t * 128:(t + 1) * 128], wb[:, kc, j * 512:(j + 1) * 512], kc == 0, kc == 7,
                     r=["hT%d" % t, "wb%d" % kc], w=["psb%d" % j])
        for kc in range(8):
            p.mm(psb[4][:, 0:16], hT[:, kc, t * 128:(t + 1) * 128], wb[:, kc, 2048:2064], kc == 0, kc == 7,
                 r=["hT%d" % t, "wb%d" % kc], w=["psb4"])
        p.cp("act", vb[t % 2][:], psb[2][:, :], r=["psb2"], w=["vb%d" % (t % 2)])
        p.dma("pool", v_o[t * 128:(t + 1) * 128, :], vb[t % 2][:], r=["vb%d" % (t % 2)], semkey="vb%d" % (t % 2))
        p.cp("act", zf[t % 2][:], psb[3][:, :], r=["psb3"], w=["zf%d" % (t % 2)])
        p.dma("pool", z_o[t * 128:(t + 1) * 128, :], zf[t % 2][:], r=["zf%d" % (t % 2)], semkey="zf%d" % (t % 2))
        p.cp("act", baraw[:, t, :], psb[4][:, 0:16], r=["psb4"], w=["baraw"])
        for j in range(2):
            sqj, t1j, t2j, qrj = sq[j], t1[j], t2[j], qr[j]
            kq = "qk%d" % j
            p.actf(sqj[:], psb[j][:, :], AF.Square, r=["psb%d" % j], w=["sq" + kq])
            p.red(ss16[:, j * 8:(j + 1) * 8], sqj[:].rearrange("p (g d) -> p g d", d=64), ALU.add, r=["sq" + kq], w=["ss16"])
        p.ts("dve", ss16[:], ss16[:], 1.0 / 64, EPS, ALU.mult, ALU.add, r=["ss16"], w=["ss16"])
        p.actf(ss16[:], ss16[:], AF.Sqrt, r=["ss16"], w=["ss16"])
        p.recip(ss16[:], ss16[:], r=["ss16"], w=["ss16"])
        for j in range(2):
            sqj, t1j, t2j, qrj = sq[j], t1[j], t2[j], qr[j]
            kq = "qk%d" % j
            s3 = sqj[:].rearrange("p (g d) -> p g d", d=64)
            a3 = t1j[:].rearrange("p (g d) -> p g d", d=64)
            b3 = t2j[:].rearrange("p (g d) -> p g d", d=64)
            q3 = qrj[:].rearrange("p (g d) -> p g d", d=64)
            p.tt("dve", s3, psb[j][:, :].rearrange("p (g d) -> p g d", d=64),
                 ss16[:, j * 8:(j + 1) * 8].unsqueeze(2).to_broadcast([128, 8, 64]), ALU.mult,
                 r=["psb%d" % j, "ss16"], w=["sq" + kq])
            p.tt("pool", s3, s3, gq[:, j * 64:(j + 1) * 64].unsqueeze(1).to_broadcast([128, 8, 64]), ALU.mult,
                 r=["sq" + kq, "gq"], w=["sq" + kq])
            p.tt("pool", a3, s3, cst[t % 2][:].unsqueeze(1).to_broadcast([128, 8, 64]), ALU.mult,
                 r=["sq" + kq, "cst%d" % (t % 2)], w=["t1" + kq])
            p.tt("dve", b3[:, :, 0:32], s3[:, :, 32:64], snt[t % 2][:, 0:32].unsqueeze(1).to_broadcast([128, 8, 32]), ALU.mult,
                 r=["sq" + kq, "snt%d" % (t % 2)], w=["t2a" + kq])
            p.tt("pool", b3[:, :, 32:64], s3[:, :, 0:32], snt[t % 2][:, 32:64].unsqueeze(1).to_broadcast([128, 8, 32]), ALU.mult,
                 r=["sq" + kq, "snt%d" % (t % 2)], w=["t2b" + kq])
            p.tt("dve", qrj[:], t1j[:], t2j[:], ALU.add, r=["t1" + kq, "t2a" + kq, "t2b" + kq], w=["qr" + kq])
        if t + 2 < NT:
            load_cs(t + 2)
        pk = "ptb%d" % (t % 2)
        for j in range(2):
            for h in range(4):
                p.tr(ptb[t % 2][:, (j * 4 + h) * 128:(j * 4 + h + 1) * 128], qr[j][:, h * 128:(h + 1) * 128], ident[:],
                     r=["qrqk%d" % j, "ident"], w=[pk])
        sl = (t // 4) % 2
        p.cp("act", qkTs[sl][:, :, (t % 4) * 128:(t % 4 + 1) * 128], ptb[t % 2][:].rearrange("p (k t) -> p k t", t=128),
             r=[pk], w=["qkTs%d" % sl])
        if t % 4 == 3:
            tg = t // 4
            p.dma("pool", qkT_o[:, :, tg * 512:(tg + 1) * 512].rearrange("h p t -> p h t"), qkTs[sl][:],
                  r=["qkTs%d" % sl], semkey="qkTs%d" % sl)

    braw = baraw[:, :, 0:8]
    araw = baraw[:, :, 8:16]
    p.actf(bgo[:, :, 0:8], braw, AF.Sigmoid, r=["baraw"], w=["bgo_b"])
    xg, ab, mx = bat
    p.tt("dve", xg[:], araw, dtbb[:].unsqueeze(1).to_broadcast([128, NT, 8]), ALU.add, r=["baraw", "dtbb"], w=["xg"])
    p.ts("dve", ab[:], xg[:], -1.0, None, ALU.mult, None, r=["xg"], w=["ab"])
    p.tt("dve", ab[:], ab[:], xg[:], ALU.max, r=["ab", "xg"], w=["ab"])
    p.actf(ab[:], ab[:], AF.Exp, r=["ab"], w=["ab"], scale=-1.0)
    p.ts("dve", ab[:], ab[:], 1.0, None, ALU.add, None, r=["ab"], w=["ab"])
    p.actf(ab[:], ab[:], AF.Ln, r=["ab"], w=["ab"])
    p.ts("dve", mx[:], xg[:], 0.0, None, ALU.max, None, r=["xg"], w=["mx"])
    p.tt("dve", mx[:], mx[:], ab[:], ALU.add, r=["mx", "ab"], w=["mx"])
    p.tt("dve", bgo[:, :, 8:16], mx[:], alb[:].unsqueeze(1).to_broadcast([128, NT, 8]), ALU.mult, r=["mx", "alb"], w=["bgo_g"])
    p.dma("pool", bg_o, bgo[:], r=["bgo_b", "bgo_g"], semkey="bgo")

    cnt = 0
    for grp, (a, b) in enumerate([(1536, 3072), (3600, 5648)]):
        ncols = b - a
        load_w([(a, b)], ncols)
        for tg in range(4):
            for cc in range(ncols // 128):
                bk = cnt % 4
                for kc in range(8):
                    p.mm(psb[bk][:, :], wb[:, kc, cc * 128:(cc + 1) * 128], hT[:, kc, tg * 512:(tg + 1) * 512], kc == 0, kc == 7,
                         r=["wb%d" % kc] + ["hT%d" % tt_ for tt_ in range(tg * 4, tg * 4 + 4)], w=["psb%d" % bk])
                sl = cnt % 3
                if grp == 0:
                    p.cp("act", dst[sl][:], psb[bk][:, :], r=["psb%d" % bk], w=["dst%d" % sl])
                    p.dma("pool", dnT_o[cc, :, tg * 512:(tg + 1) * 512], dst[sl][:], r=["dst%d" % sl], semkey="dst%d" % sl)
                else:
                    p.actf(gst[sl][:], psb[bk][:, :], AF.Sigmoid, r=["psb%d" % bk], w=["gst%d" % sl])
                    p.dma("pool", gT_o[cc, :, tg * 512:(tg + 1) * 512], gst[sl][:], r=["gst%d" % sl], semkey="gst%d" % sl)
                cnt += 1
    p.emit()
    return nc


_CACHE = {}


def _rope_tables():
    pos = np.arange(S, dtype=np.float32)
    inv = (1.0 / (np.float32(10000.0) ** (np.arange(0, 64, 2, dtype=np.float32) / np.float32(64)))).astype(np.float32)
    ang = (pos[:, None] * inv[None, :]).astype(np.float32)
    ang = np.concatenate([ang, ang], axis=-1)
    cos = np.cos(ang).astype(np.float32)
    sin = np.sin(ang).astype(np.float32)
    sin_signed = sin.copy()
    sin_signed[:, :32] = -sin_signed[:, :32]
    return cos, sin_signed


def _get(name, builder):
    if name not in _CACHE:
        _CACHE[name] = builder()
    return _CACHE[name]


def run_A(inp, l, xsh):
    nc = _get("A", build_A)
    cos, sins = _get("rope", _rope_tables)
    identb = np.eye(128, dtype=np.float32).astype(NPBF)
    maps = []
    for i in range(8):
        b, t0 = i // 4, (i % 4) * NTOK
        maps.append({
            "x": np.ascontiguousarray(xsh[i]),
            "cpk": np.ascontiguousarray(inp["c"][b].reshape(8, 128).T),
            "ada_w": np.ascontiguousarray(inp["ada_w"][l]),
            "ada_b": np.ascontiguousarray(inp["ada_b"][l][None, :]),
            "g1": np.ascontiguousarray(inp["norm1_g"][l][None, :]),
            "w_in": np.ascontiguousarray(inp["w_in"][l]),
            "qkg": np.ascontiguousarray(inp["qk_norm_gain"][l].reshape(1, 128)),
            "alog": np.ascontiguousarray(inp["dn_A_log"][l].reshape(1, 8)),
            "dtb": np.ascontiguousarray(inp["dn_dt_bias"][l].reshape(1, 8)),
            "cos": np.ascontiguousarray(cos[t0:t0 + NTOK]),
            "sin": np.ascontiguousarray(sins[t0:t0 + NTOK]),
            "identb": identb,
        })
    res = run_bass_kernel_spmd(nc, maps, core_ids=list(range(8)))
    return res.results


def emit_C(nc, outer, x_in, yaT, ybT, gT, cpk, ada_w, ada_b, g2, w_a, w_b, w_out, w1, w2, identd, x_out):
    x_res = outer.enter_context(nc.sbuf_tensor("x_res", [128, NT, D], F32))
    modbc = outer.enter_context(nc.sbuf_tensor("modbcC", [128, 4096], F32))
    G2 = outer.enter_context(nc.sbuf_tensor("G2", [128, D], F32))
    ident = outer.enter_context(nc.sbuf_tensor("identC", [128, 128], BF16))

    p = Prog(nc)
    stage = [p.sb("stgC%d" % i, [128, 2048], F32) for i in range(2)]
    psb = [p.ps("psC0_%d" % i, [128, 512], F32) for i in range(4)]
    p.dma("sp", ident[:], identd, w=["ident"], semkey="ident")
    p.dma("sp", G2[:], g2.partition_broadcast(128), w=["G2"], semkey="G2")
    for t in range(NT):
        p.dma("sp", x_res[:, t, :], x_in[t * 128:(t + 1) * 128, :], w=["xr%d" % t], semkey="xres")
    emit_mod(p, nc, ada_w, ada_b, cpk, 2048, 4096, stage, psb, modbc)
    p.stt("dve", G2[:], modbc[:, 2048:3072], 1.0, G2[:], ALU.add, ALU.mult, r=["modbc", "G2"], w=["G2"])
    p.emit()
    gt1 = modbc[:, 0:1024]
    sh2 = modbc[:, 1024:2048]
    gt2 = modbc[:, 3072:4096]

    p = Prog(nc)
    stg = [p.sb("stg1_%d" % i, [128, 1024], F32) for i in range(2)]
    wab = p.sb("wab", [128, 4, D], BF16)
    wbb = p.sb("wbb", [128, 4, D], BF16)
    wob = p.sb("wob", [128, 8, D], BF16)
    yas = [p.sb("yas%d" % i, [128, 4, 512], BF16) for i in range(2)]
    ybs = [p.sb("ybs%d" % i, [128, 4, 512], BF16) for i in range(2)]
    gas = [p.sb("gas%d" % i, [128, 512], BF16) for i in range(2)]
    gbs = [p.sb("gbs%d" % i, [128, 512], BF16) for i in range(2)]
    m1 = [p.sb("m1_%d" % i, [128, 512], F32) for i in range(2)]
    m2 = [p.sb("m2_%d" % i, [128, 512], F32) for i in range(2)]
    mT = [p.sb("mT%d" % i, [128, 8, 512], BF16) for i in range(2)]
    tmp = [p.sb("tmpc%d" % i, [128, 512], F32) for i in range(2)]
    psb = [p.ps("psC1_%d" % i, [128, 512], F32) for i in range(6)]
    si = 0
    for h in range(4):
        for (wsrc, wdst, nm) in ((w_a, wab, "wab"), (w_b, wbb, "wbb")):
            sl = si % 2
            si += 1
            p.dma("sp", stg[sl][:], wsrc[h * 128:(h + 1) * 128, :], w=["stg%d" % sl], semkey="stg%d" % sl)
            p.cp("pool", wdst[:, h, :], stg[sl][:], r=["stg%d" % sl], w=[nm])
    for kc in range(8):
        sl = si % 2
        si += 1
        p.dma("sp", stg[sl][:], w_out[kc * 128:(kc + 1) * 128, :], w=["stg%d" % sl], semkey="stg%d" % sl)
        p.cp("pool", wob[:, kc, :], stg[sl][:], r=["stg%d" % sl], w=["wob"])
    cnt = 0
    for tg in range(4):
        s2 = tg % 2
        p.dma("sp", yas[s2][:], yaT[:, :, tg * 512:(tg + 1) * 512].rearrange("h p t -> p h t"), w=["yas%d" % s2], semkey="yas%d" % s2)
        p.dma("sp", ybs[s2][:], ybT[:, :, tg * 512:(tg + 1) * 512].rearrange("h p t -> p h t"), w=["ybs%d" % s2], semkey="ybs%d" % s2)
        for fc in range(8):
            s3 = fc % 2
            p.dma("sp", gas[s3][:], gT[fc, :, tg * 512:(tg + 1) * 512], w=["gas%d" % s3], semkey="gas%d" % s3)
            p.dma("sp", gbs[s3][:], gT[8 + fc, :, tg * 512:(tg + 1) * 512], w=["gbs%d" % s3], semkey="gbs%d" % s3)
            ba, bb = (0, 1) if fc % 2 == 0 else (2, 3)
            for h in range(4):
                p.mm(psb[ba][:, :], wab[:, h, fc * 128:(fc + 1) * 128], yas[s2][:, h, :], h == 0, h == 3,
                     r=["wab", "yas%d" % s2], w=["ps%d" % ba])
            for h in range(4):
                p.mm(psb[bb][:, :], wbb[:, h, fc * 128:(fc + 1) * 128], ybs[s2][:, h, :], h == 0, h == 3,
                     r=["wbb", "ybs%d" % s2], w=["ps%d" % bb])
            p.tt("dve", m1[s3][:], psb[ba][:, :], gas[s3][:], ALU.mult, r=["ps%d" % ba, "gas%d" % s3], w=["m1_%d" % s3])
            p.tt("dve", m2[s3][:], psb[bb][:, :], gbs[s3][:], ALU.mult, r=["ps%d" % bb, "gbs%d" % s3], w=["m2_%d" % s3])
            p.tt("pool", mT[s2][:, fc, :], m1[s3][:], m2[s3][:], ALU.add, r=["m1_%d" % s3, "m2_%d" % s3], w=["mT%d_%d" % (s2, fc)])
        for i in range(4):
            t = tg * 4 + i
            for j in range(2):
                bk = 4 + (cnt % 2)
                ts_ = cnt % 2
                cnt += 1
                for kc in range(8):
                    p.mm(psb[bk][:, :], mT[s2][:, kc, i * 128:(i + 1) * 128], wob[:, kc, j * 512:(j + 1) * 512], kc == 0, kc == 7,
                         r=["mT%d_%d" % (s2, kc), "wob"], w=["ps%d" % bk])
                p.tt("dve", tmp[ts_][:], psb[bk][:, :], gt1[:, j * 512:(j + 1) * 512], ALU.mult, r=["ps%d" % bk], w=["tmp%d" % ts_])
                p.tt("pool", x_res[:, t, j * 512:(j + 1) * 512], x_res[:, t, j * 512:(j + 1) * 512], tmp[ts_][:], ALU.add,
                     r=["tmp%d" % ts_], w=["xr%d_%d" % (t, j)])
    p.emit()

    p = Prog(nc)
    h2T = p.sb("h2T", [128, 8, NTOK], BF16)
    junk = p.sb("junkC", [128, D], F32)
    ssC = p.sb("ssC", [128, 1], F32)
    hb = p.sb("hbC", [128, D], BF16)
    stg = [p.sb("stg2_%d" % i, [128, 1024], F32) for i in range(2)]
    w1b = [p.sb("w1b%d" % i, [128, 8, 512], BF16) for i in range(2)]
    w2b = [p.sb("w2b%d" % i, [128, 4, D], BF16) for i in range(2)]
    aT = p.sb("aT", [128, 4, NTOK], BF16)
    rl = [p.sb("rl%d" % i, [128, 512], F32) for i in range(2)]
    tmp = [p.sb("tmpd%d" % i, [128, 512], F32) for i in range(2)]
    psb = [p.ps("psC2_%d" % i, [128, 512], F32) for i in range(6)]
    ptb = [p.ps("ptC2_%d" % i, [128, 1024], BF16) for i in range(2)]
    si = 0

    def load_fb(fb):
        nonlocal si
        s = fb % 2
        for k2 in range(4):
            sl = si % 2
            si += 1
            p.dma("sp", stg[sl][:].rearrange("p (k c) -> p k c", k=2),
                  w1[k2 * 256:(k2 + 1) * 256, fb * 512:(fb + 1) * 512].rearrange("(k p) c -> p k c", p=128),
                  w=["stg%d" % sl], semkey="stg%d" % sl)
            p.cp("pool", w1b[s][:, k2 * 2:k2 * 2 + 2, :], stg[sl][:].rearrange("p (k c) -> p k c", k=2), r=["stg%d" % sl], w=["w1b%d" % s])
        for c in range(4):
            sl = si % 2
            si += 1
            p.dma("sp", stg[sl][:], w2[fb * 512 + c * 128:fb * 512 + (c + 1) * 128, :], w=["stg%d" % sl], semkey="stg%d" % sl)
            p.cp("pool", w2b[s][:, c, :], stg[sl][:], r=["stg%d" % sl], w=["w2b%d" % s])

    load_fb(0)
    for t in range(NT):
        emit_rmsnorm_T(p, x_res[:, t, :], "xr%d" % t, G2[:], "G2", sh2, "modbc", junk, ssC, hb,
                       ptb[t % 2], "ptb%d" % (t % 2), h2T, t, ident, "C")
    cnt = 0
    c2 = 0
    for fb in range(8):
        s = fb % 2
        if fb + 1 < 8:
            load_fb(fb + 1)
        for tg in range(4):
            for c in range(4):
                bk = cnt % 4
                rs = cnt % 2
                cnt += 1
                for kc in range(8):
                    p.mm(psb[bk][:, :], w1b[s][:, kc, c * 128:(c + 1) * 128], h2T[:, kc, tg * 512:(tg + 1) * 512], kc == 0, kc == 7,
                         r=["w1b%d" % s] + ["hT%d" % tt_ for tt_ in range(tg * 4, tg * 4 + 4)], w=["ps%d" % bk])
                p.actf(rl[rs][:], psb[bk][:, :], AF.Relu, r=["ps%d" % bk], w=["rl%d" % rs])
                p.tt("pool", aT[:, c, tg * 512:(tg + 1) * 512], rl[rs][:], rl[rs][:], ALU.mult, r=["rl%d" % rs], w=["aT%d_%d" % (c, tg)])
        for t in range(NT):
            for j in range(2):
                bk = 4 + (c2 % 2)
                ts_ = c2 % 2
                c2 += 1
                for c in range(4):
                    p.mm(psb[bk][:, :], aT[:, c, t * 128:(t + 1) * 128], w2b[s][:, c, j * 512:(j + 1) * 512], c == 0, c == 3,
                         r=["aT%d_%d" % (c, t // 4), "w2b%d" % s], w=["ps%d" % bk])
                p.tt("dve", tmp[ts_][:], psb[bk][:, :], gt2[:, j * 512:(j + 1) * 512], ALU.mult, r=["ps%d" % bk], w=["tmp%d" % ts_])
                p.tt("pool", x_res[:, t, j * 512:(j + 1) * 512], x_res[:, t, j * 512:(j + 1) * 512], tmp[ts_][:], ALU.add,
                     r=["tmp%d" % ts_], w=["xr%d" % t])
    for t in range(NT):
        p.dma("sp", x_out[t * 128:(t + 1) * 128, :], x_res[:, t, :], r=["xr%d" % t], semkey="xout")
    p.emit()


def build_C():
    nc = bass.Bass("TRN2", target_bir_lowering=False)
    x = din(nc, "x", [NTOK, D], F32)
    yaT = din(nc, "yaT", [4, 128, NTOK], BF16)
    ybT = din(nc, "ybT", [4, 128, NTOK], BF16)
    gT = din(nc, "gT", [16, 128, NTOK], BF16)
    cpk = din(nc, "cpk", [128, 8], F32)
    ada_w = din(nc, "ada_w", [D, 6 * D], F32)
    ada_b = din(nc, "ada_b", [1, 6 * D], F32)
    g2 = din(nc, "g2", [1, D], F32)
    w_a = din(nc, "w_a", [512, D], F32)
    w_b = din(nc, "w_b", [512, D], F32)
    w_out = din(nc, "w_out", [D, D], F32)
    w1 = din(nc, "w1", [D, 4 * D], F32)
    w2 = din(nc, "w2", [4 * D, D], F32)
    identd = din(nc, "identb", [128, 128], BF16)
    x_out = dout(nc, "x_out", [NTOK, D], F32)
    with ExitStack() as outer:
        emit_C(nc, outer, x, yaT, ybT, gT, cpk, ada_w, ada_b, g2, w_a, w_b, w_out, w1, w2, identd, x_out)
    return nc


def run_C(inp, l, xsh, yaT, ybT, gT):
    nc = _get("C", build_C)
    identb = np.eye(128, dtype=np.float32).astype(NPBF)
    maps = []
    for i in range(8):
        b = i // 4
        maps.append({
            "x": np.ascontiguousarray(xsh[i]),
            "yaT": np.ascontiguousarray(yaT[i]),
            "ybT": np.ascontiguousarray(ybT[i]),
            "gT": np.ascontiguousarray(gT[i]),
            "cpk": np.ascontiguousarray(inp["c"][b].reshape(8, 128).T),
            "ada_w": np.ascontiguousarray(inp["ada_w"][l]),
            "ada_b": np.ascontiguousarray(inp["ada_b"][l][None, :]),
            "g2": np.ascontiguousarray(inp["norm2_g"][l][None, :]),
            "w_a": np.ascontiguousarray(inp["w_branch_a"][l]),
            "w_b": np.ascontiguousarray(inp["w_branch_b"][l]),
            "w_out": np.ascontiguousarray(inp["w_out"][l]),
            "w1": np.ascontiguousarray(inp["w_mlp1"][l]),
            "w2": np.ascontiguousarray(inp["w_mlp2"][l]),
            "identb": identb,
        })
    res = run_bass_kernel_spmd(nc, maps, core_ids=list(range(8)))
    return [r["x_out"] for r in res.results]


DBG_B3 = 0
B1_DUP = 1
DBG_DUMP = 0
DBG_NG = 16
DBG_D = 0
DBG_DIR = -1
NKC = S // 128


def emit_B1(nc, qkT, vd, qkg, lqd, lamd, sublnd, identd, yaT_o):
    p = Prog(nc)
    qT = p.sb("qT", [128, S], BF16)
    kTz = [p.sb("kTz%d" % m, [128, S], BF16) for m in range(2)]
    vv = p.sb("vv", [128, NKC, 128], BF16)
    gq = p.sb("gqB", [128, 128], F32)
    gqa = p.sb("gqa", [128, 128], F32)
    mx2 = p.sb("mx2", [128, 2], F32)
    nshift = p.sb("nshift", [128, 1], F32)
    lq = p.sb("lq", [128, 4, 64], F32)
    lqp = p.sb("lqp", [128, 2, 64], F32)
    lsum = p.sb("lsum", [128, 2], F32)
    lam = p.sb("lam", [128, 1], F32)
    lamc = p.sb("lamc", [128, 2], F32)
    gsubc = p.sb("gsubc", [128, 1], F32)
    epst = p.sb("epsB1", [128, 1], F32)
    onesf = p.sb("onesfB1", [128, 128], F32)
    onesb = p.sb("onesbB1", [128, 128], BF16)
    pT = [p.sb("pT%d" % i, [128, 1024], BF16) for i in range(3)]
    racc = [[p.sb("racc%d_%d" % (m, e), [128, 1024], F32) for e in range(2)] for m in range(2)]
    r1 = p.sb("r1", [128, 512], F32)
    r2 = p.sb("r2", [128, 512], F32)
    t1 = p.sb("t1B1", [128, 512], F32)
    t2 = p.sb("t2B1", [128, 512], F32)
    sqb = p.sb("sqbB1", [128, 512], BF16)
    rs = p.sb("rsB1", [128, 512], F32)
    yst = [p.sb("yst%d" % i, [128, 512], BF16) for i in range(2)]
    sps = [p.ps("sps%d" % i, [128, 1024], F32) for i in range(2)]
    accO = [p.ps("accO%d" % m, [128, 512], F32) for m in range(2)]
    sums = [p.ps("sums%d" % m, [128, 512], F32) for m in range(2)]
    ssp = sums[0]

    p.dma("sp", gq[:], qkg.partition_broadcast(128), w=["gq"], semkey="gq")
    p.dma("sp", lq[:].rearrange("p a b -> p (a b)"), lqd.partition_broadcast(128), w=["lq"], semkey="lq")
    p.dma("sp", lamc[:], lamd.partition_broadcast(128), w=["lamc"], semkey="lamc")
    p.dma("sp", gsubc[:], sublnd.rearrange("o d -> d o"), w=["gsubc"], semkey="gsubc")
    for c4 in range(4):
        p.dma("sp", qT[:, c4 * 2048:(c4 + 1) * 2048], qkT[0, :, c4 * 2048:(c4 + 1) * 2048], w=["qT"], semkey="qT")
        p.dma("sp", kTz[0][0:64, c4 * 2048:(c4 + 1) * 2048], qkT[1, 0:64, c4 * 2048:(c4 + 1) * 2048], w=["kT0d"], semkey="kT0")
        p.dma("sp", kTz[1][64:128, c4 * 2048:(c4 + 1) * 2048], qkT[1, 64:128, c4 * 2048:(c4 + 1) * 2048], w=["kT1d"], semkey="kT1")
    p.memset("pool", kTz[0][64:128, :], 0.0, w=["kT0z"])
    p.memset("pool", kTz[1][0:64, :], 0.0, w=["kT1z"])
    for c4 in range(4):
        p.dma("sp", vv[:, c4 * 16:(c4 + 1) * 16, :],
              vd[c4 * 2048:(c4 + 1) * 2048, :].rearrange("(k p) d -> p k d", p=128), w=["vv"], semkey="vv")
    p.memset("pool", onesf[:], 1.0, w=["onesf"])
    p.memset("pool", onesb[:], 1.0, w=["onesb"])
    p.memset("pool", epst[:], EPS, w=["epst"])
    p.ts("dve", gqa[:], gq[:], -1.0, None, ALU.mult, None, r=["gq"], w=["gqa"])
    p.tt("dve", gqa[:], gqa[:], gq[:], ALU.max, r=["gqa", "gq"], w=["gqa"])
    p.op("dve", lambda e: e.tensor_reduce(mx2[:], gqa[:].rearrange("p (a b) -> p a b", a=2), AX.X, ALU.max), r=["gqa"], w=["mx2"])
    p.stt("dve", nshift[:], mx2[:, 0:1], -8.0, mx2[:, 1:2], ALU.mult, ALU.mult, r=["mx2"], w=["nshift"])
    p.tt("dve", lqp[:, 0, :], lq[:, 0, :], lq[:, 1, :], ALU.mult, r=["lq"], w=["lqp0"])
    p.tt("dve", lqp[:, 1, :], lq[:, 2, :], lq[:, 3, :], ALU.mult, r=["lq"], w=["lqp1"])
    p.red(lsum[:], lqp[:], ALU.add, r=["lqp0", "lqp1"], w=["lsum"])
    p.actf(lsum[:], lsum[:], AF.Exp, r=["lsum"], w=["lsum"])
    p.tt("dve", lam[:], lsum[:, 0:1], lsum[:, 1:2], ALU.subtract, r=["lsum"], w=["lam"])
    p.tt("dve", lam[:], lam[:], lamc[:, 0:1], ALU.add, r=["lam", "lamc"], w=["lam"])
    p.ts("dve", lam[:], lam[:], -1.0, None, ALU.mult, None, r=["lam"], w=["lam"])
    p.ts("dve", gsubc[:], gsubc[:], lamc[:, 1:2], None, ALU.mult, None, r=["gsubc", "lamc"], w=["gsubc"])

    pair = 0
    NP_ = NKC // 2
    for qg in range(16):
        for m in range(2):
            def s_pair(pi, pr):
                b = pr % 2
                for hf in range(2):
                    kc = pi * 2 + hf
                    p.mm(sps[b][:, hf * 512:(hf + 1) * 512], kTz[m][:, kc * 128:(kc + 1) * 128],
                         qT[:, qg * 512:(qg + 1) * 512], True, True, r=["qT", "kT%dd" % m, "kT%dz" % m], w=["sps%d" % b])
            s_pair(0, pair)
            for pi in range(NP_):
                b = pair % 2
                sl = pair % 3
                if pi + 1 < NP_:
                    s_pair(pi + 1, pair + 1)
                p.actf(pT[sl][:], sps[b][:, :], AF.Exp, r=["sps%d" % b, "nshift"], w=["pT%d" % sl], bias=nshift[:, 0:1], scale=0.125)
                for hf in range(2):
                    kc = pi * 2 + hf
                    p.mm(accO[m][:, :], vv[:, kc, :], pT[sl][:, hf * 512:(hf + 1) * 512], kc == 0, kc == NKC - 1,
                         r=["pT%d" % sl, "vv"], w=["accO%d" % m])
                    p.mm(sums[m][:, :], onesb[:], pT[sl][:, hf * 512:(hf + 1) * 512], kc == 0, kc == NKC - 1,
                         r=["pT%d" % sl, "onesb"], w=["sums%d" % m])
                pair += 1
        p.recip(r1[:], sums[0][:, :], r=["sums0"], w=["r1"])
        p.recip(r2[:], sums[1][:, :], r=["sums1"], w=["r2"])
        p.ts("pool", r2[:], r2[:], lam[:, 0:1], None, ALU.mult, None, r=["r2", "lam"], w=["r2"])
        p.tt("dve", t1[:], accO[0][:, :], r1[:], ALU.mult, r=["accO0", "r1"], w=["t1"])
        p.tt("dve", t2[:], accO[1][:, :], r2[:], ALU.mult, r=["accO1", "r2"], w=["t2"])
        p.tt("pool", t1[:], t1[:], t2[:], ALU.add, r=["t1", "t2"], w=["t1"])
        p.tt("pool", sqb[:], t1[:], t1[:], ALU.mult, r=["t1"], w=["sqb"])
        p.mm(ssp[:, :], onesb[:], sqb[:], True, True, r=["onesb", "sqb"], w=["sums0"])
        p.actf(rs[:], ssp[:, :], AF.Sqrt, r=["sums0", "epst"], w=["rs"], bias=epst[:, 0:1], scale=1.0 / 128)
        p.recip(rs[:], rs[:], r=["rs"], w=["rs"])
        ys = qg % 2
        p.stt("dve", yst[ys][:], t1[:], gsubc[:, 0:1], rs[:], ALU.mult, ALU.mult, r=["t1", "gsubc", "rs"], w=["yst%d" % ys])
        p.dma("sp", yaT_o[:, qg * 512:(qg + 1) * 512], yst[ys][:], r=["yst%d" % ys], semkey="yst%d" % ys)
    p.emit()


def bc(ap, shape, axis):
    return ap.unsqueeze(axis).to_broadcast(list(shape))


def emit_B2(nc, pers, dnT, cwd, identd):
    qTn, kTn, ktok, vtok = pers
    p = Prog(nc)
    ident = p.sb("identB2", [128, 128], BF16)
    onesb = p.sb("onesb", [128, 128], BF16)
    epst = p.sb("epst", [128, 1], F32)
    cw = p.sb("cw", [128, 3, 5], F32)
    xin = [p.sb("xin%d" % i, [128, 2052], F32) for i in range(2)]
    acc = [p.sb("cacc%d" % i, [128, 2048], F32) for i in range(2)]
    sqb = p.sb("sqb", [128, 2048], BF16)
    rin = p.sb("rin", [128, 2048], F32)
    vtb = p.sb("vtb", [128, 2048], BF16)
    ps = [p.ps("psB2_%d" % i, [128, 512], F32) for i in range(4)]
    ptb = [p.ps("ptB2_%d" % i, [128, 1024], BF16) for i in range(2)]
    p.dma("sp", ident[:], identd, w=["ident"], semkey="ident")
    p.dma("sp", cw[:], cwd, w=["cw"], semkey="cw")
    p.memset("pool", onesb[:], 1.0, w=["onesb"])
    p.memset("pool", epst[:], EPS, w=["epst"])
    it = 0
    tcnt = 0
    for tg in range(4):
        for X in range(3):
            sl = it % 2
            it += 1
            xk, ak = "xin%d" % sl, "cacc%d" % sl
            lo = tg * 2048 - 2
            hi = tg * 2048 + 2050
            if tg == 0:
                p.memset("pool", xin[sl][:, 0:2], 0.0, w=[xk])
                p.dma("sp", xin[sl][:, 2:2052], dnT[X, :, 0:2050], w=[xk + "d"], r=[xk], semkey=xk)
            elif tg == 3:
                p.memset("pool", xin[sl][:, 2050:2052], 0.0, w=[xk])
                p.dma("sp", xin[sl][:, 0:2050], dnT[X, :, lo:S], w=[xk + "d"], r=[xk], semkey=xk)
            else:
                p.memset("pool", xin[sl][:, 0:1], 0.0, w=[xk])
                p.dma("sp", xin[sl][:, :], dnT[X, :, lo:hi], w=[xk + "d"], r=[xk], semkey=xk)
            eng = "dve"
            a = acc[sl]
            p.ts(eng, a[:], xin[sl][:, 0:2048], cw[:, X, 0:1], None, ALU.mult, None, r=[xk + "d", "cw"], w=[ak])
            for j in range(1, 5):
                p.stt(eng, a[:], xin[sl][:, j:j + 2048], cw[:, X, j:j + 1], a[:], ALU.mult, ALU.add, r=[xk + "d", "cw", ak], w=[ak])
            p.actf(a[:], a[:], AF.Silu, r=[ak, xk + "d"], w=[ak, xk])
            if X == 2:
                p.cp("pool", vtb[:], a[:], r=[ak], w=["vtb"])
                src_b, dst, skey = vtb, vtok, "vtb"
            else:
                p.tt("pool", sqb[:], a[:], a[:], ALU.mult, r=[ak], w=["sqb"])
                for blk in range(4):
                    p.mm(ps[blk][:, :], onesb[:], sqb[:, blk * 512:(blk + 1) * 512], True, True, r=["onesb", "sqb"], w=["psn%d" % blk])
                    p.actf(rin[:, blk * 512:(blk + 1) * 512], ps[blk][:, :], AF.Sqrt, r=["psn%d" % blk, "epst"], w=["rin%d" % blk],
                           bias=epst[:, 0:1])
                p.recip(rin[:], rin[:], r=["rin%d" % b_ for b_ in range(4)], w=["rin"])
                tgt = qTn if X == 0 else kTn
                sc = (128.0 ** -0.5) if X == 0 else 1.0
                p.stt("dve", tgt[:, tg * 2048:(tg + 1) * 2048], a[:], sc, rin[:], ALU.mult, ALU.mult, r=[ak, "rin"],
                      w=["nT%d_%d" % (X, tg)] + ["rin%d" % b_ for b_ in range(4)])
                src_b, dst, skey = kTn, ktok, "nT1_%d" % tg
            if X >= 1:
                for half in range(2):
                    pb = tcnt % 2
                    tcnt += 1
                    for c in range(8):
                        cc = half * 8 + c
                        if X == 2:
                            src = vtb[:, cc * 128:(cc + 1) * 128]
                        else:
                            src = kTn[:, tg * 2048 + cc * 128:tg * 2048 + (cc + 1) * 128]
                        p.tr(ptb[pb][:, c * 128:(c + 1) * 128], src, ident[:], r=[skey, "ident"], w=["ptb%d" % pb])
                    n0 = tg * 16 + half * 8
                    p.cp("act", dst[:, n0:n0 + 8, :], ptb[pb][:].rearrange("p (k t) -> p k t", t=128), r=["ptb%d" % pb],
                         w=["tok%d_%d" % (X, n0)])
    p.emit()


def emit_B3(nc, pers, o_scr, bgd, masksd, identd):
    qTn, kTn, ktok, vtok = pers
    p = Prog(nc)
    identb = p.sb("identB3", [128, 128], BF16)
    identf = p.sb("identfB3", [128, 128], F32)
    msk = p.sb("msk", [128, 8, 128], F32)
    onesf = p.sb("onesf", [128, 128], F32)
    bg = p.sb("bgt", [128, NKC, 4], F32)
    negb = p.sb("negb", [128, NKC, 2], F32)
    EG = [p.sb("EG%d" % d, [128, NKC], F32) for d in range(2)]
    ER = [p.sb("ER%d" % d, [128, NKC], F32) for d in range(2)]
    EL = [p.sb("EL%d" % d, [128, NKC], F32) for d in range(2)]
    Sst = [p.sb("Sst%d" % d, [128, 128], F32) for d in range(2)]
    vnew = [[p.sb("vnew%d_%d" % (d, i), [128, 128], F32) for i in range(2)] for d in range(2)]
    ost = [[p.sb("ost%d_%d" % (d, i), [128, 128], F32) for i in range(2)] for d in range(2)]
    grp = {}
    for d in range(2):
        for s_ in range(2):
            for nm in ("u", "wT", "qdT", "qkT", "kdec"):
                grp[(nm, d, s_)] = p.sb("%s%d_%d" % (nm, d, s_), [128, 4, 128], F32)
    T3 = {}
    for d in range(2):
        for nm in ("gtri", "DTr", "EGr", "tmpf", "X", "XT", "A", "AT", "P", "PT"):
            T3[(nm, d)] = p.sb("%s_%d" % (nm, d), [128, 4, 128], F32)
        for nm in ("kgb", "Pb16"):
            T3[(nm, d)] = p.sb("%s_%d" % (nm, d), [128, 4, 128], BF16)
    psbD = [[p.ps("psB3_%d_%d" % (d, i), [128, 512], F32) for i in range(3)] for d in range(2)]
    psb = psbD[0]
    pssA = [p.ps("psscA%d" % d, [128, 512], F32) for d in range(2)]

    p.dma("sp", identb[:], identd, w=["identb_"], semkey="ident")
    p.cp("dve", identf[:], identb[:], r=["identb_"], w=["ident"])
    p.dma("sp", msk[:], masksd.rearrange("a p f -> p a f"), w=["msk"], semkey="msk")
    p.dma("sp", bg[:], bgd, w=["bg"], semkey="bg")
    p.memset("pool", onesf[:], 1.0, w=["onesf"])
    for d in range(2):
        p.memset("pool", Sst[d][:], 0.0, w=["S%d" % d])
    p.ts("dve", negb[:], bg[:, :, 0:2], -1.0, None, ALU.mult, None, r=["bg"], w=["negb"])
    for d in range(2):
        g = bg[:, :, 2 + d]
        mGC = msk[:, 0, :] if d == 0 else msk[:, 2, :]
        mGR = msk[:, 3, :] if d == 0 else msk[:, 1, :]
        for i, (lhs, dst, nm) in enumerate(((mGC, EG[d], "EG"), (mGR, ER[d], "ER"), (onesf[:], EL[d], "EL"))):
            p.mm(psbD[d][i][:, 0:NKC], lhs, g, True, True, r=["msk", "onesf", "bg"], w=["ps%d_%d" % (i, d)])
            p.actf(dst[:], psbD[d][i][:, 0:NKC], AF.Exp, r=["ps%d_%d" % (i, d)], w=["%s%d" % (nm, d)])

    sh3 = [128, 4, 128]

    def fl(t):
        return t[:].rearrange("p a b -> p (a b)")

    def d3_stages(d, G):
        s_ = G % 2
        n0 = 4 * G if d == 0 else 60 - 4 * G
        MI = msk[:, 0, :] if d == 0 else msk[:, 2, :]
        MS = msk[:, 1, :] if d == 0 else msk[:, 3, :]
        MG = msk[:, 3, :] if d == 0 else msk[:, 1, :]
        u, wT, qdT, qkT, kdec = [grp[(nm, d, s_)] for nm in ("u", "wT", "qdT", "qkT", "kdec")]
        gk = lambda nm: "%s%d_%d" % (nm, d, s_)
        D = "_%d" % d
        psb = psbD[d]
        ptb = psbD[d][1]
        gtri, DTr, EGr, tmpf, X, XT, A, AT, P_, PT, kgb, Pb16 = [T3[(nm, d)] for nm in
            ("gtri", "DTr", "EGr", "tmpf", "X", "XT", "A", "AT", "P", "PT", "kgb", "Pb16")]
        Y, Xc, XcT, Y2 = tmpf, DTr, EGr, gtri
        kY, kXc, kXcT, kY2 = "tmpf" + D, "DTr" + D, "EGr" + D, "gtri" + D
        kX, kXT, kA, kAT, kP, kPT = ["%s%s" % (nm, D) for nm in ("X", "XT", "A", "AT", "P", "PT")]
        st = []

        def mm4(bank, bkey, lhs, rhs, r):
            for j in range(4):
                p.mm(psb[bank][:, j * 128:(j + 1) * 128], lhs[:, j, :], rhs[:, j, :], True, True, r=r, w=[bkey])

        def s1():
            p.tt("pool", gtri[:], bc(MI, sh3, 1), bc(bg[:, n0:n0 + 4, 2 + d], sh3, 2), ALU.mult, r=["msk", "bg"], w=[kY2])
            p.mm(psb[0][:, :], MG, fl(gtri), True, True, r=["msk", kY2], w=["ps0" + D])
            p.mm(psb[1][:, :], onesf[:], fl(gtri), True, True, r=["onesf", kY2], w=["ps1" + D])
            p.actf(fl(DTr), psb[0][:, :], AF.Exp, r=["ps0" + D], w=[kXc])
            p.actf(fl(EGr), psb[1][:, :], AF.Exp, r=["ps1" + D], w=[kXcT])
        st.append(s1)

        def s2():
            for j in range(4):
                c0 = (n0 + j) * 128
                p.mm(psb[2][:, j * 128:(j + 1) * 128], kTn[:, c0:c0 + 128], kTn[:, c0:c0 + 128], True, True, r=[], w=["ps2" + D])
            for j in range(4):
                c0 = (n0 + j) * 128
                p.mm(psb[0][:, j * 128:(j + 1) * 128], kTn[:, c0:c0 + 128], qTn[:, c0:c0 + 128], True, True, r=[], w=["ps0" + D])
            ps2v = psb[2][:, :].rearrange("p (a b) -> p a b", a=4)
            ps0v = psb[0][:, :].rearrange("p (a b) -> p a b", a=4)
            p.tt("dve", tmpf[:], ps2v, DTr[:], ALU.mult, r=["ps2" + D, kXc], w=[kY])
            p.tt("dve", qkT[:], ps0v, DTr[:], ALU.mult, r=["ps0" + D, kXc], w=[gk("qkT")])
            p.tt("pool", tmpf[:], tmpf[:], bc(negb[:, n0:n0 + 4, d], sh3, 2), ALU.mult, r=[kY, "negb"], w=[kY])
            p.tt("pool", X[:], tmpf[:], bc(MS, sh3, 1), ALU.mult, r=[kY, "msk"], w=[kX])
            p.tt("pool", qkT[:], qkT[:], bc(MI, sh3, 1), ALU.mult, r=[gk("qkT"), "msk"], w=[gk("qkT")])
            p.tt("pool", fl(qdT), qTn[:, n0 * 128:(n0 + 4) * 128], fl(EGr), ALU.mult, r=[kXcT], w=[gk("qdT")])
            p.tt("pool", kgb[:], ktok[:, n0:n0 + 4, :], bc(EG[d][:, n0:n0 + 4], sh3, 2), ALU.mult, r=["EG%d" % d], w=["kgb" + D])
            p.tt("pool", kdec[:], ktok[:, n0:n0 + 4, :], bc(ER[d][:, n0:n0 + 4], sh3, 2), ALU.mult, r=["ER%d" % d], w=[gk("kdec")])
            for j in range(4):
                p.tr(ptb[:, j * 128:(j + 1) * 128], X[:, j, :], identf[:], r=[kX, "ident"], w=["ps1" + D])
            p.cp("act", fl(XT), ptb[:, :], r=["ps1" + D], w=[kXT])
        st.append(s2)

        def s3():
            BD = bc(msk[:, 4, :], sh3, 1)
            p.tt("pool", A[:], X[:], BD, ALU.mult, r=[kX, "msk"], w=[kA])
            p.tt("pool", AT[:], XT[:], BD, ALU.mult, r=[kXT, "msk"], w=[kAT])
            p.tt("pool", P_[:], A[:], bc(identf[:], sh3, 1), ALU.add, r=[kA, "ident"], w=[kP])
            p.tt("pool", PT[:], AT[:], bc(identf[:], sh3, 1), ALU.add, r=[kAT, "ident"], w=[kPT])
        st.append(s3)

        def base(k):
            def f():
                mm4(0, "ps0" + D, AT, A, [kA, kAT])
                mm4(1, "ps1" + D, A, AT, [kA, kAT])
                p.cp("dve", fl(A), psb[0][:, :], r=["ps0" + D], w=[kA])
                p.cp("act", fl(AT), psb[1][:, :], r=["ps1" + D], w=[kAT])
                mm4(2, "ps2" + D, AT, P_, [kAT, kP])
                mm4(0, "ps0" + D, A, PT, [kA, kPT])
                p.tt("dve", fl(P_), fl(P_), psb[2][:, :], ALU.add, r=[kP, "ps2" + D], w=[kP])
                p.tt("dve", fl(PT), fl(PT), psb[0][:, :], ALU.add, r=[kPT, "ps0" + D], w=[kPT])
            return f
        for k in range(1, 4):
            st.append(base(k))

        def merge(mi):
            def f():
                last = (mi == 2)
                CM = bc(msk[:, 5 + mi, :], sh3, 1)
                p.tt("pool", Xc[:], X[:], CM, ALU.mult, r=[kX, "msk"], w=[kXc])
                p.tt("pool", XcT[:], XT[:], CM, ALU.mult, r=[kXT, "msk"], w=[kXcT])
                mm4(0, "ps0" + D, XcT, P_, [kXcT, kP])
                p.cp("act", fl(Y), psb[0][:, :], r=["ps0" + D], w=[kY])
                if not last:
                    mm4(1, "ps1" + D, Xc, PT, [kXc, kPT])
                    p.cp("dve", fl(Y2), psb[1][:, :], r=["ps1" + D], w=[kY2])
                mm4(2, "ps2" + D, PT, Y, [kPT, kY])
                if not last:
                    mm4(0, "ps0" + D, P_, Y2, [kP, kY2])
                p.tt("dve", fl(P_), fl(P_), psb[2][:, :], ALU.add, r=[kP, "ps2" + D], w=[kP])
                if not last:
                    p.tt("dve", fl(PT), fl(PT), psb[0][:, :], ALU.add, r=[kPT, "ps0" + D], w=[kPT])
            return f
        for mi in range(3):
            st.append(merge(mi))

        def s_fin():
            p.cp("pool", Pb16[:], P_[:], r=[kP], w=["Pb16" + D])
            for j in range(4):
                p.mm(psb[2][:, j * 128:(j + 1) * 128], Pb16[:, j, :], vtok[:, n0 + j, :], True, True, r=["Pb16" + D], w=["ps2" + D])
                p.mm(psb[1][:, j * 128:(j + 1) * 128], kgb[:, j, :], Pb16[:, j, :], True, True, r=["Pb16" + D, "kgb" + D], w=["ps1" + D])
            p.tt("dve", u[:], psb[2][:, :].rearrange("p (a b) -> p a b", a=4), bc(bg[:, n0:n0 + 4, d], sh3, 2), ALU.mult,
                 r=["ps2" + D, "bg"], w=[gk("u")])
            p.cp("act", fl(wT), psb[1][:, :], r=["ps1" + D], w=[gk("wT")])
        st.append(s_fin)
        return st

    vcnt = [0, 0]

    def scan_step(d, G, j):
        s_ = G % 2
        n0 = 4 * G if d == 0 else 60 - 4 * G
        n = n0 + j
        u, wT, qdT, qkT, kdec = [grp[(nm, d, s_)] for nm in ("u", "wT", "qdT", "qkT", "kdec")]
        gk = lambda nm: "%s%d_%d" % (nm, d, s_)
        vs = vcnt[d] % 2
        vcnt[d] += 1
        vn = vnew[d][vs]
        vk = "vnew%d_%d" % (d, vs)
        os_ = ost[d][vs]
        ok = "ost%d_%d" % (d, vs)
        ps1, ps3, ps2 = pssA[d][:, 0:128], pssA[d][:, 128:256], pssA[d][:, 256:384]
        kS = "S%d" % d
        kA = "scA%d" % d
        p.mm(ps1, wT[:, j, :], Sst[d][:], True, True, r=[gk("wT"), kS], w=[kA])
        p.stt("dve", vn[:], ps1, negb[:, n, d:d + 1], u[:, j, :], ALU.mult, ALU.add, r=[kA, "negb", gk("u")], w=[vk])
        p.mm(ps3, kdec[:, j, :], vn[:], True, True, r=[gk("kdec"), vk], w=[kA])
        p.mm(ps2, qdT[:, j, :], Sst[d][:], True, False, r=[gk("qdT"), kS], w=[kA], skip=True)
        p.mm(ps2, qkT[:, j, :], vn[:], False, True, r=[gk("qkT"), vk], w=[kA], skip=True)
        p.stt("dve", Sst[d][:], Sst[d][:], EL[d][:, n:n + 1], ps3, ALU.mult, ALU.add, r=[kS, "EL%d" % d, kA], w=[kS])
        p.cp("dve", os_[:], ps2, r=[kA], w=[ok])
        p.dma("sp", o_scr[d, n, :, :], os_[:], r=[ok], semkey=ok)

    NG = DBG_NG
    lists = [d3_stages(0, 0), d3_stages(1, 0)]
    while lists[0] or lists[1]:
        for L in lists:
            if L:
                L.pop(0)()
    for G in range(NG):
        nxt = [d3_stages(0, G + 1), d3_stages(1, G + 1)] if G + 1 < NG else [[], []]
        steps = []
        for j in range(4):
            steps.append((0, G, j))
            steps.append((1, G, 3 - j))
        while steps or nxt[0] or nxt[1]:
            if nxt[0]:
                nxt[0].pop(0)()
            if steps:
                scan_step(*steps.pop(0))
            if nxt[1]:
                nxt[1].pop(0)()
            if steps:
                scan_step(*steps.pop(0))
    p.emit()


def emit_B4(nc, o_scr, zd, ogd, identd, ybT_o):
    p = Prog(nc)
    ident = p.sb("identB4", [128, 128], BF16)
    og = p.sb("og", [128, 128], F32)
    zt = p.sb("zt", [128, NKC, 128], F32)
    sq = p.sb("sqB4", [128, 4, 128], F32)
    of_ = [p.sb("ofB4_%d" % i, [128, 4, 128], F32) for i in range(2)]
    ob_ = [p.sb("obB4_%d" % i, [128, 4, 128], F32) for i in range(2)]
    ss = p.sb("ssB4", [128, 4], F32)
    yb = p.sb("ybB4", [128, 4, 128], BF16)
    yst = [p.sb("ystB4_%d" % i, [128, 512], BF16) for i in range(2)]
    ptb = p.ps("ptB4", [128, 1024], BF16)
    p.dma("sp", ident[:], identd, w=["ident"], semkey="ident")
    p.dma("sp", og[:], ogd.partition_broadcast(128), w=["og"], semkey="og")
    for c4 in range(4):
        p.dma("sp", zt[:, c4 * 16:(c4 + 1) * 16, :], zd[c4 * 2048:(c4 + 1) * 2048, :].rearrange("(k p) d -> p k d", p=128),
              w=["zt%d" % c4], semkey="zt%d" % c4)
    for c4 in range(4):
        v = zt[:, c4 * 16:(c4 + 1) * 16, :].rearrange("p a b -> p (a b)")
        p.actf(v, v, AF.Silu, r=["zt%d" % c4], w=["zt%d" % c4])
    for G in range(16):
        n0 = 4 * G
        gs = G % 2
        p.dma("sp", of_[gs][:], o_scr[0, n0:n0 + 4, :, :].rearrange("n p e -> p n e"), w=["of%d" % gs], semkey="of%d" % gs)
        p.dma("sp", ob_[gs][:], o_scr[1, n0:n0 + 4, :, :].rearrange("n p e -> p n e"), w=["ob%d" % gs], semkey="ob%d" % gs)
        p.tt("pool", of_[gs][:], of_[gs][:], ob_[gs][:], ALU.add, r=["of%d" % gs, "ob%d" % gs], w=["of%d" % gs])
        o = of_[gs][:]
        p.tt("pool", sq[:], o, o, ALU.mult, r=["of%d" % gs], w=["sq"])
        p.red(ss[:], sq[:], ALU.add, r=["sq"], w=["ss"])
        p.ts("dve", ss[:], ss[:], 1.0 / 128, EPS, ALU.mult, ALU.add, r=["ss"], w=["ss"])
        p.actf(ss[:], ss[:], AF.Sqrt, r=["ss"], w=["ss"])
        p.recip(ss[:], ss[:], r=["ss"], w=["ss"])
        p.tt("dve", sq[:], o, bc(ss[:], [128, 4, 128], 2), ALU.mult, r=["ss", "sq", "of%d" % gs], w=["sq"])
        p.tt("pool", sq[:], sq[:], bc(og[:], [128, 4, 128], 1), ALU.mult, r=["sq", "og"], w=["sq"])
        p.tt("pool", yb[:], sq[:], zt[:, n0:n0 + 4, :], ALU.mult, r=["sq", "zt%d" % (G // 4)], w=["yb"])
        for j in range(4):
            p.tr(ptb[:, j * 128:(j + 1) * 128], yb[:, j, :], ident[:], r=["yb", "ident"], w=["ptb"])
        ys = G % 2
        p.cp("dve", yst[ys][:], ptb[:, 0:512], r=["ptb"], w=["yst%d" % ys])
        p.dma("sp", ybT_o[:, n0 * 128:(n0 + 4) * 128], yst[ys][:], r=["yst%d" % ys], semkey="ystB4_%d" % ys)
    p.emit()


def build_B(parts=("B1", "B2", "B3", "B4")):
    nc = bass.Bass("TRN2", target_bir_lowering=False)
    qkT = din(nc, "qkT", [2, 128, S], BF16)
    vd = din(nc, "v", [S, 128], BF16)
    qkg = din(nc, "qkg", [1, 128], F32)
    lqd = din(nc, "lq_d", [1, 256], F32)
    lamd = din(nc, "lamc_d", [1, 2], F32)
    sub = din(nc, "subln", [1, 128], F32)
    identd = din(nc, "identb", [128, 128], BF16)
    dnT = din(nc, "dnT", [3, 128, S], F32)
    cwd = din(nc, "cw_d", [128, 3, 5], F32)
    bgd = din(nc, "bg_d", [128, NKC, 4], F32)
    masksd = din(nc, "masks", [8, 128, 128], F32)
    zd = din(nc, "z", [S, 128], F32)
    ogd = din(nc, "og_d", [1, 128], F32)
    yaT = dout(nc, "yaT", [128, S], BF16)
    ybT = dout(nc, "ybT", [128, S], BF16)
    if "B1" in parts:
        emit_B1(nc, qkT, vd, qkg, lqd, lamd, sub, identd, yaT)
    with ExitStack() as outer:
        pers = [outer.enter_context(nc.sbuf_tensor(nm, shp, BF16)) for nm, shp in
                (("qTn", [128, S]), ("kTn", [128, S]), ("ktok", [128, NKC, 128]), ("vtok", [128, NKC, 128]))]
        o_scr = nc.dram_tensor("o_scr", [2, NKC, 128, 128], F32).ap()
        if "B2" in parts:
            emit_B2(nc, pers, dnT, cwd, identd)
        if "B3" in parts:
            emit_B3(nc, pers, o_scr, bgd, masksd, identd)
        if DBG_DUMP:
            p = Prog(nc)
            for i, (nm, shp, dt_, t_) in enumerate((("dq", [128, S], BF16, pers[0]), ("dk", [128, S], BF16, pers[1]),
                                            ("dkt", [128, NKC, 128], BF16, pers[2]), ("dvt", [128, NKC, 128], BF16, pers[3]))):
                o_ = dout(nc, nm, shp, dt_)
                p.dma("sp", o_, t_[:], semkey="dump%d" % i)
            p.emit()
        if "B4" in parts:
            emit_B4(nc, o_scr, zd, ogd, identd, ybT)
    return nc


def _masks():
    i = np.arange(128)
    pp, ff = i[:, None], i[None, :]
    bd = lambda b_: (pp // b_) == (ff // b_)
    return np.stack([pp <= ff, pp < ff, pp >= ff, pp > ff, bd(16), bd(32) & ~bd(16), bd(64) & ~bd(32), ~bd(64)]).astype(np.float32)


def run_B(inp, l, resA):
    nc = _get("B", build_B)
    identb = np.eye(128, dtype=np.float32).astype(NPBF)
    masks = _masks()
    li = 0.8 - 0.6 * math.exp(-0.3 * l)
    lamc = np.array([[li, 1.0 - li]], np.float32)
    cw = inp["dn_conv"][l]
    maps = []
    for b in range(NB):
        rs = resA[b * 4:(b + 1) * 4]
        bgc = np.concatenate([r["bg"] for r in rs], axis=1)
        for h in range(4):
            qT = np.concatenate([r["qkT"][h] for r in rs], axis=1)
            kT = np.concatenate([r["qkT"][4 + h] for r in rs], axis=1)
            maps.append({
                "qkT": np.ascontiguousarray(np.stack([qT, kT])),
                "v": np.ascontiguousarray(np.concatenate([r["v"][:, h * 128:(h + 1) * 128] for r in rs], axis=0)),
                "qkg": np.ascontiguousarray(inp["qk_norm_gain"][l].reshape(1, 128)),
                "lq_d": np.ascontiguousarray(inp["diff_lambda"][l].reshape(1, 256)),
                "lamc_d": lamc,
                "subln": np.ascontiguousarray(inp["diff_subln_gain"][l][None, :]),
                "identb": identb,
                "dnT": np.ascontiguousarray(np.stack([np.concatenate([r["dnT"][X * 4 + h] for r in rs], axis=1) for X in range(3)])),
                "cw_d": np.ascontiguousarray(np.stack([cw[:, X * 512 + h * 128:X * 512 + (h + 1) * 128].T for X in range(3)], axis=1)),
                "bg_d": np.ascontiguousarray(bgc[:, :, [h, 4 + h, 8 + h, 12 + h]]),
                "masks": masks,
                "z": np.ascontiguousarray(np.concatenate([r["z"][:, h * 128:(h + 1) * 128] for r in rs], axis=0)),
                "og_d": np.ascontiguousarray(inp["dn_out_gain"][l][None, :]),
            })
    res = run_bass_kernel_spmd(nc, maps, core_ids=list(range(8)))
    return res.results


def kernel(**inp):
    inp = {k: np.asarray(v) for k, v in inp.items()}
    x = inp["x"]
    xsh = [x[i // 4, (i % 4) * NTOK:(i % 4 + 1) * NTOK] for i in range(8)]
    for l in range(2):
        resA = run_A(inp, l, xsh)
        resB = run_B(inp, l, resA)
        yaT, ybT, gT = [], [], []
        for i in range(8):
            b, t0 = i // 4, (i % 4) * NTOK
            yaT.append(np.stack([resB[b * 4 + h]["yaT"][:, t0:t0 + NTOK] for h in range(4)]))
            ybT.append(np.stack([resB[b * 4 + h]["ybT"][:, t0:t0 + NTOK] for h in range(4)]))
            gT.append(resA[i]["gT"])
        xsh = run_C(inp, l, xsh, yaT, ybT, gT)
    out = np.empty((NB, S, D), np.float32)
    for i in range(8):
        out[i // 4, (i % 4) * NTOK:(i % 4 + 1) * NTOK] = xsh[i]
    return out
```

```python
from contextlib import ExitStack
import math
import numpy as np
import ml_dtypes
import concourse.bass as bass
import concourse.mybir as mybir
from concourse.bass_utils import run_bass_kernel_spmd

F32 = mybir.dt.float32
BF16 = mybir.dt.bfloat16
ALU = mybir.AluOpType
AF = mybir.ActivationFunctionType
AX = mybir.AxisListType
NPBF = ml_dtypes.bfloat16

ENGS = ("pe", "act", "dve", "pool", "sp")

D = 1024
S = 8192
NB = 2
NTOK = 2048
NT = NTOK // 128
EPS = 1e-6
INC = 5648


class Op:
    __slots__ = ("eng", "fn", "deps", "is_dma", "semkey", "cum", "sig", "waits", "has_dep")

    def __init__(self, eng, fn, is_dma=False, semkey=None):
        self.eng = eng
        self.fn = fn
        self.deps = []
        self.is_dma = is_dma
        self.semkey = semkey
        self.cum = 0
        self.sig = 0
        self.has_dep = False
        self.waits = []


class Prog:
    def __init__(self, nc):
        self.nc = nc
        self.es = ExitStack()
        self.ops = {e: [] for e in ENGS}
        self.lastw = {}
        self.readers = {}
        self.dma_cum = {}

    def sb(self, name, shape, dtype):
        return self.es.enter_context(self.nc.sbuf_tensor(name, list(shape), dtype))

    def ps(self, name, shape, dtype=F32):
        return self.es.enter_context(self.nc.psum_tensor(name, list(shape), dtype))

    def _rec(self, op, r, w):
        deps = []
        for k in r:
            lw = self.lastw.get(k)
            if lw is not None:
                deps.append(lw)
        for k in w:
            lw = self.lastw.get(k)
            rd = self.readers.get(k, [])
            if rd:
                deps.extend(rd)
            elif lw is not None:
                if not (op.is_dma and lw.is_dma and lw.semkey == op.semkey):
                    deps.append(lw)
        for k in r:
            self.readers.setdefault(k, []).append(op)
        for k in w:
            self.lastw[k] = op
            self.readers[k] = []
        seen = set()
        for d in deps:
            if d is op or id(d) in seen:
                continue
            seen.add(id(d))
            if d.eng == "pe" and op.eng == "pe" and not d.is_dma and not op.is_dma:
                continue
            op.deps.append(d)
            d.has_dep = True
        self.ops[op.eng].append(op)
        return op

    def op(self, eng, fn, r=(), w=()):
        return self._rec(Op(eng, fn), r, w)

    def dma(self, q, out, in_, r=(), w=(), semkey=None, **kw):
        assert semkey is not None
        o = Op(q, lambda e: e.dma_start(out=out, in_=in_, **kw), is_dma=True, semkey=semkey)
        self._rec(o, r, w)
        self.dma_cum[semkey] = self.dma_cum.get(semkey, 0) + 16
        o.cum = self.dma_cum[semkey]
        return o

    def mm(self, out, lhsT, rhs, start, stop, r, w, skip=False):
        if skip:
            return self.op("pe", lambda e: e.matmul(out, lhsT, rhs, start=start, stop=stop, skip_group_check=True), r, w)
        return self.op("pe", lambda e: e.matmul(out, lhsT, rhs, start=start, stop=stop), r, w)

    def tr(self, out, in_, ident, r, w):
        return self.op("pe", lambda e: e.transpose(out, in_, ident), r, w)

    def actf(self, out, in_, func, r, w, bias=None, scale=None, accum=None):
        kw = {}
        if bias is not None:
            kw["bias"] = bias
        if scale is not None:
            kw["scale"] = scale
        if accum is not None:
            kw["accum_out"] = accum
        return self.op("act", lambda e: e.activation(out, in_, func, **kw), r, w)

    def tt(self, eng, out, a, b, op, r, w):
        return self.op(eng, lambda e: e.tensor_tensor(out, a, b, op), r, w)

    def ts(self, eng, out, a, s1, s2, op0, op1, r, w):
        if op1 is None:
            return self.op(eng, lambda e: e.tensor_scalar(out, a, s1, None, op0), r, w)
        return self.op(eng, lambda e: e.tensor_scalar(out, a, s1, s2, op0, op1), r, w)

    def stt(self, eng, out, in0, scalar, in1, op0, op1, r, w):
        return self.op(eng, lambda e: e.scalar_tensor_tensor(out, in0, scalar, in1, op0, op1), r, w)

    def cp(self, eng, out, in_, r, w):
        if eng == "act":
            return self.op("act", lambda e: e.copy(out, in_), r, w)
        return self.op(eng, lambda e: e.tensor_copy(out, in_), r, w)

    def red(self, out, in_, op, r, w):
        return self.op("dve", lambda e: e.tensor_reduce(out, in_, AX.X, op), r, w)

    def recip(self, out, in_, r, w):
        return self.op("dve", lambda e: e.reciprocal(out, in_), r, w)

    def memset(self, eng, out, val, w):
        return self.op(eng, lambda e: e.memset(out, val), (), w)

    def emit(self):
        nc = self.nc
        es = self.es
        esem = {e: es.enter_context(nc.semaphore("s_" + e)) for e in ENGS}
        dsem = {k: es.enter_context(nc.semaphore("d_%d" % i)) for i, k in enumerate(self.dma_cum)}
        for e in ENGS:
            c = 0
            for o in self.ops[e]:
                if not o.is_dma and o.has_dep:
                    c += 1
                    o.sig = c
        for e in ENGS:
            waited = {}
            for o in self.ops[e]:
                need = {}
                for d in o.deps:
                    if d.is_dma:
                        key = ("d", d.semkey)
                        val = d.cum
                    else:
                        key = ("e", d.eng)
                        val = d.sig
                    if val > need.get(key, 0):
                        need[key] = val
                for key, val in need.items():
                    if waited.get(key, 0) >= val:
                        continue
                    waited[key] = val
                    sem = dsem[key[1]] if key[0] == "d" else esem[key[1]]
                    o.waits.append((sem, val))
        final = [(dsem[k], v) for k, v in self.dma_cum.items()]
        block = es.enter_context(nc.Block())

        def run(eng, name):
            for o in self.ops[name]:
                for sem, val in o.waits:
                    eng.wait_ge(sem, val)
                ins = o.fn(eng)
                if o.is_dma:
                    ins.then_inc(dsem[o.semkey], 16)
                elif o.has_dep:
                    ins.then_inc(esem[name], 1)
            if name == "sp":
                for sem, val in final:
                    eng.wait_ge(sem, val)

        @block.tensor
        def _(eng):
            run(eng, "pe")

        @block.scalar
        def _(eng):
            run(eng, "act")

        @block.vector
        def _(eng):
            run(eng, "dve")

        @block.gpsimd
        def _(eng):
            run(eng, "pool")

        @block.sync
        def _(eng):
            run(eng, "sp")

        es.close()


def din(nc, name, shape, dt):
    return nc.dram_tensor(name, list(shape), dt, kind="ExternalInput").ap()


def dout(nc, name, shape, dt):
    return nc.dram_tensor(name, list(shape), dt, kind="ExternalOutput").ap()


def emit_mod(p, nc, ada_w, ada_b, cpk, col0, ncols, stage, psb, modbc):
    cact = p.sb("cact", [128, 8], F32)
    crep = p.sb("crep", [128, 8, 128], F32)
    p.dma("sp", cact[:], cpk, w=["cact"], semkey="cact")
    p.dma("sp", modbc[:, 0:ncols], ada_b[:, col0:col0 + ncols].partition_broadcast(128), w=["modbc"], semkey="modbc")
    p.actf(cact[:], cact[:], AF.Silu, r=["cact"], w=["cact"])
    p.cp("dve", crep[:], cact[:].unsqueeze(2).to_broadcast([128, 8, 128]), r=["cact"], w=["crep"])
    npass = (ncols + 2047) // 2048
    for ps_ in range(npass):
        c0 = ps_ * 2048
        cw = min(2048, ncols - c0)
        for kc in range(8):
            sl = kc % 2
            p.dma("sp", stage[sl][:, 0:cw], ada_w[kc * 128:(kc + 1) * 128, col0 + c0:col0 + c0 + cw],
                  w=["stg%d" % sl], semkey="stg%d" % sl)
            for j in range(cw // 512):
                p.mm(psb[j][:, :], crep[:, kc, :], stage[sl][:, j * 512:(j + 1) * 512], kc == 0, kc == 7,
                     r=["crep", "stg%d" % sl], w=["psb%d" % j])
        for j in range(cw // 512):
            cs = c0 + j * 512
            p.tt("dve", modbc[:, cs:cs + 512], modbc[:, cs:cs + 512], psb[j][:, :], ALU.add,
                 r=["psb%d" % j, "modbc"], w=["modbc"])


def emit_rmsnorm_T(p, xt, xkey, G, gkey, SH, shkey, junk, ssk, hb, ptb, ptbkey, hT, t, ident, tag):
    ss = ssk
    p.actf(junk[:], xt, AF.Square, r=[xkey], w=["junk" + tag, "ss" + tag], accum=ss[:, 0:1])
    p.ts("dve", ss[:, 0:1], ss[:, 0:1], 1.0 / D, EPS, ALU.mult, ALU.add, r=["ss" + tag], w=["ss" + tag])
    p.actf(ss[:, 0:1], ss[:, 0:1], AF.Sqrt, r=["ss" + tag], w=["ss" + tag])
    p.recip(ss[:, 0:1], ss[:, 0:1], r=["ss" + tag], w=["ss" + tag])
    p.stt("dve", junk[:], xt, ss[:, 0:1], G, ALU.mult, ALU.mult, r=[xkey, "ss" + tag, gkey], w=["junk" + tag])
    p.tt("dve", hb[:], junk[:], SH, ALU.add, r=["junk" + tag, shkey], w=["hb" + tag])
    for kc in range(8):
        p.tr(ptb[:, kc * 128:(kc + 1) * 128], hb[:, kc * 128:(kc + 1) * 128], ident[:], r=["hb" + tag, "ident"], w=[ptbkey])
    p.cp("act", hT[:, :, t * 128:(t + 1) * 128], ptb[:].rearrange("p (k t) -> p k t", t=128), r=[ptbkey], w=["hT%d" % t])


def emit_rms_a(p, xt, xkey, G, gkey, SH, shkey, junk, ss, hb, hbkey, tag):
    p.actf(junk[:], xt, AF.Square, r=[xkey], w=["junk" + tag, "ss" + tag], accum=ss[:, 0:1])
    p.ts("dve", ss[:, 0:1], ss[:, 0:1], 1.0 / D, EPS, ALU.mult, ALU.add, r=["ss" + tag], w=["ss" + tag])
    p.actf(ss[:, 0:1], ss[:, 0:1], AF.Sqrt, r=["ss" + tag], w=["ss" + tag])
    p.recip(ss[:, 0:1], ss[:, 0:1], r=["ss" + tag], w=["ss" + tag])
    p.stt("dve", junk[:], xt, ss[:, 0:1], G, ALU.mult, ALU.mult, r=[xkey, "ss" + tag, gkey], w=["junk" + tag])
    p.tt("dve", hb[:], junk[:], SH, ALU.add, r=["junk" + tag, shkey], w=[hbkey])


def emit_rms_b(p, hb, hbkey, ptb, ptbkey, hT, t, ident):
    for kc in range(8):
        p.tr(ptb[:, kc * 128:(kc + 1) * 128], hb[:, kc * 128:(kc + 1) * 128], ident[:], r=[hbkey, "ident"], w=[ptbkey])
    p.cp("act", hT[:, :, t * 128:(t + 1) * 128], ptb[:].rearrange("p (k t) -> p k t", t=128), r=[ptbkey], w=["hT%d" % t])


def build_A():
    nc = bass.Bass("TRN2", target_bir_lowering=False)
    x = din(nc, "x", [NTOK, D], F32)
    cpk = din(nc, "cpk", [128, 8], F32)
    ada_w = din(nc, "ada_w", [D, 6 * D], F32)
    ada_b = din(nc, "ada_b", [1, 6 * D], F32)
    g1 = din(nc, "g1", [1, D], F32)
    w_in = din(nc, "w_in", [D, INC], F32)
    qkg = din(nc, "qkg", [1, 128], F32)
    alog = din(nc, "alog", [1, 8], F32)
    dtb = din(nc, "dtb", [1, 8], F32)
    cosd = din(nc, "cos", [NTOK, 64], F32)
    sind = din(nc, "sin", [NTOK, 64], F32)
    identd = din(nc, "identb", [128, 128], BF16)
    qkT_o = dout(nc, "qkT", [8, 128, NTOK], BF16)
    v_o = dout(nc, "v", [NTOK, 512], BF16)
    dnT_o = dout(nc, "dnT", [12, 128, NTOK], F32)
    z_o = dout(nc, "z", [NTOK, 512], F32)
    bg_o = dout(nc, "bg", [128, NT, 16], F32)
    gT_o = dout(nc, "gT", [16, 128, NTOK], BF16)

    p = Prog(nc)
    ident = p.sb("ident", [128, 128], BF16)
    modbc = p.sb("modbc", [128, 2048], F32)
    G1 = p.sb("G1", [128, D], F32)
    stage = [p.sb("stg%d" % i, [128, 2064], F32) for i in range(2)]
    wb = p.sb("wb", [128, 8, 2064], BF16)
    hT = p.sb("hT", [128, 8, NTOK], BF16)
    xs = [p.sb("xs%d" % i, [128, D], F32) for i in range(2)]
    junk = p.sb("junk", [128, D], F32)
    ssA = p.sb("ssA", [128, 1], F32)
    hb2 = [p.sb("hb%d" % i, [128, D], BF16) for i in range(2)]
    gq = p.sb("gq", [128, 128], F32)
    cst = [p.sb("cst%d" % i, [128, 64], F32) for i in range(2)]
    snt = [p.sb("snt%d" % i, [128, 64], F32) for i in range(2)]
    alb = p.sb("alb", [128, 8], F32)
    dtbb = p.sb("dtbb", [128, 8], F32)
    sq_ = [[p.sb("sq%d_%d" % (pr, i), [128, 512], F32) for i in range(2)] for pr in range(2)]
    t1_ = [[p.sb("t1_%d_%d" % (pr, i), [128, 512], F32) for i in range(2)] for pr in range(2)]
    t2_ = [[p.sb("t2_%d_%d" % (pr, i), [128, 512], F32) for i in range(2)] for pr in range(2)]
    qr_ = [[p.sb("qr%d_%d" % (pr, i), [128, 512], BF16) for i in range(2)] for pr in range(2)]
    ss16_ = [p.sb("ss16_%d" % pr, [128, 16], F32) for pr in range(2)]
    vb = [p.sb("vb%d" % i, [128, 512], BF16) for i in range(2)]
    zf = [p.sb("zf%d" % i, [128, 512], F32) for i in range(2)]
    baraw = p.sb("baraw", [128, NT, 16], F32)
    bgo = p.sb("bgo", [128, NT, 16], F32)
    bat = [p.sb("bat%d" % i, [128, NT, 8], F32) for i in range(3)]
    qkTs = [p.sb("qkTs%d" % i, [128, 8, 512], BF16) for i in range(2)]
    dst = [p.sb("dst%d" % i, [128, 512], F32) for i in range(3)]
    gst = [p.sb("gst%d" % i, [128, 512], BF16) for i in range(3)]
    psb = [p.ps("psb%d" % i, [128, 512], F32) for i in range(6)]
    ptb = [p.ps("ptb%d" % i, [128, 1024], BF16) for i in range(2)]

    p.dma("sp", ident[:], identd, w=["ident"], semkey="ident")
    p.dma("sp", G1[:], g1.partition_broadcast(128), w=["GA"], semkey="G1")
    p.dma("sp", gq[:], qkg.partition_broadcast(128), w=["gq"], semkey="gq")
    p.dma("sp", alb[:], alog.partition_broadcast(128), w=["alb"], semkey="alb")
    p.dma("sp", dtbb[:], dtb.partition_broadcast(128), w=["dtbb"], semkey="dtbb")
    emit_mod(p, nc, ada_w, ada_b, cpk, 0, 2048, stage, psb, modbc)
    p.stt("dve", G1[:], modbc[:, 1024:2048], 1.0, G1[:], ALU.add, ALU.mult, r=["modbc", "GA"], w=["GA"])
    p.actf(alb[:], alb[:], AF.Exp, r=["alb"], w=["alb"])
    p.ts("dve", alb[:], alb[:], -1.0, None, ALU.mult, None, r=["alb"], w=["alb"])

    def load_w(colranges, ncols):
        for kc in range(8):
            sl = kc % 2
            o = 0
            for (a, b) in colranges:
                p.dma("sp", stage[sl][:, o:o + (b - a)], w_in[kc * 128:(kc + 1) * 128, a:b],
                      w=["stg%d" % sl], semkey="stg%d" % sl)
                o += b - a
            p.cp("act", wb[:, kc, 0:ncols], stage[sl][:, 0:ncols], r=["stg%d" % sl], w=["wb%d" % kc])

    def load_x(t):
        p.dma("sp", xs[t % 2][:], x[t * 128:(t + 1) * 128, :], w=["xs%d" % (t % 2)], semkey="xs%d" % (t % 2))

    load_x(0)
    load_x(1)
    load_w([(0, 1536), (3072, 3600)], 2064)

    def rms_a(t):
        emit_rms_a(p, xs[t % 2][:], "xs%d" % (t % 2), G1[:], "GA", modbc[:, 0:1024], "modbc", junk, ssA, hb2[t % 2], "hb%d" % (t % 2), "A")
        if t + 2 < NT:
            load_x(t + 2)

    def rms_b(t):
        emit_rms_b(p, hb2[t % 2], "hb%d" % (t % 2), ptb[t % 2], "ptb%d" % (t % 2), hT, t, ident)

    rms_a(0)
    rms_b(0)
    rms_a(1)

    def load_cs(t):
        p.dma("sp", cst[t % 2][:], cosd[t * 128:(t + 1) * 128, :], w=["cst%d" % (t % 2)], semkey="cst%d" % (t % 2))
        p.dma("sp", snt[t % 2][:], sind[t * 128:(t + 1) * 128, :], w=["snt%d" % (t % 2)], semkey="snt%d" % (t % 2))

    load_cs(0)
    load_cs(1)
    def emit_tr(t):
            pk = "ptb%d" % (t % 2)
            for j in range(2):
                for h in range(4):
                    p.tr(ptb[t % 2][:, (j * 4 + h) * 128:(j * 4 + h + 1) * 128], qr_[t % 2][j][:, h * 128:(h + 1) * 128], ident[:],
                         r=["qrqk%d_%d" % (j, t % 2), "ident"], w=[pk])
            sl = (t // 4) % 2
            p.cp("act", qkTs[sl][:, :, (t % 4) * 128:(t % 4 + 1) * 128], ptb[t % 2][:].rearrange("p (k t) -> p k t", t=128),
                 r=[pk], w=["qkTs%d" % sl])
            if t % 4 == 3:
                tg = t // 4
                p.dma("pool", qkT_o[:, :, tg * 512:(tg + 1) * 512].rearrange("h p t -> p h t"), qkTs[sl][:],
                      r=["qkTs%d" % sl], semkey="qkTs%d" % sl)

    wkeys = ["wb%d" % kc for kc in range(8)]
    for t in range(NT):
        for j in range(4):
            for kc in range(8):
                p.mm(psb[j][:, :], hT[:, kc, t * 128:(t + 1) * 128], wb[:, kc, j * 512:(j + 1) * 512], kc == 0, kc == 7,
                     r=["hT%d" % t, "wb%d" % kc], w=["psb%d" % j])
        for kc in range(8):
            p.mm(psb[4][:, 0:16], hT[:, kc, t * 128:(t + 1) * 128], wb[:, kc, 2048:2064], kc == 0, kc == 7,
                 r=["hT%d" % t, "wb%d" % kc], w=["psb4"])
        if t > 0:
            emit_tr(t - 1)
        if t + 1 < NT:
            rms_b(t + 1)
        p.cp("act", vb[t % 2][:], psb[2][:, :], r=["psb2"], w=["vb%d" % (t % 2)])
        p.dma("pool", v_o[t * 128:(t + 1) * 128, :], vb[t % 2][:], r=["vb%d" % (t % 2)], semkey="vb%d" % (t % 2))
        p.cp("act", zf[t % 2][:], psb[3][:, :], r=["psb3"], w=["zf%d" % (t % 2)])
        p.dma("pool", z_o[t * 128:(t + 1) * 128, :], zf[t % 2][:], r=["zf%d" % (t % 2)], semkey="zf%d" % (t % 2))
        p.cp("act", baraw[:, t, :], psb[4][:, 0:16], r=["psb4"], w=["baraw"])
        PR = t % 2
        sq, t1, t2, qr, ss16 = sq_[PR], t1_[PR], t2_[PR], qr_[PR], ss16_[PR]
        SS = "ss16_%d" % PR
        for j in range(2):
            sqj, t1j, t2j, qrj = sq[j], t1[j], t2[j], qr[j]
            kq = "qk%d_%d" % (j, PR)
            p.actf(sqj[:], psb[j][:, :], AF.Square, r=["psb%d" % j], w=["sq" + kq])
            p.red(ss16[:, j * 8:(j + 1) * 8], sqj[:].rearrange("p (g d) -> p g d", d=64), ALU.add, r=["sq" + kq], w=[SS])
        p.ts("dve", ss16[:], ss16[:], 1.0 / 64, EPS, ALU.mult, ALU.add, r=[SS], w=[SS])
        p.actf(ss16[:], ss16[:], AF.Sqrt, r=[SS], w=[SS])
        p.recip(ss16[:], ss16[:], r=[SS], w=[SS])
        for j in range(2):
            sqj, t1j, t2j, qrj = sq[j], t1[j], t2[j], qr[j]
            kq = "qk%d_%d" % (j, PR)
            s3 = sqj[:].rearrange("p (g d) -> p g d", d=64)
            a3 = t1j[:].rearrange("p (g d) -> p g d", d=64)
            b3 = t2j[:].rearrange("p (g d) -> p g d", d=64)
            q3 = qrj[:].rearrange("p (g d) -> p g d", d=64)
            p.tt("dve", s3, psb[j][:, :].rearrange("p (g d) -> p g d", d=64),
                 ss16[:, j * 8:(j + 1) * 8].unsqueeze(2).to_broadcast([128, 8, 64]), ALU.mult,
                 r=["psb%d" % j, SS], w=["sq" + kq])
            p.tt("pool", s3, s3, gq[:, j * 64:(j + 1) * 64].unsqueeze(1).to_broadcast([128, 8, 64]), ALU.mult,
                 r=["sq" + kq, "gq"], w=["sq" + kq])
            p.tt("pool", a3, s3, cst[t % 2][:].unsqueeze(1).to_broadcast([128, 8, 64]), ALU.mult,
                 r=["sq" + kq, "cst%d" % (t % 2)], w=["t1" + kq])
            p.tt("dve", b3[:, :, 0:32], s3[:, :, 32:64], snt[t % 2][:, 0:32].unsqueeze(1).to_broadcast([128, 8, 32]), ALU.mult,
                 r=["sq" + kq, "snt%d" % (t % 2)], w=["t2a" + kq])
            p.tt("pool", b3[:, :, 32:64], s3[:, :, 0:32], snt[t % 2][:, 32:64].unsqueeze(1).to_broadcast([128, 8, 32]), ALU.mult,
                 r=["sq" + kq, "snt%d" % (t % 2)], w=["t2b" + kq])
            p.tt("dve", qrj[:], t1j[:], t2j[:], ALU.add, r=["t1" + kq, "t2a" + kq, "t2b" + kq], w=["qr" + kq])
        if t + 2 < NT:
            load_cs(t + 2)
            rms_a(t + 2)
    emit_tr(NT - 1)

    braw = baraw[:, :, 0:8]
    araw = baraw[:, :, 8:16]
    p.actf(bgo[:, :, 0:8], braw, AF.Sigmoid, r=["baraw"], w=["bgo_b"])
    xg, ab, mx = bat
    p.tt("dve", xg[:], araw, dtbb[:].unsqueeze(1).to_broadcast([128, NT, 8]), ALU.add, r=["baraw", "dtbb"], w=["xg"])
    p.ts("dve", ab[:], xg[:], -1.0, None, ALU.mult, None, r=["xg"], w=["ab"])
    p.tt("dve", ab[:], ab[:], xg[:], ALU.max, r=["ab", "xg"], w=["ab"])
    p.actf(ab[:], ab[:], AF.Exp, r=["ab"], w=["ab"], scale=-1.0)
    p.ts("dve", ab[:], ab[:], 1.0, None, ALU.add, None, r=["ab"], w=["ab"])
    p.actf(ab[:], ab[:], AF.Ln, r=["ab"], w=["ab"])
    p.ts("dve", mx[:], xg[:], 0.0, None, ALU.max, None, r=["xg"], w=["mx"])
    p.tt("dve", mx[:], mx[:], ab[:], ALU.add, r=["mx", "ab"], w=["mx"])
    p.tt("dve", bgo[:, :, 8:16], mx[:], alb[:].unsqueeze(1).to_broadcast([128, NT, 8]), ALU.mult, r=["mx", "alb"], w=["bgo_g"])
    p.dma("pool", bg_o, bgo[:], r=["bgo_b", "bgo_g"], semkey="bgo")

    cnt = 0
    for grp, (a, b) in enumerate([(1536, 3072), (3600, 5648)]):
        ncols = b - a
        load_w([(a, b)], ncols)
        for tg in range(4):
            for cc in range(ncols // 128):
                bk = cnt % 4
                for kc in range(8):
                    p.mm(psb[bk][:, :], wb[:, kc, cc * 128:(cc + 1) * 128], hT[:, kc, tg * 512:(tg + 1) * 512], kc == 0, kc == 7,
                         r=["wb%d" % kc] + ["hT%d" % tt_ for tt_ in range(tg * 4, tg * 4 + 4)], w=["psb%d" % bk])
                sl = cnt % 3
                if grp == 0:
                    p.cp("act", dst[sl][:], psb[bk][:, :], r=["psb%d" % bk], w=["dst%d" % sl])
                    p.dma("pool", dnT_o[cc, :, tg * 512:(tg + 1) * 512], dst[sl][:], r=["dst%d" % sl], semkey="dst%d" % sl)
                else:
                    p.actf(gst[sl][:], psb[bk][:, :], AF.Sigmoid, r=["psb%d" % bk], w=["gst%d" % sl])
                    p.dma("pool", gT_o[cc, :, tg * 512:(tg + 1) * 512], gst[sl][:], r=["gst%d" % sl], semkey="gst%d" % sl)
                cnt += 1
    p.emit()
    return nc


_CACHE = {}


def _rope_tables():
    pos = np.arange(S, dtype=np.float32)
    inv = (1.0 / (np.float32(10000.0) ** (np.arange(0, 64, 2, dtype=np.float32) / np.float32(64)))).astype(np.float32)
    ang = (pos[:, None] * inv[None, :]).astype(np.float32)
    ang = np.concatenate([ang, ang], axis=-1)
    cos = np.cos(ang).astype(np.float32)
    sin = np.sin(ang).astype(np.float32)
    sin_signed = sin.copy()
    sin_signed[:, :32] = -sin_signed[:, :32]
    return cos, sin_signed


def _get(name, builder):
    if name not in _CACHE:
        _CACHE[name] = builder()
    return _CACHE[name]


def run_A(inp, l, xsh):
    nc = _get("A", build_A)
    cos, sins = _get("rope", _rope_tables)
    identb = np.eye(128, dtype=np.float32).astype(NPBF)
    maps = []
    for i in range(8):
        b, t0 = i // 4, (i % 4) * NTOK
        maps.append({
            "x": np.ascontiguousarray(xsh[i]),
            "cpk": np.ascontiguousarray(inp["c"][b].reshape(8, 128).T),
            "ada_w": np.ascontiguousarray(inp["ada_w"][l]),
            "ada_b": np.ascontiguousarray(inp["ada_b"][l][None, :]),
            "g1": np.ascontiguousarray(inp["norm1_g"][l][None, :]),
            "w_in": np.ascontiguousarray(inp["w_in"][l]),
            "qkg": np.ascontiguousarray(inp["qk_norm_gain"][l].reshape(1, 128)),
            "alog": np.ascontiguousarray(inp["dn_A_log"][l].reshape(1, 8)),
            "dtb": np.ascontiguousarray(inp["dn_dt_bias"][l].reshape(1, 8)),
            "cos": np.ascontiguousarray(cos[t0:t0 + NTOK]),
            "sin": np.ascontiguousarray(sins[t0:t0 + NTOK]),
            "identb": identb,
        })
    res = run_bass_kernel_spmd(nc, maps, core_ids=list(range(8)))
    return res.results


def emit_C(nc, outer, x_in, yaT, ybT, gT, cpk, ada_w, ada_b, g2, w_a, w_b, w_out, w1, w2, identd, x_out):
    x_res = outer.enter_context(nc.sbuf_tensor("x_res", [128, NT, D], F32))
    modbc = outer.enter_context(nc.sbuf_tensor("modbcC", [128, 4096], F32))
    G2 = outer.enter_context(nc.sbuf_tensor("G2", [128, D], F32))
    ident = outer.enter_context(nc.sbuf_tensor("identC", [128, 128], BF16))

    p = Prog(nc)
    stage = [p.sb("stgC%d" % i, [128, 2048], F32) for i in range(2)]
    psb = [p.ps("psC0_%d" % i, [128, 512], F32) for i in range(4)]
    p.dma("sp", ident[:], identd, w=["ident"], semkey="ident")
    p.dma("sp", G2[:], g2.partition_broadcast(128), w=["G2"], semkey="G2")
    for t in range(NT):
        p.dma("sp", x_res[:, t, :], x_in[t * 128:(t + 1) * 128, :], w=["xr%d" % t], semkey="xres")
    emit_mod(p, nc, ada_w, ada_b, cpk, 2048, 4096, stage, psb, modbc)
    p.stt("dve", G2[:], modbc[:, 2048:3072], 1.0, G2[:], ALU.add, ALU.mult, r=["modbc", "G2"], w=["G2"])
    p.emit()
    gt1 = modbc[:, 0:1024]
    sh2 = modbc[:, 1024:2048]
    gt2 = modbc[:, 3072:4096]

    p = Prog(nc)
    stg = [p.sb("stg1_%d" % i, [128, 1024], F32) for i in range(2)]
    wab = p.sb("wab", [128, 4, D], BF16)
    wbb = p.sb("wbb", [128, 4, D], BF16)
    wob = p.sb("wob", [128, 8, D], BF16)
    yas = [p.sb("yas%d" % i, [128, 4, 512], BF16) for i in range(2)]
    ybs = [p.sb("ybs%d" % i, [128, 4, 512], BF16) for i in range(2)]
    gas = [p.sb("gas%d" % i, [128, 512], BF16) for i in range(2)]
    gbs = [p.sb("gbs%d" % i, [128, 512], BF16) for i in range(2)]
    m1 = [p.sb("m1_%d" % i, [128, 512], F32) for i in range(2)]
    m2 = [p.sb("m2_%d" % i, [128, 512], F32) for i in range(2)]
    mT = [p.sb("mT%d" % i, [128, 8, 512], BF16) for i in range(2)]
    tmp = [p.sb("tmpc%d" % i, [128, 512], F32) for i in range(2)]
    psb = [p.ps("psC1_%d" % i, [128, 512], F32) for i in range(6)]
    si = 0
    for h in range(4):
        for (wsrc, wdst, nm) in ((w_a, wab, "wab"), (w_b, wbb, "wbb")):
            sl = si % 2
            si += 1
            p.dma("sp", stg[sl][:], wsrc[h * 128:(h + 1) * 128, :], w=["stg%d" % sl], semkey="stg%d" % sl)
            p.cp("act", wdst[:, h, :], stg[sl][:], r=["stg%d" % sl], w=[nm])
    for kc in range(8):
        sl = si % 2
        si += 1
        p.dma("sp", stg[sl][:], w_out[kc * 128:(kc + 1) * 128, :], w=["stg%d" % sl], semkey="stg%d" % sl)
        p.cp("act", wob[:, kc, :], stg[sl][:], r=["stg%d" % sl], w=["wob"])
    cnt = 0
    for tg in range(4):
        s2 = tg % 2
        p.dma("sp", yas[s2][:], yaT[:, :, tg * 512:(tg + 1) * 512].rearrange("h p t -> p h t"), w=["yas%d" % s2], semkey="yas%d" % s2)
        p.dma("sp", ybs[s2][:], ybT[:, :, tg * 512:(tg + 1) * 512].rearrange("h p t -> p h t"), w=["ybs%d" % s2], semkey="ybs%d" % s2)
        for fc in range(8):
            s3 = fc % 2
            p.dma("sp", gas[s3][:], gT[fc, :, tg * 512:(tg + 1) * 512], w=["gas%d" % s3], semkey="gas%d" % s3)
            p.dma("sp", gbs[s3][:], gT[8 + fc, :, tg * 512:(tg + 1) * 512], w=["gbs%d" % s3], semkey="gbs%d" % s3)
            ba, bb = (0, 1) if fc % 2 == 0 else (2, 3)
            for h in range(4):
                p.mm(psb[ba][:, :], wab[:, h, fc * 128:(fc + 1) * 128], yas[s2][:, h, :], h == 0, h == 3,
                     r=["wab", "yas%d" % s2], w=["ps%d" % ba])
            for h in range(4):
                p.mm(psb[bb][:, :], wbb[:, h, fc * 128:(fc + 1) * 128], ybs[s2][:, h, :], h == 0, h == 3,
                     r=["wbb", "ybs%d" % s2], w=["ps%d" % bb])
            p.tt("dve", m1[s3][:], psb[ba][:, :], gas[s3][:], ALU.mult, r=["ps%d" % ba, "gas%d" % s3], w=["m1_%d" % s3])
            p.tt("dve", m2[s3][:], psb[bb][:, :], gbs[s3][:], ALU.mult, r=["ps%d" % bb, "gbs%d" % s3], w=["m2_%d" % s3])
            p.tt("pool", mT[s2][:, fc, :], m1[s3][:], m2[s3][:], ALU.add, r=["m1_%d" % s3, "m2_%d" % s3], w=["mT%d_%d" % (s2, fc)])
        for i in range(4):
            t = tg * 4 + i
            for j in range(2):
                bk = 4 + (cnt % 2)
                ts_ = cnt % 2
                cnt += 1
                for kc in range(8):
                    p.mm(psb[bk][:, :], mT[s2][:, kc, i * 128:(i + 1) * 128], wob[:, kc, j * 512:(j + 1) * 512], kc == 0, kc == 7,
                         r=["mT%d_%d" % (s2, kc), "wob"], w=["ps%d" % bk])
                p.tt("dve", tmp[ts_][:], psb[bk][:, :], gt1[:, j * 512:(j + 1) * 512], ALU.mult, r=["ps%d" % bk], w=["tmp%d" % ts_])
                p.tt("pool", x_res[:, t, j * 512:(j + 1) * 512], x_res[:, t, j * 512:(j + 1) * 512], tmp[ts_][:], ALU.add,
                     r=["tmp%d" % ts_], w=["xr%d_%d" % (t, j)])
    p.emit()

    p = Prog(nc)
    h2T = p.sb("h2T", [128, 8, NTOK], BF16)
    junk = p.sb("junkC", [128, D], F32)
    ssC = p.sb("ssC", [128, 1], F32)
    hb = p.sb("hbC", [128, D], BF16)
    stg = [p.sb("stg2_%d" % i, [128, 1024], F32) for i in range(2)]
    w1b = [p.sb("w1b%d" % i, [128, 8, 512], BF16) for i in range(2)]
    w2b = [p.sb("w2b%d" % i, [128, 4, D], BF16) for i in range(2)]
    aT = p.sb("aT", [128, 4, NTOK], BF16)
    rl = [p.sb("rl%d" % i, [128, 512], F32) for i in range(2)]
    tmp = [p.sb("tmpd%d" % i, [128, 512], F32) for i in range(2)]
    psb = [p.ps("psC2_%d" % i, [128, 512], F32) for i in range(6)]
    ptb = [p.ps("ptC2_%d" % i, [128, 1024], BF16) for i in range(2)]
    si = 0

    def load_fb(fb):
        nonlocal si
        s = fb % 2
        for k2 in range(4):
            sl = si % 2
            si += 1
            p.dma("sp", stg[sl][:].rearrange("p (k c) -> p k c", k=2),
                  w1[k2 * 256:(k2 + 1) * 256, fb * 512:(fb + 1) * 512].rearrange("(k p) c -> p k c", p=128),
                  w=["stg%d" % sl], semkey="stg%d" % sl)
            p.cp("act", w1b[s][:, k2 * 2:k2 * 2 + 2, :], stg[sl][:].rearrange("p (k c) -> p k c", k=2), r=["stg%d" % sl], w=["w1b%d" % s])
        for c in range(4):
            sl = si % 2
            si += 1
            p.dma("sp", stg[sl][:], w2[fb * 512 + c * 128:fb * 512 + (c + 1) * 128, :], w=["stg%d" % sl], semkey="stg%d" % sl)
            p.cp("act", w2b[s][:, c, :], stg[sl][:], r=["stg%d" % sl], w=["w2b%d" % s])

    load_fb(0)
    hb2 = [hb, p.sb("hbC2", [128, D], BF16)]
    for t in range(NT):
        emit_rms_a(p, x_res[:, t, :], "xr%d" % t, G2[:], "G2", sh2, "modbc", junk, ssC, hb2[t % 2], "hbC%d" % (t % 2), "C")
        emit_rms_b(p, hb2[t % 2], "hbC%d" % (t % 2), ptb[t % 2], "ptb%d" % (t % 2), h2T, t, ident)
        if t == 3:
            break
    cnt = 0
    c2 = 0
    for fb in range(8):
        s = fb % 2
        if fb + 1 < 8:
            load_fb(fb + 1)
        for tg in range(4):
            for c in range(4):
                bk = cnt % 4
                rs = cnt % 2
                cnt += 1
                tn = (tg + 1) * 4 + c
                if fb == 0 and tn < NT:
                    emit_rms_a(p, x_res[:, tn, :], "xr%d" % tn, G2[:], "G2", sh2, "modbc", junk, ssC, hb2[tn % 2], "hbC%d" % (tn % 2), "C")
                for kc in range(8):
                    p.mm(psb[bk][:, :], w1b[s][:, kc, c * 128:(c + 1) * 128], h2T[:, kc, tg * 512:(tg + 1) * 512], kc == 0, kc == 7,
                         r=["w1b%d" % s] + ["hT%d" % tt_ for tt_ in range(tg * 4, tg * 4 + 4)], w=["ps%d" % bk])
                if fb == 0 and tn < NT:
                    emit_rms_b(p, hb2[tn % 2], "hbC%d" % (tn % 2), ptb[tn % 2], "ptb%d" % (tn % 2), h2T, tn, ident)
                p.actf(rl[rs][:], psb[bk][:, :], AF.Relu, r=["ps%d" % bk], w=["rl%d" % rs])
                p.actf(aT[:, c, tg * 512:(tg + 1) * 512], rl[rs][:], AF.Square, r=["rl%d" % rs], w=["aT%d_%d" % (c, tg)])
        for t in range(NT):
            for j in range(2):
                bk = 4 + (c2 % 2)
                ts_ = c2 % 2
                c2 += 1
                for c in range(4):
                    p.mm(psb[bk][:, :], aT[:, c, t * 128:(t + 1) * 128], w2b[s][:, c, j * 512:(j + 1) * 512], c == 0, c == 3,
                         r=["aT%d_%d" % (c, t // 4), "w2b%d" % s], w=["ps%d" % bk])
                p.tt("dve", tmp[ts_][:], psb[bk][:, :], gt2[:, j * 512:(j + 1) * 512], ALU.mult, r=["ps%d" % bk], w=["tmp%d" % ts_])
                p.tt("pool", x_res[:, t, j * 512:(j + 1) * 512], x_res[:, t, j * 512:(j + 1) * 512], tmp[ts_][:], ALU.add,
                     r=["tmp%d" % ts_], w=["xr%d" % t])
    for t in range(NT):
        p.dma("sp", x_out[t * 128:(t + 1) * 128, :], x_res[:, t, :], r=["xr%d" % t], semkey="xout")
    p.emit()


def build_C():
    nc = bass.Bass("TRN2", target_bir_lowering=False)
    x = din(nc, "x", [NTOK, D], F32)
    yaT = din(nc, "yaT", [4, 128, NTOK], BF16)
    ybT = din(nc, "ybT", [4, 128, NTOK], BF16)
    gT = din(nc, "gT", [16, 128, NTOK], BF16)
    cpk = din(nc, "cpk", [128, 8], F32)
    ada_w = din(nc, "ada_w", [D, 6 * D], F32)
    ada_b = din(nc, "ada_b", [1, 6 * D], F32)
    g2 = din(nc, "g2", [1, D], F32)
    w_a = din(nc, "w_a", [512, D], F32)
    w_b = din(nc, "w_b", [512, D], F32)
    w_out = din(nc, "w_out", [D, D], F32)
    w1 = din(nc, "w1", [D, 4 * D], F32)
    w2 = din(nc, "w2", [4 * D, D], F32)
    identd = din(nc, "identb", [128, 128], BF16)
    x_out = dout(nc, "x_out", [NTOK, D], F32)
    with ExitStack() as outer:
        emit_C(nc, outer, x, yaT, ybT, gT, cpk, ada_w, ada_b, g2, w_a, w_b, w_out, w1, w2, identd, x_out)
    return nc


def run_C(inp, l, xsh, yaT, ybT, gT):
    nc = _get("C", build_C)
    identb = np.eye(128, dtype=np.float32).astype(NPBF)
    maps = []
    for i in range(8):
        b = i // 4
        maps.append({
            "x": np.ascontiguousarray(xsh[i]),
            "yaT": np.ascontiguousarray(yaT[i]),
            "ybT": np.ascontiguousarray(ybT[i]),
            "gT": np.ascontiguousarray(gT[i]),
            "cpk": np.ascontiguousarray(inp["c"][b].reshape(8, 128).T),
            "ada_w": np.ascontiguousarray(inp["ada_w"][l]),
            "ada_b": np.ascontiguousarray(inp["ada_b"][l][None, :]),
            "g2": np.ascontiguousarray(inp["norm2_g"][l][None, :]),
            "w_a": np.ascontiguousarray(inp["w_branch_a"][l]),
            "w_b": np.ascontiguousarray(inp["w_branch_b"][l]),
            "w_out": np.ascontiguousarray(inp["w_out"][l]),
            "w1": np.ascontiguousarray(inp["w_mlp1"][l]),
            "w2": np.ascontiguousarray(inp["w_mlp2"][l]),
            "identb": identb,
        })
    res = run_bass_kernel_spmd(nc, maps, core_ids=list(range(8)))
    return [r["x_out"] for r in res.results]


DBG_B3 = 0
B1_DUP = 1
DBG_DUMP = 0
DBG_NG = 16
DBG_D = 0
DBG_DIR = -1
NKC = S // 128


def emit_B1(nc, qkT, vd, qkg, lqd, lamd, sublnd, identd, yaT_o):
    p = Prog(nc)
    qT = p.sb("qT", [128, S], BF16)
    kTz = [p.sb("kTz%d" % m, [128, S], BF16) for m in range(2)]
    vv = p.sb("vv", [128, NKC, 128], BF16)
    gq = p.sb("gqB", [128, 128], F32)
    gqa = p.sb("gqa", [128, 128], F32)
    mx2 = p.sb("mx2", [128, 2], F32)
    nshift = p.sb("nshift", [128, 1], F32)
    lq = p.sb("lq", [128, 4, 64], F32)
    lqp = p.sb("lqp", [128, 2, 64], F32)
    lsum = p.sb("lsum", [128, 2], F32)
    lam = p.sb("lam", [128, 1], F32)
    lamc = p.sb("lamc", [128, 2], F32)
    gsubc = p.sb("gsubc", [128, 1], F32)
    epst = p.sb("epsB1", [128, 1], F32)
    onesf = p.sb("onesfB1", [128, 128], F32)
    onesb = p.sb("onesbB1", [128, 128], BF16)
    pT = [p.sb("pT%d" % i, [128, 1024], BF16) for i in range(3)]
    racc = [[p.sb("racc%d_%d" % (m, e), [128, 1024], F32) for e in range(2)] for m in range(2)]
    r1 = p.sb("r1", [128, 512], F32)
    r2 = p.sb("r2", [128, 512], F32)
    t1 = p.sb("t1B1", [128, 512], F32)
    t2 = p.sb("t2B1", [128, 512], F32)
    sqb = p.sb("sqbB1", [128, 512], BF16)
    rs = p.sb("rsB1", [128, 512], F32)
    yst = [p.sb("yst%d" % i, [128, 512], BF16) for i in range(2)]
    sps = [p.ps("sps%d" % i, [128, 1024], F32) for i in range(2)]
    accO = [p.ps("accO%d" % m, [128, 512], F32) for m in range(2)]
    sums = [p.ps("sums%d" % m, [128, 512], F32) for m in range(2)]
    ssp = sums[0]

    p.dma("sp", gq[:], qkg.partition_broadcast(128), w=["gq"], semkey="gq")
    p.dma("sp", lq[:].rearrange("p a b -> p (a b)"), lqd.partition_broadcast(128), w=["lq"], semkey="lq")
    p.dma("sp", lamc[:], lamd.partition_broadcast(128), w=["lamc"], semkey="lamc")
    p.dma("sp", gsubc[:], sublnd.rearrange("o d -> d o"), w=["gsubc"], semkey="gsubc")
    for c4 in range(4):
        p.dma("sp", qT[:, c4 * 2048:(c4 + 1) * 2048], qkT[0, :, c4 * 2048:(c4 + 1) * 2048], w=["qT"], semkey="qT")
        p.dma("sp", kTz[0][0:64, c4 * 2048:(c4 + 1) * 2048], qkT[1, 0:64, c4 * 2048:(c4 + 1) * 2048], w=["kT0d"], semkey="kT0")
        p.dma("sp", kTz[1][64:128, c4 * 2048:(c4 + 1) * 2048], qkT[1, 64:128, c4 * 2048:(c4 + 1) * 2048], w=["kT1d"], semkey="kT1")
    p.memset("pool", kTz[0][64:128, :], 0.0, w=["kT0z"])
    p.memset("pool", kTz[1][0:64, :], 0.0, w=["kT1z"])
    for c4 in range(4):
        p.dma("sp", vv[:, c4 * 16:(c4 + 1) * 16, :],
              vd[c4 * 2048:(c4 + 1) * 2048, :].rearrange("(k p) d -> p k d", p=128), w=["vv"], semkey="vv")
    p.memset("pool", onesf[:], 1.0, w=["onesf"])
    p.memset("pool", onesb[:], 1.0, w=["onesb"])
    p.memset("pool", epst[:], EPS, w=["epst"])
    p.ts("dve", gqa[:], gq[:], -1.0, None, ALU.mult, None, r=["gq"], w=["gqa"])
    p.tt("dve", gqa[:], gqa[:], gq[:], ALU.max, r=["gqa", "gq"], w=["gqa"])
    p.op("dve", lambda e: e.tensor_reduce(mx2[:], gqa[:].rearrange("p (a b) -> p a b", a=2), AX.X, ALU.max), r=["gqa"], w=["mx2"])
    p.stt("dve", nshift[:], mx2[:, 0:1], -8.0, mx2[:, 1:2], ALU.mult, ALU.mult, r=["mx2"], w=["nshift"])
    p.tt("dve", lqp[:, 0, :], lq[:, 0, :], lq[:, 1, :], ALU.mult, r=["lq"], w=["lqp0"])
    p.tt("dve", lqp[:, 1, :], lq[:, 2, :], lq[:, 3, :], ALU.mult, r=["lq"], w=["lqp1"])
    p.red(lsum[:], lqp[:], ALU.add, r=["lqp0", "lqp1"], w=["lsum"])
    p.actf(lsum[:], lsum[:], AF.Exp, r=["lsum"], w=["lsum"])
    p.tt("dve", lam[:], lsum[:, 0:1], lsum[:, 1:2], ALU.subtract, r=["lsum"], w=["lam"])
    p.tt("dve", lam[:], lam[:], lamc[:, 0:1], ALU.add, r=["lam", "lamc"], w=["lam"])
    p.ts("dve", lam[:], lam[:], -1.0, None, ALU.mult, None, r=["lam"], w=["lam"])
    p.ts("dve", gsubc[:], gsubc[:], lamc[:, 1:2], None, ALU.mult, None, r=["gsubc", "lamc"], w=["gsubc"])

    pair = 0
    NP_ = NKC // 2
    for qg in range(16):
        for m in range(2):
            def s_pair(pi, pr):
                b = pr % 2
                for hf in range(2):
                    kc = pi * 2 + hf
                    p.mm(sps[b][:, hf * 512:(hf + 1) * 512], kTz[m][:, kc * 128:(kc + 1) * 128],
                         qT[:, qg * 512:(qg + 1) * 512], True, True, r=["qT", "kT%dd" % m, "kT%dz" % m], w=["sps%d" % b])
            s_pair(0, pair)
            for pi in range(NP_):
                b = pair % 2
                sl = pair % 3
                if pi + 1 < NP_:
                    s_pair(pi + 1, pair + 1)
                p.actf(pT[sl][:], sps[b][:, :], AF.Exp, r=["sps%d" % b, "nshift"], w=["pT%d" % sl], bias=nshift[:, 0:1], scale=0.125)
                for hf in range(2):
                    kc = pi * 2 + hf
                    p.mm(accO[m][:, :], vv[:, kc, :], pT[sl][:, hf * 512:(hf + 1) * 512], kc == 0, kc == NKC - 1,
                         r=["pT%d" % sl, "vv"], w=["accO%d" % m])
                    p.mm(sums[m][:, :], onesb[:], pT[sl][:, hf * 512:(hf + 1) * 512], kc == 0, kc == NKC - 1,
                         r=["pT%d" % sl, "onesb"], w=["sums%d" % m])
                pair += 1
        p.recip(r1[:], sums[0][:, :], r=["sums0"], w=["r1"])
        p.recip(r2[:], sums[1][:, :], r=["sums1"], w=["r2"])
        p.ts("pool", r2[:], r2[:], lam[:, 0:1], None, ALU.mult, None, r=["r2", "lam"], w=["r2"])
        p.tt("dve", t1[:], accO[0][:, :], r1[:], ALU.mult, r=["accO0", "r1"], w=["t1"])
        p.tt("dve", t2[:], accO[1][:, :], r2[:], ALU.mult, r=["accO1", "r2"], w=["t2"])
        p.tt("pool", t1[:], t1[:], t2[:], ALU.add, r=["t1", "t2"], w=["t1"])
        p.tt("pool", sqb[:], t1[:], t1[:], ALU.mult, r=["t1"], w=["sqb"])
        p.mm(ssp[:, :], onesb[:], sqb[:], True, True, r=["onesb", "sqb"], w=["sums0"])
        p.actf(rs[:], ssp[:, :], AF.Sqrt, r=["sums0", "epst"], w=["rs"], bias=epst[:, 0:1], scale=1.0 / 128)
        p.recip(rs[:], rs[:], r=["rs"], w=["rs"])
        ys = qg % 2
        p.stt("dve", yst[ys][:], t1[:], gsubc[:, 0:1], rs[:], ALU.mult, ALU.mult, r=["t1", "gsubc", "rs"], w=["yst%d" % ys])
        p.dma("sp", yaT_o[:, qg * 512:(qg + 1) * 512], yst[ys][:], r=["yst%d" % ys], semkey="yst%d" % ys)
    p.emit()


def bc(ap, shape, axis):
    return ap.unsqueeze(axis).to_broadcast(list(shape))


def emit_B2(nc, pers, dnT, cwd, identd):
    qTn, kTn, ktok, vtok = pers
    p = Prog(nc)
    ident = p.sb("identB2", [128, 128], BF16)
    onesb = p.sb("onesb", [128, 128], BF16)
    epst = p.sb("epst", [128, 1], F32)
    cw = p.sb("cw", [128, 3, 5], F32)
    xin = [p.sb("xin%d" % i, [128, 2052], F32) for i in range(2)]
    acc = [p.sb("cacc%d" % i, [128, 2048], F32) for i in range(2)]
    sqb = p.sb("sqb", [128, 2048], BF16)
    rin = p.sb("rin", [128, 2048], F32)
    vtb = p.sb("vtb", [128, 2048], BF16)
    ps = [p.ps("psB2_%d" % i, [128, 512], F32) for i in range(4)]
    ptb = [p.ps("ptB2_%d" % i, [128, 1024], BF16) for i in range(2)]
    p.dma("sp", ident[:], identd, w=["ident"], semkey="ident")
    p.dma("sp", cw[:], cwd, w=["cw"], semkey="cw")
    p.memset("pool", onesb[:], 1.0, w=["onesb"])
    p.memset("pool", epst[:], EPS, w=["epst"])
    it = 0
    tcnt = 0
    for tg in range(4):
        for X in range(3):
            sl = it % 2
            it += 1
            xk, ak = "xin%d" % sl, "cacc%d" % sl
            lo = tg * 2048 - 2
            hi = tg * 2048 + 2050
            if tg == 0:
                p.memset("pool", xin[sl][:, 0:2], 0.0, w=[xk])
                p.dma("sp", xin[sl][:, 2:2052], dnT[X, :, 0:2050], w=[xk + "d"], r=[xk], semkey=xk)
            elif tg == 3:
                p.memset("pool", xin[sl][:, 2050:2052], 0.0, w=[xk])
                p.dma("sp", xin[sl][:, 0:2050], dnT[X, :, lo:S], w=[xk + "d"], r=[xk], semkey=xk)
            else:
                p.memset("pool", xin[sl][:, 0:1], 0.0, w=[xk])
                p.dma("sp", xin[sl][:, :], dnT[X, :, lo:hi], w=[xk + "d"], r=[xk], semkey=xk)
            eng = "dve"
            a = acc[sl]
            p.ts(eng, a[:], xin[sl][:, 0:2048], cw[:, X, 0:1], None, ALU.mult, None, r=[xk + "d", "cw"], w=[ak])
            for j in range(1, 5):
                p.stt(eng, a[:], xin[sl][:, j:j + 2048], cw[:, X, j:j + 1], a[:], ALU.mult, ALU.add, r=[xk + "d", "cw", ak], w=[ak])
            p.actf(a[:], a[:], AF.Silu, r=[ak, xk + "d"], w=[ak, xk])
            if X == 2:
                p.cp("pool", vtb[:], a[:], r=[ak], w=["vtb"])
                src_b, dst, skey = vtb, vtok, "vtb"
            else:
                p.tt("pool", sqb[:], a[:], a[:], ALU.mult, r=[ak], w=["sqb"])
                for blk in range(4):
                    p.mm(ps[blk][:, :], onesb[:], sqb[:, blk * 512:(blk + 1) * 512], True, True, r=["onesb", "sqb"], w=["psn%d" % blk])
                    p.actf(rin[:, blk * 512:(blk + 1) * 512], ps[blk][:, :], AF.Sqrt, r=["psn%d" % blk, "epst"], w=["rin%d" % blk],
                           bias=epst[:, 0:1])
                p.recip(rin[:], rin[:], r=["rin%d" % b_ for b_ in range(4)], w=["rin"])
                tgt = qTn if X == 0 else kTn
                sc = (128.0 ** -0.5) if X == 0 else 1.0
                p.stt("dve", tgt[:, tg * 2048:(tg + 1) * 2048], a[:], sc, rin[:], ALU.mult, ALU.mult, r=[ak, "rin"],
                      w=["nT%d_%d" % (X, tg)] + ["rin%d" % b_ for b_ in range(4)])
                src_b, dst, skey = kTn, ktok, "nT1_%d" % tg
            if X >= 1:
                for half in range(2):
                    pb = tcnt % 2
                    tcnt += 1
                    for c in range(8):
                        cc = half * 8 + c
                        if X == 2:
                            src = vtb[:, cc * 128:(cc + 1) * 128]
                        else:
                            src = kTn[:, tg * 2048 + cc * 128:tg * 2048 + (cc + 1) * 128]
                        p.tr(ptb[pb][:, c * 128:(c + 1) * 128], src, ident[:], r=[skey, "ident"], w=["ptb%d" % pb])
                    n0 = tg * 16 + half * 8
                    p.cp("act", dst[:, n0:n0 + 8, :], ptb[pb][:].rearrange("p (k t) -> p k t", t=128), r=["ptb%d" % pb],
                         w=["tok%d_%d" % (X, n0)])
    p.emit()


def emit_B3(nc, pers, o_scr, bgd, masksd, identd):
    qTn, kTn, ktok, vtok = pers
    p = Prog(nc)
    identb = p.sb("identB3", [128, 128], BF16)
    identf = p.sb("identfB3", [128, 128], F32)
    msk = p.sb("msk", [128, 8, 128], F32)
    onesf = p.sb("onesf", [128, 128], F32)
    bg = p.sb("bgt", [128, NKC, 4], F32)
    negb = p.sb("negb", [128, NKC, 2], F32)
    EG = [p.sb("EG%d" % d, [128, NKC], F32) for d in range(2)]
    ER = [p.sb("ER%d" % d, [128, NKC], F32) for d in range(2)]
    EL = [p.sb("EL%d" % d, [128, NKC], F32) for d in range(2)]
    Sst = [p.sb("Sst%d" % d, [128, 128], F32) for d in range(2)]
    vnew = [[p.sb("vnew%d_%d" % (d, i), [128, 128], F32) for i in range(2)] for d in range(2)]
    ost = [[p.sb("ost%d_%d" % (d, i), [128, 128], F32) for i in range(2)] for d in range(2)]
    grp = {}
    for d in range(2):
        for s_ in range(2):
            for nm in ("u", "wT", "qdT", "qkT", "kdec"):
                grp[(nm, d, s_)] = p.sb("%s%d_%d" % (nm, d, s_), [128, 4, 128], F32)
    T3 = {}
    for d in range(2):
        for nm in ("gtri", "DTr", "EGr", "tmpf", "X", "XT", "A", "AT", "P", "PT"):
            T3[(nm, d)] = p.sb("%s_%d" % (nm, d), [128, 4, 128], F32)
        for nm in ("kgb", "Pb16"):
            T3[(nm, d)] = p.sb("%s_%d" % (nm, d), [128, 4, 128], BF16)
    psbD = [[p.ps("psB3_%d_%d" % (d, i), [128, 512], F32) for i in range(3)] for d in range(2)]
    psb = psbD[0]
    pssA = [p.ps("psscA%d" % d, [128, 512], F32) for d in range(2)]

    p.dma("sp", identb[:], identd, w=["identb_"], semkey="ident")
    p.cp("dve", identf[:], identb[:], r=["identb_"], w=["ident"])
    p.dma("sp", msk[:], masksd.rearrange("a p f -> p a f"), w=["msk"], semkey="msk")
    p.dma("sp", bg[:], bgd, w=["bg"], semkey="bg")
    p.memset("pool", onesf[:], 1.0, w=["onesf"])
    for d in range(2):
        p.memset("pool", Sst[d][:], 0.0, w=["S%d" % d])
    p.ts("dve", negb[:], bg[:, :, 0:2], -1.0, None, ALU.mult, None, r=["bg"], w=["negb"])
    for d in range(2):
        g = bg[:, :, 2 + d]
        mGC = msk[:, 0, :] if d == 0 else msk[:, 2, :]
        mGR = msk[:, 3, :] if d == 0 else msk[:, 1, :]
        for i, (lhs, dst, nm) in enumerate(((mGC, EG[d], "EG"), (mGR, ER[d], "ER"), (onesf[:], EL[d], "EL"))):
            p.mm(psbD[d][i][:, 0:NKC], lhs, g, True, True, r=["msk", "onesf", "bg"], w=["ps%d_%d" % (i, d)])
            p.actf(dst[:], psbD[d][i][:, 0:NKC], AF.Exp, r=["ps%d_%d" % (i, d)], w=["%s%d" % (nm, d)])

    sh3 = [128, 4, 128]

    def fl(t):
        return t[:].rearrange("p a b -> p (a b)")

    def d3_stages(d, G):
        s_ = G % 2
        n0 = 4 * G if d == 0 else 60 - 4 * G
        MI = msk[:, 0, :] if d == 0 else msk[:, 2, :]
        MS = msk[:, 1, :] if d == 0 else msk[:, 3, :]
        MG = msk[:, 3, :] if d == 0 else msk[:, 1, :]
        u, wT, qdT, qkT, kdec = [grp[(nm, d, s_)] for nm in ("u", "wT", "qdT", "qkT", "kdec")]
        gk = lambda nm: "%s%d_%d" % (nm, d, s_)
        D = "_%d" % d
        psb = psbD[d]
        ptb = psbD[d][1]
        gtri, DTr, EGr, tmpf, X, XT, A, AT, P_, PT, kgb, Pb16 = [T3[(nm, d)] for nm in
            ("gtri", "DTr", "EGr", "tmpf", "X", "XT", "A", "AT", "P", "PT", "kgb", "Pb16")]
        Y, Xc, XcT, Y2 = tmpf, DTr, EGr, gtri
        kY, kXc, kXcT, kY2 = "tmpf" + D, "DTr" + D, "EGr" + D, "gtri" + D
        kX, kXT, kA, kAT, kP, kPT = ["%s%s" % (nm, D) for nm in ("X", "XT", "A", "AT", "P", "PT")]
        st = []

        def mm4(bank, bkey, lhs, rhs, r):
            for j in range(4):
                p.mm(psb[bank][:, j * 128:(j + 1) * 128], lhs[:, j, :], rhs[:, j, :], True, True, r=r, w=[bkey])

        def s1():
            p.tt("pool", gtri[:], bc(MI, sh3, 1), bc(bg[:, n0:n0 + 4, 2 + d], sh3, 2), ALU.mult, r=["msk", "bg"], w=[kY2])
            p.mm(psb[0][:, :], MG, fl(gtri), True, True, r=["msk", kY2], w=["ps0" + D])
            p.mm(psb[1][:, :], onesf[:], fl(gtri), True, True, r=["onesf", kY2], w=["ps1" + D])
            p.actf(fl(DTr), psb[0][:, :], AF.Exp, r=["ps0" + D], w=[kXc])
            p.actf(fl(EGr), psb[1][:, :], AF.Exp, r=["ps1" + D], w=[kXcT])
        st.append(s1)

        def s2():
            for j in range(4):
                c0 = (n0 + j) * 128
                p.mm(psb[2][:, j * 128:(j + 1) * 128], kTn[:, c0:c0 + 128], kTn[:, c0:c0 + 128], True, True, r=[], w=["ps2" + D])
            for j in range(4):
                c0 = (n0 + j) * 128
                p.mm(psb[0][:, j * 128:(j + 1) * 128], kTn[:, c0:c0 + 128], qTn[:, c0:c0 + 128], True, True, r=[], w=["ps0" + D])
            ps2v = psb[2][:, :].rearrange("p (a b) -> p a b", a=4)
            ps0v = psb[0][:, :].rearrange("p (a b) -> p a b", a=4)
            p.tt("dve", tmpf[:], ps2v, DTr[:], ALU.mult, r=["ps2" + D, kXc], w=[kY])
            p.tt("dve", qkT[:], ps0v, DTr[:], ALU.mult, r=["ps0" + D, kXc], w=[gk("qkT")])
            p.tt("pool", tmpf[:], tmpf[:], bc(negb[:, n0:n0 + 4, d], sh3, 2), ALU.mult, r=[kY, "negb"], w=[kY])
            p.tt("pool", X[:], tmpf[:], bc(MS, sh3, 1), ALU.mult, r=[kY, "msk"], w=[kX])
            p.tt("pool", qkT[:], qkT[:], bc(MI, sh3, 1), ALU.mult, r=[gk("qkT"), "msk"], w=[gk("qkT")])
            p.tt("pool", fl(qdT), qTn[:, n0 * 128:(n0 + 4) * 128], fl(EGr), ALU.mult, r=[kXcT], w=[gk("qdT")])
            p.tt("pool", kgb[:], ktok[:, n0:n0 + 4, :], bc(EG[d][:, n0:n0 + 4], sh3, 2), ALU.mult, r=["EG%d" % d], w=["kgb" + D])
            p.tt("pool", kdec[:], ktok[:, n0:n0 + 4, :], bc(ER[d][:, n0:n0 + 4], sh3, 2), ALU.mult, r=["ER%d" % d], w=[gk("kdec")])
            for j in range(4):
                p.tr(ptb[:, j * 128:(j + 1) * 128], X[:, j, :], identf[:], r=[kX, "ident"], w=["ps1" + D])
            p.cp("act", fl(XT), ptb[:, :], r=["ps1" + D], w=[kXT])
        st.append(s2)

        def s3():
            BD = bc(msk[:, 4, :], sh3, 1)
            p.tt("pool", A[:], X[:], BD, ALU.mult, r=[kX, "msk"], w=[kA])
            p.tt("pool", AT[:], XT[:], BD, ALU.mult, r=[kXT, "msk"], w=[kAT])
            p.tt("pool", P_[:], A[:], bc(identf[:], sh3, 1), ALU.add, r=[kA, "ident"], w=[kP])
            p.tt("pool", PT[:], AT[:], bc(identf[:], sh3, 1), ALU.add, r=[kAT, "ident"], w=[kPT])
        st.append(s3)

        def base(k):
            def f():
                mm4(0, "ps0" + D, AT, A, [kA, kAT])
                mm4(1, "ps1" + D, A, AT, [kA, kAT])
                p.cp("dve", fl(A), psb[0][:, :], r=["ps0" + D], w=[kA])
                p.cp("act", fl(AT), psb[1][:, :], r=["ps1" + D], w=[kAT])
                mm4(2, "ps2" + D, AT, P_, [kAT, kP])
                mm4(0, "ps0" + D, A, PT, [kA, kPT])
                p.tt("dve", fl(P_), fl(P_), psb[2][:, :], ALU.add, r=[kP, "ps2" + D], w=[kP])
                p.tt("dve", fl(PT), fl(PT), psb[0][:, :], ALU.add, r=[kPT, "ps0" + D], w=[kPT])
            return f
        for k in range(1, 4):
            st.append(base(k))

        def merge(mi):
            def f():
                last = (mi == 2)
                CM = bc(msk[:, 5 + mi, :], sh3, 1)
                p.tt("pool", Xc[:], X[:], CM, ALU.mult, r=[kX, "msk"], w=[kXc])
                p.tt("pool", XcT[:], XT[:], CM, ALU.mult, r=[kXT, "msk"], w=[kXcT])
                mm4(0, "ps0" + D, XcT, P_, [kXcT, kP])
                p.cp("act", fl(Y), psb[0][:, :], r=["ps0" + D], w=[kY])
                if not last:
                    mm4(1, "ps1" + D, Xc, PT, [kXc, kPT])
                    p.cp("dve", fl(Y2), psb[1][:, :], r=["ps1" + D], w=[kY2])
                mm4(2, "ps2" + D, PT, Y, [kPT, kY])
                if not last:
                    mm4(0, "ps0" + D, P_, Y2, [kP, kY2])
                p.tt("dve", fl(P_), fl(P_), psb[2][:, :], ALU.add, r=[kP, "ps2" + D], w=[kP])
                if not last:
                    p.tt("dve", fl(PT), fl(PT), psb[0][:, :], ALU.add, r=[kPT, "ps0" + D], w=[kPT])
            return f
        for mi in range(3):
            st.append(merge(mi))

        def s_fin():
            p.cp("pool", Pb16[:], P_[:], r=[kP], w=["Pb16" + D])
            for j in range(4):
                p.mm(psb[2][:, j * 128:(j + 1) * 128], Pb16[:, j, :], vtok[:, n0 + j, :], True, True, r=["Pb16" + D], w=["ps2" + D])
                p.mm(psb[1][:, j * 128:(j + 1) * 128], kgb[:, j, :], Pb16[:, j, :], True, True, r=["Pb16" + D, "kgb" + D], w=["ps1" + D])
            p.tt("dve", u[:], psb[2][:, :].rearrange("p (a b) -> p a b", a=4), bc(bg[:, n0:n0 + 4, d], sh3, 2), ALU.mult,
                 r=["ps2" + D, "bg"], w=[gk("u")])
            p.cp("act", fl(wT), psb[1][:, :], r=["ps1" + D], w=[gk("wT")])
        st.append(s_fin)
        return st

    vcnt = [0, 0]

    def scan_step(d, G, j):
        s_ = G % 2
        n0 = 4 * G if d == 0 else 60 - 4 * G
        n = n0 + j
        u, wT, qdT, qkT, kdec = [grp[(nm, d, s_)] for nm in ("u", "wT", "qdT", "qkT", "kdec")]
        gk = lambda nm: "%s%d_%d" % (nm, d, s_)
        vs = vcnt[d] % 2
        vcnt[d] += 1
        vn = vnew[d][vs]
        vk = "vnew%d_%d" % (d, vs)
        os_ = ost[d][vs]
        ok = "ost%d_%d" % (d, vs)
        ps1, ps3, ps2 = pssA[d][:, 0:128], pssA[d][:, 128:256], pssA[d][:, 256:384]
        kS = "S%d" % d
        kA = "scA%d" % d
        p.mm(ps1, wT[:, j, :], Sst[d][:], True, True, r=[gk("wT"), kS], w=[kA])
        p.stt("dve", vn[:], ps1, negb[:, n, d:d + 1], u[:, j, :], ALU.mult, ALU.add, r=[kA, "negb", gk("u")], w=[vk])
        p.mm(ps3, kdec[:, j, :], vn[:], True, True, r=[gk("kdec"), vk], w=[kA])
        p.mm(ps2, qdT[:, j, :], Sst[d][:], True, False, r=[gk("qdT"), kS], w=[kA], skip=True)
        p.mm(ps2, qkT[:, j, :], vn[:], False, True, r=[gk("qkT"), vk], w=[kA], skip=True)
        p.stt("dve", Sst[d][:], Sst[d][:], EL[d][:, n:n + 1], ps3, ALU.mult, ALU.add, r=[kS, "EL%d" % d, kA], w=[kS])
        p.cp("dve", os_[:], ps2, r=[kA], w=[ok])
        p.dma("sp", o_scr[d, n, :, :], os_[:], r=[ok], semkey=ok)

    NG = DBG_NG
    lists = [d3_stages(0, 0), d3_stages(1, 0)]
    while lists[0] or lists[1]:
        for L in lists:
            if L:
                L.pop(0)()
    for G in range(NG):
        nxt = [d3_stages(0, G + 1), d3_stages(1, G + 1)] if G + 1 < NG else [[], []]
        steps = []
        for j in range(4):
            steps.append((0, G, j))
            steps.append((1, G, 3 - j))
        while steps or nxt[0] or nxt[1]:
            if nxt[0]:
                nxt[0].pop(0)()
            if steps:
                scan_step(*steps.pop(0))
            if nxt[1]:
                nxt[1].pop(0)()
            if steps:
                scan_step(*steps.pop(0))
    p.emit()


def emit_B4(nc, o_scr, zd, ogd, identd, ybT_o):
    p = Prog(nc)
    ident = p.sb("identB4", [128, 128], BF16)
    og = p.sb("og", [128, 128], F32)
    zt = p.sb("zt", [128, NKC, 128], F32)
    sq = p.sb("sqB4", [128, 4, 128], F32)
    of_ = [p.sb("ofB4_%d" % i, [128, 4, 128], F32) for i in range(2)]
    ob_ = [p.sb("obB4_%d" % i, [128, 4, 128], F32) for i in range(2)]
    ss = p.sb("ssB4", [128, 4], F32)
    yb = p.sb("ybB4", [128, 4, 128], BF16)
    yst = [p.sb("ystB4_%d" % i, [128, 512], BF16) for i in range(2)]
    ptb = p.ps("ptB4", [128, 1024], BF16)
    p.dma("sp", ident[:], identd, w=["ident"], semkey="ident")
    p.dma("sp", og[:], ogd.partition_broadcast(128), w=["og"], semkey="og")
    for c4 in range(4):
        p.dma("sp", zt[:, c4 * 16:(c4 + 1) * 16, :], zd[c4 * 2048:(c4 + 1) * 2048, :].rearrange("(k p) d -> p k d", p=128),
              w=["zt%d" % c4], semkey="zt%d" % c4)
    for c4 in range(4):
        v = zt[:, c4 * 16:(c4 + 1) * 16, :].rearrange("p a b -> p (a b)")
        p.actf(v, v, AF.Silu, r=["zt%d" % c4], w=["zt%d" % c4])
    for G in range(16):
        n0 = 4 * G
        gs = G % 2
        p.dma("sp", of_[gs][:], o_scr[0, n0:n0 + 4, :, :].rearrange("n p e -> p n e"), w=["of%d" % gs], semkey="of%d" % gs)
        p.dma("sp", ob_[gs][:], o_scr[1, n0:n0 + 4, :, :].rearrange("n p e -> p n e"), w=["ob%d" % gs], semkey="ob%d" % gs)
        p.tt("pool", of_[gs][:], of_[gs][:], ob_[gs][:], ALU.add, r=["of%d" % gs, "ob%d" % gs], w=["of%d" % gs])
        o = of_[gs][:]
        p.tt("pool", sq[:], o, o, ALU.mult, r=["of%d" % gs], w=["sq"])
        p.red(ss[:], sq[:], ALU.add, r=["sq"], w=["ss"])
        p.ts("dve", ss[:], ss[:], 1.0 / 128, EPS, ALU.mult, ALU.add, r=["ss"], w=["ss"])
        p.actf(ss[:], ss[:], AF.Sqrt, r=["ss"], w=["ss"])
        p.recip(ss[:], ss[:], r=["ss"], w=["ss"])
        p.tt("dve", sq[:], o, bc(ss[:], [128, 4, 128], 2), ALU.mult, r=["ss", "sq", "of%d" % gs], w=["sq"])
        p.tt("pool", sq[:], sq[:], bc(og[:], [128, 4, 128], 1), ALU.mult, r=["sq", "og"], w=["sq"])
        p.tt("pool", yb[:], sq[:], zt[:, n0:n0 + 4, :], ALU.mult, r=["sq", "zt%d" % (G // 4)], w=["yb"])
        for j in range(4):
            p.tr(ptb[:, j * 128:(j + 1) * 128], yb[:, j, :], ident[:], r=["yb", "ident"], w=["ptb"])
        ys = G % 2
        p.cp("dve", yst[ys][:], ptb[:, 0:512], r=["ptb"], w=["yst%d" % ys])
        p.dma("sp", ybT_o[:, n0 * 128:(n0 + 4) * 128], yst[ys][:], r=["yst%d" % ys], semkey="ystB4_%d" % ys)
    p.emit()


def build_B(parts=("B1", "B2", "B3", "B4")):
    nc = bass.Bass("TRN2", target_bir_lowering=False)
    qkT = din(nc, "qkT", [2, 128, S], BF16)
    vd = din(nc, "v", [S, 128], BF16)
    qkg = din(nc, "qkg", [1, 128], F32)
    lqd = din(nc, "lq_d", [1, 256], F32)
    lamd = din(nc, "lamc_d", [1, 2], F32)
    sub = din(nc, "subln", [1, 128], F32)
    identd = din(nc, "identb", [128, 128], BF16)
    dnT = din(nc, "dnT", [3, 128, S], F32)
    cwd = din(nc, "cw_d", [128, 3, 5], F32)
    bgd = din(nc, "bg_d", [128, NKC, 4], F32)
    masksd = din(nc, "masks", [8, 128, 128], F32)
    zd = din(nc, "z", [S, 128], F32)
    ogd = din(nc, "og_d", [1, 128], F32)
    yaT = dout(nc, "yaT", [128, S], BF16)
    ybT = dout(nc, "ybT", [128, S], BF16)
    if "B1" in parts:
        emit_B1(nc, qkT, vd, qkg, lqd, lamd, sub, identd, yaT)
    with ExitStack() as outer:
        pers = [outer.enter_context(nc.sbuf_tensor(nm, shp, BF16)) for nm, shp in
                (("qTn", [128, S]), ("kTn", [128, S]), ("ktok", [128, NKC, 128]), ("vtok", [128, NKC, 128]))]
        o_scr = nc.dram_tensor("o_scr", [2, NKC, 128, 128], F32).ap()
        if "B2" in parts:
            emit_B2(nc, pers, dnT, cwd, identd)
        if "B3" in parts:
            emit_B3(nc, pers, o_scr, bgd, masksd, identd)
        if DBG_DUMP:
            p = Prog(nc)
            for i, (nm, shp, dt_, t_) in enumerate((("dq", [128, S], BF16, pers[0]), ("dk", [128, S], BF16, pers[1]),
                                            ("dkt", [128, NKC, 128], BF16, pers[2]), ("dvt", [128, NKC, 128], BF16, pers[3]))):
                o_ = dout(nc, nm, shp, dt_)
                p.dma("sp", o_, t_[:], semkey="dump%d" % i)
            p.emit()
        if "B4" in parts:
            emit_B4(nc, o_scr, zd, ogd, identd, ybT)
    return nc


def _masks():
    i = np.arange(128)
    pp, ff = i[:, None], i[None, :]
    bd = lambda b_: (pp // b_) == (ff // b_)
    return np.stack([pp <= ff, pp < ff, pp >= ff, pp > ff, bd(16), bd(32) & ~bd(16), bd(64) & ~bd(32), ~bd(64)]).astype(np.float32)


def run_B(inp, l, resA):
    nc = _get("B", build_B)
    identb = np.eye(128, dtype=np.float32).astype(NPBF)
    masks = _masks()
    li = 0.8 - 0.6 * math.exp(-0.3 * l)
    lamc = np.array([[li, 1.0 - li]], np.float32)
    cw = inp["dn_conv"][l]
    maps = []
    for b in range(NB):
        rs = resA[b * 4:(b + 1) * 4]
        bgc = np.concatenate([r["bg"] for r in rs], axis=1)
        for h in range(4):
            qT = np.concatenate([r["qkT"][h] for r in rs], axis=1)
            kT = np.concatenate([r["qkT"][4 + h] for r in rs], axis=1)
            maps.append({
                "qkT": np.ascontiguousarray(np.stack([qT, kT])),
                "v": np.ascontiguousarray(np.concatenate([r["v"][:, h * 128:(h + 1) * 128] for r in rs], axis=0)),
                "qkg": np.ascontiguousarray(inp["qk_norm_gain"][l].reshape(1, 128)),
                "lq_d": np.ascontiguousarray(inp["diff_lambda"][l].reshape(1, 256)),
                "lamc_d": lamc,
                "subln": np.ascontiguousarray(inp["diff_subln_gain"][l][None, :]),
                "identb": identb,
                "dnT": np.ascontiguousarray(np.stack([np.concatenate([r["dnT"][X * 4 + h] for r in rs], axis=1) for X in range(3)])),
                "cw_d": np.ascontiguousarray(np.stack([cw[:, X * 512 + h * 128:X * 512 + (h + 1) * 128].T for X in range(3)], axis=1)),
                "bg_d": np.ascontiguousarray(bgc[:, :, [h, 4 + h, 8 + h, 12 + h]]),
                "masks": masks,
                "z": np.ascontiguousarray(np.concatenate([r["z"][:, h * 128:(h + 1) * 128] for r in rs], axis=0)),
                "og_d": np.ascontiguousarray(inp["dn_out_gain"][l][None, :]),
            })
    res = run_bass_kernel_spmd(nc, maps, core_ids=list(range(8)))
    return res.results


def kernel(**inp):
    inp = {k: np.asarray(v) for k, v in inp.items()}
    x = inp["x"]
    xsh = [x[i // 4, (i % 4) * NTOK:(i % 4 + 1) * NTOK] for i in range(8)]
    for l in range(2):
        resA = run_A(inp, l, xsh)
        resB = run_B(inp, l, resA)
        yaT, ybT, gT = [], [], []
        for i in range(8):
            b, t0 = i // 4, (i % 4) * NTOK
            yaT.append(np.stack([resB[b * 4 + h]["yaT"][:, t0:t0 + NTOK] for h in range(4)]))
            ybT.append(np.stack([resB[b * 4 + h]["ybT"][:, t0:t0 + NTOK] for h in range(4)]))
            gT.append(resA[i]["gT"])
        xsh = run_C(inp, l, xsh, yaT, ybT, gT)
    out = np.empty((NB, S, D), np.float32)
    for i in range(8):
        out[i // 4, (i % 4) * NTOK:(i % 4 + 1) * NTOK] = xsh[i]
    return out
```

```python
from contextlib import ExitStack
import math
import numpy as np
import ml_dtypes
import concourse.bass as bass
import concourse.mybir as mybir
from concourse.bass_utils import run_bass_kernel_spmd

F32 = mybir.dt.float32
BF16 = mybir.dt.bfloat16
ALU = mybir.AluOpType
AF = mybir.ActivationFunctionType
AX = mybir.AxisListType
NPBF = ml_dtypes.bfloat16

ENGS = ("pe", "act", "dve", "pool", "sp")

D = 1024
S = 8192
NB = 2
NTOK = 2048
NT = NTOK // 128
EPS = 1e-6
INC = 5648


class Op:
    __slots__ = ("eng", "fn", "deps", "is_dma", "semkey", "cum", "sig", "waits", "has_dep")

    def __init__(self, eng, fn, is_dma=False, semkey=None):
        self.eng = eng
        self.fn = fn
        self.deps = []
        self.is_dma = is_dma
        self.semkey = semkey
        self.cum = 0
        self.sig = 0
        self.has_dep = False
        self.waits = []


class Prog:
    def __init__(self, nc):
        self.nc = nc
        self.es = ExitStack()
        self.ops = {e: [] for e in ENGS}
        self.lastw = {}
        self.readers = {}
        self.dma_cum = {}

    def sb(self, name, shape, dtype):
        return self.es.enter_context(self.nc.sbuf_tensor(name, list(shape), dtype))

    def ps(self, name, shape, dtype=F32):
        return self.es.enter_context(self.nc.psum_tensor(name, list(shape), dtype))

    def _rec(self, op, r, w):
        deps = []
        for k in r:
            lw = self.lastw.get(k)
            if lw is not None:
                deps.append(lw)
        for k in w:
            lw = self.lastw.get(k)
            rd = self.readers.get(k, [])
            if rd:
                deps.extend(rd)
            elif lw is not None:
                if not (op.is_dma and lw.is_dma and lw.semkey == op.semkey):
                    deps.append(lw)
        for k in r:
            self.readers.setdefault(k, []).append(op)
        for k in w:
            self.lastw[k] = op
            self.readers[k] = []
        seen = set()
        for d in deps:
            if d is op or id(d) in seen:
                continue
            seen.add(id(d))
            if d.eng == "pe" and op.eng == "pe" and not d.is_dma and not op.is_dma:
                continue
            op.deps.append(d)
            d.has_dep = True
        self.ops[op.eng].append(op)
        return op

    def op(self, eng, fn, r=(), w=()):
        return self._rec(Op(eng, fn), r, w)

    def dma(self, q, out, in_, r=(), w=(), semkey=None, **kw):
        assert semkey is not None
        o = Op(q, lambda e: e.dma_start(out=out, in_=in_, **kw), is_dma=True, semkey=semkey)
        self._rec(o, r, w)
        self.dma_cum[semkey] = self.dma_cum.get(semkey, 0) + 16
        o.cum = self.dma_cum[semkey]
        return o

    def mm(self, out, lhsT, rhs, start, stop, r, w, skip=False):
        if skip:
            return self.op("pe", lambda e: e.matmul(out, lhsT, rhs, start=start, stop=stop, skip_group_check=True), r, w)
        return self.op("pe", lambda e: e.matmul(out, lhsT, rhs, start=start, stop=stop), r, w)

    def tr(self, out, in_, ident, r, w):
        return self.op("pe", lambda e: e.transpose(out, in_, ident), r, w)

    def actf(self, out, in_, func, r, w, bias=None, scale=None, accum=None):
        kw = {}
        if bias is not None:
            kw["bias"] = bias
        if scale is not None:
            kw["scale"] = scale
        if accum is not None:
            kw["accum_out"] = accum
        return self.op("act", lambda e: e.activation(out, in_, func, **kw), r, w)

    def tt(self, eng, out, a, b, op, r, w):
        return self.op(eng, lambda e: e.tensor_tensor(out, a, b, op), r, w)

    def ts(self, eng, out, a, s1, s2, op0, op1, r, w):
        if op1 is None:
            return self.op(eng, lambda e: e.tensor_scalar(out, a, s1, None, op0), r, w)
        return self.op(eng, lambda e: e.tensor_scalar(out, a, s1, s2, op0, op1), r, w)

    def stt(self, eng, out, in0, scalar, in1, op0, op1, r, w):
        return self.op(eng, lambda e: e.scalar_tensor_tensor(out, in0, scalar, in1, op0, op1), r, w)

    def cp(self, eng, out, in_, r, w):
        if eng == "act":
            return self.op("act", lambda e: e.copy(out, in_), r, w)
        return self.op(eng, lambda e: e.tensor_copy(out, in_), r, w)

    def red(self, out, in_, op, r, w):
        return self.op("dve", lambda e: e.tensor_reduce(out, in_, AX.X, op), r, w)

    def recip(self, out, in_, r, w):
        return self.op("dve", lambda e: e.reciprocal(out, in_), r, w)

    def memset(self, eng, out, val, w):
        return self.op(eng, lambda e: e.memset(out, val), (), w)

    def emit(self):
        nc = self.nc
        es = self.es
        esem = {e: es.enter_context(nc.semaphore("s_" + e)) for e in ENGS}
        dsem = {k: es.enter_context(nc.semaphore("d_%d" % i)) for i, k in enumerate(self.dma_cum)}
        for e in ENGS:
            c = 0
            for o in self.ops[e]:
                if not o.is_dma and o.has_dep:
                    c += 1
                    o.sig = c
        for e in ENGS:
            waited = {}
            for o in self.ops[e]:
                need = {}
                for d in o.deps:
                    if d.is_dma:
                        key = ("d", d.semkey)
                        val = d.cum
                    else:
                        key = ("e", d.eng)
                        val = d.sig
                    if val > need.get(key, 0):
                        need[key] = val
                for key, val in need.items():
                    if waited.get(key, 0) >= val:
                        continue
                    waited[key] = val
                    sem = dsem[key[1]] if key[0] == "d" else esem[key[1]]
                    o.waits.append((sem, val))
        final = [(dsem[k], v) for k, v in self.dma_cum.items()]
        block = es.enter_context(nc.Block())

        def run(eng, name):
            for o in self.ops[name]:
                for sem, val in o.waits:
                    eng.wait_ge(sem, val)
                ins = o.fn(eng)
                if o.is_dma:
                    ins.then_inc(dsem[o.semkey], 16)
                elif o.has_dep:
                    ins.then_inc(esem[name], 1)
            if name == "sp":
                for sem, val in final:
                    eng.wait_ge(sem, val)

        @block.tensor
        def _(eng):
            run(eng, "pe")

        @block.scalar
        def _(eng):
            run(eng, "act")

        @block.vector
        def _(eng):
            run(eng, "dve")

        @block.gpsimd
        def _(eng):
            run(eng, "pool")

        @block.sync
        def _(eng):
            run(eng, "sp")

        es.close()


def din(nc, name, shape, dt):
    return nc.dram_tensor(name, list(shape), dt, kind="ExternalInput").ap()


def dout(nc, name, shape, dt):
    return nc.dram_tensor(name, list(shape), dt, kind="ExternalOutput").ap()


def emit_mod(p, nc, ada_w, ada_b, cpk, col0, ncols, stage, psb, modbc):
    cact = p.sb("cact", [128, 8], F32)
    crep = p.sb("crep", [128, 8, 128], F32)
    p.dma("sp", cact[:], cpk, w=["cact"], semkey="cact")
    p.dma("sp", modbc[:, 0:ncols], ada_b[:, col0:col0 + ncols].partition_broadcast(128), w=["modbc"], semkey="modbc")
    p.actf(cact[:], cact[:], AF.Silu, r=["cact"], w=["cact"])
    p.cp("dve", crep[:], cact[:].unsqueeze(2).to_broadcast([128, 8, 128]), r=["cact"], w=["crep"])
    npass = (ncols + 2047) // 2048
    for ps_ in range(npass):
        c0 = ps_ * 2048
        cw = min(2048, ncols - c0)
        for kc in range(8):
            sl = kc % 2
            p.dma("sp", stage[sl][:, 0:cw], ada_w[kc * 128:(kc + 1) * 128, col0 + c0:col0 + c0 + cw],
                  w=["stg%d" % sl], semkey="stg%d" % sl)
            for j in range(cw // 512):
                p.mm(psb[j][:, :], crep[:, kc, :], stage[sl][:, j * 512:(j + 1) * 512], kc == 0, kc == 7,
                     r=["crep", "stg%d" % sl], w=["psb%d" % j])
        for j in range(cw // 512):
            cs = c0 + j * 512
            p.tt("dve", modbc[:, cs:cs + 512], modbc[:, cs:cs + 512], psb[j][:, :], ALU.add,
                 r=["psb%d" % j, "modbc"], w=["modbc"])


def emit_rmsnorm_T(p, xt, xkey, G, gkey, SH, shkey, junk, ssk, hb, ptb, ptbkey, hT, t, ident, tag):
    ss = ssk
    p.actf(junk[:], xt, AF.Square, r=[xkey], w=["junk" + tag, "ss" + tag], accum=ss[:, 0:1])
    p.ts("dve", ss[:, 0:1], ss[:, 0:1], 1.0 / D, EPS, ALU.mult, ALU.add, r=["ss" + tag], w=["ss" + tag])
    p.actf(ss[:, 0:1], ss[:, 0:1], AF.Sqrt, r=["ss" + tag], w=["ss" + tag])
    p.recip(ss[:, 0:1], ss[:, 0:1], r=["ss" + tag], w=["ss" + tag])
    p.stt("dve", junk[:], xt, ss[:, 0:1], G, ALU.mult, ALU.mult, r=[xkey, "ss" + tag, gkey], w=["junk" + tag])
    p.tt("dve", hb[:], junk[:], SH, ALU.add, r=["junk" + tag, shkey], w=["hb" + tag])
    for kc in range(8):
        p.tr(ptb[:, kc * 128:(kc + 1) * 128], hb[:, kc * 128:(kc + 1) * 128], ident[:], r=["hb" + tag, "ident"], w=[ptbkey])
    p.cp("act", hT[:, :, t * 128:(t + 1) * 128], ptb[:].rearrange("p (k t) -> p k t", t=128), r=[ptbkey], w=["hT%d" % t])


def emit_rms_a(p, xt, xkey, G, gkey, SH, shkey, junk, ss, hb, hbkey, tag):
    p.actf(junk[:], xt, AF.Square, r=[xkey], w=["junk" + tag, "ss" + tag], accum=ss[:, 0:1])
    p.ts("dve", ss[:, 0:1], ss[:, 0:1], 1.0 / D, EPS, ALU.mult, ALU.add, r=["ss" + tag], w=["ss" + tag])
    p.actf(ss[:, 0:1], ss[:, 0:1], AF.Sqrt, r=["ss" + tag], w=["ss" + tag])
    p.recip(ss[:, 0:1], ss[:, 0:1], r=["ss" + tag], w=["ss" + tag])
    p.stt("dve", junk[:], xt, ss[:, 0:1], G, ALU.mult, ALU.mult, r=[xkey, "ss" + tag, gkey], w=["junk" + tag])
    p.tt("dve", hb[:], junk[:], SH, ALU.add, r=["junk" + tag, shkey], w=[hbkey])


def emit_rms_b(p, hb, hbkey, ptb, ptbkey, hT, t, ident):
    for kc in range(8):
        p.tr(ptb[:, kc * 128:(kc + 1) * 128], hb[:, kc * 128:(kc + 1) * 128], ident[:], r=[hbkey, "ident"], w=[ptbkey])
    p.cp("act", hT[:, :, t * 128:(t + 1) * 128], ptb[:].rearrange("p (k t) -> p k t", t=128), r=[ptbkey], w=["hT%d" % t])


def build_A():
    nc = bass.Bass("TRN2", target_bir_lowering=False)
    x = din(nc, "x", [NTOK, D], F32)
    cpk = din(nc, "cpk", [128, 8], F32)
    ada_w = din(nc, "ada_w", [D, 6 * D], F32)
    ada_b = din(nc, "ada_b", [1, 6 * D], F32)
    g1 = din(nc, "g1", [1, D], F32)
    w_in = din(nc, "w_in", [D, INC], F32)
    qkg = din(nc, "qkg", [1, 128], F32)
    alog = din(nc, "alog", [1, 8], F32)
    dtb = din(nc, "dtb", [1, 8], F32)
    cosd = din(nc, "cos", [NTOK, 64], F32)
    sind = din(nc, "sin", [NTOK, 64], F32)
    identd = din(nc, "identb", [128, 128], BF16)
    qkT_o = dout(nc, "qkT", [8, 128, NTOK], BF16)
    v_o = dout(nc, "v", [NTOK, 512], BF16)
    dnT_o = dout(nc, "dnT", [12, 128, NTOK], F32)
    z_o = dout(nc, "z", [NTOK, 512], F32)
    bg_o = dout(nc, "bg", [128, NT, 16], F32)
    gT_o = dout(nc, "gT", [16, 128, NTOK], BF16)

    p = Prog(nc)
    ident = p.sb("ident", [128, 128], BF16)
    modbc = p.sb("modbc", [128, 2048], F32)
    G1 = p.sb("G1", [128, D], F32)
    stage = [p.sb("stg%d" % i, [128, 2064], F32) for i in range(2)]
    wb = p.sb("wb", [128, 8, 2064], BF16)
    hT = p.sb("hT", [128, 8, NTOK], BF16)
    xs = [p.sb("xs%d" % i, [128, D], F32) for i in range(2)]
    junk = p.sb("junk", [128, D], F32)
    ssA = p.sb("ssA", [128, 1], F32)
    hb2 = [p.sb("hb%d" % i, [128, D], BF16) for i in range(2)]
    gq = p.sb("gq", [128, 128], F32)
    cst = [p.sb("cst%d" % i, [128, 64], F32) for i in range(2)]
    snt = [p.sb("snt%d" % i, [128, 64], F32) for i in range(2)]
    alb = p.sb("alb", [128, 8], F32)
    dtbb = p.sb("dtbb", [128, 8], F32)
    sq_ = [[p.sb("sq%d_%d" % (pr, i), [128, 512], F32) for i in range(2)] for pr in range(2)]
    t1_ = [[p.sb("t1_%d_%d" % (pr, i), [128, 512], F32) for i in range(2)] for pr in range(2)]
    t2_ = [[p.sb("t2_%d_%d" % (pr, i), [128, 512], F32) for i in range(2)] for pr in range(2)]
    qr_ = [[p.sb("qr%d_%d" % (pr, i), [128, 512], BF16) for i in range(2)] for pr in range(2)]
    ss16_ = [p.sb("ss16_%d" % pr, [128, 16], F32) for pr in range(2)]
    vb = [p.sb("vb%d" % i, [128, 512], BF16) for i in range(2)]
    zf = [p.sb("zf%d" % i, [128, 512], F32) for i in range(2)]
    baraw = p.sb("baraw", [128, NT, 16], F32)
    bgo = p.sb("bgo", [128, NT, 16], F32)
    bat = [p.sb("bat%d" % i, [128, NT, 8], F32) for i in range(3)]
    qkTs = [p.sb("qkTs%d" % i, [128, 8, 512], BF16) for i in range(2)]
    dst = [p.sb("dst%d" % i, [128, 512], F32) for i in range(3)]
    gst = [p.sb("gst%d" % i, [128, 512], BF16) for i in range(3)]
    psb = [p.ps("psb%d" % i, [128, 512], F32) for i in range(6)]
    ptb = [p.ps("ptb%d" % i, [128, 1024], BF16) for i in range(2)]

    p.dma("sp", ident[:], identd, w=["ident"], semkey="ident")
    p.dma("sp", G1[:], g1.partition_broadcast(128), w=["GA"], semkey="G1")
    p.dma("sp", gq[:], qkg.partition_broadcast(128), w=["gq"], semkey="gq")
    p.dma("sp", alb[:], alog.partition_broadcast(128), w=["alb"], semkey="alb")
    p.dma("sp", dtbb[:], dtb.partition_broadcast(128), w=["dtbb"], semkey="dtbb")
    emit_mod(p, nc, ada_w, ada_b, cpk, 0, 2048, stage, psb, modbc)
    p.stt("dve", G1[:], modbc[:, 1024:2048], 1.0, G1[:], ALU.add, ALU.mult, r=["modbc", "GA"], w=["GA"])
    p.actf(alb[:], alb[:], AF.Exp, r=["alb"], w=["alb"])
    p.ts("dve", alb[:], alb[:], -1.0, None, ALU.mult, None, r=["alb"], w=["alb"])

    def load_w(colranges, ncols):
        for kc in range(8):
            sl = kc % 2
            o = 0
            for (a, b) in colranges:
                p.dma("sp", stage[sl][:, o:o + (b - a)], w_in[kc * 128:(kc + 1) * 128, a:b],
                      w=["stg%d" % sl], semkey="stg%d" % sl)
                o += b - a
            p.cp("act", wb[:, kc, 0:ncols], stage[sl][:, 0:ncols], r=["stg%d" % sl], w=["wb%d" % kc])

    def load_x(t):
        p.dma("sp", xs[t % 2][:], x[t * 128:(t + 1) * 128, :], w=["xs%d" % (t % 2)], semkey="xs%d" % (t % 2))

    load_x(0)
    load_x(1)
    load_w([(0, 1536), (3072, 3600)], 2064)

    def rms_a(t):
        emit_rms_a(p, xs[t % 2][:], "xs%d" % (t % 2), G1[:], "GA", modbc[:, 0:1024], "modbc", junk, ssA, hb2[t % 2], "hb%d" % (t % 2), "A")
        if t + 2 < NT:
            load_x(t + 2)

    def rms_b(t):
        emit_rms_b(p, hb2[t % 2], "hb%d" % (t % 2), ptb[t % 2], "ptb%d" % (t % 2), hT, t, ident)

    rms_a(0)
    rms_b(0)
    rms_a(1)

    def load_cs(t):
        p.dma("sp", cst[t % 2][:], cosd[t * 128:(t + 1) * 128, :], w=["cst%d" % (t % 2)], semkey="cst%d" % (t % 2))
        p.dma("sp", snt[t % 2][:], sind[t * 128:(t + 1) * 128, :], w=["snt%d" % (t % 2)], semkey="snt%d" % (t % 2))

    load_cs(0)
    load_cs(1)
    def emit_tr(t):
            pk = "ptb%d" % (t % 2)
            for j in range(2):
                for h in range(4):
                    p.tr(ptb[t % 2][:, (j * 4 + h) * 128:(j * 4 + h + 1) * 128], qr_[t % 2][j][:, h * 128:(h + 1) * 128], ident[:],
                         r=["qrqk%d_%d" % (j, t % 2), "ident"], w=[pk])
            sl = (t // 4) % 2
            p.cp("act", qkTs[sl][:, :, (t % 4) * 128:(t % 4 + 1) * 128], ptb[t % 2][:].rearrange("p (k t) -> p k t", t=128),
                 r=[pk], w=["qkTs%d" % sl])
            if t % 4 == 3:
                tg = t // 4
                p.dma("pool", qkT_o[:, :, tg * 512:(tg + 1) * 512].rearrange("h p t -> p h t"), qkTs[sl][:],
                      r=["qkTs%d" % sl], semkey="qkTs%d" % sl)

    wkeys = ["wb%d" % kc for kc in range(8)]
    for t in range(NT):
        for j in range(4):
            for kc in range(8):
                p.mm(psb[j][:, :], hT[:, kc, t * 128:(t + 1) * 128], wb[:, kc, j * 512:(j + 1) * 512], kc == 0, kc == 7,
                     r=["hT%d" % t, "wb%d" % kc], w=["psb%d" % j])
        for kc in range(8):
            p.mm(psb[4][:, 0:16], hT[:, kc, t * 128:(t + 1) * 128], wb[:, kc, 2048:2064], kc == 0, kc == 7,
                 r=["hT%d" % t, "wb%d" % kc], w=["psb4"])
        if t > 0:
            emit_tr(t - 1)
        if t + 1 < NT:
            rms_b(t + 1)
        p.cp("act", vb[t % 2][:], psb[2][:, :], r=["psb2"], w=["vb%d" % (t % 2)])
        p.dma("pool", v_o[t * 128:(t + 1) * 128, :], vb[t % 2][:], r=["vb%d" % (t % 2)], semkey="vb%d" % (t % 2))
        p.cp("act", zf[t % 2][:], psb[3][:, :], r=["psb3"], w=["zf%d" % (t % 2)])
        p.dma("pool", z_o[t * 128:(t + 1) * 128, :], zf[t % 2][:], r=["zf%d" % (t % 2)], semkey="zf%d" % (t % 2))
        p.cp("act", baraw[:, t, :], psb[4][:, 0:16], r=["psb4"], w=["baraw"])
        PR = t % 2
        sq, t1, t2, qr, ss16 = sq_[PR], t1_[PR], t2_[PR], qr_[PR], ss16_[PR]
        SS = "ss16_%d" % PR
        for j in range(2):
            sqj, t1j, t2j, qrj = sq[j], t1[j], t2[j], qr[j]
            kq = "qk%d_%d" % (j, PR)
            p.actf(sqj[:], psb[j][:, :], AF.Square, r=["psb%d" % j], w=["sq" + kq])
            p.red(ss16[:, j * 8:(j + 1) * 8], sqj[:].rearrange("p (g d) -> p g d", d=64), ALU.add, r=["sq" + kq], w=[SS])
        p.ts("dve", ss16[:], ss16[:], 1.0 / 64, EPS, ALU.mult, ALU.add, r=[SS], w=[SS])
        p.actf(ss16[:], ss16[:], AF.Sqrt, r=[SS], w=[SS])
        p.recip(ss16[:], ss16[:], r=[SS], w=[SS])
        for j in range(2):
            sqj, t1j, t2j, qrj = sq[j], t1[j], t2[j], qr[j]
            kq = "qk%d_%d" % (j, PR)
            s3 = sqj[:].rearrange("p (g d) -> p g d", d=64)
            a3 = t1j[:].rearrange("p (g d) -> p g d", d=64)
            b3 = t2j[:].rearrange("p (g d) -> p g d", d=64)
            q3 = qrj[:].rearrange("p (g d) -> p g d", d=64)
            p.tt("dve", s3, psb[j][:, :].rearrange("p (g d) -> p g d", d=64),
                 ss16[:, j * 8:(j + 1) * 8].unsqueeze(2).to_broadcast([128, 8, 64]), ALU.mult,
                 r=["psb%d" % j, SS], w=["sq" + kq])
            p.tt("pool", s3, s3, gq[:, j * 64:(j + 1) * 64].unsqueeze(1).to_broadcast([128, 8, 64]), ALU.mult,
                 r=["sq" + kq, "gq"], w=["sq" + kq])
            p.tt("pool", a3, s3, cst[t % 2][:].unsqueeze(1).to_broadcast([128, 8, 64]), ALU.mult,
                 r=["sq" + kq, "cst%d" % (t % 2)], w=["t1" + kq])
            p.tt("dve", b3[:, :, 0:32], s3[:, :, 32:64], snt[t % 2][:, 0:32].unsqueeze(1).to_broadcast([128, 8, 32]), ALU.mult,
                 r=["sq" + kq, "snt%d" % (t % 2)], w=["t2a" + kq])
            p.tt("pool", b3[:, :, 32:64], s3[:, :, 0:32], snt[t % 2][:, 32:64].unsqueeze(1).to_broadcast([128, 8, 32]), ALU.mult,
                 r=["sq" + kq, "snt%d" % (t % 2)], w=["t2b" + kq])
            p.tt("dve", qrj[:], t1j[:], t2j[:], ALU.add, r=["t1" + kq, "t2a" + kq, "t2b" + kq], w=["qr" + kq])
        if t + 2 < NT:
            load_cs(t + 2)
            rms_a(t + 2)
    emit_tr(NT - 1)

    braw = baraw[:, :, 0:8]
    araw = baraw[:, :, 8:16]
    p.actf(bgo[:, :, 0:8], braw, AF.Sigmoid, r=["baraw"], w=["bgo_b"])
    xg, ab, mx = bat
    p.tt("dve", xg[:], araw, dtbb[:].unsqueeze(1).to_broadcast([128, NT, 8]), ALU.add, r=["baraw", "dtbb"], w=["xg"])
    p.ts("dve", ab[:], xg[:], -1.0, None, ALU.mult, None, r=["xg"], w=["ab"])
    p.tt("dve", ab[:], ab[:], xg[:], ALU.max, r=["ab", "xg"], w=["ab"])
    p.actf(ab[:], ab[:], AF.Exp, r=["ab"], w=["ab"], scale=-1.0)
    p.ts("dve", ab[:], ab[:], 1.0, None, ALU.add, None, r=["ab"], w=["ab"])
    p.actf(ab[:], ab[:], AF.Ln, r=["ab"], w=["ab"])
    p.ts("dve", mx[:], xg[:], 0.0, None, ALU.max, None, r=["xg"], w=["mx"])
    p.tt("dve", mx[:], mx[:], ab[:], ALU.add, r=["mx", "ab"], w=["mx"])
    p.tt("dve", bgo[:, :, 8:16], mx[:], alb[:].unsqueeze(1).to_broadcast([128, NT, 8]), ALU.mult, r=["mx", "alb"], w=["bgo_g"])
    p.dma("pool", bg_o, bgo[:], r=["bgo_b", "bgo_g"], semkey="bgo")

    cnt = 0
    for grp, (a, b) in enumerate([(1536, 3072), (3600, 5648)]):
        ncols = b - a
        load_w([(a, b)], ncols)
        for tg in range(4):
            for cc in range(ncols // 128):
                bk = cnt % 4
                for kc in range(8):
                    p.mm(psb[bk][:, :], wb[:, kc, cc * 128:(cc + 1) * 128], hT[:, kc, tg * 512:(tg + 1) * 512], kc == 0, kc == 7,
                         r=["wb%d" % kc] + ["hT%d" % tt_ for tt_ in range(tg * 4, tg * 4 + 4)], w=["psb%d" % bk])
                sl = cnt % 3
                if grp == 0:
                    p.cp("act", dst[sl][:], psb[bk][:, :], r=["psb%d" % bk], w=["dst%d" % sl])
                    p.dma("pool", dnT_o[cc, :, tg * 512:(tg + 1) * 512], dst[sl][:], r=["dst%d" % sl], semkey="dst%d" % sl)
                else:
                    p.actf(gst[sl][:], psb[bk][:, :], AF.Sigmoid, r=["psb%d" % bk], w=["gst%d" % sl])
                    p.dma("pool", gT_o[cc, :, tg * 512:(tg + 1) * 512], gst[sl][:], r=["gst%d" % sl], semkey="gst%d" % sl)
                cnt += 1
    p.emit()
    return nc


_CACHE = {}


def _rope_tables():
    pos = np.arange(S, dtype=np.float32)
    inv = (1.0 / (np.float32(10000.0) ** (np.arange(0, 64, 2, dtype=np.float32) / np.float32(64)))).astype(np.float32)
    ang = (pos[:, None] * inv[None, :]).astype(np.float32)
    ang = np.concatenate([ang, ang], axis=-1)
    cos = np.cos(ang).astype(np.float32)
    sin = np.sin(ang).astype(np.float32)
    sin_signed = sin.copy()
    sin_signed[:, :32] = -sin_signed[:, :32]
    return cos, sin_signed


def _get(name, builder):
    if name not in _CACHE:
        _CACHE[name] = builder()
    return _CACHE[name]


def run_A(inp, l, xsh):
    nc = _get("A", build_A)
    cos, sins = _get("rope", _rope_tables)
    identb = np.eye(128, dtype=np.float32).astype(NPBF)
    maps = []
    for i in range(8):
        b, t0 = i // 4, (i % 4) * NTOK
        maps.append({
            "x": np.ascontiguousarray(xsh[i]),
            "cpk": np.ascontiguousarray(inp["c"][b].reshape(8, 128).T),
            "ada_w": np.ascontiguousarray(inp["ada_w"][l]),
            "ada_b": np.ascontiguousarray(inp["ada_b"][l][None, :]),
            "g1": np.ascontiguousarray(inp["norm1_g"][l][None, :]),
            "w_in": np.ascontiguousarray(inp["w_in"][l]),
            "qkg": np.ascontiguousarray(inp["qk_norm_gain"][l].reshape(1, 128)),
            "alog": np.ascontiguousarray(inp["dn_A_log"][l].reshape(1, 8)),
            "dtb": np.ascontiguousarray(inp["dn_dt_bias"][l].reshape(1, 8)),
            "cos": np.ascontiguousarray(cos[t0:t0 + NTOK]),
            "sin": np.ascontiguousarray(sins[t0:t0 + NTOK]),
            "identb": identb,
        })
    res = run_bass_kernel_spmd(nc, maps, core_ids=list(range(8)))
    return res.results


def emit_C(nc, outer, x_in, yaT, ybT, gT, cpk, ada_w, ada_b, g2, w_a, w_b, w_out, w1, w2, identd, x_out):
    x_res = outer.enter_context(nc.sbuf_tensor("x_res", [128, NT, D], F32))
    modbc = outer.enter_context(nc.sbuf_tensor("modbcC", [128, 4096], F32))
    G2 = outer.enter_context(nc.sbuf_tensor("G2", [128, D], F32))
    ident = outer.enter_context(nc.sbuf_tensor("identC", [128, 128], BF16))

    p = Prog(nc)
    stage = [p.sb("stgC%d" % i, [128, 2048], F32) for i in range(2)]
    psb = [p.ps("psC0_%d" % i, [128, 512], F32) for i in range(4)]
    p.dma("sp", ident[:], identd, w=["ident"], semkey="ident")
    p.dma("sp", G2[:], g2.partition_broadcast(128), w=["G2"], semkey="G2")
    for t in range(NT):
        p.dma("sp", x_res[:, t, :], x_in[t * 128:(t + 1) * 128, :], w=["xr%d" % t], semkey="xres")
    emit_mod(p, nc, ada_w, ada_b, cpk, 2048, 4096, stage, psb, modbc)
    p.stt("dve", G2[:], modbc[:, 2048:3072], 1.0, G2[:], ALU.add, ALU.mult, r=["modbc", "G2"], w=["G2"])
    p.emit()
    gt1 = modbc[:, 0:1024]
    sh2 = modbc[:, 1024:2048]
    gt2 = modbc[:, 3072:4096]

    p = Prog(nc)
    stg = [p.sb("stg1_%d" % i, [128, 1024], F32) for i in range(2)]
    wab = p.sb("wab", [128, 4, D], BF16)
    wbb = p.sb("wbb", [128, 4, D], BF16)
    wob = p.sb("wob", [128, 8, D], BF16)
    yas = [p.sb("yas%d" % i, [128, 4, 512], BF16) for i in range(2)]
    ybs = [p.sb("ybs%d" % i, [128, 4, 512], BF16) for i in range(2)]
    gas = [p.sb("gas%d" % i, [128, 512], BF16) for i in range(2)]
    gbs = [p.sb("gbs%d" % i, [128, 512], BF16) for i in range(2)]
    m1 = [p.sb("m1_%d" % i, [128, 512], F32) for i in range(2)]
    m2 = [p.sb("m2_%d" % i, [128, 512], F32) for i in range(2)]
    mT = [p.sb("mT%d" % i, [128, 8, 512], BF16) for i in range(2)]
    tmp = [p.sb("tmpc%d" % i, [128, 512], F32) for i in range(2)]
    psb = [p.ps("psC1_%d" % i, [128, 512], F32) for i in range(6)]
    si = 0
    for h in range(4):
        for (wsrc, wdst, nm) in ((w_a, wab, "wab"), (w_b, wbb, "wbb")):
            sl = si % 2
            si += 1
            p.dma("sp", stg[sl][:], wsrc[h * 128:(h + 1) * 128, :], w=["stg%d" % sl], semkey="stg%d" % sl)
            p.cp("act", wdst[:, h, :], stg[sl][:], r=["stg%d" % sl], w=[nm])
    for kc in range(8):
        sl = si % 2
        si += 1
        p.dma("sp", stg[sl][:], w_out[kc * 128:(kc + 1) * 128, :], w=["stg%d" % sl], semkey="stg%d" % sl)
        p.cp("act", wob[:, kc, :], stg[sl][:], r=["stg%d" % sl], w=["wob"])
    cnt = 0
    for tg in range(4):
        s2 = tg % 2
        p.dma("sp", yas[s2][:], yaT[:, :, tg * 512:(tg + 1) * 512].rearrange("h p t -> p h t"), w=["yas%d" % s2], semkey="yas%d" % s2)
        p.dma("sp", ybs[s2][:], ybT[:, :, tg * 512:(tg + 1) * 512].rearrange("h p t -> p h t"), w=["ybs%d" % s2], semkey="ybs%d" % s2)
        for fc in range(8):
            s3 = fc % 2
            p.dma("sp", gas[s3][:], gT[fc, :, tg * 512:(tg + 1) * 512], w=["gas%d" % s3], semkey="gas%d" % s3)
            p.dma("sp", gbs[s3][:], gT[8 + fc, :, tg * 512:(tg + 1) * 512], w=["gbs%d" % s3], semkey="gbs%d" % s3)
            ba, bb = (0, 1) if fc % 2 == 0 else (2, 3)
            for h in range(4):
                p.mm(psb[ba][:, :], wab[:, h, fc * 128:(fc + 1) * 128], yas[s2][:, h, :], h == 0, h == 3,
                     r=["wab", "yas%d" % s2], w=["ps%d" % ba])
            for h in range(4):
                p.mm(psb[bb][:, :], wbb[:, h, fc * 128:(fc + 1) * 128], ybs[s2][:, h, :], h == 0, h == 3,
                     r=["wbb", "ybs%d" % s2], w=["ps%d" % bb])
            p.tt("dve", m1[s3][:], psb[ba][:, :], gas[s3][:], ALU.mult, r=["ps%d" % ba, "gas%d" % s3], w=["m1_%d" % s3])
            p.tt("dve", m2[s3][:], psb[bb][:, :], gbs[s3][:], ALU.mult, r=["ps%d" % bb, "gbs%d" % s3], w=["m2_%d" % s3])
            p.tt("pool", mT[s2][:, fc, :], m1[s3][:], m2[s3][:], ALU.add, r=["m1_%d" % s3, "m2_%d" % s3], w=["mT%d_%d" % (s2, fc)])
        for i in range(4):
            t = tg * 4 + i
            for j in range(2):
                bk = 4 + (cnt % 2)
                ts_ = cnt % 2
                cnt += 1
                for kc in range(8):
                    p.mm(psb[bk][:, :], mT[s2][:, kc, i * 128:(i + 1) * 128], wob[:, kc, j * 512:(j + 1) * 512], kc == 0, kc == 7,
                         r=["mT%d_%d" % (s2, kc), "wob"], w=["ps%d" % bk])
                p.tt("dve", tmp[ts_][:], psb[bk][:, :], gt1[:, j * 512:(j + 1) * 512], ALU.mult, r=["ps%d" % bk], w=["tmp%d" % ts_])
                p.tt("pool", x_res[:, t, j * 512:(j + 1) * 512], x_res[:, t, j * 512:(j + 1) * 512], tmp[ts_][:], ALU.add,
                     r=["tmp%d" % ts_], w=["xr%d_%d" % (t, j)])
    p.emit()

    p = Prog(nc)
    h2T = p.sb("h2T", [128, 8, NTOK], BF16)
    junk = p.sb("junkC", [128, D], F32)
    ssC = p.sb("ssC", [128, 1], F32)
    hb = p.sb("hbC", [128, D], BF16)
    stg = [p.sb("stg2_%d" % i, [128, 1024], F32) for i in range(2)]
    w1b = [p.sb("w1b%d" % i, [128, 8, 512], BF16) for i in range(2)]
    w2b = [p.sb("w2b%d" % i, [128, 4, D], BF16) for i in range(2)]
    aT = p.sb("aT", [128, 4, NTOK], BF16)
    rl = [p.sb("rl%d" % i, [128, 512], F32) for i in range(2)]
    tmp = [p.sb("tmpd%d" % i, [128, 512], F32) for i in range(2)]
    psb = [p.ps("psC2_%d" % i, [128, 512], F32) for i in range(6)]
    ptb = [p.ps("ptC2_%d" % i, [128, 1024], BF16) for i in range(2)]
    si = 0

    def load_fb(fb):
        nonlocal si
        s = fb % 2
        for k2 in range(4):
            sl = si % 2
            si += 1
            p.dma("sp", stg[sl][:].rearrange("p (k c) -> p k c", k=2),
                  w1[k2 * 256:(k2 + 1) * 256, fb * 512:(fb + 1) * 512].rearrange("(k p) c -> p k c", p=128),
                  w=["stg%d" % sl], semkey="stg%d" % sl)
            p.cp("act", w1b[s][:, k2 * 2:k2 * 2 + 2, :], stg[sl][:].rearrange("p (k c) -> p k c", k=2), r=["stg%d" % sl], w=["w1b%d" % s])
        for c in range(4):
            sl = si % 2
            si += 1
            p.dma("sp", stg[sl][:], w2[fb * 512 + c * 128:fb * 512 + (c + 1) * 128, :], w=["stg%d" % sl], semkey="stg%d" % sl)
            p.cp("act", w2b[s][:, c, :], stg[sl][:], r=["stg%d" % sl], w=["w2b%d" % s])

    load_fb(0)
    hb2 = [hb, p.sb("hbC2", [128, D], BF16)]
    for t in range(NT):
        emit_rms_a(p, x_res[:, t, :], "xr%d" % t, G2[:], "G2", sh2, "modbc", junk, ssC, hb2[t % 2], "hbC%d" % (t % 2), "C")
        emit_rms_b(p, hb2[t % 2], "hbC%d" % (t % 2), ptb[t % 2], "ptb%d" % (t % 2), h2T, t, ident)
        if t == 3:
            break
    cnt = 0
    c2 = 0
    for fb in range(8):
        s = fb % 2
        if fb + 1 < 8:
            load_fb(fb + 1)
        for tg in range(4):
            for c in range(4):
                bk = cnt % 4
                rs = cnt % 2
                cnt += 1
                tn = (tg + 1) * 4 + c
                if fb == 0 and tn < NT:
                    emit_rms_a(p, x_res[:, tn, :], "xr%d" % tn, G2[:], "G2", sh2, "modbc", junk, ssC, hb2[tn % 2], "hbC%d" % (tn % 2), "C")
                for kc in range(8):
                    p.mm(psb[bk][:, :], w1b[s][:, kc, c * 128:(c + 1) * 128], h2T[:, kc, tg * 512:(tg + 1) * 512], kc == 0, kc == 7,
                         r=["w1b%d" % s] + ["hT%d" % tt_ for tt_ in range(tg * 4, tg * 4 + 4)], w=["ps%d" % bk])
                if fb == 0 and tn < NT:
                    emit_rms_b(p, hb2[tn % 2], "hbC%d" % (tn % 2), ptb[tn % 2], "ptb%d" % (tn % 2), h2T, tn, ident)
                p.actf(rl[rs][:], psb[bk][:, :], AF.Relu, r=["ps%d" % bk], w=["rl%d" % rs])
                p.actf(aT[:, c, tg * 512:(tg + 1) * 512], rl[rs][:], AF.Square, r=["rl%d" % rs], w=["aT%d_%d" % (c, tg)])
        for t in range(NT):
            for j in range(2):
                bk = 4 + (c2 % 2)
                ts_ = c2 % 2
                c2 += 1
                for c in range(4):
                    p.mm(psb[bk][:, :], aT[:, c, t * 128:(t + 1) * 128], w2b[s][:, c, j * 512:(j + 1) * 512], c == 0, c == 3,
                         r=["aT%d_%d" % (c, t // 4), "w2b%d" % s], w=["ps%d" % bk])
                p.tt("dve", tmp[ts_][:], psb[bk][:, :], gt2[:, j * 512:(j + 1) * 512], ALU.mult, r=["ps%d" % bk], w=["tmp%d" % ts_])
                p.tt("pool", x_res[:, t, j * 512:(j + 1) * 512], x_res[:, t, j * 512:(j + 1) * 512], tmp[ts_][:], ALU.add,
                     r=["tmp%d" % ts_], w=["xr%d" % t])
    for t in range(NT):
        p.dma("sp", x_out[t * 128:(t + 1) * 128, :], x_res[:, t, :], r=["xr%d" % t], semkey="xout")
    p.emit()


def build_C():
    nc = bass.Bass("TRN2", target_bir_lowering=False)
    x = din(nc, "x", [NTOK, D], F32)
    yaT = din(nc, "yaT", [4, 128, NTOK], BF16)
    ybT = din(nc, "ybT", [4, 128, NTOK], BF16)
    gT = din(nc, "gT", [16, 128, NTOK], BF16)
    cpk = din(nc, "cpk", [128, 8], F32)
    ada_w = din(nc, "ada_w", [D, 6 * D], F32)
    ada_b = din(nc, "ada_b", [1, 6 * D], F32)
    g2 = din(nc, "g2", [1, D], F32)
    w_a = din(nc, "w_a", [512, D], F32)
    w_b = din(nc, "w_b", [512, D], F32)
    w_out = din(nc, "w_out", [D, D], F32)
    w1 = din(nc, "w1", [D, 4 * D], F32)
    w2 = din(nc, "w2", [4 * D, D], F32)
    identd = din(nc, "identb", [128, 128], BF16)
    x_out = dout(nc, "x_out", [NTOK, D], F32)
    with ExitStack() as outer:
        emit_C(nc, outer, x, yaT, ybT, gT, cpk, ada_w, ada_b, g2, w_a, w_b, w_out, w1, w2, identd, x_out)
    return nc


def run_C(inp, l, xsh, yaT, ybT, gT):
    nc = _get("C", build_C)
    identb = np.eye(128, dtype=np.float32).astype(NPBF)
    maps = []
    for i in range(8):
        b = i // 4
        maps.append({
            "x": np.ascontiguousarray(xsh[i]),
            "yaT": np.ascontiguousarray(yaT[i]),
            "ybT": np.ascontiguousarray(ybT[i]),
            "gT": np.ascontiguousarray(gT[i]),
            "cpk": np.ascontiguousarray(inp["c"][b].reshape(8, 128).T),
            "ada_w": np.ascontiguousarray(inp["ada_w"][l]),
            "ada_b": np.ascontiguousarray(inp["ada_b"][l][None, :]),
            "g2": np.ascontiguousarray(inp["norm2_g"][l][None, :]),
            "w_a": np.ascontiguousarray(inp["w_branch_a"][l]),
            "w_b": np.ascontiguousarray(inp["w_branch_b"][l]),
            "w_out": np.ascontiguousarray(inp["w_out"][l]),
            "w1": np.ascontiguousarray(inp["w_mlp1"][l]),
            "w2": np.ascontiguousarray(inp["w_mlp2"][l]),
            "identb": identb,
        })
    res = run_bass_kernel_spmd(nc, maps, core_ids=list(range(8)))
    return [r["x_out"] for r in res.results]


DBG_B3 = 0
B1_DUP = 1
DBG_DUMP = 0
DBG_NG = 16
DBG_D = 0
DBG_DIR = -1
NKC = S // 128


def emit_B1(nc, qkT, vd, qkg, lqd, lamd, sublnd, identd, yaT_o):
    p = Prog(nc)
    qT = p.sb("qT", [128, S], BF16)
    kTz = [p.sb("kTz%d" % m, [128, S], BF16) for m in range(2)]
    vv = p.sb("vv", [128, NKC, 128], BF16)
    gq = p.sb("gqB", [128, 128], F32)
    gqa = p.sb("gqa", [128, 128], F32)
    mx2 = p.sb("mx2", [128, 2], F32)
    nshift = p.sb("nshift", [128, 1], F32)
    lq = p.sb("lq", [128, 4, 64], F32)
    lqp = p.sb("lqp", [128, 2, 64], F32)
    lsum = p.sb("lsum", [128, 2], F32)
    lam = p.sb("lam", [128, 1], F32)
    lamc = p.sb("lamc", [128, 2], F32)
    gsubc = p.sb("gsubc", [128, 1], F32)
    epst = p.sb("epsB1", [128, 1], F32)
    onesf = p.sb("onesfB1", [128, 128], F32)
    onesb = p.sb("onesbB1", [128, 128], BF16)
    pT = [p.sb("pT%d" % i, [128, 1024], BF16) for i in range(3)]
    racc = [[p.sb("racc%d_%d" % (m, e), [128, 1024], F32) for e in range(2)] for m in range(2)]
    r1 = p.sb("r1", [128, 512], F32)
    r2 = p.sb("r2", [128, 512], F32)
    t1 = p.sb("t1B1", [128, 512], F32)
    t2 = p.sb("t2B1", [128, 512], F32)
    sqb = p.sb("sqbB1", [128, 512], BF16)
    rs = p.sb("rsB1", [128, 512], F32)
    yst = [p.sb("yst%d" % i, [128, 512], BF16) for i in range(2)]
    sps = [p.ps("sps%d" % i, [128, 1024], F32) for i in range(2)]
    accO = [p.ps("accO%d" % m, [128, 512], F32) for m in range(2)]
    sums = [p.ps("sums%d" % m, [128, 512], F32) for m in range(2)]
    ssp = sums[0]

    p.dma("sp", gq[:], qkg.partition_broadcast(128), w=["gq"], semkey="gq")
    p.dma("sp", lq[:].rearrange("p a b -> p (a b)"), lqd.partition_broadcast(128), w=["lq"], semkey="lq")
    p.dma("sp", lamc[:], lamd.partition_broadcast(128), w=["lamc"], semkey="lamc")
    p.dma("sp", gsubc[:], sublnd.rearrange("o d -> d o"), w=["gsubc"], semkey="gsubc")
    for c4 in range(4):
        p.dma("sp", qT[:, c4 * 2048:(c4 + 1) * 2048], qkT[0, :, c4 * 2048:(c4 + 1) * 2048], w=["qT"], semkey="qT")
        p.dma("sp", kTz[0][0:64, c4 * 2048:(c4 + 1) * 2048], qkT[1, 0:64, c4 * 2048:(c4 + 1) * 2048], w=["kT0d"], semkey="kT0")
        p.dma("sp", kTz[1][64:128, c4 * 2048:(c4 + 1) * 2048], qkT[1, 64:128, c4 * 2048:(c4 + 1) * 2048], w=["kT1d"], semkey="kT1")
    p.memset("pool", kTz[0][64:128, :], 0.0, w=["kT0z"])
    p.memset("pool", kTz[1][0:64, :], 0.0, w=["kT1z"])
    for c4 in range(4):
        p.dma("sp", vv[:, c4 * 16:(c4 + 1) * 16, :],
              vd[c4 * 2048:(c4 + 1) * 2048, :].rearrange("(k p) d -> p k d", p=128), w=["vv"], semkey="vv")
    p.memset("pool", onesf[:], 1.0, w=["onesf"])
    p.memset("pool", onesb[:], 1.0, w=["onesb"])
    p.memset("pool", epst[:], EPS, w=["epst"])
    p.ts("dve", gqa[:], gq[:], -1.0, None, ALU.mult, None, r=["gq"], w=["gqa"])
    p.tt("dve", gqa[:], gqa[:], gq[:], ALU.max, r=["gqa", "gq"], w=["gqa"])
    p.op("dve", lambda e: e.tensor_reduce(mx2[:], gqa[:].rearrange("p (a b) -> p a b", a=2), AX.X, ALU.max), r=["gqa"], w=["mx2"])
    p.stt("dve", nshift[:], mx2[:, 0:1], -8.0, mx2[:, 1:2], ALU.mult, ALU.mult, r=["mx2"], w=["nshift"])
    p.tt("dve", lqp[:, 0, :], lq[:, 0, :], lq[:, 1, :], ALU.mult, r=["lq"], w=["lqp0"])
    p.tt("dve", lqp[:, 1, :], lq[:, 2, :], lq[:, 3, :], ALU.mult, r=["lq"], w=["lqp1"])
    p.red(lsum[:], lqp[:], ALU.add, r=["lqp0", "lqp1"], w=["lsum"])
    p.actf(lsum[:], lsum[:], AF.Exp, r=["lsum"], w=["lsum"])
    p.tt("dve", lam[:], lsum[:, 0:1], lsum[:, 1:2], ALU.subtract, r=["lsum"], w=["lam"])
    p.tt("dve", lam[:], lam[:], lamc[:, 0:1], ALU.add, r=["lam", "lamc"], w=["lam"])
    p.ts("dve", lam[:], lam[:], -1.0, None, ALU.mult, None, r=["lam"], w=["lam"])
    p.ts("dve", gsubc[:], gsubc[:], lamc[:, 1:2], None, ALU.mult, None, r=["gsubc", "lamc"], w=["gsubc"])

    pair = 0
    NP_ = NKC // 2
    for qg in range(16):
        for m in range(2):
            def s_pair(pi, pr):
                b = pr % 2
                for hf in range(2):
                    kc = pi * 2 + hf
                    p.mm(sps[b][:, hf * 512:(hf + 1) * 512], kTz[m][:, kc * 128:(kc + 1) * 128],
                         qT[:, qg * 512:(qg + 1) * 512], True, True, r=["qT", "kT%dd" % m, "kT%dz" % m], w=["sps%d" % b])
            s_pair(0, pair)
            for pi in range(NP_):
                b = pair % 2
                sl = pair % 3
                if pi + 1 < NP_:
                    s_pair(pi + 1, pair + 1)
                p.actf(pT[sl][:], sps[b][:, :], AF.Exp, r=["sps%d" % b, "nshift"], w=["pT%d" % sl], bias=nshift[:, 0:1], scale=0.125)
                for hf in range(2):
                    kc = pi * 2 + hf
                    p.mm(accO[m][:, :], vv[:, kc, :], pT[sl][:, hf * 512:(hf + 1) * 512], kc == 0, kc == NKC - 1,
                         r=["pT%d" % sl, "vv"], w=["accO%d" % m])
                    p.mm(sums[m][:, :], onesb[:], pT[sl][:, hf * 512:(hf + 1) * 512], kc == 0, kc == NKC - 1,
                         r=["pT%d" % sl, "onesb"], w=["sums%d" % m])
                pair += 1
        p.recip(r1[:], sums[0][:, :], r=["sums0"], w=["r1"])
        p.recip(r2[:], sums[1][:, :], r=["sums1"], w=["r2"])
        p.ts("pool", r2[:], r2[:], lam[:, 0:1], None, ALU.mult, None, r=["r2", "lam"], w=["r2"])
        p.tt("dve", t1[:], accO[0][:, :], r1[:], ALU.mult, r=["accO0", "r1"], w=["t1"])
        p.tt("dve", t2[:], accO[1][:, :], r2[:], ALU.mult, r=["accO1", "r2"], w=["t2"])
        p.tt("pool", t1[:], t1[:], t2[:], ALU.add, r=["t1", "t2"], w=["t1"])
        p.tt("pool", sqb[:], t1[:], t1[:], ALU.mult, r=["t1"], w=["sqb"])
        p.mm(ssp[:, :], onesb[:], sqb[:], True, True, r=["onesb", "sqb"], w=["sums0"])
        p.actf(rs[:], ssp[:, :], AF.Sqrt, r=["sums0", "epst"], w=["rs"], bias=epst[:, 0:1], scale=1.0 / 128)
        p.recip(rs[:], rs[:], r=["rs"], w=["rs"])
        ys = qg % 2
        p.stt("dve", yst[ys][:], t1[:], gsubc[:, 0:1], rs[:], ALU.mult, ALU.mult, r=["t1", "gsubc", "rs"], w=["yst%d" % ys])
        p.dma("sp", yaT_o[:, qg * 512:(qg + 1) * 512], yst[ys][:], r=["yst%d" % ys], semkey="yst%d" % ys)
    p.emit()


def bc(ap, shape, axis):
    return ap.unsqueeze(axis).to_broadcast(list(shape))


def emit_B2(nc, pers, dnT, cwd, identd):
    qTn, kTn, ktok, vtok = pers
    p = Prog(nc)
    ident = p.sb("identB2", [128, 128], BF16)
    onesb = p.sb("onesb", [128, 128], BF16)
    epst = p.sb("epst", [128, 1], F32)
    cw = p.sb("cw", [128, 3, 5], F32)
    xin = [p.sb("xin%d" % i, [128, 2052], F32) for i in range(2)]
    acc = [p.sb("cacc%d" % i, [128, 2048], F32) for i in range(2)]
    sqb = p.sb("sqb", [128, 2048], BF16)
    rin_ = [p.sb("rin%d" % i, [128, 2048], F32) for i in range(2)]
    vtb = p.sb("vtb", [128, 2048], BF16)
    ps = [p.ps("psB2_%d" % i, [128, 512], F32) for i in range(4)]
    ptb = [p.ps("ptB2_%d" % i, [128, 1024], BF16) for i in range(2)]
    p.dma("sp", ident[:], identd, w=["ident"], semkey="ident")
    p.dma("sp", cw[:], cwd, w=["cw"], semkey="cw")
    p.memset("pool", onesb[:], 1.0, w=["onesb"])
    p.memset("pool", epst[:], EPS, w=["epst"])
    tcnt = [0]

    def part1(it):
        tg, X = it // 3, it % 3
        sl = it % 2
        xk, ak = "xin%d" % sl, "cacc%d" % sl
        lo = tg * 2048 - 2
        hi = tg * 2048 + 2050
        if tg == 0:
            p.memset("pool", xin[sl][:, 0:2], 0.0, w=[xk])
            p.dma("sp", xin[sl][:, 2:2052], dnT[X, :, 0:2050], w=[xk + "d"], r=[xk], semkey=xk)
        elif tg == 3:
            p.memset("pool", xin[sl][:, 2050:2052], 0.0, w=[xk])
            p.dma("sp", xin[sl][:, 0:2050], dnT[X, :, lo:S], w=[xk + "d"], r=[xk], semkey=xk)
        else:
            p.memset("pool", xin[sl][:, 0:1], 0.0, w=[xk])
            p.dma("sp", xin[sl][:, :], dnT[X, :, lo:hi], w=[xk + "d"], r=[xk], semkey=xk)
        a = acc[sl]
        p.ts("dve", a[:], xin[sl][:, 0:2048], cw[:, X, 0:1], None, ALU.mult, None, r=[xk + "d", "cw"], w=[ak])
        for j in range(1, 5):
            p.stt("dve", a[:], xin[sl][:, j:j + 2048], cw[:, X, j:j + 1], a[:], ALU.mult, ALU.add, r=[xk + "d", "cw", ak], w=[ak])
        p.actf(a[:], a[:], AF.Silu, r=[ak, xk + "d"], w=[ak, xk])
        if X != 2:
            rin = rin_[sl]
            p.tt("pool", sqb[:], a[:], a[:], ALU.mult, r=[ak], w=["sqb"])
            for blk in range(4):
                p.mm(ps[blk][:, :], onesb[:], sqb[:, blk * 512:(blk + 1) * 512], True, True, r=["onesb", "sqb"], w=["psn%d" % blk])
                p.actf(rin[:, blk * 512:(blk + 1) * 512], ps[blk][:, :], AF.Sqrt, r=["psn%d" % blk, "epst"], w=["rin%d" % sl],
                       bias=epst[:, 0:1])

    def part2(it):
        tg, X = it // 3, it % 3
        sl = it % 2
        ak = "cacc%d" % sl
        a = acc[sl]
        if X == 2:
            p.cp("pool", vtb[:], a[:], r=[ak], w=["vtb"])
            dst, skey = vtok, "vtb"
        else:
            rin = rin_[sl]
            p.recip(rin[:], rin[:], r=["rin%d" % sl], w=["rin%d" % sl])
            tgt = qTn if X == 0 else kTn
            sc = (128.0 ** -0.5) if X == 0 else 1.0
            p.stt("dve", tgt[:, tg * 2048:(tg + 1) * 2048], a[:], sc, rin[:], ALU.mult, ALU.mult, r=[ak, "rin%d" % sl],
                  w=["nT%d_%d" % (X, tg)])
            dst, skey = ktok, "nT1_%d" % tg
        if X >= 1:
            for half in range(2):
                pb = tcnt[0] % 2
                tcnt[0] += 1
                for c in range(8):
                    cc = half * 8 + c
                    if X == 2:
                        src = vtb[:, cc * 128:(cc + 1) * 128]
                    else:
                        src = kTn[:, tg * 2048 + cc * 128:tg * 2048 + (cc + 1) * 128]
                    p.tr(ptb[pb][:, c * 128:(c + 1) * 128], src, ident[:], r=[skey, "ident"], w=["ptb%d" % pb])
                n0 = tg * 16 + half * 8
                p.cp("act", dst[:, n0:n0 + 8, :], ptb[pb][:].rearrange("p (k t) -> p k t", t=128), r=["ptb%d" % pb],
                     w=["tok%d_%d" % (X, n0)])

    NIT = 12
    part1(0)
    for it in range(NIT):
        if it + 1 < NIT:
            part1(it + 1)
        part2(it)
    p.emit()


def emit_B3(nc, pers, o_scr, bgd, masksd, identd):
    qTn, kTn, ktok, vtok = pers
    p = Prog(nc)
    identb = p.sb("identB3", [128, 128], BF16)
    identf = p.sb("identfB3", [128, 128], F32)
    msk = p.sb("msk", [128, 8, 128], F32)
    onesf = p.sb("onesf", [128, 128], F32)
    bg = p.sb("bgt", [128, NKC, 4], F32)
    negb = p.sb("negb", [128, NKC, 2], F32)
    EG = [p.sb("EG%d" % d, [128, NKC], F32) for d in range(2)]
    ER = [p.sb("ER%d" % d, [128, NKC], F32) for d in range(2)]
    EL = [p.sb("EL%d" % d, [128, NKC], F32) for d in range(2)]
    Sst = [p.sb("Sst%d" % d, [128, 128], F32) for d in range(2)]
    vnew = [[p.sb("vnew%d_%d" % (d, i), [128, 128], F32) for i in range(2)] for d in range(2)]
    ost = [[p.sb("ost%d_%d" % (d, i), [128, 128], F32) for i in range(2)] for d in range(2)]
    grp = {}
    for d in range(2):
        for s_ in range(2):
            for nm in ("u", "wT", "qdT", "qkT", "kdec"):
                grp[(nm, d, s_)] = p.sb("%s%d_%d" % (nm, d, s_), [128, 4, 128], F32)
    T3 = {}
    for d in range(2):
        for nm in ("gtri", "DTr", "EGr", "tmpf", "X", "XT", "A", "AT", "P", "PT"):
            T3[(nm, d)] = p.sb("%s_%d" % (nm, d), [128, 4, 128], F32)
        for nm in ("kgb", "Pb16"):
            T3[(nm, d)] = p.sb("%s_%d" % (nm, d), [128, 4, 128], BF16)
    psbD = [[p.ps("psB3_%d_%d" % (d, i), [128, 512], F32) for i in range(3)] for d in range(2)]
    psb = psbD[0]
    pssA = [p.ps("psscA%d" % d, [128, 512], F32) for d in range(2)]

    p.dma("sp", identb[:], identd, w=["identb_"], semkey="ident")
    p.cp("dve", identf[:], identb[:], r=["identb_"], w=["ident"])
    p.dma("sp", msk[:], masksd.rearrange("a p f -> p a f"), w=["msk"], semkey="msk")
    p.dma("sp", bg[:], bgd, w=["bg"], semkey="bg")
    p.memset("pool", onesf[:], 1.0, w=["onesf"])
    for d in range(2):
        p.memset("pool", Sst[d][:], 0.0, w=["S%d" % d])
    p.ts("dve", negb[:], bg[:, :, 0:2], -1.0, None, ALU.mult, None, r=["bg"], w=["negb"])
    for d in range(2):
        g = bg[:, :, 2 + d]
        mGC = msk[:, 0, :] if d == 0 else msk[:, 2, :]
        mGR = msk[:, 3, :] if d == 0 else msk[:, 1, :]
        for i, (lhs, dst, nm) in enumerate(((mGC, EG[d], "EG"), (mGR, ER[d], "ER"), (onesf[:], EL[d], "EL"))):
            p.mm(psbD[d][i][:, 0:NKC], lhs, g, True, True, r=["msk", "onesf", "bg"], w=["ps%d_%d" % (i, d)])
            p.actf(dst[:], psbD[d][i][:, 0:NKC], AF.Exp, r=["ps%d_%d" % (i, d)], w=["%s%d" % (nm, d)])

    sh3 = [128, 4, 128]

    def fl(t):
        return t[:].rearrange("p a b -> p (a b)")

    def d3_stages(d, G):
        s_ = G % 2
        n0 = 4 * G if d == 0 else 60 - 4 * G
        MI = msk[:, 0, :] if d == 0 else msk[:, 2, :]
        MS = msk[:, 1, :] if d == 0 else msk[:, 3, :]
        MG = msk[:, 3, :] if d == 0 else msk[:, 1, :]
        u, wT, qdT, qkT, kdec = [grp[(nm, d, s_)] for nm in ("u", "wT", "qdT", "qkT", "kdec")]
        gk = lambda nm: "%s%d_%d" % (nm, d, s_)
        D = "_%d" % d
        psb = psbD[d]
        ptb = psbD[d][1]
        gtri, DTr, EGr, tmpf, X, XT, A, AT, P_, PT, kgb, Pb16 = [T3[(nm, d)] for nm in
            ("gtri", "DTr", "EGr", "tmpf", "X", "XT", "A", "AT", "P", "PT", "kgb", "Pb16")]
        Y, Xc, XcT, Y2 = tmpf, DTr, EGr, gtri
        kY, kXc, kXcT, kY2 = "tmpf" + D, "DTr" + D, "EGr" + D, "gtri" + D
        kX, kXT, kA, kAT, kP, kPT = ["%s%s" % (nm, D) for nm in ("X", "XT", "A", "AT", "P", "PT")]
        st = []

        def mm4(bank, bkey, lhs, rhs, r):
            for j in range(4):
                p.mm(psb[bank][:, j * 128:(j + 1) * 128], lhs[:, j, :], rhs[:, j, :], True, True, r=r, w=[bkey])

        def s1():
            p.tt("pool", gtri[:], bc(MI, sh3, 1), bc(bg[:, n0:n0 + 4, 2 + d], sh3, 2), ALU.mult, r=["msk", "bg"], w=[kY2])
            p.mm(psb[0][:, :], MG, fl(gtri), True, True, r=["msk", kY2], w=["ps0" + D])
            p.mm(psb[1][:, :], onesf[:], fl(gtri), True, True, r=["onesf", kY2], w=["ps1" + D])
            p.actf(fl(DTr), psb[0][:, :], AF.Exp, r=["ps0" + D], w=[kXc])
            p.actf(fl(EGr), psb[1][:, :], AF.Exp, r=["ps1" + D], w=[kXcT])
        st.append(s1)

        def s2():
            for j in range(4):
                c0 = (n0 + j) * 128
                p.mm(psb[2][:, j * 128:(j + 1) * 128], kTn[:, c0:c0 + 128], kTn[:, c0:c0 + 128], True, True, r=[], w=["ps2" + D])
            for j in range(4):
                c0 = (n0 + j) * 128
                p.mm(psb[0][:, j * 128:(j + 1) * 128], kTn[:, c0:c0 + 128], qTn[:, c0:c0 + 128], True, True, r=[], w=["ps0" + D])
            ps2v = psb[2][:, :].rearrange("p (a b) -> p a b", a=4)
            ps0v = psb[0][:, :].rearrange("p (a b) -> p a b", a=4)
            p.tt("dve", tmpf[:], ps2v, DTr[:], ALU.mult, r=["ps2" + D, kXc], w=[kY])
            p.tt("dve", qkT[:], ps0v, DTr[:], ALU.mult, r=["ps0" + D, kXc], w=[gk("qkT")])
            p.tt("pool", tmpf[:], tmpf[:], bc(negb[:, n0:n0 + 4, d], sh3, 2), ALU.mult, r=[kY, "negb"], w=[kY])
            p.tt("pool", X[:], tmpf[:], bc(MS, sh3, 1), ALU.mult, r=[kY, "msk"], w=[kX])
            p.tt("pool", qkT[:], qkT[:], bc(MI, sh3, 1), ALU.mult, r=[gk("qkT"), "msk"], w=[gk("qkT")])
            p.tt("pool", fl(qdT), qTn[:, n0 * 128:(n0 + 4) * 128], fl(EGr), ALU.mult, r=[kXcT], w=[gk("qdT")])
            p.tt("pool", kgb[:], ktok[:, n0:n0 + 4, :], bc(EG[d][:, n0:n0 + 4], sh3, 2), ALU.mult, r=["EG%d" % d], w=["kgb" + D])
            p.tt("pool", kdec[:], ktok[:, n0:n0 + 4, :], bc(ER[d][:, n0:n0 + 4], sh3, 2), ALU.mult, r=["ER%d" % d], w=[gk("kdec")])
            for j in range(4):
                p.tr(ptb[:, j * 128:(j + 1) * 128], X[:, j, :], identf[:], r=[kX, "ident"], w=["ps1" + D])
            p.cp("act", fl(XT), ptb[:, :], r=["ps1" + D], w=[kXT])
        st.append(s2)

        def s3():
            BD = bc(msk[:, 4, :], sh3, 1)
            p.tt("pool", A[:], X[:], BD, ALU.mult, r=[kX, "msk"], w=[kA])
            p.tt("pool", AT[:], XT[:], BD, ALU.mult, r=[kXT, "msk"], w=[kAT])
            p.tt("pool", P_[:], A[:], bc(identf[:], sh3, 1), ALU.add, r=[kA, "ident"], w=[kP])
            p.tt("pool", PT[:], AT[:], bc(identf[:], sh3, 1), ALU.add, r=[kAT, "ident"], w=[kPT])
        st.append(s3)

        def base(k):
            def f():
                mm4(0, "ps0" + D, AT, A, [kA, kAT])
                mm4(1, "ps1" + D, A, AT, [kA, kAT])
                p.cp("dve", fl(A), psb[0][:, :], r=["ps0" + D], w=[kA])
                p.cp("act", fl(AT), psb[1][:, :], r=["ps1" + D], w=[kAT])
                mm4(2, "ps2" + D, AT, P_, [kAT, kP])
                mm4(0, "ps0" + D, A, PT, [kA, kPT])
                p.tt("dve", fl(P_), fl(P_), psb[2][:, :], ALU.add, r=[kP, "ps2" + D], w=[kP])
                p.tt("dve", fl(PT), fl(PT), psb[0][:, :], ALU.add, r=[kPT, "ps0" + D], w=[kPT])
            return f
        for k in range(1, 4):
            st.append(base(k))

        def merge(mi):
            def f():
                last = (mi == 2)
                CM = bc(msk[:, 5 + mi, :], sh3, 1)
                p.tt("pool", Xc[:], X[:], CM, ALU.mult, r=[kX, "msk"], w=[kXc])
                p.tt("pool", XcT[:], XT[:], CM, ALU.mult, r=[kXT, "msk"], w=[kXcT])
                mm4(0, "ps0" + D, XcT, P_, [kXcT, kP])
                p.cp("act", fl(Y), psb[0][:, :], r=["ps0" + D], w=[kY])
                if not last:
                    mm4(1, "ps1" + D, Xc, PT, [kXc, kPT])
                    p.cp("dve", fl(Y2), psb[1][:, :], r=["ps1" + D], w=[kY2])
                mm4(2, "ps2" + D, PT, Y, [kPT, kY])
                if not last:
                    mm4(0, "ps0" + D, P_, Y2, [kP, kY2])
                p.tt("dve", fl(P_), fl(P_), psb[2][:, :], ALU.add, r=[kP, "ps2" + D], w=[kP])
                if not last:
                    p.tt("dve", fl(PT), fl(PT), psb[0][:, :], ALU.add, r=[kPT, "ps0" + D], w=[kPT])
            return f
        for mi in range(3):
            st.append(merge(mi))

        def s_fin():
            p.cp("pool", Pb16[:], P_[:], r=[kP], w=["Pb16" + D])
            for j in range(4):
                p.mm(psb[2][:, j * 128:(j + 1) * 128], Pb16[:, j, :], vtok[:, n0 + j, :], True, True, r=["Pb16" + D], w=["ps2" + D])
                p.mm(psb[1][:, j * 128:(j + 1) * 128], kgb[:, j, :], Pb16[:, j, :], True, True, r=["Pb16" + D, "kgb" + D], w=["ps1" + D])
            p.tt("dve", u[:], psb[2][:, :].rearrange("p (a b) -> p a b", a=4), bc(bg[:, n0:n0 + 4, d], sh3, 2), ALU.mult,
                 r=["ps2" + D, "bg"], w=[gk("u")])
            p.cp("act", fl(wT), psb[1][:, :], r=["ps1" + D], w=[gk("wT")])
        st.append(s_fin)
        return st

    vcnt = [0, 0]

    def scan_step(d, G, j):
        s_ = G % 2
        n0 = 4 * G if d == 0 else 60 - 4 * G
        n = n0 + j
        u, wT, qdT, qkT, kdec = [grp[(nm, d, s_)] for nm in ("u", "wT", "qdT", "qkT", "kdec")]
        gk = lambda nm: "%s%d_%d" % (nm, d, s_)
        vs = vcnt[d] % 2
        vcnt[d] += 1
        vn = vnew[d][vs]
        vk = "vnew%d_%d" % (d, vs)
        os_ = ost[d][vs]
        ok = "ost%d_%d" % (d, vs)
        ps1, ps3, ps2 = pssA[d][:, 0:128], pssA[d][:, 128:256], pssA[d][:, 256:384]
        kS = "S%d" % d
        kA = "scA%d" % d
        p.mm(ps1, wT[:, j, :], Sst[d][:], True, True, r=[gk("wT"), kS], w=[kA])
        p.stt("dve", vn[:], ps1, negb[:, n, d:d + 1], u[:, j, :], ALU.mult, ALU.add, r=[kA, "negb", gk("u")], w=[vk])
        p.mm(ps3, kdec[:, j, :], vn[:], True, True, r=[gk("kdec"), vk], w=[kA])
        p.mm(ps2, qdT[:, j, :], Sst[d][:], True, False, r=[gk("qdT"), kS], w=[kA], skip=True)
        p.mm(ps2, qkT[:, j, :], vn[:], False, True, r=[gk("qkT"), vk], w=[kA], skip=True)
        p.stt("dve", Sst[d][:], Sst[d][:], EL[d][:, n:n + 1], ps3, ALU.mult, ALU.add, r=[kS, "EL%d" % d, kA], w=[kS])
        p.cp("dve", os_[:], ps2, r=[kA], w=[ok])
        p.dma("sp", o_scr[d, n, :, :], os_[:], r=[ok], semkey=ok)

    NG = DBG_NG
    lists = [d3_stages(0, 0), d3_stages(1, 0)]
    while lists[0] or lists[1]:
        for L in lists:
            if L:
                L.pop(0)()
    for G in range(NG):
        nxt = [d3_stages(0, G + 1), d3_stages(1, G + 1)] if G + 1 < NG else [[], []]
        steps = []
        for j in range(4):
            steps.append((0, G, j))
            steps.append((1, G, 3 - j))
        while steps or nxt[0] or nxt[1]:
            if nxt[0]:
                nxt[0].pop(0)()
            if steps:
                scan_step(*steps.pop(0))
            if nxt[1]:
                nxt[1].pop(0)()
            if steps:
                scan_step(*steps.pop(0))
    p.emit()


def emit_B4(nc, o_scr, zd, ogd, identd, ybT_o):
    p = Prog(nc)
    ident = p.sb("identB4", [128, 128], BF16)
    og = p.sb("og", [128, 128], F32)
    zt = p.sb("zt", [128, NKC, 128], F32)
    sq = p.sb("sqB4", [128, 4, 128], F32)
    of_ = [p.sb("ofB4_%d" % i, [128, 4, 128], F32) for i in range(2)]
    ob_ = [p.sb("obB4_%d" % i, [128, 4, 128], F32) for i in range(2)]
    ss = p.sb("ssB4", [128, 4], F32)
    yb = p.sb("ybB4", [128, 4, 128], BF16)
    yst = [p.sb("ystB4_%d" % i, [128, 512], BF16) for i in range(2)]
    ptb = p.ps("ptB4", [128, 1024], BF16)
    p.dma("sp", ident[:], identd, w=["ident"], semkey="ident")
    p.dma("sp", og[:], ogd.partition_broadcast(128), w=["og"], semkey="og")
    for c4 in range(4):
        p.dma("sp", zt[:, c4 * 16:(c4 + 1) * 16, :], zd[c4 * 2048:(c4 + 1) * 2048, :].rearrange("(k p) d -> p k d", p=128),
              w=["zt%d" % c4], semkey="zt%d" % c4)
    for c4 in range(4):
        v = zt[:, c4 * 16:(c4 + 1) * 16, :].rearrange("p a b -> p (a b)")
        p.actf(v, v, AF.Silu, r=["zt%d" % c4], w=["zt%d" % c4])
    for G in range(16):
        n0 = 4 * G
        gs = G % 2
        p.dma("sp", of_[gs][:], o_scr[0, n0:n0 + 4, :, :].rearrange("n p e -> p n e"), w=["of%d" % gs], semkey="of%d" % gs)
        p.dma("sp", ob_[gs][:], o_scr[1, n0:n0 + 4, :, :].rearrange("n p e -> p n e"), w=["ob%d" % gs], semkey="ob%d" % gs)
        p.tt("pool", of_[gs][:], of_[gs][:], ob_[gs][:], ALU.add, r=["of%d" % gs, "ob%d" % gs], w=["of%d" % gs])
        o = of_[gs][:]
        p.tt("pool", sq[:], o, o, ALU.mult, r=["of%d" % gs], w=["sq"])
        p.red(ss[:], sq[:], ALU.add, r=["sq"], w=["ss"])
        p.ts("dve", ss[:], ss[:], 1.0 / 128, EPS, ALU.mult, ALU.add, r=["ss"], w=["ss"])
        p.actf(ss[:], ss[:], AF.Sqrt, r=["ss"], w=["ss"])
        p.recip(ss[:], ss[:], r=["ss"], w=["ss"])
        p.tt("dve", sq[:], o, bc(ss[:], [128, 4, 128], 2), ALU.mult, r=["ss", "sq", "of%d" % gs], w=["sq"])
        p.tt("pool", sq[:], sq[:], bc(og[:], [128, 4, 128], 1), ALU.mult, r=["sq", "og"], w=["sq"])
        p.tt("pool", yb[:], sq[:], zt[:, n0:n0 + 4, :], ALU.mult, r=["sq", "zt%d" % (G // 4)], w=["yb"])
        for j in range(4):
            p.tr(ptb[:, j * 128:(j + 1) * 128], yb[:, j, :], ident[:], r=["yb", "ident"], w=["ptb"])
        ys = G % 2
        p.cp("dve", yst[ys][:], ptb[:, 0:512], r=["ptb"], w=["yst%d" % ys])
        p.dma("sp", ybT_o[:, n0 * 128:(n0 + 4) * 128], yst[ys][:], r=["yst%d" % ys], semkey="ystB4_%d" % ys)
    p.emit()


def build_B(parts=("B1", "B2", "B3", "B4")):
    nc = bass.Bass("TRN2", target_bir_lowering=False)
    qkT = din(nc, "qkT", [2, 128, S], BF16)
    vd = din(nc, "v", [S, 128], BF16)
    qkg = din(nc, "qkg", [1, 128], F32)
    lqd = din(nc, "lq_d", [1, 256], F32)
    lamd = din(nc, "lamc_d", [1, 2], F32)
    sub = din(nc, "subln", [1, 128], F32)
    identd = din(nc, "identb", [128, 128], BF16)
    dnT = din(nc, "dnT", [3, 128, S], F32)
    cwd = din(nc, "cw_d", [128, 3, 5], F32)
    bgd = din(nc, "bg_d", [128, NKC, 4], F32)
    masksd = din(nc, "masks", [8, 128, 128], F32)
    zd = din(nc, "z", [S, 128], F32)
    ogd = din(nc, "og_d", [1, 128], F32)
    yaT = dout(nc, "yaT", [128, S], BF16)
    ybT = dout(nc, "ybT", [128, S], BF16)
    if "B1" in parts:
        emit_B1(nc, qkT, vd, qkg, lqd, lamd, sub, identd, yaT)
    with ExitStack() as outer:
        pers = [outer.enter_context(nc.sbuf_tensor(nm, shp, BF16)) for nm, shp in
                (("qTn", [128, S]), ("kTn", [128, S]), ("ktok", [128, NKC, 128]), ("vtok", [128, NKC, 128]))]
        o_scr = nc.dram_tensor("o_scr", [2, NKC, 128, 128], F32).ap()
        if "B2" in parts:
            emit_B2(nc, pers, dnT, cwd, identd)
        if "B3" in parts:
            emit_B3(nc, pers, o_scr, bgd, masksd, identd)
        if DBG_DUMP:
            p = Prog(nc)
            for i, (nm, shp, dt_, t_) in enumerate((("dq", [128, S], BF16, pers[0]), ("dk", [128, S], BF16, pers[1]),
                                            ("dkt", [128, NKC, 128], BF16, pers[2]), ("dvt", [128, NKC, 128], BF16, pers[3]))):
                o_ = dout(nc, nm, shp, dt_)
                p.dma("sp", o_, t_[:], semkey="dump%d" % i)
            p.emit()
        if "B4" in parts:
            emit_B4(nc, o_scr, zd, ogd, identd, ybT)
    return nc


def _masks():
    i = np.arange(128)
    pp, ff = i[:, None], i[None, :]
    bd = lambda b_: (pp // b_) == (ff // b_)
    return np.stack([pp <= ff, pp < ff, pp >= ff, pp > ff, bd(16), bd(32) & ~bd(16), bd(64) & ~bd(32), ~bd(64)]).astype(np.float32)


def run_B(inp, l, resA):
    nc = _get("B", build_B)
    identb = np.eye(128, dtype=np.float32).astype(NPBF)
    masks = _masks()
    li = 0.8 - 0.6 * math.exp(-0.3 * l)
    lamc = np.array([[li, 1.0 - li]], np.float32)
    cw = inp["dn_conv"][l]
    maps = []
    for b in range(NB):
        rs = resA[b * 4:(b + 1) * 4]
        bgc = np.concatenate([r["bg"] for r in rs], axis=1)
        for h in range(4):
            qT = np.concatenate([r["qkT"][h] for r in rs], axis=1)
            kT = np.concatenate([r["qkT"][4 + h] for r in rs], axis=1)
            maps.append({
                "qkT": np.ascontiguousarray(np.stack([qT, kT])),
                "v": np.ascontiguousarray(np.concatenate([r["v"][:, h * 128:(h + 1) * 128] for r in rs], axis=0)),
                "qkg": np.ascontiguousarray(inp["qk_norm_gain"][l].reshape(1, 128)),
                "lq_d": np.ascontiguousarray(inp["diff_lambda"][l].reshape(1, 256)),
                "lamc_d": lamc,
                "subln": np.ascontiguousarray(inp["diff_subln_gain"][l][None, :]),
                "identb": identb,
                "dnT": np.ascontiguousarray(np.stack([np.concatenate([r["dnT"][X * 4 + h] for r in rs], axis=1) for X in range(3)])),
                "cw_d": np.ascontiguousarray(np.stack([cw[:, X * 512 + h * 128:X * 512 + (h + 1) * 128].T for X in range(3)], axis=1)),
                "bg_d": np.ascontiguousarray(bgc[:, :, [h, 4 + h, 8 + h, 12 + h]]),
                "masks": masks,
                "z": np.ascontiguousarray(np.concatenate([r["z"][:, h * 128:(h + 1) * 128] for r in rs], axis=0)),
                "og_d": np.ascontiguousarray(inp["dn_out_gain"][l][None, :]),
            })
    res = run_bass_kernel_spmd(nc, maps, core_ids=list(range(8)))
    return res.results


def kernel(**inp):
    inp = {k: np.asarray(v) for k, v in inp.items()}
    x = inp["x"]
    xsh = [x[i // 4, (i % 4) * NTOK:(i % 4 + 1) * NTOK] for i in range(8)]
    for l in range(2):
        resA = run_A(inp, l, xsh)
        resB = run_B(inp, l, resA)
        yaT, ybT, gT = [], [], []
        for i in range(8):
            b, t0 = i // 4, (i % 4) * NTOK
            yaT.append(np.stack([resB[b * 4 + h]["yaT"][:, t0:t0 + NTOK] for h in range(4)]))
            ybT.append(np.stack([resB[b * 4 + h]["ybT"][:, t0:t0 + NTOK] for h in range(4)]))
            gT.append(resA[i]["gT"])
        xsh = run_C(inp, l, xsh, yaT, ybT, gT)
    out = np.empty((NB, S, D), np.float32)
    for i in range(8):
        out[i // 4, (i % 4) * NTOK:(i % 4 + 1) * NTOK] = xsh[i]
    return out
```

```python
from contextlib import ExitStack
import math
import numpy as np
import ml_dtypes
import concourse.bass as bass
import concourse.mybir as mybir
from concourse.bass_utils import run_bass_kernel_spmd

F32 = mybir.dt.float32
BF16 = mybir.dt.bfloat16
ALU = mybir.AluOpType
AF = mybir.ActivationFunctionType
AX = mybir.AxisListType
NPBF = ml_dtypes.bfloat16

ENGS = ("pe", "act", "dve", "pool", "sp")

D = 1024
S = 8192
NB = 2
NTOK = 2048
NT = NTOK // 128
EPS = 1e-6
INC = 5648


class Op:
    __slots__ = ("eng", "fn", "deps", "is_dma", "semkey", "cum", "sig", "waits", "has_dep")

    def __init__(self, eng, fn, is_dma=False, semkey=None):
        self.eng = eng
        self.fn = fn
        self.deps = []
        self.is_dma = is_dma
        self.semkey = semkey
        self.cum = 0
        self.sig = 0
        self.has_dep = False
        self.waits = []


class Prog:
    def __init__(self, nc):
        self.nc = nc
        self.es = ExitStack()
        self.ops = {e: [] for e in ENGS}
        self.lastw = {}
        self.readers = {}
        self.dma_cum = {}

    def sb(self, name, shape, dtype):
        return self.es.enter_context(self.nc.sbuf_tensor(name, list(shape), dtype))

    def ps(self, name, shape, dtype=F32):
        return self.es.enter_context(self.nc.psum_tensor(name, list(shape), dtype))

    def _rec(self, op, r, w):
        deps = []
        for k in r:
            lw = self.lastw.get(k)
            if lw is not None:
                deps.append(lw)
        for k in w:
            lw = self.lastw.get(k)
            rd = self.readers.get(k, [])
            if rd:
                deps.extend(rd)
            elif lw is not None:
                if not (op.is_dma and lw.is_dma and lw.semkey == op.semkey):
                    deps.append(lw)
        for k in r:
            self.readers.setdefault(k, []).append(op)
        for k in w:
            self.lastw[k] = op
            self.readers[k] = []
        seen = set()
        for d in deps:
            if d is op or id(d) in seen:
                continue
            seen.add(id(d))
            if d.eng == "pe" and op.eng == "pe" and not d.is_dma and not op.is_dma:
                continue
            op.deps.append(d)
            d.has_dep = True
        self.ops[op.eng].append(op)
        return op

    def op(self, eng, fn, r=(), w=()):
        return self._rec(Op(eng, fn), r, w)

    def dma(self, q, out, in_, r=(), w=(), semkey=None, **kw):
        assert semkey is not None
        o = Op(q, lambda e: e.dma_start(out=out, in_=in_, **kw), is_dma=True, semkey=semkey)
        self._rec(o, r, w)
        self.dma_cum[semkey] = self.dma_cum.get(semkey, 0) + 16
        o.cum = self.dma_cum[semkey]
        return o

    def mm(self, out, lhsT, rhs, start, stop, r, w, skip=False):
        if skip:
            return self.op("pe", lambda e: e.matmul(out, lhsT, rhs, start=start, stop=stop, skip_group_check=True), r, w)
        return self.op("pe", lambda e: e.matmul(out, lhsT, rhs, start=start, stop=stop), r, w)

    def tr(self, out, in_, ident, r, w):
        return self.op("pe", lambda e: e.transpose(out, in_, ident), r, w)

    def actf(self, out, in_, func, r, w, bias=None, scale=None, accum=None):
        kw = {}
        if bias is not None:
            kw["bias"] = bias
        if scale is not None:
            kw["scale"] = scale
        if accum is not None:
            kw["accum_out"] = accum
        return self.op("act", lambda e: e.activation(out, in_, func, **kw), r, w)

    def tt(self, eng, out, a, b, op, r, w):
        return self.op(eng, lambda e: e.tensor_tensor(out, a, b, op), r, w)

    def ts(self, eng, out, a, s1, s2, op0, op1, r, w):
        if op1 is None:
            return self.op(eng, lambda e: e.tensor_scalar(out, a, s1, None, op0), r, w)
        return self.op(eng, lambda e: e.tensor_scalar(out, a, s1, s2, op0, op1), r, w)

    def stt(self, eng, out, in0, scalar, in1, op0, op1, r, w):
        return self.op(eng, lambda e: e.scalar_tensor_tensor(out, in0, scalar, in1, op0, op1), r, w)

    def cp(self, eng, out, in_, r, w):
        if eng == "act":
            return self.op("act", lambda e: e.copy(out, in_), r, w)
        return self.op(eng, lambda e: e.tensor_copy(out, in_), r, w)

    def red(self, out, in_, op, r, w):
        return self.op("dve", lambda e: e.tensor_reduce(out, in_, AX.X, op), r, w)

    def recip(self, out, in_, r, w):
        return self.op("dve", lambda e: e.reciprocal(out, in_), r, w)

    def memset(self, eng, out, val, w):
        return self.op(eng, lambda e: e.memset(out, val), (), w)

    def emit(self):
        nc = self.nc
        es = self.es
        esem = {e: es.enter_context(nc.semaphore("s_" + e)) for e in ENGS}
        dsem = {k: es.enter_context(nc.semaphore("d_%d" % i)) for i, k in enumerate(self.dma_cum)}
        for e in ENGS:
            c = 0
            for o in self.ops[e]:
                if not o.is_dma and o.has_dep:
                    c += 1
                    o.sig = c
        for e in ENGS:
            waited = {}
            for o in self.ops[e]:
                need = {}
                for d in o.deps:
                    if d.is_dma:
                        key = ("d", d.semkey)
                        val = d.cum
                    else:
                        key = ("e", d.eng)
                        val = d.sig
                    if val > need.get(key, 0):
                        need[key] = val
                for key, val in need.items():
                    if waited.get(key, 0) >= val:
                        continue
                    waited[key] = val
                    sem = dsem[key[1]] if key[0] == "d" else esem[key[1]]
                    o.waits.append((sem, val))
        final = [(dsem[k], v) for k, v in self.dma_cum.items()]
        block = es.enter_context(nc.Block())

        def run(eng, name):
            for o in self.ops[name]:
                for sem, val in o.waits:
                    eng.wait_ge(sem, val)
                ins = o.fn(eng)
                if o.is_dma:
                    ins.then_inc(dsem[o.semkey], 16)
                elif o.has_dep:
                    ins.then_inc(esem[name], 1)
            if name == "sp":
                for sem, val in final:
                    eng.wait_ge(sem, val)

        @block.tensor
        def _(eng):
            run(eng, "pe")

        @block.scalar
        def _(eng):
            run(eng, "act")

        @block.vector
        def _(eng):
            run(eng, "dve")

        @block.gpsimd
        def _(eng):
            run(eng, "pool")

        @block.sync
        def _(eng):
            run(eng, "sp")

        es.close()


def din(nc, name, shape, dt):
    return nc.dram_tensor(name, list(shape), dt, kind="ExternalInput").ap()


def dout(nc, name, shape, dt):
    return nc.dram_tensor(name, list(shape), dt, kind="ExternalOutput").ap()


def emit_mod(p, nc, ada_w, ada_b, cpk, col0, ncols, stage, psb, modbc):
    cact = p.sb("cact", [128, 8], F32)
    crep = p.sb("crep", [128, 8, 128], F32)
    p.dma("sp", cact[:], cpk, w=["cact"], semkey="cact")
    p.dma("sp", modbc[:, 0:ncols], ada_b[:, col0:col0 + ncols].partition_broadcast(128), w=["modbc"], semkey="modbc")
    p.actf(cact[:], cact[:], AF.Silu, r=["cact"], w=["cact"])
    p.cp("dve", crep[:], cact[:].unsqueeze(2).to_broadcast([128, 8, 128]), r=["cact"], w=["crep"])
    npass = (ncols + 2047) // 2048
    for ps_ in range(npass):
        c0 = ps_ * 2048
        cw = min(2048, ncols - c0)
        for kc in range(8):
            sl = kc % 2
            p.dma("sp", stage[sl][:, 0:cw], ada_w[kc * 128:(kc + 1) * 128, col0 + c0:col0 + c0 + cw],
                  w=["stg%d" % sl], semkey="stg%d" % sl)
            for j in range(cw // 512):
                p.mm(psb[j][:, :], crep[:, kc, :], stage[sl][:, j * 512:(j + 1) * 512], kc == 0, kc == 7,
                     r=["crep", "stg%d" % sl], w=["psb%d" % j])
        for j in range(cw // 512):
            cs = c0 + j * 512
            p.tt("dve", modbc[:, cs:cs + 512], modbc[:, cs:cs + 512], psb[j][:, :], ALU.add,
                 r=["psb%d" % j, "modbc"], w=["modbc"])


def emit_rmsnorm_T(p, xt, xkey, G, gkey, SH, shkey, junk, ssk, hb, ptb, ptbkey, hT, t, ident, tag):
    ss = ssk
    p.actf(junk[:], xt, AF.Square, r=[xkey], w=["junk" + tag, "ss" + tag], accum=ss[:, 0:1])
    p.ts("dve", ss[:, 0:1], ss[:, 0:1], 1.0 / D, EPS, ALU.mult, ALU.add, r=["ss" + tag], w=["ss" + tag])
    p.actf(ss[:, 0:1], ss[:, 0:1], AF.Sqrt, r=["ss" + tag], w=["ss" + tag])
    p.recip(ss[:, 0:1], ss[:, 0:1], r=["ss" + tag], w=["ss" + tag])
    p.stt("dve", junk[:], xt, ss[:, 0:1], G, ALU.mult, ALU.mult, r=[xkey, "ss" + tag, gkey], w=["junk" + tag])
    p.tt("dve", hb[:], junk[:], SH, ALU.add, r=["junk" + tag, shkey], w=["hb" + tag])
    for kc in range(8):
        p.tr(ptb[:, kc * 128:(kc + 1) * 128], hb[:, kc * 128:(kc + 1) * 128], ident[:], r=["hb" + tag, "ident"], w=[ptbkey])
    p.cp("act", hT[:, :, t * 128:(t + 1) * 128], ptb[:].rearrange("p (k t) -> p k t", t=128), r=[ptbkey], w=["hT%d" % t])


def emit_rms_a(p, xt, xkey, G, gkey, SH, shkey, junk, ss, hb, hbkey, tag):
    p.actf(junk[:], xt, AF.Square, r=[xkey], w=["junk" + tag, "ss" + tag], accum=ss[:, 0:1])
    p.ts("dve", ss[:, 0:1], ss[:, 0:1], 1.0 / D, EPS, ALU.mult, ALU.add, r=["ss" + tag], w=["ss" + tag])
    p.actf(ss[:, 0:1], ss[:, 0:1], AF.Sqrt, r=["ss" + tag], w=["ss" + tag])
    p.recip(ss[:, 0:1], ss[:, 0:1], r=["ss" + tag], w=["ss" + tag])
    p.stt("dve", junk[:], xt, ss[:, 0:1], G, ALU.mult, ALU.mult, r=[xkey, "ss" + tag, gkey], w=["junk" + tag])
    p.tt("dve", hb[:], junk[:], SH, ALU.add, r=["junk" + tag, shkey], w=[hbkey])


def emit_rms_b(p, hb, hbkey, ptb, ptbkey, hT, t, ident):
    for kc in range(8):
        p.tr(ptb[:, kc * 128:(kc + 1) * 128], hb[:, kc * 128:(kc + 1) * 128], ident[:], r=[hbkey, "ident"], w=[ptbkey])
    p.cp("act", hT[:, :, t * 128:(t + 1) * 128], ptb[:].rearrange("p (k t) -> p k t", t=128), r=[ptbkey], w=["hT%d" % t])


def build_A():
    nc = bass.Bass("TRN2", target_bir_lowering=False)
    x = din(nc, "x", [NTOK, D], F32)
    cpk = din(nc, "cpk", [128, 8], F32)
    ada_w = din(nc, "ada_w", [D, 6 * D], F32)
    ada_b = din(nc, "ada_b", [1, 6 * D], F32)
    g1 = din(nc, "g1", [1, D], F32)
    w_in = din(nc, "w_in", [D, INC], F32)
    qkg = din(nc, "qkg", [1, 128], F32)
    alog = din(nc, "alog", [1, 8], F32)
    dtb = din(nc, "dtb", [1, 8], F32)
    cosd = din(nc, "cos", [NTOK, 64], F32)
    sind = din(nc, "sin", [NTOK, 64], F32)
    identd = din(nc, "identb", [128, 128], BF16)
    qkT_o = dout(nc, "qkT", [8, 128, NTOK], BF16)
    v_o = dout(nc, "v", [NTOK, 512], BF16)
    dnT_o = dout(nc, "dnT", [12, 128, NTOK], F32)
    z_o = dout(nc, "z", [NTOK, 512], F32)
    bg_o = dout(nc, "bg", [128, NT, 16], F32)
    gT_o = dout(nc, "gT", [16, 128, NTOK], BF16)

    p = Prog(nc)
    ident = p.sb("ident", [128, 128], BF16)
    modbc = p.sb("modbc", [128, 2048], F32)
    G1 = p.sb("G1", [128, D], F32)
    stage = [p.sb("stg%d" % i, [128, 2064], F32) for i in range(2)]
    wb = p.sb("wb", [128, 8, 2064], BF16)
    hT = p.sb("hT", [128, 8, NTOK], BF16)
    xs = [p.sb("xs%d" % i, [128, D], F32) for i in range(2)]
    junk = p.sb("junk", [128, D], F32)
    ssA = p.sb("ssA", [128, 1], F32)
    hb2 = [p.sb("hb%d" % i, [128, D], BF16) for i in range(2)]
    gq = p.sb("gq", [128, 128], F32)
    cst = [p.sb("cst%d" % i, [128, 64], F32) for i in range(2)]
    snt = [p.sb("snt%d" % i, [128, 64], F32) for i in range(2)]
    alb = p.sb("alb", [128, 8], F32)
    dtbb = p.sb("dtbb", [128, 8], F32)
    sq_ = [[p.sb("sq%d_%d" % (pr, i), [128, 512], F32) for i in range(2)] for pr in range(2)]
    t1_ = [[p.sb("t1_%d_%d" % (pr, i), [128, 512], F32) for i in range(2)] for pr in range(2)]
    t2_ = [[p.sb("t2_%d_%d" % (pr, i), [128, 512], F32) for i in range(2)] for pr in range(2)]
    qr_ = [[p.sb("qr%d_%d" % (pr, i), [128, 512], BF16) for i in range(2)] for pr in range(2)]
    ss16_ = [p.sb("ss16_%d" % pr, [128, 16], F32) for pr in range(2)]
    vb = [p.sb("vb%d" % i, [128, 512], BF16) for i in range(2)]
    zf = [p.sb("zf%d" % i, [128, 512], F32) for i in range(2)]
    baraw = p.sb("baraw", [128, NT, 16], F32)
    bgo = p.sb("bgo", [128, NT, 16], F32)
    bat = [p.sb("bat%d" % i, [128, NT, 8], F32) for i in range(3)]
    qkTs = [p.sb("qkTs%d" % i, [128, 8, 512], BF16) for i in range(2)]
    dst = [p.sb("dst%d" % i, [128, 512], F32) for i in range(3)]
    gst = [p.sb("gst%d" % i, [128, 512], BF16) for i in range(3)]
    psb = [p.ps("psb%d" % i, [128, 512], F32) for i in range(6)]
    ptb = [p.ps("ptb%d" % i, [128, 1024], BF16) for i in range(2)]

    p.dma("sp", ident[:], identd, w=["ident"], semkey="ident")
    p.dma("sp", G1[:], g1.partition_broadcast(128), w=["GA"], semkey="G1")
    p.dma("sp", gq[:], qkg.partition_broadcast(128), w=["gq"], semkey="gq")
    p.dma("sp", alb[:], alog.partition_broadcast(128), w=["alb"], semkey="alb")
    p.dma("sp", dtbb[:], dtb.partition_broadcast(128), w=["dtbb"], semkey="dtbb")
    emit_mod(p, nc, ada_w, ada_b, cpk, 0, 2048, stage, psb, modbc)
    p.stt("dve", G1[:], modbc[:, 1024:2048], 1.0, G1[:], ALU.add, ALU.mult, r=["modbc", "GA"], w=["GA"])
    p.actf(alb[:], alb[:], AF.Exp, r=["alb"], w=["alb"])
    p.ts("dve", alb[:], alb[:], -1.0, None, ALU.mult, None, r=["alb"], w=["alb"])

    def load_w(colranges, ncols):
        for kc in range(8):
            sl = kc % 2
            o = 0
            for (a, b) in colranges:
                p.dma("sp", stage[sl][:, o:o + (b - a)], w_in[kc * 128:(kc + 1) * 128, a:b],
                      w=["stg%d" % sl], semkey="stg%d" % sl)
                o += b - a
            p.cp("act", wb[:, kc, 0:ncols], stage[sl][:, 0:ncols], r=["stg%d" % sl], w=["wb%d" % kc])

    def load_x(t):
        p.dma("sp", xs[t % 2][:], x[t * 128:(t + 1) * 128, :], w=["xs%d" % (t % 2)], semkey="xs%d" % (t % 2))

    load_x(0)
    load_x(1)
    load_w([(0, 1536), (3072, 3600)], 2064)

    def rms_a(t):
        emit_rms_a(p, xs[t % 2][:], "xs%d" % (t % 2), G1[:], "GA", modbc[:, 0:1024], "modbc", junk, ssA, hb2[t % 2], "hb%d" % (t % 2), "A")
        if t + 2 < NT:
            load_x(t + 2)

    def rms_b(t):
        emit_rms_b(p, hb2[t % 2], "hb%d" % (t % 2), ptb[t % 2], "ptb%d" % (t % 2), hT, t, ident)

    rms_a(0)
    rms_b(0)
    rms_a(1)

    def load_cs(t):
        p.dma("sp", cst[t % 2][:], cosd[t * 128:(t + 1) * 128, :], w=["cst%d" % (t % 2)], semkey="cst%d" % (t % 2))
        p.dma("sp", snt[t % 2][:], sind[t * 128:(t + 1) * 128, :], w=["snt%d" % (t % 2)], semkey="snt%d" % (t % 2))

    load_cs(0)
    load_cs(1)
    def emit_tr(t):
            pk = "ptb%d" % (t % 2)
            for j in range(2):
                for h in range(4):
                    p.tr(ptb[t % 2][:, (j * 4 + h) * 128:(j * 4 + h + 1) * 128], qr_[t % 2][j][:, h * 128:(h + 1) * 128], ident[:],
                         r=["qrqk%d_%d" % (j, t % 2), "ident"], w=[pk])
            sl = (t // 4) % 2
            p.cp("act", qkTs[sl][:, :, (t % 4) * 128:(t % 4 + 1) * 128], ptb[t % 2][:].rearrange("p (k t) -> p k t", t=128),
                 r=[pk], w=["qkTs%d" % sl])
            if t % 4 == 3:
                tg = t // 4
                p.dma("pool", qkT_o[:, :, tg * 512:(tg + 1) * 512].rearrange("h p t -> p h t"), qkTs[sl][:],
                      r=["qkTs%d" % sl], semkey="qkTs%d" % sl)

    wkeys = ["wb%d" % kc for kc in range(8)]
    for t in range(NT):
        for j in range(4):
            for kc in range(8):
                p.mm(psb[j][:, :], hT[:, kc, t * 128:(t + 1) * 128], wb[:, kc, j * 512:(j + 1) * 512], kc == 0, kc == 7,
                     r=["hT%d" % t, "wb%d" % kc], w=["psb%d" % j])
        for kc in range(8):
            p.mm(psb[4][:, 0:16], hT[:, kc, t * 128:(t + 1) * 128], wb[:, kc, 2048:2064], kc == 0, kc == 7,
                 r=["hT%d" % t, "wb%d" % kc], w=["psb4"])
        if t > 0:
            emit_tr(t - 1)
        if t + 1 < NT:
            rms_b(t + 1)
        p.cp("act", vb[t % 2][:], psb[2][:, :], r=["psb2"], w=["vb%d" % (t % 2)])
        p.dma("pool", v_o[t * 128:(t + 1) * 128, :], vb[t % 2][:], r=["vb%d" % (t % 2)], semkey="vb%d" % (t % 2))
        p.cp("act", zf[t % 2][:], psb[3][:, :], r=["psb3"], w=["zf%d" % (t % 2)])
        p.dma("pool", z_o[t * 128:(t + 1) * 128, :], zf[t % 2][:], r=["zf%d" % (t % 2)], semkey="zf%d" % (t % 2))
        p.cp("act", baraw[:, t, :], psb[4][:, 0:16], r=["psb4"], w=["baraw"])
        PR = t % 2
        sq, t1, t2, qr, ss16 = sq_[PR], t1_[PR], t2_[PR], qr_[PR], ss16_[PR]
        SS = "ss16_%d" % PR
        for j in range(2):
            sqj, t1j, t2j, qrj = sq[j], t1[j], t2[j], qr[j]
            kq = "qk%d_%d" % (j, PR)
            p.actf(sqj[:], psb[j][:, :], AF.Square, r=["psb%d" % j], w=["sq" + kq])
            p.red(ss16[:, j * 8:(j + 1) * 8], sqj[:].rearrange("p (g d) -> p g d", d=64), ALU.add, r=["sq" + kq], w=[SS])
        p.ts("dve", ss16[:], ss16[:], 1.0 / 64, EPS, ALU.mult, ALU.add, r=[SS], w=[SS])
        p.actf(ss16[:], ss16[:], AF.Sqrt, r=[SS], w=[SS])
        p.recip(ss16[:], ss16[:], r=[SS], w=[SS])
        for j in range(2):
            sqj, t1j, t2j, qrj = sq[j], t1[j], t2[j], qr[j]
            kq = "qk%d_%d" % (j, PR)
            s3 = sqj[:].rearrange("p (g d) -> p g d", d=64)
            a3 = t1j[:].rearrange("p (g d) -> p g d", d=64)
            b3 = t2j[:].rearrange("p (g d) -> p g d", d=64)
            q3 = qrj[:].rearrange("p (g d) -> p g d", d=64)
            p.tt("dve", s3, psb[j][:, :].rearrange("p (g d) -> p g d", d=64),
                 ss16[:, j * 8:(j + 1) * 8].unsqueeze(2).to_broadcast([128, 8, 64]), ALU.mult,
                 r=["psb%d" % j, SS], w=["sq" + kq])
            p.tt("pool", s3, s3, gq[:, j * 64:(j + 1) * 64].unsqueeze(1).to_broadcast([128, 8, 64]), ALU.mult,
                 r=["sq" + kq, "gq"], w=["sq" + kq])
            p.tt("pool", a3, s3, cst[t % 2][:].unsqueeze(1).to_broadcast([128, 8, 64]), ALU.mult,
                 r=["sq" + kq, "cst%d" % (t % 2)], w=["t1" + kq])
            p.tt("dve", b3[:, :, 0:32], s3[:, :, 32:64], snt[t % 2][:, 0:32].unsqueeze(1).to_broadcast([128, 8, 32]), ALU.mult,
                 r=["sq" + kq, "snt%d" % (t % 2)], w=["t2a" + kq])
            p.tt("pool", b3[:, :, 32:64], s3[:, :, 0:32], snt[t % 2][:, 32:64].unsqueeze(1).to_broadcast([128, 8, 32]), ALU.mult,
                 r=["sq" + kq, "snt%d" % (t % 2)], w=["t2b" + kq])
            p.tt("dve", qrj[:], t1j[:], t2j[:], ALU.add, r=["t1" + kq, "t2a" + kq, "t2b" + kq], w=["qr" + kq])
        if t + 2 < NT:
            load_cs(t + 2)
            rms_a(t + 2)
    emit_tr(NT - 1)

    braw = baraw[:, :, 0:8]
    araw = baraw[:, :, 8:16]
    p.actf(bgo[:, :, 0:8], braw, AF.Sigmoid, r=["baraw"], w=["bgo_b"])
    xg, ab, mx = bat
    p.tt("dve", xg[:], araw, dtbb[:].unsqueeze(1).to_broadcast([128, NT, 8]), ALU.add, r=["baraw", "dtbb"], w=["xg"])
    p.ts("dve", ab[:], xg[:], -1.0, None, ALU.mult, None, r=["xg"], w=["ab"])
    p.tt("dve", ab[:], ab[:], xg[:], ALU.max, r=["ab", "xg"], w=["ab"])
    p.actf(ab[:], ab[:], AF.Exp, r=["ab"], w=["ab"], scale=-1.0)
    p.ts("dve", ab[:], ab[:], 1.0, None, ALU.add, None, r=["ab"], w=["ab"])
    p.actf(ab[:], ab[:], AF.Ln, r=["ab"], w=["ab"])
    p.ts("dve", mx[:], xg[:], 0.0, None, ALU.max, None, r=["xg"], w=["mx"])
    p.tt("dve", mx[:], mx[:], ab[:], ALU.add, r=["mx", "ab"], w=["mx"])
    p.tt("dve", bgo[:, :, 8:16], mx[:], alb[:].unsqueeze(1).to_broadcast([128, NT, 8]), ALU.mult, r=["mx", "alb"], w=["bgo_g"])
    p.dma("pool", bg_o, bgo[:], r=["bgo_b", "bgo_g"], semkey="bgo")

    cnt = 0
    for grp, (a, b) in enumerate([(1536, 3072), (3600, 5648)]):
        ncols = b - a
        load_w([(a, b)], ncols)
        for tg in range(4):
            for cc in range(ncols // 128):
                bk = cnt % 4
                for kc in range(8):
                    p.mm(psb[bk][:, :], wb[:, kc, cc * 128:(cc + 1) * 128], hT[:, kc, tg * 512:(tg + 1) * 512], kc == 0, kc == 7,
                         r=["wb%d" % kc] + ["hT%d" % tt_ for tt_ in range(tg * 4, tg * 4 + 4)], w=["psb%d" % bk])
                sl = cnt % 3
                if grp == 0:
                    p.cp("act", dst[sl][:], psb[bk][:, :], r=["psb%d" % bk], w=["dst%d" % sl])
                    p.dma("pool", dnT_o[cc, :, tg * 512:(tg + 1) * 512], dst[sl][:], r=["dst%d" % sl], semkey="dst%d" % sl)
                else:
                    p.actf(gst[sl][:], psb[bk][:, :], AF.Sigmoid, r=["psb%d" % bk], w=["gst%d" % sl])
                    p.dma("pool", gT_o[cc, :, tg * 512:(tg + 1) * 512], gst[sl][:], r=["gst%d" % sl], semkey="gst%d" % sl)
                cnt += 1
    p.emit()
    return nc


_CACHE = {}


def _rope_tables():
    pos = np.arange(S, dtype=np.float32)
    inv = (1.0 / (np.float32(10000.0) ** (np.arange(0, 64, 2, dtype=np.float32) / np.float32(64)))).astype(np.float32)
    ang = (pos[:, None] * inv[None, :]).astype(np.float32)
    ang = np.concatenate([ang, ang], axis=-1)
    cos = np.cos(ang).astype(np.float32)
    sin = np.sin(ang).astype(np.float32)
    sin_signed = sin.copy()
    sin_signed[:, :32] = -sin_signed[:, :32]
    return cos, sin_signed


def _get(name, builder):
    if name not in _CACHE:
        _CACHE[name] = builder()
    return _CACHE[name]


def run_A(inp, l, xsh):
    nc = _get("A", build_A)
    cos, sins = _get("rope", _rope_tables)
    identb = np.eye(128, dtype=np.float32).astype(NPBF)
    maps = []
    for i in range(8):
        b, t0 = i // 4, (i % 4) * NTOK
        maps.append({
            "x": np.ascontiguousarray(xsh[i]),
            "cpk": np.ascontiguousarray(inp["c"][b].reshape(8, 128).T),
            "ada_w": np.ascontiguousarray(inp["ada_w"][l]),
            "ada_b": np.ascontiguousarray(inp["ada_b"][l][None, :]),
            "g1": np.ascontiguousarray(inp["norm1_g"][l][None, :]),
            "w_in": np.ascontiguousarray(inp["w_in"][l]),
            "qkg": np.ascontiguousarray(inp["qk_norm_gain"][l].reshape(1, 128)),
            "alog": np.ascontiguousarray(inp["dn_A_log"][l].reshape(1, 8)),
            "dtb": np.ascontiguousarray(inp["dn_dt_bias"][l].reshape(1, 8)),
            "cos": np.ascontiguousarray(cos[t0:t0 + NTOK]),
            "sin": np.ascontiguousarray(sins[t0:t0 + NTOK]),
            "identb": identb,
        })
    res = run_bass_kernel_spmd(nc, maps, core_ids=list(range(8)))
    return res.results


def emit_C(nc, outer, x_in, yaT, ybT, gT, cpk, ada_w, ada_b, g2, w_a, w_b, w_out, w1, w2, identd, x_out):
    x_res = outer.enter_context(nc.sbuf_tensor("x_res", [128, NT, D], F32))
    modbc = outer.enter_context(nc.sbuf_tensor("modbcC", [128, 4096], F32))
    G2 = outer.enter_context(nc.sbuf_tensor("G2", [128, D], F32))
    ident = outer.enter_context(nc.sbuf_tensor("identC", [128, 128], BF16))

    p = Prog(nc)
    stage = [p.sb("stgC%d" % i, [128, 2048], F32) for i in range(2)]
    psb = [p.ps("psC0_%d" % i, [128, 512], F32) for i in range(4)]
    p.dma("sp", ident[:], identd, w=["ident"], semkey="ident")
    p.dma("sp", G2[:], g2.partition_broadcast(128), w=["G2"], semkey="G2")
    for t in range(NT):
        p.dma("sp", x_res[:, t, :], x_in[t * 128:(t + 1) * 128, :], w=["xr%d" % t], semkey="xres")
    emit_mod(p, nc, ada_w, ada_b, cpk, 2048, 4096, stage, psb, modbc)
    p.stt("dve", G2[:], modbc[:, 2048:3072], 1.0, G2[:], ALU.add, ALU.mult, r=["modbc", "G2"], w=["G2"])
    p.emit()
    gt1 = modbc[:, 0:1024]
    sh2 = modbc[:, 1024:2048]
    gt2 = modbc[:, 3072:4096]

    p = Prog(nc)
    stg = [p.sb("stg1_%d" % i, [128, 1024], F32) for i in range(2)]
    wab = p.sb("wab", [128, 4, D], BF16)
    wbb = p.sb("wbb", [128, 4, D], BF16)
    wob = p.sb("wob", [128, 8, D], BF16)
    yas = [p.sb("yas%d" % i, [128, 4, 512], BF16) for i in range(2)]
    ybs = [p.sb("ybs%d" % i, [128, 4, 512], BF16) for i in range(2)]
    gas = [p.sb("gas%d" % i, [128, 512], BF16) for i in range(2)]
    gbs = [p.sb("gbs%d" % i, [128, 512], BF16) for i in range(2)]
    m1 = [p.sb("m1_%d" % i, [128, 512], F32) for i in range(2)]
    m2 = [p.sb("m2_%d" % i, [128, 512], F32) for i in range(2)]
    mT = [p.sb("mT%d" % i, [128, 8, 512], BF16) for i in range(2)]
    tmp = [p.sb("tmpc%d" % i, [128, 512], F32) for i in range(2)]
    psb = [p.ps("psC1_%d" % i, [128, 512], F32) for i in range(6)]
    si = 0
    for h in range(4):
        for (wsrc, wdst, nm) in ((w_a, wab, "wab"), (w_b, wbb, "wbb")):
            sl = si % 2
            si += 1
            p.dma("sp", stg[sl][:], wsrc[h * 128:(h + 1) * 128, :], w=["stg%d" % sl], semkey="stg%d" % sl)
            p.cp("act", wdst[:, h, :], stg[sl][:], r=["stg%d" % sl], w=[nm])
    for kc in range(8):
        sl = si % 2
        si += 1
        p.dma("sp", stg[sl][:], w_out[kc * 128:(kc + 1) * 128, :], w=["stg%d" % sl], semkey="stg%d" % sl)
        p.cp("act", wob[:, kc, :], stg[sl][:], r=["stg%d" % sl], w=["wob"])
    cnt = 0
    for tg in range(4):
        s2 = tg % 2
        p.dma("sp", yas[s2][:], yaT[:, :, tg * 512:(tg + 1) * 512].rearrange("h p t -> p h t"), w=["yas%d" % s2], semkey="yas%d" % s2)
        p.dma("sp", ybs[s2][:], ybT[:, :, tg * 512:(tg + 1) * 512].rearrange("h p t -> p h t"), w=["ybs%d" % s2], semkey="ybs%d" % s2)
        for fc in range(8):
            s3 = fc % 2
            p.dma("sp", gas[s3][:], gT[fc, :, tg * 512:(tg + 1) * 512], w=["gas%d" % s3], semkey="gas%d" % s3)
            p.dma("sp", gbs[s3][:], gT[8 + fc, :, tg * 512:(tg + 1) * 512], w=["gbs%d" % s3], semkey="gbs%d" % s3)
            ba, bb = (0, 1) if fc % 2 == 0 else (2, 3)
            for h in range(4):
                p.mm(psb[ba][:, :], wab[:, h, fc * 128:(fc + 1) * 128], yas[s2][:, h, :], h == 0, h == 3,
                     r=["wab", "yas%d" % s2], w=["ps%d" % ba])
            for h in range(4):
                p.mm(psb[bb][:, :], wbb[:, h, fc * 128:(fc + 1) * 128], ybs[s2][:, h, :], h == 0, h == 3,
                     r=["wbb", "ybs%d" % s2], w=["ps%d" % bb])
            p.tt("dve", m1[s3][:], psb[ba][:, :], gas[s3][:], ALU.mult, r=["ps%d" % ba, "gas%d" % s3], w=["m1_%d" % s3])
            p.tt("dve", m2[s3][:], psb[bb][:, :], gbs[s3][:], ALU.mult, r=["ps%d" % bb, "gbs%d" % s3], w=["m2_%d" % s3])
            p.tt("pool", mT[s2][:, fc, :], m1[s3][:], m2[s3][:], ALU.add, r=["m1_%d" % s3, "m2_%d" % s3], w=["mT%d_%d" % (s2, fc)])
        for i in range(4):
            t = tg * 4 + i
            for j in range(2):
                bk = 4 + (cnt % 2)
                ts_ = cnt % 2
                cnt += 1
                for kc in range(8):
                    p.mm(psb[bk][:, :], mT[s2][:, kc, i * 128:(i + 1) * 128], wob[:, kc, j * 512:(j + 1) * 512], kc == 0, kc == 7,
                         r=["mT%d_%d" % (s2, kc), "wob"], w=["ps%d" % bk])
                p.tt("dve", tmp[ts_][:], psb[bk][:, :], gt1[:, j * 512:(j + 1) * 512], ALU.mult, r=["ps%d" % bk], w=["tmp%d" % ts_])
                p.tt("pool", x_res[:, t, j * 512:(j + 1) * 512], x_res[:, t, j * 512:(j + 1) * 512], tmp[ts_][:], ALU.add,
                     r=["tmp%d" % ts_], w=["xr%d_%d" % (t, j)])
    p.emit()

    p = Prog(nc)
    h2T = p.sb("h2T", [128, 8, NTOK], BF16)
    junk = p.sb("junkC", [128, D], F32)
    ssC = p.sb("ssC", [128, 1], F32)
    hb = p.sb("hbC", [128, D], BF16)
    stg = [p.sb("stg2_%d" % i, [128, 1024], F32) for i in range(2)]
    w1b = [p.sb("w1b%d" % i, [128, 8, 512], BF16) for i in range(2)]
    w2b = [p.sb("w2b%d" % i, [128, 4, D], BF16) for i in range(2)]
    aT = p.sb("aT", [128, 4, NTOK], BF16)
    rl = [p.sb("rl%d" % i, [128, 512], F32) for i in range(2)]
    tmp = [p.sb("tmpd%d" % i, [128, 512], F32) for i in range(2)]
    psb = [p.ps("psC2_%d" % i, [128, 512], F32) for i in range(6)]
    ptb = [p.ps("ptC2_%d" % i, [128, 1024], BF16) for i in range(2)]
    si = 0

    def load_fb(fb):
        nonlocal si
        s = fb % 2
        for k2 in range(4):
            sl = si % 2
            si += 1
            p.dma("sp", stg[sl][:].rearrange("p (k c) -> p k c", k=2),
                  w1[k2 * 256:(k2 + 1) * 256, fb * 512:(fb + 1) * 512].rearrange("(k p) c -> p k c", p=128),
                  w=["stg%d" % sl], semkey="stg%d" % sl)
            p.cp("act", w1b[s][:, k2 * 2:k2 * 2 + 2, :], stg[sl][:].rearrange("p (k c) -> p k c", k=2), r=["stg%d" % sl], w=["w1b%d" % s])
        for c in range(4):
            sl = si % 2
            si += 1
            p.dma("sp", stg[sl][:], w2[fb * 512 + c * 128:fb * 512 + (c + 1) * 128, :], w=["stg%d" % sl], semkey="stg%d" % sl)
            p.cp("act", w2b[s][:, c, :], stg[sl][:], r=["stg%d" % sl], w=["w2b%d" % s])

    load_fb(0)
    hb2 = [hb, p.sb("hbC2", [128, D], BF16)]
    for t in range(NT):
        emit_rms_a(p, x_res[:, t, :], "xr%d" % t, G2[:], "G2", sh2, "modbc", junk, ssC, hb2[t % 2], "hbC%d" % (t % 2), "C")
        emit_rms_b(p, hb2[t % 2], "hbC%d" % (t % 2), ptb[t % 2], "ptb%d" % (t % 2), h2T, t, ident)
        if t == 3:
            break
    cnt = 0
    c2 = 0
    for fb in range(8):
        s = fb % 2
        if fb + 1 < 8:
            load_fb(fb + 1)
        for tg in range(4):
            for c in range(4):
                bk = cnt % 4
                rs = cnt % 2
                cnt += 1
                tn = (tg + 1) * 4 + c
                if fb == 0 and tn < NT:
                    emit_rms_a(p, x_res[:, tn, :], "xr%d" % tn, G2[:], "G2", sh2, "modbc", junk, ssC, hb2[tn % 2], "hbC%d" % (tn % 2), "C")
                for kc in range(8):
                    p.mm(psb[bk][:, :], w1b[s][:, kc, c * 128:(c + 1) * 128], h2T[:, kc, tg * 512:(tg + 1) * 512], kc == 0, kc == 7,
                         r=["w1b%d" % s] + ["hT%d" % tt_ for tt_ in range(tg * 4, tg * 4 + 4)], w=["ps%d" % bk])
                if fb == 0 and tn < NT:
                    emit_rms_b(p, hb2[tn % 2], "hbC%d" % (tn % 2), ptb[tn % 2], "ptb%d" % (tn % 2), h2T, tn, ident)
                p.actf(rl[rs][:], psb[bk][:, :], AF.Relu, r=["ps%d" % bk], w=["rl%d" % rs])
                p.actf(aT[:, c, tg * 512:(tg + 1) * 512], rl[rs][:], AF.Square, r=["rl%d" % rs], w=["aT%d_%d" % (c, tg)])
        for t in range(NT):
            for j in range(2):
                bk = 4 + (c2 % 2)
                ts_ = c2 % 2
                c2 += 1
                for c in range(4):
                    p.mm(psb[bk][:, :], aT[:, c, t * 128:(t + 1) * 128], w2b[s][:, c, j * 512:(j + 1) * 512], c == 0, c == 3,
                         r=["aT%d_%d" % (c, t // 4), "w2b%d" % s], w=["ps%d" % bk])
                p.tt("dve", tmp[ts_][:], psb[bk][:, :], gt2[:, j * 512:(j + 1) * 512], ALU.mult, r=["ps%d" % bk], w=["tmp%d" % ts_])
                p.tt("pool", x_res[:, t, j * 512:(j + 1) * 512], x_res[:, t, j * 512:(j + 1) * 512], tmp[ts_][:], ALU.add,
                     r=["tmp%d" % ts_], w=["xr%d" % t])
    for t in range(NT):
        p.dma("sp", x_out[t * 128:(t + 1) * 128, :], x_res[:, t, :], r=["xr%d" % t], semkey="xout")
    p.emit()


def build_C():
    nc = bass.Bass("TRN2", target_bir_lowering=False)
    x = din(nc, "x", [NTOK, D], F32)
    yaT = din(nc, "yaT", [4, 128, NTOK], BF16)
    ybT = din(nc, "ybT", [4, 128, NTOK], BF16)
    gT = din(nc, "gT", [16, 128, NTOK], BF16)
    cpk = din(nc, "cpk", [128, 8], F32)
    ada_w = din(nc, "ada_w", [D, 6 * D], F32)
    ada_b = din(nc, "ada_b", [1, 6 * D], F32)
    g2 = din(nc, "g2", [1, D], F32)
    w_a = din(nc, "w_a", [512, D], F32)
    w_b = din(nc, "w_b", [512, D], F32)
    w_out = din(nc, "w_out", [D, D], F32)
    w1 = din(nc, "w1", [D, 4 * D], F32)
    w2 = din(nc, "w2", [4 * D, D], F32)
    identd = din(nc, "identb", [128, 128], BF16)
    x_out = dout(nc, "x_out", [NTOK, D], F32)
    with ExitStack() as outer:
        emit_C(nc, outer, x, yaT, ybT, gT, cpk, ada_w, ada_b, g2, w_a, w_b, w_out, w1, w2, identd, x_out)
    return nc


def run_C(inp, l, xsh, yaT, ybT, gT):
    nc = _get("C", build_C)
    identb = np.eye(128, dtype=np.float32).astype(NPBF)
    maps = []
    for i in range(8):
        b = i // 4
        maps.append({
            "x": np.ascontiguousarray(xsh[i]),
            "yaT": np.ascontiguousarray(yaT[i]),
            "ybT": np.ascontiguousarray(ybT[i]),
            "gT": np.ascontiguousarray(gT[i]),
            "cpk": np.ascontiguousarray(inp["c"][b].reshape(8, 128).T),
            "ada_w": np.ascontiguousarray(inp["ada_w"][l]),
            "ada_b": np.ascontiguousarray(inp["ada_b"][l][None, :]),
            "g2": np.ascontiguousarray(inp["norm2_g"][l][None, :]),
            "w_a": np.ascontiguousarray(inp["w_branch_a"][l]),
            "w_b": np.ascontiguousarray(inp["w_branch_b"][l]),
            "w_out": np.ascontiguousarray(inp["w_out"][l]),
            "w1": np.ascontiguousarray(inp["w_mlp1"][l]),
            "w2": np.ascontiguousarray(inp["w_mlp2"][l]),
            "identb": identb,
        })
    res = run_bass_kernel_spmd(nc, maps, core_ids=list(range(8)))
    return [r["x_out"] for r in res.results]


DBG_B3 = 0
B1_DUP = 1
DBG_DUMP = 0
DBG_NG = 16
DBG_D = 0
DBG_DIR = -1
NKC = S // 128


def emit_B1(nc, qkT, vd, qkg, lqd, lamd, sublnd, identd, yaT_o):
    p = Prog(nc)
    qT = p.sb("qT", [128, S], BF16)
    kTz = [p.sb("kTz%d" % m, [128, S], BF16) for m in range(2)]
    vv = p.sb("vv", [128, NKC, 128], BF16)
    gq = p.sb("gqB", [128, 128], F32)
    gqa = p.sb("gqa", [128, 128], F32)
    mx2 = p.sb("mx2", [128, 2], F32)
    nshift = p.sb("nshift", [128, 1], F32)
    lq = p.sb("lq", [128, 4, 64], F32)
    lqp = p.sb("lqp", [128, 2, 64], F32)
    lsum = p.sb("lsum", [128, 2], F32)
    lam = p.sb("lam", [128, 1], F32)
    lamc = p.sb("lamc", [128, 2], F32)
    gsubc = p.sb("gsubc", [128, 1], F32)
    epst = p.sb("epsB1", [128, 1], F32)
    onesf = p.sb("onesfB1", [128, 128], F32)
    onesb = p.sb("onesbB1", [128, 128], BF16)
    pT = [p.sb("pT%d" % i, [128, 1024], BF16) for i in range(4)]
    racc = [[p.sb("racc%d_%d" % (m, e), [128, 1024], F32) for e in range(2)] for m in range(2)]
    r1 = p.sb("r1", [128, 512], F32)
    r2 = p.sb("r2", [128, 512], F32)
    t1 = p.sb("t1B1", [128, 512], F32)
    t2 = p.sb("t2B1", [128, 512], F32)
    sqb = p.sb("sqbB1", [128, 512], BF16)
    rs = p.sb("rsB1", [128, 512], F32)
    yst = [p.sb("yst%d" % i, [128, 512], BF16) for i in range(2)]
    sps = [p.ps("sps%d" % i, [128, 1024], F32) for i in range(2)]
    accO = [p.ps("accO%d" % m, [128, 512], F32) for m in range(2)]
    sums = [p.ps("sums%d" % m, [128, 512], F32) for m in range(2)]
    ssp = sums[0]

    p.dma("sp", gq[:], qkg.partition_broadcast(128), w=["gq"], semkey="gq")
    p.dma("sp", lq[:].rearrange("p a b -> p (a b)"), lqd.partition_broadcast(128), w=["lq"], semkey="lq")
    p.dma("sp", lamc[:], lamd.partition_broadcast(128), w=["lamc"], semkey="lamc")
    p.dma("sp", gsubc[:], sublnd.rearrange("o d -> d o"), w=["gsubc"], semkey="gsubc")
    for c4 in range(4):
        p.dma("sp", qT[:, c4 * 2048:(c4 + 1) * 2048], qkT[0, :, c4 * 2048:(c4 + 1) * 2048], w=["qT"], semkey="qT")
        p.dma("sp", kTz[0][0:64, c4 * 2048:(c4 + 1) * 2048], qkT[1, 0:64, c4 * 2048:(c4 + 1) * 2048], w=["kT0d"], semkey="kT0")
        p.dma("sp", kTz[1][64:128, c4 * 2048:(c4 + 1) * 2048], qkT[1, 64:128, c4 * 2048:(c4 + 1) * 2048], w=["kT1d"], semkey="kT1")
    p.memset("pool", kTz[0][64:128, :], 0.0, w=["kT0z"])
    p.memset("pool", kTz[1][0:64, :], 0.0, w=["kT1z"])
    for c4 in range(4):
        p.dma("sp", vv[:, c4 * 16:(c4 + 1) * 16, :],
              vd[c4 * 2048:(c4 + 1) * 2048, :].rearrange("(k p) d -> p k d", p=128), w=["vv"], semkey="vv")
    p.memset("pool", onesf[:], 1.0, w=["onesf"])
    p.memset("pool", onesb[:], 1.0, w=["onesb"])
    p.memset("pool", epst[:], EPS, w=["epst"])
    p.ts("dve", gqa[:], gq[:], -1.0, None, ALU.mult, None, r=["gq"], w=["gqa"])
    p.tt("dve", gqa[:], gqa[:], gq[:], ALU.max, r=["gqa", "gq"], w=["gqa"])
    p.op("dve", lambda e: e.tensor_reduce(mx2[:], gqa[:].rearrange("p (a b) -> p a b", a=2), AX.X, ALU.max), r=["gqa"], w=["mx2"])
    p.stt("dve", nshift[:], mx2[:, 0:1], -8.0, mx2[:, 1:2], ALU.mult, ALU.mult, r=["mx2"], w=["nshift"])
    p.tt("dve", lqp[:, 0, :], lq[:, 0, :], lq[:, 1, :], ALU.mult, r=["lq"], w=["lqp0"])
    p.tt("dve", lqp[:, 1, :], lq[:, 2, :], lq[:, 3, :], ALU.mult, r=["lq"], w=["lqp1"])
    p.red(lsum[:], lqp[:], ALU.add, r=["lqp0", "lqp1"], w=["lsum"])
    p.actf(lsum[:], lsum[:], AF.Exp, r=["lsum"], w=["lsum"])
    p.tt("dve", lam[:], lsum[:, 0:1], lsum[:, 1:2], ALU.subtract, r=["lsum"], w=["lam"])
    p.tt("dve", lam[:], lam[:], lamc[:, 0:1], ALU.add, r=["lam", "lamc"], w=["lam"])
    p.ts("dve", lam[:], lam[:], -1.0, None, ALU.mult, None, r=["lam"], w=["lam"])
    p.ts("dve", gsubc[:], gsubc[:], lamc[:, 1:2], None, ALU.mult, None, r=["gsubc", "lamc"], w=["gsubc"])

    pair = 0
    NP_ = NKC // 2
    for qg in range(16):
        for m in range(2):
            def s_pair(pi, pr):
                b = pr % 2
                for hf in range(2):
                    kc = pi * 2 + hf
                    p.mm(sps[b][:, hf * 512:(hf + 1) * 512], kTz[m][:, kc * 128:(kc + 1) * 128],
                         qT[:, qg * 512:(qg + 1) * 512], True, True, r=["qT", "kT%dd" % m, "kT%dz" % m], w=["sps%d" % b])
            s_pair(0, pair)
            for pi in range(NP_):
                b = pair % 2
                sl = pair % 4
                if pi + 1 < NP_:
                    s_pair(pi + 1, pair + 1)
                p.actf(pT[sl][:], sps[b][:, :], AF.Exp, r=["sps%d" % b, "nshift"], w=["pT%d" % sl], bias=nshift[:, 0:1], scale=0.125)
                for hf in range(2):
                    kc = pi * 2 + hf
                    p.mm(accO[m][:, :], vv[:, kc, :], pT[sl][:, hf * 512:(hf + 1) * 512], kc == 0, kc == NKC - 1,
                         r=["pT%d" % sl, "vv"], w=["accO%d" % m])
                    p.mm(sums[m][:, :], onesb[:], pT[sl][:, hf * 512:(hf + 1) * 512], kc == 0, kc == NKC - 1,
                         r=["pT%d" % sl, "onesb"], w=["sums%d" % m])
                pair += 1
        p.recip(r1[:], sums[0][:, :], r=["sums0"], w=["r1"])
        p.recip(r2[:], sums[1][:, :], r=["sums1"], w=["r2"])
        p.ts("pool", r2[:], r2[:], lam[:, 0:1], None, ALU.mult, None, r=["r2", "lam"], w=["r2"])
        p.tt("dve", t1[:], accO[0][:, :], r1[:], ALU.mult, r=["accO0", "r1"], w=["t1"])
        p.tt("dve", t2[:], accO[1][:, :], r2[:], ALU.mult, r=["accO1", "r2"], w=["t2"])
        p.tt("pool", t1[:], t1[:], t2[:], ALU.add, r=["t1", "t2"], w=["t1"])
        p.tt("pool", sqb[:], t1[:], t1[:], ALU.mult, r=["t1"], w=["sqb"])
        p.mm(ssp[:, :], onesb[:], sqb[:], True, True, r=["onesb", "sqb"], w=["sums0"])
        p.actf(rs[:], ssp[:, :], AF.Sqrt, r=["sums0", "epst"], w=["rs"], bias=epst[:, 0:1], scale=1.0 / 128)
        p.recip(rs[:], rs[:], r=["rs"], w=["rs"])
        ys = qg % 2
        p.stt("dve", yst[ys][:], t1[:], gsubc[:, 0:1], rs[:], ALU.mult, ALU.mult, r=["t1", "gsubc", "rs"], w=["yst%d" % ys])
        p.dma("sp", yaT_o[:, qg * 512:(qg + 1) * 512], yst[ys][:], r=["yst%d" % ys], semkey="yst%d" % ys)
    p.emit()


def bc(ap, shape, axis):
    return ap.unsqueeze(axis).to_broadcast(list(shape))


def emit_B2(nc, pers, dnT, cwd, identd):
    qTn, kTn, ktok, vtok = pers
    p = Prog(nc)
    ident = p.sb("identB2", [128, 128], BF16)
    onesb = p.sb("onesb", [128, 128], BF16)
    epst = p.sb("epst", [128, 1], F32)
    cw = p.sb("cw", [128, 3, 5], F32)
    xin = [p.sb("xin%d" % i, [128, 2052], F32) for i in range(2)]
    acc = [p.sb("cacc%d" % i, [128, 2048], F32) for i in range(2)]
    sqb = p.sb("sqb", [128, 2048], BF16)
    rin_ = [p.sb("rin%d" % i, [128, 2048], F32) for i in range(2)]
    vtb = p.sb("vtb", [128, 2048], BF16)
    ps = [p.ps("psB2_%d" % i, [128, 512], F32) for i in range(4)]
    ptb = [p.ps("ptB2_%d" % i, [128, 1024], BF16) for i in range(2)]
    p.dma("sp", ident[:], identd, w=["ident"], semkey="ident")
    p.dma("sp", cw[:], cwd, w=["cw"], semkey="cw")
    p.memset("pool", onesb[:], 1.0, w=["onesb"])
    p.memset("pool", epst[:], EPS, w=["epst"])
    tcnt = [0]

    def part1(it):
        tg, X = it // 3, it % 3
        sl = it % 2
        xk, ak = "xin%d" % sl, "cacc%d" % sl
        lo = tg * 2048 - 2
        hi = tg * 2048 + 2050
        if tg == 0:
            p.memset("pool", xin[sl][:, 0:2], 0.0, w=[xk])
            p.dma("sp", xin[sl][:, 2:2052], dnT[X, :, 0:2050], w=[xk + "d"], r=[xk], semkey=xk)
        elif tg == 3:
            p.memset("pool", xin[sl][:, 2050:2052], 0.0, w=[xk])
            p.dma("sp", xin[sl][:, 0:2050], dnT[X, :, lo:S], w=[xk + "d"], r=[xk], semkey=xk)
        else:
            p.memset("pool", xin[sl][:, 0:1], 0.0, w=[xk])
            p.dma("sp", xin[sl][:, :], dnT[X, :, lo:hi], w=[xk + "d"], r=[xk], semkey=xk)
        a = acc[sl]
        p.ts("dve", a[:], xin[sl][:, 0:2048], cw[:, X, 0:1], None, ALU.mult, None, r=[xk + "d", "cw"], w=[ak])
        for j in range(1, 5):
            p.stt("dve", a[:], xin[sl][:, j:j + 2048], cw[:, X, j:j + 1], a[:], ALU.mult, ALU.add, r=[xk + "d", "cw", ak], w=[ak])
        p.actf(a[:], a[:], AF.Silu, r=[ak, xk + "d"], w=[ak, xk])
        if X != 2:
            rin = rin_[sl]
            p.tt("pool", sqb[:], a[:], a[:], ALU.mult, r=[ak], w=["sqb"])
            for blk in range(4):
                p.mm(ps[blk][:, :], onesb[:], sqb[:, blk * 512:(blk + 1) * 512], True, True, r=["onesb", "sqb"], w=["psn%d" % blk])
                p.actf(rin[:, blk * 512:(blk + 1) * 512], ps[blk][:, :], AF.Sqrt, r=["psn%d" % blk, "epst"], w=["rin%d" % sl],
                       bias=epst[:, 0:1])

    def part2(it):
        tg, X = it // 3, it % 3
        sl = it % 2
        ak = "cacc%d" % sl
        a = acc[sl]
        if X == 2:
            p.cp("pool", vtb[:], a[:], r=[ak], w=["vtb"])
            dst, skey = vtok, "vtb"
        else:
            rin = rin_[sl]
            p.recip(rin[:], rin[:], r=["rin%d" % sl], w=["rin%d" % sl])
            tgt = qTn if X == 0 else kTn
            sc = (128.0 ** -0.5) if X == 0 else 1.0
            p.stt("dve", tgt[:, tg * 2048:(tg + 1) * 2048], a[:], sc, rin[:], ALU.mult, ALU.mult, r=[ak, "rin%d" % sl],
                  w=["nT%d_%d" % (X, tg)])
            dst, skey = ktok, "nT1_%d" % tg
        if X >= 1:
            for half in range(2):
                pb = tcnt[0] % 2
                tcnt[0] += 1
                for c in range(8):
                    cc = half * 8 + c
                    if X == 2:
                        src = vtb[:, cc * 128:(cc + 1) * 128]
                    else:
                        src = kTn[:, tg * 2048 + cc * 128:tg * 2048 + (cc + 1) * 128]
                    p.tr(ptb[pb][:, c * 128:(c + 1) * 128], src, ident[:], r=[skey, "ident"], w=["ptb%d" % pb])
                n0 = tg * 16 + half * 8
                p.cp("act", dst[:, n0:n0 + 8, :], ptb[pb][:].rearrange("p (k t) -> p k t", t=128), r=["ptb%d" % pb],
                     w=["tok%d_%d" % (X, n0)])

    NIT = 12
    part1(0)
    for it in range(NIT):
        if it + 1 < NIT:
            part1(it + 1)
        part2(it)
    p.emit()


def emit_B3(nc, pers, o_scr, bgd, masksd, identd):
    qTn, kTn, ktok, vtok = pers
    p = Prog(nc)
    identb = p.sb("identB3", [128, 128], BF16)
    identf = p.sb("identfB3", [128, 128], F32)
    msk = p.sb("msk", [128, 8, 128], F32)
    onesf = p.sb("onesf", [128, 128], F32)
    bg = p.sb("bgt", [128, NKC, 4], F32)
    negb = p.sb("negb", [128, NKC, 2], F32)
    EG = [p.sb("EG%d" % d, [128, NKC], F32) for d in range(2)]
    ER = [p.sb("ER%d" % d, [128, NKC], F32) for d in range(2)]
    EL = [p.sb("EL%d" % d, [128, NKC], F32) for d in range(2)]
    Sst = [p.sb("Sst%d" % d, [128, 128], F32) for d in range(2)]
    vnew = [[p.sb("vnew%d_%d" % (d, i), [128, 128], F32) for i in range(2)] for d in range(2)]
    ost = [[p.sb("ost%d_%d" % (d, i), [128, 128], F32) for i in range(2)] for d in range(2)]
    grp = {}
    for d in range(2):
        for s_ in range(2):
            for nm in ("u", "wT", "qdT", "qkT", "kdec"):
                grp[(nm, d, s_)] = p.sb("%s%d_%d" % (nm, d, s_), [128, 4, 128], F32)
    T3 = {}
    for d in range(2):
        for nm in ("gtri", "DTr", "EGr", "tmpf", "X", "XT", "A", "AT", "P", "PT"):
            T3[(nm, d)] = p.sb("%s_%d" % (nm, d), [128, 4, 128], F32)
        for nm in ("kgb", "Pb16"):
            T3[(nm, d)] = p.sb("%s_%d" % (nm, d), [128, 4, 128], BF16)
    psbD = [[p.ps("psB3_%d_%d" % (d, i), [128, 512], F32) for i in range(3)] for d in range(2)]
    psb = psbD[0]
    pssA = [p.ps("psscA%d" % d, [128, 512], F32) for d in range(2)]

    p.dma("sp", identb[:], identd, w=["identb_"], semkey="ident")
    p.cp("dve", identf[:], identb[:], r=["identb_"], w=["ident"])
    p.dma("sp", msk[:], masksd.rearrange("a p f -> p a f"), w=["msk"], semkey="msk")
    p.dma("sp", bg[:], bgd, w=["bg"], semkey="bg")
    p.memset("pool", onesf[:], 1.0, w=["onesf"])
    for d in range(2):
        p.memset("pool", Sst[d][:], 0.0, w=["S%d" % d])
    p.ts("dve", negb[:], bg[:, :, 0:2], -1.0, None, ALU.mult, None, r=["bg"], w=["negb"])
    for d in range(2):
        g = bg[:, :, 2 + d]
        mGC = msk[:, 0, :] if d == 0 else msk[:, 2, :]
        mGR = msk[:, 3, :] if d == 0 else msk[:, 1, :]
        for i, (lhs, dst, nm) in enumerate(((mGC, EG[d], "EG"), (mGR, ER[d], "ER"), (onesf[:], EL[d], "EL"))):
            p.mm(psbD[d][i][:, 0:NKC], lhs, g, True, True, r=["msk", "onesf", "bg"], w=["ps%d_%d" % (i, d)])
            p.actf(dst[:], psbD[d][i][:, 0:NKC], AF.Exp, r=["ps%d_%d" % (i, d)], w=["%s%d" % (nm, d)])

    sh3 = [128, 4, 128]

    def fl(t):
        return t[:].rearrange("p a b -> p (a b)")

    def d3_stages(d, G):
        s_ = G % 2
        n0 = 4 * G if d == 0 else 60 - 4 * G
        MI = msk[:, 0, :] if d == 0 else msk[:, 2, :]
        MS = msk[:, 1, :] if d == 0 else msk[:, 3, :]
        MG = msk[:, 3, :] if d == 0 else msk[:, 1, :]
        u, wT, qdT, qkT, kdec = [grp[(nm, d, s_)] for nm in ("u", "wT", "qdT", "qkT", "kdec")]
        gk = lambda nm: "%s%d_%d" % (nm, d, s_)
        D = "_%d" % d
        psb = psbD[d]
        ptb = psbD[d][1]
        gtri, DTr, EGr, tmpf, X, XT, A, AT, P_, PT, kgb, Pb16 = [T3[(nm, d)] for nm in
            ("gtri", "DTr", "EGr", "tmpf", "X", "XT", "A", "AT", "P", "PT", "kgb", "Pb16")]
        Y, Xc, XcT, Y2 = tmpf, DTr, EGr, gtri
        kY, kXc, kXcT, kY2 = "tmpf" + D, "DTr" + D, "EGr" + D, "gtri" + D
        kX, kXT, kA, kAT, kP, kPT = ["%s%s" % (nm, D) for nm in ("X", "XT", "A", "AT", "P", "PT")]
        st = []

        def mm4(bank, bkey, lhs, rhs, r):
            for j in range(4):
                p.mm(psb[bank][:, j * 128:(j + 1) * 128], lhs[:, j, :], rhs[:, j, :], True, True, r=r, w=[bkey])

        def s1():
            p.tt("pool", gtri[:], bc(MI, sh3, 1), bc(bg[:, n0:n0 + 4, 2 + d], sh3, 2), ALU.mult, r=["msk", "bg"], w=[kY2])
            p.mm(psb[0][:, :], MG, fl(gtri), True, True, r=["msk", kY2], w=["ps0" + D])
            p.mm(psb[1][:, :], onesf[:], fl(gtri), True, True, r=["onesf", kY2], w=["ps1" + D])
            p.actf(fl(DTr), psb[0][:, :], AF.Exp, r=["ps0" + D], w=[kXc])
            p.actf(fl(EGr), psb[1][:, :], AF.Exp, r=["ps1" + D], w=[kXcT])
        st.append(s1)

        def s2():
            for j in range(4):
                c0 = (n0 + j) * 128
                p.mm(psb[2][:, j * 128:(j + 1) * 128], kTn[:, c0:c0 + 128], kTn[:, c0:c0 + 128], True, True, r=[], w=["ps2" + D])
            for j in range(4):
                c0 = (n0 + j) * 128
                p.mm(psb[0][:, j * 128:(j + 1) * 128], kTn[:, c0:c0 + 128], qTn[:, c0:c0 + 128], True, True, r=[], w=["ps0" + D])
            ps2v = psb[2][:, :].rearrange("p (a b) -> p a b", a=4)
            ps0v = psb[0][:, :].rearrange("p (a b) -> p a b", a=4)
            p.tt("dve", tmpf[:], ps2v, DTr[:], ALU.mult, r=["ps2" + D, kXc], w=[kY])
            p.tt("dve", qkT[:], ps0v, DTr[:], ALU.mult, r=["ps0" + D, kXc], w=[gk("qkT")])
            p.tt("pool", tmpf[:], tmpf[:], bc(negb[:, n0:n0 + 4, d], sh3, 2), ALU.mult, r=[kY, "negb"], w=[kY])
            p.tt("pool", X[:], tmpf[:], bc(MS, sh3, 1), ALU.mult, r=[kY, "msk"], w=[kX])
            p.tt("pool", qkT[:], qkT[:], bc(MI, sh3, 1), ALU.mult, r=[gk("qkT"), "msk"], w=[gk("qkT")])
            p.tt("pool", fl(qdT), qTn[:, n0 * 128:(n0 + 4) * 128], fl(EGr), ALU.mult, r=[kXcT], w=[gk("qdT")])
            p.tt("pool", kgb[:], ktok[:, n0:n0 + 4, :], bc(EG[d][:, n0:n0 + 4], sh3, 2), ALU.mult, r=["EG%d" % d], w=["kgb" + D])
            p.tt("pool", kdec[:], ktok[:, n0:n0 + 4, :], bc(ER[d][:, n0:n0 + 4], sh3, 2), ALU.mult, r=["ER%d" % d], w=[gk("kdec")])
            for j in range(4):
                p.tr(ptb[:, j * 128:(j + 1) * 128], X[:, j, :], identf[:], r=[kX, "ident"], w=["ps1" + D])
            p.cp("act", fl(XT), ptb[:, :], r=["ps1" + D], w=[kXT])
        st.append(s2)

        def s3():
            BD = bc(msk[:, 4, :], sh3, 1)
            p.tt("pool", A[:], X[:], BD, ALU.mult, r=[kX, "msk"], w=[kA])
            p.tt("pool", AT[:], XT[:], BD, ALU.mult, r=[kXT, "msk"], w=[kAT])
            p.tt("pool", P_[:], A[:], bc(identf[:], sh3, 1), ALU.add, r=[kA, "ident"], w=[kP])
            p.tt("pool", PT[:], AT[:], bc(identf[:], sh3, 1), ALU.add, r=[kAT, "ident"], w=[kPT])
        st.append(s3)

        def base(k):
            def f():
                mm4(0, "ps0" + D, AT, A, [kA, kAT])
                mm4(1, "ps1" + D, A, AT, [kA, kAT])
                p.cp("dve", fl(A), psb[0][:, :], r=["ps0" + D], w=[kA])
                p.cp("act", fl(AT), psb[1][:, :], r=["ps1" + D], w=[kAT])
                mm4(2, "ps2" + D, AT, P_, [kAT, kP])
                mm4(0, "ps0" + D, A, PT, [kA, kPT])
                p.tt("dve", fl(P_), fl(P_), psb[2][:, :], ALU.add, r=[kP, "ps2" + D], w=[kP])
                p.tt("dve", fl(PT), fl(PT), psb[0][:, :], ALU.add, r=[kPT, "ps0" + D], w=[kPT])
            return f
        for k in range(1, 4):
            st.append(base(k))

        def merge(mi):
            def f():
                last = (mi == 2)
                CM = bc(msk[:, 5 + mi, :], sh3, 1)
                p.tt("pool", Xc[:], X[:], CM, ALU.mult, r=[kX, "msk"], w=[kXc])
                p.tt("pool", XcT[:], XT[:], CM, ALU.mult, r=[kXT, "msk"], w=[kXcT])
                mm4(0, "ps0" + D, XcT, P_, [kXcT, kP])
                p.cp("act", fl(Y), psb[0][:, :], r=["ps0" + D], w=[kY])
                if not last:
                    mm4(1, "ps1" + D, Xc, PT, [kXc, kPT])
                    p.cp("dve", fl(Y2), psb[1][:, :], r=["ps1" + D], w=[kY2])
                mm4(2, "ps2" + D, PT, Y, [kPT, kY])
                if not last:
                    mm4(0, "ps0" + D, P_, Y2, [kP, kY2])
                p.tt("dve", fl(P_), fl(P_), psb[2][:, :], ALU.add, r=[kP, "ps2" + D], w=[kP])
                if not last:
                    p.tt("dve", fl(PT), fl(PT), psb[0][:, :], ALU.add, r=[kPT, "ps0" + D], w=[kPT])
            return f
        for mi in range(3):
            st.append(merge(mi))

        def s_fin():
            p.cp("pool", Pb16[:], P_[:], r=[kP], w=["Pb16" + D])
            for j in range(4):
                p.mm(psb[2][:, j * 128:(j + 1) * 128], Pb16[:, j, :], vtok[:, n0 + j, :], True, True, r=["Pb16" + D], w=["ps2" + D])
                p.mm(psb[1][:, j * 128:(j + 1) * 128], kgb[:, j, :], Pb16[:, j, :], True, True, r=["Pb16" + D, "kgb" + D], w=["ps1" + D])
            p.tt("dve", u[:], psb[2][:, :].rearrange("p (a b) -> p a b", a=4), bc(bg[:, n0:n0 + 4, d], sh3, 2), ALU.mult,
                 r=["ps2" + D, "bg"], w=[gk("u")])
            p.cp("act", fl(wT), psb[1][:, :], r=["ps1" + D], w=[gk("wT")])
        st.append(s_fin)
        return st

    vcnt = [0, 0]

    def scan_step(d, G, j):
        s_ = G % 2
        n0 = 4 * G if d == 0 else 60 - 4 * G
        n = n0 + j
        u, wT, qdT, qkT, kdec = [grp[(nm, d, s_)] for nm in ("u", "wT", "qdT", "qkT", "kdec")]
        gk = lambda nm: "%s%d_%d" % (nm, d, s_)
        vs = vcnt[d] % 2
        vcnt[d] += 1
        vn = vnew[d][vs]
        vk = "vnew%d_%d" % (d, vs)
        os_ = ost[d][vs]
        ok = "ost%d_%d" % (d, vs)
        ps1, ps3, ps2 = pssA[d][:, 0:128], pssA[d][:, 128:256], pssA[d][:, 256:384]
        kS = "S%d" % d
        kA = "scA%d" % d
        p.mm(ps1, wT[:, j, :], Sst[d][:], True, True, r=[gk("wT"), kS], w=[kA])
        p.stt("dve", vn[:], ps1, negb[:, n, d:d + 1], u[:, j, :], ALU.mult, ALU.add, r=[kA, "negb", gk("u")], w=[vk])
        p.mm(ps3, kdec[:, j, :], vn[:], True, True, r=[gk("kdec"), vk], w=[kA])
        p.mm(ps2, qdT[:, j, :], Sst[d][:], True, False, r=[gk("qdT"), kS], w=[kA], skip=True)
        p.mm(ps2, qkT[:, j, :], vn[:], False, True, r=[gk("qkT"), vk], w=[kA], skip=True)
        p.stt("dve", Sst[d][:], Sst[d][:], EL[d][:, n:n + 1], ps3, ALU.mult, ALU.add, r=[kS, "EL%d" % d, kA], w=[kS])
        p.cp("dve", os_[:], ps2, r=[kA], w=[ok])
        p.dma("sp", o_scr[d, n, :, :], os_[:], r=[ok], semkey=ok)

    NG = DBG_NG
    lists = [d3_stages(0, 0), d3_stages(1, 0)]
    while lists[0] or lists[1]:
        for L in lists:
            if L:
                L.pop(0)()
    for G in range(NG):
        nxt = [d3_stages(0, G + 1), d3_stages(1, G + 1)] if G + 1 < NG else [[], []]
        steps = []
        for j in range(4):
            steps.append((0, G, j))
            steps.append((1, G, 3 - j))
        while steps or nxt[0] or nxt[1]:
            if nxt[0]:
                nxt[0].pop(0)()
            if steps:
                scan_step(*steps.pop(0))
            if nxt[1]:
                nxt[1].pop(0)()
            if steps:
                scan_step(*steps.pop(0))
    p.emit()


def emit_B4(nc, o_scr, zd, ogd, identd, ybT_o):
    p = Prog(nc)
    ident = p.sb("identB4", [128, 128], BF16)
    og = p.sb("og", [128, 128], F32)
    zt = p.sb("zt", [128, NKC, 128], F32)
    sq_ = [p.sb("sqB4_%d" % i, [128, 4, 128], F32) for i in range(2)]
    of_ = [p.sb("ofB4_%d" % i, [128, 4, 128], F32) for i in range(2)]
    ob_ = [p.sb("obB4_%d" % i, [128, 4, 128], F32) for i in range(2)]
    ss_ = [p.sb("ssB4_%d" % i, [128, 4], F32) for i in range(2)]
    yb_ = [p.sb("ybB4_%d" % i, [128, 4, 128], BF16) for i in range(2)]
    yst = [p.sb("ystB4_%d" % i, [128, 512], BF16) for i in range(2)]
    ptb = p.ps("ptB4", [128, 1024], BF16)
    p.dma("sp", ident[:], identd, w=["ident"], semkey="ident")
    p.dma("sp", og[:], ogd.partition_broadcast(128), w=["og"], semkey="og")
    for c4 in range(4):
        p.dma("sp", zt[:, c4 * 16:(c4 + 1) * 16, :], zd[c4 * 2048:(c4 + 1) * 2048, :].rearrange("(k p) d -> p k d", p=128),
              w=["zt%d" % c4], semkey="zt%d" % c4)
    for c4 in range(4):
        v = zt[:, c4 * 16:(c4 + 1) * 16, :].rearrange("p a b -> p (a b)")
        p.actf(v, v, AF.Silu, r=["zt%d" % c4], w=["zt%d" % c4])
    for G in range(16):
        n0 = 4 * G
        gs = G % 2
        sq, ss, yb = sq_[gs], ss_[gs], yb_[gs]
        KQ, KS, KY = "sq%d" % gs, "ss%d" % gs, "yb%d" % gs
        p.dma("sp", of_[gs][:], o_scr[0, n0:n0 + 4, :, :].rearrange("n p e -> p n e"), w=["of%d" % gs], semkey="of%d" % gs)
        p.dma("sp", ob_[gs][:], o_scr[1, n0:n0 + 4, :, :].rearrange("n p e -> p n e"), w=["ob%d" % gs], semkey="ob%d" % gs)
        p.tt("pool", of_[gs][:], of_[gs][:], ob_[gs][:], ALU.add, r=["of%d" % gs, "ob%d" % gs], w=["of%d" % gs])
        o = of_[gs][:]
        p.tt("pool", sq[:], o, o, ALU.mult, r=["of%d" % gs], w=[KQ])
        p.red(ss[:], sq[:], ALU.add, r=[KQ], w=[KS])
        p.ts("dve", ss[:], ss[:], 1.0 / 128, EPS, ALU.mult, ALU.add, r=[KS], w=[KS])
        p.actf(ss[:], ss[:], AF.Sqrt, r=[KS], w=[KS])
        p.recip(ss[:], ss[:], r=[KS], w=[KS])
        p.tt("dve", sq[:], o, bc(ss[:], [128, 4, 128], 2), ALU.mult, r=[KS, KQ, "of%d" % gs], w=[KQ])
        p.tt("pool", sq[:], sq[:], bc(og[:], [128, 4, 128], 1), ALU.mult, r=[KQ, "og"], w=[KQ])
        p.tt("pool", yb[:], sq[:], zt[:, n0:n0 + 4, :], ALU.mult, r=[KQ, "zt%d" % (G // 4)], w=[KY])
        for j in range(4):
            p.tr(ptb[:, j * 128:(j + 1) * 128], yb[:, j, :], ident[:], r=[KY, "ident"], w=["ptb"])
        ys = G % 2
        p.cp("dve", yst[ys][:], ptb[:, 0:512], r=["ptb"], w=["yst%d" % ys])
        p.dma("sp", ybT_o[:, n0 * 128:(n0 + 4) * 128], yst[ys][:], r=["yst%d" % ys], semkey="ystB4_%d" % ys)
    p.emit()


def build_B(parts=("B1", "B2", "B3", "B4")):
    nc = bass.Bass("TRN2", target_bir_lowering=False)
    qkT = din(nc, "qkT", [2, 128, S], BF16)
    vd = din(nc, "v", [S, 128], BF16)
    qkg = din(nc, "qkg", [1, 128], F32)
    lqd = din(nc, "lq_d", [1, 256], F32)
    lamd = din(nc, "lamc_d", [1, 2], F32)
    sub = din(nc, "subln", [1, 128], F32)
    identd = din(nc, "identb", [128, 128], BF16)
    dnT = din(nc, "dnT", [3, 128, S], F32)
    cwd = din(nc, "cw_d", [128, 3, 5], F32)
    bgd = din(nc, "bg_d", [128, NKC, 4], F32)
    masksd = din(nc, "masks", [8, 128, 128], F32)
    zd = din(nc, "z", [S, 128], F32)
    ogd = din(nc, "og_d", [1, 128], F32)
    yaT = dout(nc, "yaT", [128, S], BF16)
    ybT = dout(nc, "ybT", [128, S], BF16)
    if "B1" in parts:
        emit_B1(nc, qkT, vd, qkg, lqd, lamd, sub, identd, yaT)
    with ExitStack() as outer:
        pers = [outer.enter_context(nc.sbuf_tensor(nm, shp, BF16)) for nm, shp in
                (("qTn", [128, S]), ("kTn", [128, S]), ("ktok", [128, NKC, 128]), ("vtok", [128, NKC, 128]))]
        o_scr = nc.dram_tensor("o_scr", [2, NKC, 128, 128], F32).ap()
        if "B2" in parts:
            emit_B2(nc, pers, dnT, cwd, identd)
        if "B3" in parts:
            emit_B3(nc, pers, o_scr, bgd, masksd, identd)
        if DBG_DUMP:
            p = Prog(nc)
            for i, (nm, shp, dt_, t_) in enumerate((("dq", [128, S], BF16, pers[0]), ("dk", [128, S], BF16, pers[1]),
                                            ("dkt", [128, NKC, 128], BF16, pers[2]), ("dvt", [128, NKC, 128], BF16, pers[3]))):
                o_ = dout(nc, nm, shp, dt_)
                p.dma("sp", o_, t_[:], semkey="dump%d" % i)
            p.emit()
        if "B4" in parts:
            emit_B4(nc, o_scr, zd, ogd, identd, ybT)
    return nc


def _masks():
    i = np.arange(128)
    pp, ff = i[:, None], i[None, :]
    bd = lambda b_: (pp // b_) == (ff // b_)
    return np.stack([pp <= ff, pp < ff, pp >= ff, pp > ff, bd(16), bd(32) & ~bd(16), bd(64) & ~bd(32), ~bd(64)]).astype(np.float32)


def run_B(inp, l, resA):
    nc = _get("B", build_B)
    identb = np.eye(128, dtype=np.float32).astype(NPBF)
    masks = _masks()
    li = 0.8 - 0.6 * math.exp(-0.3 * l)
    lamc = np.array([[li, 1.0 - li]], np.float32)
    cw = inp["dn_conv"][l]
    maps = []
    for b in range(NB):
        rs = resA[b * 4:(b + 1) * 4]
        bgc = np.concatenate([r["bg"] for r in rs], axis=1)
        for h in range(4):
            qT = np.concatenate([r["qkT"][h] for r in rs], axis=1)
            kT = np.concatenate([r["qkT"][4 + h] for r in rs], axis=1)
            maps.append({
                "qkT": np.ascontiguousarray(np.stack([qT, kT])),
                "v": np.ascontiguousarray(np.concatenate([r["v"][:, h * 128:(h + 1) * 128] for r in rs], axis=0)),
                "qkg": np.ascontiguousarray(inp["qk_norm_gain"][l].reshape(1, 128)),
                "lq_d": np.ascontiguousarray(inp["diff_lambda"][l].reshape(1, 256)),
                "lamc_d": lamc,
                "subln": np.ascontiguousarray(inp["diff_subln_gain"][l][None, :]),
                "identb": identb,
                "dnT": np.ascontiguousarray(np.stack([np.concatenate([r["dnT"][X * 4 + h] for r in rs], axis=1) for X in range(3)])),
                "cw_d": np.ascontiguousarray(np.stack([cw[:, X * 512 + h * 128:X * 512 + (h + 1) * 128].T for X in range(3)], axis=1)),
                "bg_d": np.ascontiguousarray(bgc[:, :, [h, 4 + h, 8 + h, 12 + h]]),
                "masks": masks,
                "z": np.ascontiguousarray(np.concatenate([r["z"][:, h * 128:(h + 1) * 128] for r in rs], axis=0)),
                "og_d": np.ascontiguousarray(inp["dn_out_gain"][l][None, :]),
            })
    res = run_bass_kernel_spmd(nc, maps, core_ids=list(range(8)))
    return res.results


def kernel(**inp):
    inp = {k: np.asarray(v) for k, v in inp.items()}
    x = inp["x"]
    xsh = [x[i // 4, (i % 4) * NTOK:(i % 4 + 1) * NTOK] for i in range(8)]
    for l in range(2):
        resA = run_A(inp, l, xsh)
        resB = run_B(inp, l, resA)
        yaT, ybT, gT = [], [], []
        for i in range(8):
            b, t0 = i // 4, (i % 4) * NTOK
            yaT.append(np.stack([resB[b * 4 + h]["yaT"][:, t0:t0 + NTOK] for h in range(4)]))
            ybT.append(np.stack([resB[b * 4 + h]["ybT"][:, t0:t0 + NTOK] for h in range(4)]))
            gT.append(resA[i]["gT"])
        xsh = run_C(inp, l, xsh, yaT, ybT, gT)
    out = np.empty((NB, S, D), np.float32)
    for i in range(8):
        out[i // 4, (i % 4) * NTOK:(i % 4 + 1) * NTOK] = xsh[i]
    return out
```
